# Optimizing a Trainium2 kernel written in Bass

```python
import math
import jax, jax.numpy as jnp
from jax import lax
import numpy as np

D_MODEL = 1024
BATCH = 4
SEQ = 4096
DEPTH = 2

GRID_W = 64
CTX_LEN = 256
HEAD_DIM = 64
D_FF = 2816
N_MOD = 9
RMS_EPS = 1e-6
RWKV_WIDTH = D_MODEL // 2
RWKV_HEADS = RWKV_WIDTH // HEAD_DIM
RWKV_DECAY_LORA = 64
RWKV_ICLR_LORA = 64
RWKV_GATE_LORA = 128
RWKV_PROJ = 3 * RWKV_WIDTH + RWKV_DECAY_LORA + RWKV_ICLR_LORA + RWKV_GATE_LORA
GN_EPS = 64e-5
S5_WIDTH = D_MODEL - RWKV_WIDTH
S5_GROUP_CH = 16
S5_GROUPS = S5_WIDTH // S5_GROUP_CH
S5_STATE = 64
AB_PROJ = RWKV_PROJ + S5_WIDTH
ATTN_HEADS = D_MODEL // HEAD_DIM
ATTN_KV_HEADS = 4
ATTN_GROUP = ATTN_HEADS // ATTN_KV_HEADS
ATTN_WINDOW = 128
ATTN_BLOCK = 128
ATTN_Q_W = ATTN_HEADS * HEAD_DIM
ATTN_KV_W = ATTN_KV_HEADS * HEAD_DIM
C_PROJ = ATTN_Q_W + 2 * ATTN_KV_W
ROPE_AXIS_DIM = HEAD_DIM // 2
ROPE_BASE = 10000.0
NEG_INF = -1e30

kernel_name = "hybrid_rwkv7_s5_swa_macaron_dit"


def _rmsnorm(h, g):
    h32 = h.astype(jnp.float32)
    h32 = h32 * lax.rsqrt(jnp.mean(h32 * h32, axis=-1, keepdims=True) + RMS_EPS)
    return (h32 * g).astype(h.dtype)


def _modnorm(h, g, shift, scale):
    return _rmsnorm(h, g) * (1.0 + scale) + shift


def _swiglu(h, w1, w2):
    gate, up = jnp.split(h @ w1, 2, axis=-1)
    return (jax.nn.silu(gate) * up) @ w2


def _shift_mix(p, mu_prev, mu_next):
    prev = jnp.pad(p[:, :-1], ((0, 0), (1, 0), (0, 0)))
    nxt = jnp.pad(p[:, 1:], ((0, 0), (0, 1), (0, 0)))
    return p + mu_prev * (prev - p) + mu_next * (nxt - p)


def _rwkv_scan(seq, reverse):
    bsz = seq[0].shape[1]

    def step(state, inp):
        r_t, w_t, k_t, v_t, a_t, b_t = inp
        sa = jnp.einsum('bhvk,bhk->bhv', state, a_t)
        state = state * w_t[:, :, None, :] + sa[..., None] * b_t[:, :, None, :] + v_t[..., None] * k_t[:, :, None, :]
        return state, jnp.einsum('bhvk,bhk->bhv', state, r_t)

    s0 = jnp.zeros((bsz, RWKV_HEADS, HEAD_DIM, HEAD_DIM), jnp.float32)
    _, y = lax.scan(step, s0, seq, reverse=reverse)
    return y


def _rwkv7(p, lc, w0, w2, a0, a2, g2, k_k, k_a, r_k, lnx_g, lnx_b):
    f32 = jnp.float32
    p = p.astype(f32)
    bsz, L, _ = p.shape
    W = RWKV_WIDTH
    r, k, v, wl, al, gl = jnp.split(
        p, [W, 2 * W, 3 * W, 3 * W + RWKV_DECAY_LORA, 3 * W + RWKV_DECAY_LORA + RWKV_ICLR_LORA], axis=-1)
    heads = lambda t: t.reshape(bsz, L, RWKV_HEADS, HEAD_DIM)
    tm = lambda t: jnp.swapaxes(t, 0, 1)
    kk = heads(k * k_k)
    kk = kk * lax.rsqrt(jnp.maximum(jnp.sum(kk * kk, axis=-1, keepdims=True), 1e-12))
    tanh_wl = jnp.tanh(wl)
    r_h, v_h = heads(r), heads(v)
    y = 0.0
    k_sum = 0.0
    for d in range(2):
        w = -jax.nn.softplus(-(w0[d] + tanh_wl @ w2[d])) - 0.5
        decay = jnp.exp(-jnp.exp(w))
        a = jax.nn.sigmoid(a0[d] + al @ a2[d])
        kd = heads(k * (1.0 + (a - 1.0) * k_a))
        seq = tuple(tm(t) for t in (r_h, heads(decay), kd, v_h, -kk, kk * heads(a)))
        if d == 1:
            seq = tuple(jnp.roll(t, -lc, axis=0) for t in seq)
        yd = _rwkv_scan(seq, reverse=(d == 1))
        if d == 1:
            yd = jnp.roll(yd, lc, axis=0)
        y = y + yd
        k_sum = k_sum + kd
    y = jnp.swapaxes(y, 0, 1)
    mu = jnp.mean(y, axis=-1, keepdims=True)
    var = jnp.mean(jnp.square(y - mu), axis=-1, keepdims=True)
    y = (y - mu) * lax.rsqrt(var + GN_EPS) * lnx_g.reshape(RWKV_HEADS, HEAD_DIM) + lnx_b.reshape(RWKV_HEADS, HEAD_DIM)
    bonus = jnp.sum(r_h * (0.5 * k_sum) * r_k, axis=-1, keepdims=True) * v_h
    g = jax.nn.sigmoid(gl) @ g2
    return (y + bonus).reshape(bsz, L, W) * g


def _complex_affine_combine(e1, e2):
    a1r, a1i, b1r, b1i = e1
    a2r, a2i, b2r, b2i = e2
    return (a2r * a1r - a2i * a1i, a2r * a1i + a2i * a1r,
            a2r * b1r - a2i * b1i + b2r, a2r * b1i + a2i * b1r + b2i)


def _s5_scan(u_tm, a_re, a_im, log_step, b_re, b_im, c_re, c_im, reverse):
    f32 = jnp.float32
    lam_re = jnp.minimum(a_re.astype(f32), -1e-4)
    lam_im = a_im.astype(f32)
    dt = jnp.exp(log_step.astype(f32))[:, None]
    mag = jnp.exp(lam_re * dt)
    ab_re, ab_im = mag * jnp.cos(lam_im * dt), mag * jnp.sin(lam_im * dt)
    den = lam_re * lam_re + lam_im * lam_im
    f_re = ((ab_re - 1.0) * lam_re + ab_im * lam_im) / den
    f_im = (ab_im * lam_re - (ab_re - 1.0) * lam_im) / den
    b_re, b_im = b_re.astype(f32), b_im.astype(f32)
    bb_re = f_re[..., None] * b_re - f_im[..., None] * b_im
    bb_im = f_re[..., None] * b_im + f_im[..., None] * b_re
    bu_re = jnp.einsum('tbgi,gpi->tbgp', u_tm, bb_re)
    bu_im = jnp.einsum('tbgi,gpi->tbgp', u_tm, bb_im)
    L = u_tm.shape[0]
    a_re_t = jnp.broadcast_to(ab_re, (L, 1) + ab_re.shape)
    a_im_t = jnp.broadcast_to(ab_im, (L, 1) + ab_im.shape)
    _, _, x_re, x_im = lax.associative_scan(
        _complex_affine_combine, (a_re_t, a_im_t, bu_re, bu_im), reverse=reverse, axis=0)
    return (jnp.einsum('tbgp,gip->tbgi', x_re, c_re.astype(f32))
            - jnp.einsum('tbgp,gip->tbgi', x_im, c_im.astype(f32)))


def _s5(u, lc, a_re, a_im, log_step, b_re, b_im, c_re, c_im, d_skip, glu_w, glu_b):
    u = u.astype(jnp.float32)
    bsz, L, _ = u.shape
    u_tm = jnp.swapaxes(u.reshape(bsz, L, S5_GROUPS, S5_GROUP_CH), 0, 1)
    y_f = _s5_scan(u_tm, a_re[0], a_im[0], log_step[0], b_re[0], b_im[0], c_re[0], c_im[0], reverse=False)
    y_b = jnp.roll(_s5_scan(jnp.roll(u_tm, -lc, axis=0), a_re[1], a_im[1], log_step[1],
                            b_re[1], b_im[1], c_re[1], c_im[1], reverse=True), lc, axis=0)
    y = jnp.swapaxes(y_f + y_b, 0, 1).reshape(bsz, L, S5_WIDTH) + d_skip * u
    z = jax.nn.gelu(y)
    return z * jax.nn.sigmoid(z @ glu_w + glu_b)


def _mixer_ab(h_lat, h_ctx, in_w, out_w, mu, w0, w2, a0, a2, g2, k_k, k_a, r_k, lnx_g, lnx_b,
              s_are, s_aim, s_step, s_bre, s_bim, s_cre, s_cim, s_d, glu_w, glu_b):
    lc = h_ctx.shape[1]
    p = jnp.concatenate([h_ctx, h_lat], axis=1) @ in_w
    pa, pb = p[..., :RWKV_PROJ], p[..., RWKV_PROJ:]
    pa = jnp.concatenate([_shift_mix(pa[:, :lc], mu[0], mu[1]), _shift_mix(pa[:, lc:], mu[0], mu[1])], axis=1)
    ya = _rwkv7(pa, lc, w0, w2, a0, a2, g2, k_k, k_a, r_k, lnx_g, lnx_b)
    yb = _s5(pb, lc, s_are, s_aim, s_step, s_bre, s_bim, s_cre, s_cim, s_d, glu_w, glu_b)
    y = jnp.concatenate([ya, yb], axis=-1) @ out_w
    return y[:, lc:], y[:, :lc]


def _rope(t, cos, sin):
    half = HEAD_DIM // 2
    t1, t2 = t[..., :half], t[..., half:]
    cs, sn = cos[None, :, None, :], sin[None, :, None, :]
    return jnp.concatenate([t1 * cs - t2 * sn, t2 * cs + t1 * sn], axis=-1)


def _window_attention(q, k, v, kc, vc, sink):
    f32 = jnp.float32
    bsz, n, _, _ = q.shape
    lc = kc.shape[1]
    nb = n // ATTN_BLOCK
    qb = q.reshape(bsz, nb, ATTN_BLOCK, ATTN_KV_HEADS, ATTN_GROUP, HEAD_DIM)
    pad = ((0, 0), (ATTN_BLOCK, ATTN_BLOCK), (0, 0), (0, 0))
    kp = jnp.pad(k, pad).reshape(bsz, nb + 2, ATTN_BLOCK, ATTN_KV_HEADS, HEAD_DIM)
    vp = jnp.pad(v, pad).reshape(bsz, nb + 2, ATTN_BLOCK, ATTN_KV_HEADS, HEAD_DIM)
    kw = jnp.concatenate([kp[:, :-2], kp[:, 1:-1], kp[:, 2:]], axis=2)
    vw = jnp.concatenate([vp[:, :-2], vp[:, 1:-1], vp[:, 2:]], axis=2)
    qi = jnp.arange(ATTN_BLOCK)[:, None]
    mj = jnp.arange(3 * ATTN_BLOCK)[None, :] - ATTN_BLOCK
    kj = jnp.arange(nb)[:, None, None] * ATTN_BLOCK + mj[None]
    valid = (jnp.abs(mj - qi)[None] <= ATTN_WINDOW) & (kj >= 0) & (kj < n)
    sink_l = sink.astype(f32).reshape(ATTN_KV_HEADS, ATTN_GROUP)[None, :, :, None, None]
    scale = HEAD_DIM ** -0.5
    m_win = 3 * ATTN_BLOCK

    def one_block(args):
        qblk, kblk, vblk, vmask = args
        s_win = jnp.einsum('bqkgd,bmkd->bkgqm', qblk, kblk).astype(f32) * scale
        s_win = jnp.where(vmask, s_win, NEG_INF)
        s_ctx = jnp.einsum('bqkgd,bckd->bkgqc', qblk, kc).astype(f32) * scale
        s_sink = jnp.broadcast_to(sink_l, s_win.shape[:-1] + (1,))
        prob = jax.nn.softmax(jnp.concatenate([s_win, s_ctx, s_sink], axis=-1), axis=-1)
        return (jnp.einsum('bkgqm,bmkd->bqkgd', prob[..., :m_win], vblk)
                + jnp.einsum('bkgqc,bckd->bqkgd', prob[..., m_win:m_win + lc], vc))

    out = lax.map(one_block, (jnp.moveaxis(qb, 1, 0), jnp.moveaxis(kw, 1, 0), jnp.moveaxis(vw, 1, 0), valid))
    return jnp.moveaxis(out, 0, 1).reshape(bsz, n, ATTN_Q_W)


def _context_attention(qc, kc, vc, sink):
    f32 = jnp.float32
    bsz, lc = qc.shape[:2]
    qg = qc.reshape(bsz, lc, ATTN_KV_HEADS, ATTN_GROUP, HEAD_DIM)
    s = jnp.einsum('bckgd,bjkd->bkgcj', qg, kc).astype(f32) * HEAD_DIM ** -0.5
    s_sink = jnp.broadcast_to(sink.astype(f32).reshape(ATTN_KV_HEADS, ATTN_GROUP)[None, :, :, None, None],
                              s.shape[:-1] + (1,))
    prob = jax.nn.softmax(jnp.concatenate([s, s_sink], axis=-1), axis=-1)
    o = jnp.einsum('bkgcj,bjkd->bckgd', prob[..., :lc], vc)
    return o.reshape(bsz, lc, ATTN_Q_W)


def _mixer_c(h_lat, h_ctx, in_w, out_w, sink, cos, sin, need_ctx):
    bsz, n, _ = h_lat.shape
    lc = h_ctx.shape[1]
    p = h_lat @ in_w
    q = _rope(p[..., :ATTN_Q_W].reshape(bsz, n, ATTN_HEADS, HEAD_DIM), cos, sin)
    k = _rope(p[..., ATTN_Q_W:ATTN_Q_W + ATTN_KV_W].reshape(bsz, n, ATTN_KV_HEADS, HEAD_DIM), cos, sin)
    v = p[..., ATTN_Q_W + ATTN_KV_W:].reshape(bsz, n, ATTN_KV_HEADS, HEAD_DIM)
    pc = h_ctx @ in_w[:, ATTN_Q_W:]
    kc = pc[..., :ATTN_KV_W].reshape(bsz, lc, ATTN_KV_HEADS, HEAD_DIM)
    vc = pc[..., ATTN_KV_W:].reshape(bsz, lc, ATTN_KV_HEADS, HEAD_DIM)
    y_lat = _window_attention(q, k, v, kc, vc, sink) @ out_w
    if not need_ctx:
        return y_lat, None
    qc = (h_ctx @ in_w[:, :ATTN_Q_W]).reshape(bsz, lc, ATTN_HEADS, HEAD_DIM)
    return y_lat, _context_attention(qc, kc, vc, sink) @ out_w


def setup_inputs(seed: int = 0) -> dict:
    key = jax.random.key(seed)
    ks = iter(jax.random.split(key, 48))
    f32 = jnp.float32
    D = D_MODEL
    ne, no = (DEPTH + 1) // 2, DEPTH // 2

    def nrm(shape, scale=1.0):
        return scale * jax.random.normal(next(ks), shape, f32)

    def uni(shape, lo, hi):
        return jax.random.uniform(next(ks), shape, f32, lo, hi)

    ramp = jnp.arange(RWKV_WIDTH, dtype=f32) / (RWKV_WIDTH - 1)
    return {
        "x": nrm((BATCH, SEQ, D)),
        "c": nrm((BATCH, D)),
        "ctx": nrm((BATCH, CTX_LEN, D)),
        "c_ctx": nrm((D,)),
        "norm_g": 1.0 + nrm((DEPTH, 3, D), 0.05),
        "mod_w": nrm((DEPTH, D, N_MOD * D), 0.5 * D ** -0.5),
        "mod_b": nrm((DEPTH, N_MOD * D), 0.02),
        "ffn_w1": nrm((DEPTH, 2, D, 2 * D_FF), D ** -0.5),
        "ffn_w2": nrm((DEPTH, 2, D_FF, D), D_FF ** -0.5),
        "ab_in_w": nrm((ne, D, AB_PROJ), D ** -0.5),
        "ab_out_w": nrm((ne, D, D), D ** -0.5),
        "rwkv_mu": uni((ne, 2, RWKV_PROJ), 0.05, 0.45),
        "rwkv_w0": (-5.5 + 5.0 * ramp ** 0.85) + nrm((ne, 2, RWKV_WIDTH), 0.1),
        "rwkv_w2": nrm((ne, 2, RWKV_DECAY_LORA, RWKV_WIDTH), 0.5 * RWKV_DECAY_LORA ** -0.5),
        "rwkv_a0": nrm((ne, 2, RWKV_WIDTH), 0.1),
        "rwkv_a2": nrm((ne, 2, RWKV_ICLR_LORA, RWKV_WIDTH), 0.5 * RWKV_ICLR_LORA ** -0.5),
        "rwkv_g2": nrm((ne, RWKV_GATE_LORA, RWKV_WIDTH), RWKV_GATE_LORA ** -0.5),
        "rwkv_k_k": 0.85 + nrm((ne, RWKV_WIDTH), 0.05),
        "rwkv_k_a": 1.0 + nrm((ne, RWKV_WIDTH), 0.05),
        "rwkv_r_k": nrm((ne, RWKV_HEADS, HEAD_DIM), 0.1),
        "rwkv_lnx_g": 1.0 + nrm((ne, RWKV_WIDTH), 0.05),
        "rwkv_lnx_b": nrm((ne, RWKV_WIDTH), 0.02),
        "s5_a_re": -0.5 + nrm((ne, 2, S5_GROUPS, S5_STATE), 0.01),
        "s5_a_im": math.pi * jnp.arange(S5_STATE, dtype=f32) + nrm((ne, 2, S5_GROUPS, S5_STATE), 0.01),
        "s5_log_step": uni((ne, 2, S5_GROUPS), math.log(1e-3), math.log(1e-1)),
        "s5_b_re": nrm((ne, 2, S5_GROUPS, S5_STATE, S5_GROUP_CH), (2 * S5_GROUP_CH) ** -0.5),
        "s5_b_im": nrm((ne, 2, S5_GROUPS, S5_STATE, S5_GROUP_CH), (2 * S5_GROUP_CH) ** -0.5),
        "s5_c_re": nrm((ne, 2, S5_GROUPS, S5_GROUP_CH, S5_STATE), S5_STATE ** -0.5),
        "s5_c_im": nrm((ne, 2, S5_GROUPS, S5_GROUP_CH, S5_STATE), S5_STATE ** -0.5),
        "s5_d": nrm((ne, S5_WIDTH), 0.5),
        "s5_glu_w": nrm((ne, S5_WIDTH, S5_WIDTH), S5_WIDTH ** -0.5),
        "s5_glu_b": nrm((ne, S5_WIDTH), 0.02),
        "attn_in_w": nrm((no, D, C_PROJ), D ** -0.5),
        "attn_out_w": nrm((no, D, D), D ** -0.5),
        "attn_sink": nrm((no, ATTN_HEADS), 0.5),
        "final_g": 1.0 + nrm((D,), 0.05),
    }


def reference(x, c, ctx, c_ctx, norm_g, mod_w, mod_b, ffn_w1, ffn_w2, ab_in_w, ab_out_w, rwkv_mu, rwkv_w0,
              rwkv_w2, rwkv_a0, rwkv_a2, rwkv_g2, rwkv_k_k, rwkv_k_a, rwkv_r_k, rwkv_lnx_g, rwkv_lnx_b,
              s5_a_re, s5_a_im, s5_log_step, s5_b_re, s5_b_im, s5_c_re, s5_c_im, s5_d, s5_glu_w, s5_glu_b,
              attn_in_w, attn_out_w, attn_sink, final_g):
    f32 = jnp.float32
    n_lat = x.shape[1]
    rows = n_lat // GRID_W
    row_id = jnp.repeat(jnp.arange(rows, dtype=f32), GRID_W)
    col_id = jnp.tile(jnp.arange(GRID_W, dtype=f32), rows)
    inv_freq = ROPE_BASE ** (-jnp.arange(0, ROPE_AXIS_DIM, 2, dtype=f32) / ROPE_AXIS_DIM)
    ang = jnp.concatenate([row_id[:, None] * inv_freq, col_id[:, None] * inv_freq], axis=-1)
    cos, sin = jnp.cos(ang), jnp.sin(ang)

    xc = ctx
    for l in range(DEPTH):
        last = l == DEPTH - 1
        ml = [m[:, None, :] for m in jnp.split(jax.nn.silu(c) @ mod_w[l] + mod_b[l], N_MOD, axis=-1)]
        mc = jnp.split(jax.nn.silu(c_ctx) @ mod_w[l] + mod_b[l], N_MOD, axis=-1)
        x = x + 0.5 * ml[2] * _swiglu(_modnorm(x, norm_g[l, 0], ml[0], ml[1]), ffn_w1[l, 0], ffn_w2[l, 0])
        xc = xc + 0.5 * mc[2] * _swiglu(_modnorm(xc, norm_g[l, 0], mc[0], mc[1]), ffn_w1[l, 0], ffn_w2[l, 0])
        hl = _modnorm(x, norm_g[l, 1], ml[3], ml[4])
        hc = _modnorm(xc, norm_g[l, 1], mc[3], mc[4])
        if l % 2 == 0:
            e = l // 2
            yl, yc = _mixer_ab(hl, hc, ab_in_w[e], ab_out_w[e], rwkv_mu[e], rwkv_w0[e], rwkv_w2[e], rwkv_a0[e],
                               rwkv_a2[e], rwkv_g2[e], rwkv_k_k[e], rwkv_k_a[e], rwkv_r_k[e], rwkv_lnx_g[e],
                               rwkv_lnx_b[e], s5_a_re[e], s5_a_im[e], s5_log_step[e], s5_b_re[e], s5_b_im[e],
                               s5_c_re[e], s5_c_im[e], s5_d[e], s5_glu_w[e], s5_glu_b[e])
        else:
            o = l // 2
            yl, yc = _mixer_c(hl, hc, attn_in_w[o], attn_out_w[o], attn_sink[o], cos, sin, not last)
        x = x + ml[5] * yl
        x = x + 0.5 * ml[8] * _swiglu(_modnorm(x, norm_g[l, 2], ml[6], ml[7]), ffn_w1[l, 1], ffn_w2[l, 1])
        if not last:
            xc = xc + mc[5] * yc
            xc = xc + 0.5 * mc[8] * _swiglu(_modnorm(xc, norm_g[l, 2], mc[6], mc[7]), ffn_w1[l, 1], ffn_w2[l, 1])
    return _rmsnorm(x, final_g)
```

```python
import numpy as np
import concourse.bass as bass
import concourse.mybir as mybir
from concourse.bass_utils import run_bass_kernel_spmd

F32 = mybir.dt.float32
BF16 = mybir.dt.bfloat16
ALU = mybir.AluOpType
AF = mybir.ActivationFunctionType
AX = mybir.AxisListType


class T:
    def __init__(self, h, name=""):
        self.h = h
        self.name = name
        self.last_w = None
        self.readers = []

    def __getitem__(self, idx):
        return V(self, self.h[idx])

    def re(self, pat, **kw):
        return self[:].re(pat, **kw)


class V:
    def __init__(self, t, ap):
        self.t = t
        self.ap = ap

    def __getitem__(self, idx):
        return V(self.t, self.ap[idx])

    def re(self, pat, **kw):
        return V(self.t, self.ap.rearrange(pat, **kw))

    def bc(self, shape):
        return V(self.t, self.ap.to_broadcast(shape))


def _ap(x):
    return x.ap if isinstance(x, V) else x


def _ts(xs):
    out = []
    for x in xs:
        if isinstance(x, V):
            out.append(x.t)
        elif isinstance(x, T):
            out.append(x)
    return out


class Sched:
    ENG = ["pe", "act", "dve", "pool", "sync"]

    def __init__(self, nc, n_dma_sems=6, same_engine_sync=True):
        self.nc = nc
        self.q = {e: [] for e in self.ENG}
        self.cnt = {e: 0 for e in self.ENG}
        self.unsig = {e: False for e in self.ENG}
        self.sem = {e: nc.alloc_semaphore(f"s_{e}") for e in ["pe", "act", "dve", "pool"]}
        self.waited = {e: {} for e in self.ENG}
        self.same_engine_sync = same_engine_sync
        self.dsem = {}
        self.dcnt = {}
        self.drr = {}
        for qn in ["sync", "pool", "act"]:
            self.dsem[qn] = [nc.alloc_semaphore(f"d_{qn}{i}") for i in range(n_dma_sems)]
            self.dcnt[qn] = [0] * n_dma_sems
            self.drr[qn] = 0
        self.n_inst = 0
        self.uid = 0

    def sb(self, shape, dt=F32, name=None):
        self.uid += 1
        name = name or f"t{self.uid}"
        return T(self.nc.alloc_sbuf_tensor(f"{name}_{self.uid}", list(shape), dt), name)

    def ps(self, shape, dt=F32, name=None):
        self.uid += 1
        name = name or f"p{self.uid}"
        return T(self.nc.alloc_psum_tensor(f"{name}_{self.uid}", list(shape), dt), name)

    def dram(self, name, shape, dt=F32, kind="Internal"):
        return T(self.nc.dram_tensor(name, list(shape), dt, kind=kind), name)

    def _collect(self, eng, reads, writes):
        toks = []
        for t in _ts(reads):
            if t.last_w is not None:
                toks.append(t.last_w)
        for t in _ts(writes):
            if t.last_w is not None:
                toks.append(t.last_w)
            toks.extend(t.readers)
        best = {}
        for (kind, key, sem, val) in toks:
            if kind == "eng" and key == eng:
                if eng in ("pe", "sync") or not self.same_engine_sync:
                    continue
            k = id(sem)
            if k not in best or best[k][1] < val:
                best[k] = (sem, val)
        waits = []
        for k, (sem, val) in best.items():
            if self.waited[eng].get(k, 0) >= val:
                continue
            self.waited[eng][k] = val
            waits.append((sem, val))
        return waits

    def _mark(self, tok, reads, writes):
        for t in _ts(reads):
            t.readers.append(tok)
        for t in _ts(writes):
            t.last_w = tok
            t.readers = []

    def op(self, eng, fn, reads, writes, sig=True):
        waits = self._collect(eng, reads, writes)
        if sig:
            self.cnt[eng] += 1
            tok = ("eng", eng, self.sem[eng], self.cnt[eng])
            self.unsig[eng] = False
        else:
            tok = ("eng", eng, self.sem[eng], self.cnt[eng] + 1)
            self.unsig[eng] = True
        self.q[eng].append((fn, waits, (self.sem[eng], 1) if sig else None))
        self._mark(tok, reads, writes)
        self.n_inst += 1

    def dma(self, qn, out, in_, extra_reads=(), extra_writes=(), **kw):
        eng = qn
        i = self.drr[qn]
        self.drr[qn] = (i + 1) % len(self.dsem[qn])
        sem = self.dsem[qn][i]
        reads = [in_] + list(extra_reads)
        writes = [out] + list(extra_writes)
        waits = self._collect(eng, reads, writes)
        prev = self.dcnt[qn][i]
        if prev > 0 and self.waited[eng].get(id(sem), 0) < prev:
            self.waited[eng][id(sem)] = prev
            waits.append((sem, prev))
        self.dcnt[qn][i] += 16
        tok = ("dma", qn, sem, self.dcnt[qn][i])
        o, a = _ap(out), _ap(in_)
        self.q[eng].append((lambda e: e.dma_start(out=o, in_=a, **kw), waits, (sem, 16)))
        self._mark(tok, reads, writes)
        self.n_inst += 1
        return tok

    def mm(self, out, lhsT, rhs, start=True, stop=True, sig=None):
        if sig is None:
            sig = stop
        o, l, r = _ap(out), _ap(lhsT), _ap(rhs)
        self.op("pe", lambda e: e.matmul(o, l, r, start=start, stop=stop), [lhsT, rhs], [out], sig=sig)

    def tr(self, out, in_, ident, sig=True):
        o, i, d = _ap(out), _ap(in_), _ap(ident)
        self.op("pe", lambda e: e.transpose(o, i, d), [in_, ident], [out], sig=sig)

    def act(self, out, in_, func, bias=None, scale=1.0, accum=None, eng="act"):
        o, i = _ap(out), _ap(in_)
        kw = {}
        reads = [in_]
        writes = [out]
        if bias is not None:
            kw["bias"] = _ap(bias)
            reads.append(bias)
        kw["scale"] = _ap(scale)
        if isinstance(scale, V):
            reads.append(scale)
        if accum is not None:
            kw["accum_out"] = _ap(accum)
            writes.append(accum)
        self.op("act", lambda e: e.activation(o, i, func, **kw), reads, writes)

    def tt(self, eng, out, in0, in1, op):
        o, a, b = _ap(out), _ap(in0), _ap(in1)
        self.op(eng, lambda e: e.tensor_tensor(o, a, b, op), [in0, in1], [out])

    def ts(self, eng, out, in0, s1, op0, s2=None, op1=None, accum=None):
        o, a = _ap(out), _ap(in0)
        reads = [in0] + [s for s in (s1, s2) if isinstance(s, V)]
        writes = [out] + ([accum] if accum is not None else [])
        kw = {}
        if op1 is not None:
            kw["op1"] = op1
        if accum is not None:
            kw["accum_out"] = _ap(accum)
        self.op(eng, lambda e: e.tensor_scalar(o, a, _ap(s1), _ap(s2) if s2 is not None else None, op0, **kw), reads, writes)

    def stt(self, eng, out, in0, scalar, in1, op0, op1):
        o, a, b = _ap(out), _ap(in0), _ap(in1)
        reads = [in0, in1] + ([scalar] if isinstance(scalar, V) else [])
        self.op(eng, lambda e: e.scalar_tensor_tensor(o, a, _ap(scalar), b, op0, op1), reads, [out])

    def red(self, eng, out, in_, op, axis=AX.X):
        o, a = _ap(out), _ap(in_)
        self.op(eng, lambda e: e.tensor_reduce(o, a, axis, op), [in_], [out])

    def copy(self, eng, out, in_):
        o, a = _ap(out), _ap(in_)
        if eng == "act":
            self.op(eng, lambda e: e.copy(o, a), [in_], [out])
        else:
            self.op(eng, lambda e: e.tensor_copy(o, a), [in_], [out])

    def memset(self, eng, out, val):
        o = _ap(out)
        self.op(eng, lambda e: e.memset(o, val), [], [out])

    def scan(self, out, d0, d1, init, op0=ALU.mult, op1=ALU.add):
        o, a, b, i = _ap(out), _ap(d0), _ap(d1), _ap(init)
        reads = [d0, d1] + ([init] if isinstance(init, V) else [])
        self.op("dve", lambda e: e.tensor_tensor_scan(o, a, b, i, op0, op1), reads, [out])

    def recip(self, out, in_):
        o, a = _ap(out), _ap(in_)
        self.op("dve", lambda e: e.reciprocal(o, a), [in_], [out])

    def finish(self, final_tiles):
        nc = self.nc
        toks = []
        for t in final_tiles:
            if t.last_w is not None:
                toks.append(t.last_w)
        fin = []
        best = {}
        for (_, _, sem, val) in toks:
            if id(sem) not in best or best[id(sem)][1] < val:
                best[id(sem)] = (sem, val)
        for qn in self.dsem:
            for s, c in zip(self.dsem[qn], self.dcnt[qn]):
                if c > 0:
                    best[id(s)] = (s, max(c, best.get(id(s), (s, 0))[1]))
        for e in ("pe", "act", "dve", "pool"):
            if self.cnt[e] > 0 or self.unsig[e]:
                assert not self.unsig[e], f"engine {e} ends with unsignaled instruction"
                best[id(self.sem[e])] = (self.sem[e], self.cnt[e])
        fin = list(best.values())
        q = self.q
        with nc.Block() as block:
            def replay(lst):
                def f(e):
                    for (fn, waits, inc) in lst:
                        for (sem, val) in waits:
                            e.wait_ge(sem, val)
                        ins = fn(e)
                        if inc is not None:
                            ins.then_inc(inc[0], inc[1])
                return f

            @block.tensor
            def _(e):
                replay(q["pe"])(e)

            @block.scalar
            def _(e):
                replay(q["act"])(e)

            @block.vector
            def _(e):
                replay(q["dve"])(e)

            @block.gpsimd
            def _(e):
                replay(q["pool"])(e)

            @block.sync
            def _(e):
                replay(q["sync"])(e)
                for (sem, val) in fin:
                    e.wait_ge(sem, val)
        return nc


NCH = 17
TOK = NCH * 128
D = 1024
DFF = 2816
NFT = 22


def setup_consts(S, ident_d):
    idf = S.sb([128, 128], F32, "idf")
    idb = S.sb([128, 128], BF16, "idb")
    S.dma("sync", idf[:], ident_d)
    S.copy("dve", idb[:], idf[:])
    return idf, idb


def mod_vectors(S, cT_d, modw_d, modbT_d, ngT_d, wst, pa):
    cT = S.sb([128, 8, 2], F32, "cT")
    sc = S.sb([128, 8, 2], F32, "sc")
    S.dma("sync", cT[:], cT_d)
    S.act(sc[:], cT[:], AF.Silu)
    modbT = S.sb([128, 72], F32, "modbT")
    S.dma("sync", modbT[:], modbT_d)
    ngT = S.sb([128, 3, 8], F32, "ngT")
    S.dma("sync", ngT[:], ngT_d)
    modT = S.sb([128, 72, 2], F32, "modT")
    pm = pa[0]
    for nb in range(36):
        w = wst[nb % 2]
        S.dma("sync" if nb % 2 == 0 else "pool", w[:], modw_d[:, nb * 256:(nb + 1) * 256].re("(k p) n -> p k n", p=128))
        for ct in range(2):
            t = nb * 2 + ct
            for k in range(8):
                S.mm(pm[:, t * 2:t * 2 + 2], w[:, k, ct * 128:(ct + 1) * 128], sc[:, k, :], start=(k == 0), stop=(k == 7),
                     sig=(k == 7 and t % 2 == 1))
    for j in range(2):
        S.tt("dve", modT[:, :, j], pm[:, 0:144].re("p (t j) -> p t j", j=2)[:, :, j], modbT[:], ALU.add)
    return mod_derive(S, modT, ngT)


def mod_derive(S, modT, ngT):
    G = S.sb([128, 3, 8, 2], F32, "G")
    SH = S.sb([128, 3, 8, 2], F32, "SH")
    GATE = S.sb([128, 3, 8, 2], F32, "GATE")
    for i in range(3):
        for j in range(2):
            S.stt("dve", G[:, i, :, j], modT[:, (3 * i + 1) * 8:(3 * i + 2) * 8, j], 1.0, ngT[:, i, :], ALU.add, ALU.mult)
            S.copy("dve", SH[:, i, :, j], modT[:, (3 * i) * 8:(3 * i + 1) * 8, j])
            S.copy("dve", GATE[:, i, :, j], modT[:, (3 * i + 2) * 8:(3 * i + 3) * 8, j])
    return dict(G=G, SH=SH, GATE=GATE, modT=modT)


def gate_bcast(S, gate_col, idf, ones, ps, scale, name):
    out = S.sb([128, 1024], F32, name)
    dg = S.sb([128, 128], F32, name + "_dg")
    for k in range(8):
        S.ts("dve", dg[:], idf[:], gate_col[:, k:k + 1], ALU.mult)
        S.mm(ps[:, (k % 4) * 128:(k % 4 + 1) * 128], ones[:], dg[:], start=True, stop=True)
        S.ts("dve", out[:, k * 128:(k + 1) * 128], ps[:, (k % 4) * 128:(k % 4 + 1) * 128], float(scale), ALU.mult)
    return out


class Ctx:
    pass


def norm_to_hT(S, C, xt, hT, col0, G, SH, j):
    ss = C.small[C.si % 4]; C.si += 1
    junk = C.junk
    S.act(junk[:], xt, AF.Square, accum=ss[:, 0:1])
    S.ts("dve", ss[:, 1:2], ss[:, 0:1], 1.0 / D, ALU.mult, 1e-6, ALU.add)
    S.act(ss[:, 3:4], ss[:, 1:2], AF.Sqrt)
    S.recip(ss[:, 2:3], ss[:, 3:4])
    xn = C.xn[C.xi % 2]; C.xi += 1
    S.act(xn[:], xt, AF.Copy, scale=ss[:, 2:3])
    pt = C.pt[C.pti % 2]; C.pti += 1
    for k in range(8):
        S.tr(pt[:, k * 128:(k + 1) * 128], xn[:, k * 128:(k + 1) * 128], C.idb[:], sig=(k == 7))
    for k in range(8):
        eng = "dve" if k % 2 == 0 else "pool"
        eng = "dve"
        S.ts(eng, hT[:, k, col0:col0 + 128], pt[:, k * 128:(k + 1) * 128], G[:, k:k + 1], ALU.mult, SH[:, k:k + 1], ALU.add)


def ffn(S, C, x_src, x_dst, w1_d, w2_d, mv, ni, groups, after_chunk=None, jf=lambda ci: 0 if ci == 0 else 1):
    G, SH = mv["G"], mv["SH"]
    w2b = C.w2b
    first = True
    for grp in groups:
        nt = len(grp) * 128
        for li, ci in enumerate(grp):
            xt = C.xt[C.xti % 3]; C.xti += 1
            S.dma("sync", xt[:], x_src[ci * 128:(ci + 1) * 128, :])
            j = jf(ci)
            norm_to_hT(S, C, xt[:], C.hT, li * 128, G[:, ni, :, j], SH[:, ni, :, j], j)
        for ft in range(NFT):
            wst = C.wst[ft % 2]
            wb = C.w1b[ft % 2]
            S.dma("sync", wst[:, :, 0:128], w1_d[:, ft * 128:(ft + 1) * 128].re("(k p) n -> p k n", p=128))
            S.dma("pool", wst[:, :, 128:256], w1_d[:, DFF + ft * 128:DFF + (ft + 1) * 128].re("(k p) n -> p k n", p=128))
            S.copy("pool", wb[:], wst[:])
            if first:
                w2s = C.w2st[ft % 2]
                S.dma("sync", w2s[:], w2_d[ft * 128:(ft + 1) * 128, :])
                S.copy("pool", w2b[:, ft, :], w2s[:])
            for b0 in range(0, nt, 512):
                bw = min(512, nt - b0)
                pg = C.pa[C.pai % 4]; C.pai += 1
                pu = C.pa[C.pai % 4]; C.pai += 1
                for k in range(8):
                    S.mm(pg[:, 0:bw], wb[:, k, 0:128], C.hT[:, k, b0:b0 + bw], start=(k == 0), stop=(k == 7))
                for k in range(8):
                    S.mm(pu[:, 0:bw], wb[:, k, 128:256], C.hT[:, k, b0:b0 + bw], start=(k == 0), stop=(k == 7))
                sg = C.sg[C.sgi % 2]; C.sgi += 1
                S.act(sg[:, 0:bw], pg[:, 0:bw], AF.Silu)
                S.tt("dve", C.actT[:, ft, b0:b0 + bw], sg[:, 0:bw], pu[:, 0:bw], ALU.mult)
        first = False
        for li, ci in enumerate(grp):
            xt = C.xt[C.xti % 3]; C.xti += 1
            S.dma("sync", xt[:], x_src[ci * 128:(ci + 1) * 128, :])
            pc = C.pc
            for h in range(2):
                for ft in range(NFT):
                    S.mm(pc[:, h * 512:(h + 1) * 512], C.actT[:, ft, li * 128:(li + 1) * 128], w2b[:, ft, h * 512:(h + 1) * 512],
                         start=(ft == 0), stop=(ft == NFT - 1))
            gb = C.gateb[(ni, jf(ci))]
            tmp = C.tmp
            S.tt("dve", tmp[:], pc[:], gb[:], ALU.mult)
            S.tt("pool", xt[:], xt[:], tmp[:], ALU.add)
            if x_dst is not None:
                S.dma("pool", x_dst[ci * 128:(ci + 1) * 128, :], xt[:])
            if after_chunk is not None:
                after_chunk(ci, li, xt)
        yield grp


def alloc_common(S, nc):
    C = Ctx()
    C.small = [S.sb([128, 4], F32, f"small{i}") for i in range(4)]; C.si = 0
    C.junk = S.sb([128, 1024], BF16, "junk")
    C.xn = [S.sb([128, 1024], BF16, f"xn{i}") for i in range(2)]; C.xi = 0
    C.pt = [S.ps([128, 1024], BF16, f"pt{i}") for i in range(2)]; C.pti = 0
    C.pa = [S.ps([128, 512], F32, f"pa{i}") for i in range(4)]; C.pai = 0
    C.pc = S.ps([128, 1024], F32, "pc")
    C.xt = [S.sb([128, 1024], F32, f"xt{i}") for i in range(3)]; C.xti = 0
    C.hT = S.sb([128, 8, 1152], BF16, "hT")
    C.actT = S.sb([128, NFT, 1152], BF16, "actT")
    C.w2b = S.sb([128, NFT, 1024], BF16, "w2b")
    C.wst = [S.sb([128, 8, 256], F32, f"wst{i}") for i in range(2)]
    C.w1b = [S.sb([128, 8, 256], BF16, f"w1b{i}") for i in range(2)]
    C.w2st = [S.sb([128, 1024], F32, f"w2st{i}") for i in range(2)]
    C.sg = [S.sb([128, 512], F32, f"sg{i}") for i in range(2)]; C.sgi = 0
    C.tmp = S.sb([128, 1024], F32, "tmp")
    C.ones = S.sb([128, 128], F32, "ones")
    S.memset("dve", C.ones[:], 1.0)
    return C


GROUPS = [list(range(0, 9)), list(range(9, 17))]


def build_p1():
    nc = bass.Bass("TRN2", target_bir_lowering=False)
    S = Sched(nc)
    di = lambda n, s: S.dram(n, s, F32, kind="ExternalInput")
    xin = di("xin", [TOK, D]); cT = di("cT", [128, 8, 2]); modw = di("modw", [D, 9216]); modbT = di("modbT", [128, 72])
    ngT = di("ngT", [128, 3, 8]); w1 = di("w1", [D, 2 * DFF]); w2 = di("w2", [DFF, D]); win = di("win", [D, 2304])
    ident = di("ident", [128, 128])
    x1 = S.dram("x1", [TOK, D], F32, kind="ExternalOutput")
    p = S.dram("p", [TOK, 2304], F32, kind="ExternalOutput")
    pT = S.dram("pT", [768, TOK], F32, kind="ExternalOutput")
    modo = S.dram("modo", [128, 144], F32, kind="ExternalOutput")
    C = alloc_common(S, nc)
    C.idf, C.idb = setup_consts(S, ident[:])
    mv = mod_vectors(S, cT[:], modw, modbT[:], ngT[:], C.wst, C.pa)
    S.dma("sync", modo[:], mv["modT"][:].re("p t j -> p (t j)"))
    C.gateb = {}
    for j in range(2):
        C.gateb[(0, j)] = gate_bcast(S, mv["GATE"][:, 0, :, j], C.idf, C.ones, C.pa[1 + j], 0.5, f"gb0{j}")
    pst = [S.sb([128, 512], F32, f"pst{i}") for i in range(2)]
    psti = [0]

    def after(ci, li, xt):
        j = 0 if ci == 0 else 1
        norm_to_hT(S, C, xt[:], C.hT, li * 128, mv["G"][:, 1, :, j], mv["SH"][:, 1, :, j], j)

    for grp in ffn(S, C, xin, x1, w1, w2, mv, 0, GROUPS, after_chunk=after):
        nt = len(grp) * 128
        t0 = grp[0] * 128
        for cb in range(9):
            wst = C.wst[cb % 2]; wb = C.w1b[cb % 2]
            S.dma("sync", wst[:], win[:, cb * 256:(cb + 1) * 256].re("(k p) n -> p k n", p=128))
            S.copy("pool", wb[:], wst[:])
            for li, ci in enumerate(grp):
                pp = C.pa[C.pai % 4]; C.pai += 1
                for k in range(8):
                    S.mm(pp[:, 0:256], C.hT[:, k, li * 128:(li + 1) * 128], wb[:, k, :], start=(k == 0), stop=(k == 7))
                st = pst[psti[0] % 2]; psti[0] += 1
                S.copy("act", st[:, 0:256], pp[:, 0:256])
                S.dma("sync", p[ci * 128:(ci + 1) * 128, cb * 256:(cb + 1) * 256], st[:, 0:256])
            if cb >= 6:
                for ct in range(2):
                    row0 = (cb - 6) * 256 + ct * 128
                    for b0 in range(0, nt, 512):
                        bw = min(512, nt - b0)
                        pp = C.pa[C.pai % 4]; C.pai += 1
                        for k in range(8):
                            S.mm(pp[:, 0:bw], wb[:, k, ct * 128:(ct + 1) * 128], C.hT[:, k, b0:b0 + bw], start=(k == 0), stop=(k == 7))
                        st = pst[psti[0] % 2]; psti[0] += 1
                        S.copy("act", st[:, 0:bw], pp[:, 0:bw])
                        S.dma("sync", pT[row0:row0 + 128, t0 + b0:t0 + b0 + bw], st[:, 0:bw])
    S.finish([x1, p, pT, modo])
    return nc, S

import math

L = 4352
NB = L // 128
NEGC = -math.exp(-0.5)


def build_p2a(MD=BF16, nblocks=NB, stage=99):
    nc = bass.Bass("TRN2", target_bir_lowering=False)
    S = Sched(nc)
    di = lambda n, s: S.dram(n, s, F32, kind="ExternalInput")
    rkv3 = di("rkv3", [3, L, 1536])
    wl3 = di("wl3", [3, 64, L]); al3 = di("al3", [3, 64, L]); gl3 = di("gl3", [3, 128, L])
    mub_d = di("mub", [2, 128, 1536])
    mul_d = di("mul", [128, 3, 2])
    kkb_d = di("kkb", [128, 512]); kab_d = di("kab", [128, 512])
    w2a_d = di("w2a", [65, 512]); a2a_d = di("a2a", [65, 512]); g2_d = di("g2", [128, 512])
    msk_d = di("msk", [5, 128, 128])
    yd = S.dram("yd", [L, 512], F32, kind="ExternalOutput")
    kdo = S.dram("kdo", [L, 512], F32, kind="ExternalOutput")
    rvo = S.dram("rvo", [L, 1024], F32, kind="ExternalOutput")
    gto = S.dram("gto", [L, 512], F32, kind="ExternalOutput")

    def ld(dv, shape, nm, dt=F32):
        t = S.sb(shape, dt, nm)
        S.dma("sync", t[:], dv)
        return t
    mu0 = ld(mub_d[0], [128, 1536], "mu0"); mu1 = ld(mub_d[1], [128, 1536], "mu1")
    c0 = S.sb([128, 1536], F32, "c0")
    S.tt("dve", c0[:], mu0[:], mu1[:], ALU.add)
    S.ts("dve", c0[:], c0[:], -1.0, ALU.mult, 1.0, ALU.add)
    mul = ld(mul_d[:], [128, 3, 2], "mul")
    c0l = S.sb([128, 3], F32, "c0l")
    S.tt("dve", c0l[:], mul[:, :, 0], mul[:, :, 1], ALU.add)
    S.ts("dve", c0l[:], c0l[:], -1.0, ALU.mult, 1.0, ALU.add)
    kkb = ld(kkb_d[:], [128, 512], "kkb"); kab = ld(kab_d[:], [128, 512], "kab")
    omka = S.sb([128, 512], F32, "omka")
    S.ts("dve", omka[:], kab[:], -1.0, ALU.mult, 1.0, ALU.add)
    w2a = ld(w2a_d[:], [65, 512], "w2a"); a2a = ld(a2a_d[:], [65, 512], "a2a"); g2 = ld(g2_d[:], [128, 512], "g2")
    mUs = ld(msk_d[0], [128, 128], "mUs"); mUi = ld(msk_d[1], [128, 128], "mUi"); mLs = ld(msk_d[2], [128, 128], "mLs")
    idf = ld(msk_d[3], [128, 128], "idf")
    cUi = S.sb([128, 128], F32, "cUi"); cUs = S.sb([128, 128], F32, "cUs"); cLs = S.sb([128, 128], F32, "cLs")
    S.ts("dve", cUi[:], mUi[:], NEGC, ALU.mult)
    S.ts("dve", cUs[:], mUs[:], NEGC, ALU.mult)
    S.ts("dve", cLs[:], mLs[:], NEGC, ALU.mult)
    negc = S.sb([128, 1], F32, "negc")
    S.memset("dve", negc[:], NEGC)
    if MD != F32:
        idm = S.sb([128, 128], MD, "idm")
        S.copy("dve", idm[:], idf[:])
    else:
        idm = idf

    pb = [S.ps([128, 512], F32, f"pb{i}") for i in range(8)]
    pbi = [0]

    def bank():
        b = pb[pbi[0] % 8]; pbi[0] += 1
        return b

    def t512(nm, dt=F32):
        return S.sb([128, 512], dt, nm)

    rc = [S.sb([128, 1536], F32, "rc0")] * 2
    rp = [S.sb([128, 1536], F32, "rp0")] * 2
    rn_ = [S.sb([128, 1536], F32, "rn0")] * 2
    mix = S.sb([128, 1536], F32, "mix"); mt = S.sb([128, 1536], F32, "mt")
    lw = [[S.sb([64, 128], F32, f"lw{i}{j}") for j in range(3)] for i in range(2)]
    la = [[S.sb([64, 128], F32, f"la{i}{j}") for j in range(3)] for i in range(2)]
    lg = [[S.sb([128, 128], F32, f"lg{i}{j}") for j in range(3)] for i in range(2)]
    TW = S.sb([65, 128], F32, "TW"); AL = S.sb([65, 128], F32, "AL"); SG = S.sb([128, 128], F32, "SG")
    S.memset("dve", TW[:], 1.0); S.memset("dve", AL[:], 1.0)
    ltmp = S.sb([128, 128], F32, "ltmp")
    sig = t512("sig"); a_ = t512("a"); gt = t512("gt")
    kk = t512("kk"); sq = t512("sq"); ss = S.sb([128, 8], F32, "ss"); rn8 = S.sb([128, 8], F32, "rn8")
    kd = t512("kd"); tq = t512("tq"); bq = t512("bq")
    Gc = t512("G"); Gp = t512("Gp"); Gi = t512("Gi"); Ge = t512("Ge")
    A = t512("A", MD); B = t512("B", MD); K = t512("K", MD); Rq = t512("Rq", MD)
    Am = t512("Am", MD); B2m = t512("B2m", MD); K2m = t512("K2m", MD); Vm = t512("Vm", MD)
    AT = S.sb([64, 8, 128], MD, "AT"); BT = S.sb([64, 8, 128], MD, "BT"); KT = S.sb([64, 8, 128], MD, "KT")
    RT = S.sb([64, 8, 128], MD, "RT"); RTf = S.sb([64, 8, 128], F32, "RTf")
    mat = lambda nm, dt=MD: [S.sb([128, 4, 128], dt, f"{nm}{g}") for g in range(2)]
    Nm = [mat("Nm0"), mat("Nm1")]; NT = [mat("NT0"), mat("NT1")]
    Mak = mat("Mak"); Mbr = mat("Mbr"); Mkr = mat("Mkr")
    Tf = mat("Tf", F32); Tm = mat("Tm") if MD != F32 else Tf
    WTm = t512("WTm", MD); X1Tm = t512("X1Tm", MD); UlTm = t512("UlTm", MD)
    Rpf = S.sb([64, 8, 128], F32, "Rpf")
    gC = S.sb([64, 16], F32, "gC")
    dgG = S.sb([64, 8, 64], F32, "dgG")
    Pf = [S.sb([64, 8, 64], F32, f"Pf{c}") for c in range(2)]
    QTf = [S.sb([64, 8, 64], F32, f"QTf{c}") for c in range(2)]
    Yloc = t512("Yloc")
    ST = [S.sb([64, 8, 64], F32, f"ST{i}") for i in range(2)]
    S.memset("dve", ST[0][:], 0.0)
    sti = 0
    yt = [t512(f"yt{i}") for i in range(2)]
    v3 = lambda v: v.re("p (h k) -> p h k", k=64)
    m3 = lambda v: v.re("p (h t) -> p h t", t=128)

    for n in range(nblocks):
        r0 = n * 128
        i2 = n % 2
        S.dma("sync", rc[i2][:], rkv3[0, r0:r0 + 128, :])
        S.dma("pool", rp[i2][:], rkv3[1, r0:r0 + 128, :])
        S.dma("sync", rn_[i2][:], rkv3[2, r0:r0 + 128, :])
        for j in range(3):
            S.dma("pool", lw[i2][j][:], wl3[j, :, r0:r0 + 128])
            S.dma("pool", la[i2][j][:], al3[j, :, r0:r0 + 128])
            S.dma("sync", lg[i2][j][:], gl3[j, :, r0:r0 + 128])
        S.tt("dve", mix[:], rc[i2][:], c0[:], ALU.mult)
        S.tt("pool", mt[:], rp[i2][:], mu0[:], ALU.mult)
        S.tt("dve", mix[:], mix[:], mt[:], ALU.add)
        S.tt("pool", mt[:], rn_[i2][:], mu1[:], ALU.mult)
        S.tt("dve", mix[:], mix[:], mt[:], ALU.add)
        r = mix[:, 0:512]; k = mix[:, 512:1024]; v = mix[:, 1024:1536]
        S.dma("pool", rvo[r0:r0 + 128, 0:512], r)
        S.dma("pool", rvo[r0:r0 + 128, 512:1024], v)
        if stage < -2:
            continue
        for (src, ti, np_, dst, fn) in ((lw[i2], 0, 64, TW, AF.Tanh), (la[i2], 1, 64, AL, None), (lg[i2], 2, 128, SG, AF.Sigmoid)):
            S.ts("dve", ltmp[0:np_, :], src[0][:], c0l[0:np_, ti:ti + 1], ALU.mult)
            S.stt("dve", ltmp[0:np_, :], src[1][:], mul[0:np_, ti, 0:1], ltmp[0:np_, :], ALU.mult, ALU.add)
            if fn is None:
                S.stt("dve", dst[0:np_, :], src[2][:], mul[0:np_, ti, 1:2], ltmp[0:np_, :], ALU.mult, ALU.add)
            else:
                S.stt("dve", ltmp[0:np_, :], src[2][:], mul[0:np_, ti, 1:2], ltmp[0:np_, :], ALU.mult, ALU.add)
                S.act(dst[0:np_, :], ltmp[0:np_, :], fn)
        pw_ = bank(); S.mm(pw_[:], TW[:], w2a[:])
        S.act(sig[:], pw_[:], AF.Sigmoid)
        pa_ = bank(); S.mm(pa_[:], AL[:], a2a[:])
        S.act(a_[:], pa_[:], AF.Sigmoid)
        pg_ = bank(); S.mm(pg_[:], SG[:], g2[:])
        S.copy("act", gt[:], pg_[:])
        S.dma("pool", gto[r0:r0 + 128, :], gt[:])
        if stage < -1:
            continue
        S.tt("dve", kk[:], k, kkb[:], ALU.mult)
        S.tt("pool", sq[:], kk[:], kk[:], ALU.mult)
        S.red("dve", ss[:], v3(sq[:]), ALU.add)
        S.ts("dve", ss[:], ss[:], 1e-12, ALU.max)
        S.act(ss[:], ss[:], AF.Sqrt)
        S.recip(rn8[:], ss[:])
        S.tt("dve", v3(kk[:]), v3(kk[:]), rn8[:].re("p (h o) -> p h o", o=1).bc([128, 8, 64]), ALU.mult)
        if stage < -0.7:
            continue
        S.tt("pool", tq[:], a_[:], kab[:], ALU.mult)
        S.tt("pool", tq[:], tq[:], omka[:], ALU.add)
        S.tt("pool", kd[:], k, tq[:], ALU.mult)
        S.dma("pool", kdo[r0:r0 + 128, :], kd[:])
        if stage < -0.3:
            continue
        S.tt("pool", bq[:], kk[:], a_[:], ALU.mult)
        if stage < 1:
            continue
        pc1 = bank(); S.mm(pc1[:], cUi[:], sig[:])
        pc2 = bank(); S.mm(pc2[:], cUs[:], sig[:])
        pc3 = bank(); S.mm(pc3[:], cLs[:], sig[:])
        S.act(Gc[:], pc1[:], AF.Exp)
        S.act(Gi[:], pc1[:], AF.Exp, scale=-1.0)
        S.act(Gp[:], pc2[:], AF.Exp)
        S.act(Ge[:], pc3[:], AF.Exp)
        S.stt("dve", A[:], kk[:], -1.0, Gp[:], ALU.mult, ALU.mult)
        S.tt("dve", B[:], bq[:], Gi[:], ALU.mult)
        S.tt("pool", K[:], kd[:], Gi[:], ALU.mult)
        S.tt("pool", Rq[:], r, Gc[:], ALU.mult)
        S.tt("pool", B2m[:], bq[:], Ge[:], ALU.mult)
        S.tt("pool", K2m[:], kd[:], Ge[:], ALU.mult)
        Am_ = A
        if MD != F32:
            S.copy("pool", Vm[:], v)
            Vv = Vm[:]
        else:
            Vv = v
        if stage < 2:
            continue
        for (src, dst, extra) in ((A, AT, None), (B, BT, None), (K, KT, None), (Rq, RT, RTf)):
            for hg in range(2):
                p_ = bank()
                for hh in range(4):
                    h = hg * 4 + hh
                    S.mm(p_[0:64, hh * 128:(hh + 1) * 128], src[:, h * 64:(h + 1) * 64], idm[:], sig=(hh == 3))
                S.copy("act", dst[:, hg * 4:(hg + 1) * 4, :], m3(p_[0:64, :]))
                if extra is not None and MD != F32:
                    S.copy("dve", extra[:, hg * 4:(hg + 1) * 4, :], m3(p_[0:64, :]))
        RTf_ = RTf if MD != F32 else RT
        if stage < 2.5:
            continue
        def mmat(dst, LT, RTt, mask, hg):
            p_ = bank()
            for hh in range(4):
                h = hg * 4 + hh
                S.mm(p_[:, hh * 128:(hh + 1) * 128], LT[:, h, :], RTt[:, h, :], sig=(hh == 3))
            S.tt("dve", dst[hg][:], m3(p_[:]), mask[:].re("p (o t) -> p o t", o=1).bc([128, 4, 128]), ALU.mult)
        for hg in range(2):
            mmat(Nm[0], BT, AT, mUs, hg)
            mmat(NT[0], AT, BT, mLs, hg)
            mmat(Mak, KT, AT, mUs, hg)
            mmat(Mbr, BT, RT, mUi, hg)
            mmat(Mkr, KT, RT, mUi, hg)
        if stage < 3:
            continue
        for hg in range(2):
            S.tt("dve", Tf[hg][:], Nm[0][hg][:], idf[:].re("p (o t) -> p o t", o=1).bc([128, 4, 128]), ALU.add)
            if MD != F32:
                S.copy("pool", Tm[hg][:], Tf[hg][:])
            cur = 0
            for lev in range(5):
                nxt = 1 - cur
                last = lev == 4
                if not last:
                    p1 = bank()
                    for hh in range(4):
                        S.mm(p1[:, hh * 128:(hh + 1) * 128], NT[cur][hg][:, hh, :], Nm[cur][hg][:, hh, :], sig=(hh == 3))
                    S.copy("act", Nm[nxt][hg][:], m3(p1[:]))
                p2 = bank()
                for hh in range(4):
                    S.mm(p2[:, hh * 128:(hh + 1) * 128], Nm[cur][hg][:, hh, :], NT[cur][hg][:, hh, :], sig=(hh == 3))
                S.copy("act", NT[nxt][hg][:], m3(p2[:]))
                p3 = bank()
                for hh in range(4):
                    S.mm(p3[:, hh * 128:(hh + 1) * 128], NT[nxt][hg][:, hh, :], Tm[hg][:, hh, :], sig=(hh == 3))
                S.tt("dve", Tf[hg][:], Tf[hg][:], m3(p3[:]), ALU.add)
                if MD != F32:
                    S.copy("pool", Tm[hg][:], Tf[hg][:])
                cur = nxt
        if stage < 4:
            continue
        p_ = bank()
        for h in range(8):
            S.mm(p_[:, h * 64:(h + 1) * 64], Tm[h // 4][:, h % 4, :], Am_[:, h * 64:(h + 1) * 64], sig=(h == 7))
        S.copy("act", WTm[:], p_[:])
        p_ = bank()
        for h in range(8):
            S.mm(p_[:, h * 64:(h + 1) * 64], Mak[h // 4][:, h % 4, :], Vv[:, h * 64:(h + 1) * 64], sig=(h == 7))
        S.copy("dve", X1Tm[:], p_[:])
        p_ = bank()
        for h in range(8):
            S.mm(p_[:, h * 64:(h + 1) * 64], Tm[h // 4][:, h % 4, :], X1Tm[:, h * 64:(h + 1) * 64], sig=(h == 7))
        S.copy("act", UlTm[:], p_[:])
        for hg in range(2):
            p_ = bank()
            for hh in range(4):
                h = hg * 4 + hh
                S.mm(p_[0:64, hh * 128:(hh + 1) * 128], WTm[:, h * 64:(h + 1) * 64], Mbr[hg][:, hh, :], sig=(hh == 3))
            S.tt("dve", Rpf[:, hg * 4:(hg + 1) * 4, :], m3(p_[0:64, :]), RTf_[:, hg * 4:(hg + 1) * 4, :], ALU.add)
        if stage < 5:
            continue
        p_ = bank()
        for c in range(2):
            for h in range(8):
                S.mm(p_[0:64, c * 8 + h:c * 8 + h + 1], sig[c * 64:(c + 1) * 64, h * 64:(h + 1) * 64], negc[c * 64:(c + 1) * 64, 0:1],
                     sig=(c == 1 and h == 7))
        S.act(gC[:], p_[0:64, 0:16], AF.Exp)
        for c in range(2):
            pc = slice(c * 64, (c + 1) * 64)
            p_ = bank()
            for h in range(8):
                S.mm(p_[0:64, h * 64:(h + 1) * 64], WTm[pc, h * 64:(h + 1) * 64], B2m[pc, h * 64:(h + 1) * 64], sig=(h == 7))
            S.tt("pool", dgG[:], idf[0:64, 0:64].re("p (o k) -> p o k", o=1).bc([64, 8, 64]),
                 gC[:, c * 8:(c + 1) * 8].re("p (h o) -> p h o", o=1).bc([64, 8, 64]), ALU.mult)
            S.tt("dve", Pf[c][:], v3(p_[0:64, :]), dgG[:], ALU.add)
            p_ = bank()
            for h in range(8):
                S.mm(p_[0:64, h * 64:(h + 1) * 64], B2m[pc, h * 64:(h + 1) * 64], UlTm[pc, h * 64:(h + 1) * 64], start=True, stop=False, sig=False)
                S.mm(p_[0:64, h * 64:(h + 1) * 64], K2m[pc, h * 64:(h + 1) * 64], Vv[pc, h * 64:(h + 1) * 64], start=False, stop=True, sig=(h == 7))
            S.copy("act", QTf[c][:], v3(p_[0:64, :]))
        p_ = bank()
        for h in range(8):
            S.mm(p_[:, h * 64:(h + 1) * 64], Mbr[h // 4][:, h % 4, :], UlTm[:, h * 64:(h + 1) * 64], start=True, stop=False, sig=False)
            S.mm(p_[:, h * 64:(h + 1) * 64], Mkr[h // 4][:, h % 4, :], Vv[:, h * 64:(h + 1) * 64], start=False, stop=True, sig=(h == 7))
        S.copy("act", Yloc[:], p_[:])
        if stage < 6:
            continue
        ytile = yt[i2]
        for c in range(2):
            pc = slice(c * 64, (c + 1) * 64)
            st_cur = ST[sti % 2]; st_nxt = ST[(sti + 1) % 2]; sti += 1
            p_ = bank()
            for h in range(8):
                S.mm(p_[pc, h * 64:(h + 1) * 64], Rpf[:, h, c * 64:(c + 1) * 64], st_cur[:, h, :], sig=(h == 7))
            S.tt("dve", ytile[pc, :], p_[pc, :], Yloc[pc, :], ALU.add)
            p2 = bank()
            for h in range(8):
                S.mm(p2[0:64, h * 64:(h + 1) * 64], Pf[c][:, h, :], st_cur[:, h, :], sig=(h == 7))
            S.tt("dve", st_nxt[:], v3(p2[0:64, :]), QTf[c][:], ALU.add)
        S.dma("sync", yd[r0:r0 + 128, :], ytile[:])
    S.finish([yd, kdo, rvo, gto])
    return nc, S

import math

L = 4352
NST = 16
CH = 128


def build_p2b():
    nc = bass.Bass("TRN2", target_bir_lowering=False)
    S = Sched(nc)
    di = lambda n, s: S.dram(n, s, F32, kind="ExternalInput")
    uT32 = di("uT32", [32, NST, L])
    are_d = di("are", [128, NST]); aim_d = di("aim", [128, NST]); lst_d = di("lst", [128, NST])
    bre_d = di("bre", [128, NST, 32]); bim_d = di("bim", [128, NST, 32])
    cre_d = di("cre", [128, NST, 32]); cim_d = di("cim", [128, NST, 32])
    ident = di("ident", [128, 128])
    yd = S.dram("yd", [L, 512], F32, kind="ExternalOutput")

    idf = S.sb([128, 128], F32, "idf")
    S.dma("sync", idf[:], ident[:])
    ld = lambda d_, shape, nm: (lambda t: (S.dma("sync", t[:], d_[:]), t)[1])(S.sb(shape, F32, nm))
    are = ld(are_d, [128, NST], "are"); aim = ld(aim_d, [128, NST], "aim"); lst = ld(lst_d, [128, NST], "lst")
    bre = ld(bre_d, [128, NST, 32], "bre"); bim = ld(bim_d, [128, NST, 32], "bim")
    cre = ld(cre_d, [128, NST, 32], "cre"); cim = ld(cim_d, [128, NST, 32], "cim")
    ncim = S.sb([128, NST, 32], F32, "ncim")
    S.ts("dve", ncim[:], cim[:], -1.0, ALU.mult)

    sm = lambda nm: S.sb([128, NST], F32, nm)
    lre = sm("lre"); dt_ = sm("dt"); zr = sm("zr"); th = sm("th"); rho = sm("rho")
    S.ts("dve", lre[:], are[:], -1e-4, ALU.min)
    S.act(dt_[:], lst[:], AF.Exp)
    S.tt("dve", zr[:], lre[:], dt_[:], ALU.mult)
    S.tt("dve", th[:], aim[:], dt_[:], ALU.mult)
    S.act(rho[:], zr[:], AF.Exp)
    MAGIC = 12582912.0
    TWO_PI = 2.0 * math.pi

    def sin_reduced(out, ang, shift, nm):
        a = sm(nm + "a"); k = sm(nm + "k"); r = sm(nm + "r")
        S.ts("dve", a[:], ang[:], float(shift), ALU.add)
        S.ts("dve", k[:], a[:], 1.0 / TWO_PI, ALU.mult, MAGIC, ALU.add)
        S.ts("dve", k[:], k[:], MAGIC, ALU.subtract)
        S.stt("dve", r[:], k[:], -TWO_PI, a[:], ALU.mult, ALU.add)
        S.ts("dve", r[:], r[:], 3.14159, ALU.min, -3.14159, ALU.max)
        S.act(out, r[:], AF.Sin)

    cs = sm("cs"); sn = sm("sn")
    sin_reduced(sn[:], th, 0.0, "s")
    sin_reduced(cs[:], th, math.pi / 2, "c")
    abre = sm("abre"); abim = sm("abim"); den = sm("den"); rden = sm("rden"); t1 = sm("t1"); t2 = sm("t2")
    fre = sm("fre"); fim = sm("fim"); am1 = sm("am1")
    S.tt("dve", abre[:], rho[:], cs[:], ALU.mult)
    S.tt("dve", abim[:], rho[:], sn[:], ALU.mult)
    S.tt("dve", t1[:], lre[:], lre[:], ALU.mult)
    S.tt("dve", t2[:], aim[:], aim[:], ALU.mult)
    S.tt("dve", den[:], t1[:], t2[:], ALU.add)
    S.recip(rden[:], den[:])
    S.ts("dve", am1[:], abre[:], -1.0, ALU.add)
    S.tt("dve", t1[:], am1[:], lre[:], ALU.mult)
    S.tt("dve", t2[:], abim[:], aim[:], ALU.mult)
    S.tt("dve", t1[:], t1[:], t2[:], ALU.add)
    S.tt("dve", fre[:], t1[:], rden[:], ALU.mult)
    S.tt("dve", t1[:], abim[:], lre[:], ALU.mult)
    S.tt("dve", t2[:], am1[:], aim[:], ALU.mult)
    S.tt("dve", t1[:], t1[:], t2[:], ALU.subtract)
    S.tt("dve", fim[:], t1[:], rden[:], ALU.mult)
    bbre = S.sb([128, NST, 32], F32, "bbre"); bbim = S.sb([128, NST, 32], F32, "bbim")
    u1 = S.sb([128, NST, 32], F32, "u1"); u2 = S.sb([128, NST, 32], F32, "u2")
    fre_b = fre[:].re("p (t o) -> p t o", o=1).bc([128, NST, 32])
    fim_b = fim[:].re("p (t o) -> p t o", o=1).bc([128, NST, 32])
    S.tt("dve", u1[:], bre[:], fre_b, ALU.mult)
    S.tt("dve", u2[:], bim[:], fim_b, ALU.mult)
    S.tt("dve", bbre[:], u1[:], u2[:], ALU.subtract)
    S.tt("dve", u1[:], bim[:], fre_b, ALU.mult)
    S.tt("dve", u2[:], bre[:], fim_b, ALU.mult)
    S.tt("dve", bbim[:], u1[:], u2[:], ALU.add)
    pb = [S.ps([128, 512], F32, f"pb{i}") for i in range(8)]
    BBTre = S.sb([32, NST, 128], F32, "BBTre"); BBTim = S.sb([32, NST, 128], F32, "BBTim")
    for (src, dst, pi) in ((bbre, BBTre, 0), (bbim, BBTim, 1)):
        for g4 in range(4):
            p_ = pb[pi * 4 + g4]
            for jj in range(4):
                j = g4 * 4 + jj
                S.tr(p_[0:32, jj * 128:(jj + 1) * 128], src[:, j, :], idf[:], sig=(jj == 3))
            S.copy("dve", dst[:, g4 * 4:(g4 + 1) * 4, :], p_[0:32, :].re("p (a b) -> p a b", b=128))
    Ec = S.sb([128, NST, CH], F32, "Ec"); Es = S.sb([128, NST, CH], F32, "Es")
    S.copy("dve", Ec[:, :, 0], cs[:])
    S.copy("dve", Es[:, :, 0], sn[:])
    w1 = S.sb([128, NST, CH // 2], F32, "w1"); w2 = S.sb([128, NST, CH // 2], F32, "w2")
    m = 1
    while m < CH:
        cb = Ec[:, :, m - 1:m].bc([128, NST, m]); sb_ = Es[:, :, m - 1:m].bc([128, NST, m])
        S.tt("dve", w1[:, :, 0:m], Ec[:, :, 0:m], cb, ALU.mult)
        S.tt("dve", w2[:, :, 0:m], Es[:, :, 0:m], sb_, ALU.mult)
        S.tt("dve", Ec[:, :, m:2 * m], w1[:, :, 0:m], w2[:, :, 0:m], ALU.subtract)
        S.tt("dve", w1[:, :, 0:m], Ec[:, :, 0:m], sb_, ALU.mult)
        S.tt("dve", w2[:, :, 0:m], Es[:, :, 0:m], cb, ALU.mult)
        S.tt("dve", Es[:, :, m:2 * m], w1[:, :, 0:m], w2[:, :, 0:m], ALU.add)
        m *= 2
    rhob = S.sb([128, NST, CH], F32, "rhob")
    S.copy("dve", rhob[:], rho[:].re("p (t o) -> p t o", o=1).bc([128, NST, CH]))

    nchunk = L // CH
    ut = [S.sb([32, NST, CH], F32, f"ut{i}") for i in range(2)]
    Zre = S.sb([128, NST, CH], F32, "Zre"); Zim = S.sb([128, NST, CH], F32, "Zim")
    wre = S.sb([128, NST, CH], F32, "wre"); wim = S.sb([128, NST, CH], F32, "wim")
    xre = [S.sb([128, NST, CH], F32, f"xre{i}") for i in range(2)]
    xim = [S.sb([128, NST, CH], F32, f"xim{i}") for i in range(2)]
    ta = [S.sb([128, 4, CH], F32, f"ta{i}") for i in range(4)]
    tb = [S.sb([128, 4, CH], F32, f"tb{i}") for i in range(4)]
    yst = [S.sb([128, 512], F32, f"yst{i}") for i in range(2)]
    pbi = 0
    for c in range(nchunk):
        u = ut[c % 2]
        S.dma("sync", u[:], uT32[:, :, c * CH:(c + 1) * CH])
        xr, xi = xre[c % 2], xim[c % 2]
        xr_prev, xi_prev = xre[(c + 1) % 2], xim[(c + 1) % 2]
        for g4 in range(4):
            pr = pb[pbi % 6]; pbi += 1
            pi_ = pb[pbi % 6]; pbi += 1
            for jj in range(4):
                j = g4 * 4 + jj
                S.mm(pr[:, jj * CH:(jj + 1) * CH], BBTre[:, j, :], u[:, j, :], sig=False)
            for jj in range(4):
                j = g4 * 4 + jj
                S.mm(pi_[:, jj * CH:(jj + 1) * CH], BBTim[:, j, :], u[:, j, :], sig=(jj == 3))
            sl = slice(g4 * 4, (g4 + 1) * 4)
            prv = pr[:, :].re("p (a b) -> p a b", b=CH); piv = pi_[:, :].re("p (a b) -> p a b", b=CH)
            a0, a1, a2, a3 = ta
            S.tt("dve", a0[:], prv, Ec[:, sl, :], ALU.mult)
            S.tt("dve", a1[:], piv, Es[:, sl, :], ALU.mult)
            S.tt("pool", Zre[:, sl, :], a0[:], a1[:], ALU.add)
            S.tt("dve", a2[:], piv, Ec[:, sl, :], ALU.mult)
            S.tt("dve", a3[:], prv, Es[:, sl, :], ALU.mult)
            S.tt("pool", Zim[:, sl, :], a2[:], a3[:], ALU.subtract)
            for jj in range(4):
                j = g4 * 4 + jj
                for (wt, zt, xp) in ((wre, Zre, xr_prev), (wim, Zim, xi_prev)):
                    o_, d0, d1 = wt[:, j, :], rhob[:, j, :], zt[:, j, :]
                    init = 0.0 if c == 0 else xp[:, j, CH - 1:CH]
                    S.scan(o_, d0, d1, init)
            b0, b1, b2, b3 = tb
            S.tt("pool", b0[:], wre[:, sl, :], Ec[:, sl, :], ALU.mult)
            S.tt("pool", b1[:], wim[:, sl, :], Es[:, sl, :], ALU.mult)
            S.tt("pool", xr[:, sl, :], b0[:], b1[:], ALU.subtract)
            S.tt("pool", b2[:], wim[:, sl, :], Ec[:, sl, :], ALU.mult)
            S.tt("pool", b3[:], wre[:, sl, :], Es[:, sl, :], ALU.mult)
            S.tt("pool", xi[:, sl, :], b2[:], b3[:], ALU.add)
        py = pb[6 + c % 2]
        for j in range(NST):
            S.mm(py[:, j * 32:(j + 1) * 32], xr[:, j, :], cre[:, j, :], start=True, stop=False, sig=False)
            S.mm(py[:, j * 32:(j + 1) * 32], xi[:, j, :], ncim[:, j, :], start=False, stop=True, sig=(j == NST - 1))
        ys = yst[c % 2]
        S.copy("act", ys[:], py[:])
        S.dma("sync", yd[c * CH:(c + 1) * CH, :], ys[:])
    S.finish([yd])
    return nc, S


GN_EPS = 64e-5


def build_p3a():
    nc = bass.Bass("TRN2", target_bir_lowering=False)
    S = Sched(nc)
    di = lambda n, s: S.dram(n, s, F32, kind="ExternalInput")
    x1 = di("x1", [TOK, D])
    rw = di("rw", [7, TOK, 512])
    s5 = di("s5", [3, TOK, 512])
    bc_d = di("bcs", [5, 128, 512])
    gluw_d = di("gluw", [512, 512]); outw_d = di("outw", [D, D])
    modT_d = di("modT", [128, 144]); ident = di("ident", [128, 128])
    xm = S.dram("xm", [TOK, D], F32, kind="ExternalOutput")

    idf, idb = setup_consts(S, ident[:])
    ones = S.sb([128, 128], F32, "ones"); S.memset("dve", ones[:], 1.0)
    pa = [S.ps([128, 512], F32, f"pa{i}") for i in range(4)]
    pt = [S.ps([128, 1024], BF16, f"pt{i}") for i in range(2)]
    pc = S.ps([128, 1024], F32, "pc")
    modT = S.sb([128, 72, 2], F32, "modT")
    S.dma("sync", modT[:].re("p t j -> p (t j)"), modT_d[:])
    gateb = [gate_bcast(S, modT[:, 5 * 8:6 * 8, j], idf, ones, pa[j], 1.0, f"g5{j}") for j in range(2)]
    bcs = [S.sb([128, 512], F32, f"bcs{i}") for i in range(5)]
    for i in range(5):
        S.dma("sync", bcs[i][:], bc_d[i])
    lnxg, lnxb, rk, s5d, glub = bcs
    S.ts("dve", rk[:], rk[:], 0.5, ALU.mult)
    wst = S.sb([128, 4, 512], F32, "wst")
    gluw = S.sb([128, 4, 512], BF16, "gluw")
    S.dma("sync", wst[:], gluw_d.re("(k p) n -> p k n", p=128))
    S.copy("pool", gluw[:], wst[:])
    outw = S.sb([128, 8, 1024], BF16, "outw")
    wst2 = [S.sb([128, 1024], F32, f"wst2{i}") for i in range(2)]
    for k in range(8):
        S.dma("sync", wst2[k % 2][:], outw_d[k * 128:(k + 1) * 128, :])
        S.copy("pool", outw[:, k, :], wst2[k % 2][:])

    t5 = lambda nm, dt=F32: S.sb([128, 512], dt, nm)
    inr = [[t5(f"inr{i}{j}") for j in range(7)] for i in range(2)]
    ins = [[t5(f"ins{i}{j}") for j in range(3)] for i in range(2)]
    xt = [S.sb([128, 1024], F32, f"xt{i}") for i in range(2)]
    y = t5("y"); yc = t5("yc"); sq = t5("sq"); ks = t5("ks"); tq = t5("tq"); bon = t5("bon")
    s8 = S.sb([128, 8], F32, "s8"); v8 = S.sb([128, 8], F32, "v8"); b8 = S.sb([128, 8], F32, "b8")
    cat = S.sb([128, 1024], BF16, "cat"); ysum = t5("ysum"); z = t5("z"); zb = t5("zb", BF16)
    zT = S.sb([128, 4, 128], BF16, "zT"); gl = t5("gl"); catT = S.sb([128, 8, 128], BF16, "catT")
    tmp = S.sb([128, 1024], F32, "tmp")
    v3 = lambda v: v.re("p (h k) -> p h k", k=64)
    b3 = lambda t: t[:].re("p (h o) -> p h o", o=1).bc([128, 8, 64])
    for ci in range(NCH):
        j = 0 if ci == 0 else 1
        i2 = ci % 2
        rows = slice(ci * 128, (ci + 1) * 128)
        for q in range(7):
            S.dma("sync" if q % 2 == 0 else "pool", inr[i2][q][:], rw[q, rows, :])
        for q in range(3):
            S.dma("pool" if q % 2 == 0 else "sync", ins[i2][q][:], s5[q, rows, :])
        S.dma("sync", xt[i2][:], x1[rows, :])
        y0, y1, kd0, kd1, r, v, g = inr[i2]
        S.tt("dve", y[:], y0[:], y1[:], ALU.add)
        S.red("dve", s8[:], v3(y[:]), ALU.add)
        S.ts("dve", s8[:], s8[:], 1.0 / 64, ALU.mult)
        S.tt("dve", v3(yc[:]), v3(y[:]), b3(s8), ALU.subtract)
        S.tt("pool", sq[:], yc[:], yc[:], ALU.mult)
        S.red("dve", v8[:], v3(sq[:]), ALU.add)
        S.ts("dve", v8[:], v8[:], 1.0 / 64, ALU.mult, GN_EPS, ALU.add)
        S.act(v8[:], v8[:], AF.Sqrt)
        S.recip(v8[:], v8[:])
        S.tt("dve", v3(yc[:]), v3(yc[:]), b3(v8), ALU.mult)
        S.tt("pool", yc[:], yc[:], lnxg[:], ALU.mult)
        S.tt("pool", yc[:], yc[:], lnxb[:], ALU.add)
        S.tt("pool", ks[:], kd0[:], kd1[:], ALU.add)
        S.tt("pool", tq[:], r[:], ks[:], ALU.mult)
        S.tt("pool", tq[:], tq[:], rk[:], ALU.mult)
        S.red("dve", b8[:], v3(tq[:]), ALU.add)
        S.tt("dve", v3(bon[:]), v3(v[:]), b3(b8), ALU.mult)
        S.tt("dve", yc[:], yc[:], bon[:], ALU.add)
        S.tt("dve", cat[:, 0:512], yc[:], g[:], ALU.mult)
        ys0, ys1, u = ins[i2]
        S.tt("pool", ysum[:], ys0[:], ys1[:], ALU.add)
        S.tt("pool", tq[:], u[:], s5d[:], ALU.mult)
        S.tt("pool", ysum[:], ysum[:], tq[:], ALU.add)
        S.act(z[:], ysum[:], AF.Gelu)
        S.copy("pool", zb[:], z[:])
        p_ = pt[0]
        for k in range(4):
            S.tr(p_[:, k * 128:(k + 1) * 128], zb[:, k * 128:(k + 1) * 128], idb[:], sig=(k == 3))
        S.copy("dve", zT[:], p_[:, 0:512].re("p (k t) -> p k t", t=128))
        pg = pa[2]
        for k in range(4):
            S.mm(pg[:], zT[:, k, :], gluw[:, k, :], start=(k == 0), stop=(k == 3))
        S.tt("dve", gl[:], pg[:], glub[:], ALU.add)
        S.act(gl[:], gl[:], AF.Sigmoid)
        S.tt("dve", cat[:, 512:1024], z[:], gl[:], ALU.mult)
        p_ = pt[1]
        for k in range(8):
            S.tr(p_[:, k * 128:(k + 1) * 128], cat[:, k * 128:(k + 1) * 128], idb[:], sig=(k == 7))
        S.copy("dve", catT[:], p_[:].re("p (k t) -> p k t", t=128))
        for h in range(2):
            for k in range(8):
                S.mm(pc[:, h * 512:(h + 1) * 512], catT[:, k, :], outw[:, k, h * 512:(h + 1) * 512], start=(k == 0), stop=(k == 7))
        S.tt("dve", tmp[:], pc[:], gateb[j][:], ALU.mult)
        S.tt("pool", xt[i2][:], xt[i2][:], tmp[:], ALU.add)
        S.dma("pool", xm[rows, :], xt[i2][:])
    S.finish([xm])
    return nc, S


def build_p3b():
    nc = bass.Bass("TRN2", target_bir_lowering=False)
    S = Sched(nc)
    di = lambda n, s: S.dram(n, s, F32, kind="ExternalInput")
    xm = di("xm", [TOK, D])
    modT0_d = di("modT0", [128, 144]); ngT0_d = di("ngT0", [128, 3, 8])
    w1a = di("w1a", [D, 2 * DFF]); w2a = di("w2a", [DFF, D])
    cT = di("cT", [128, 8, 2]); modw = di("modw", [D, 9216]); modbT = di("modbT", [128, 72]); ngT1_d = di("ngT1", [128, 3, 8])
    w1b_ = di("w1b", [D, 2 * DFF]); w2b_ = di("w2b", [DFF, D])
    win = di("win", [D, 1536])
    rope_d = di("rope", [2, NCH, 128, 32])
    ident = di("ident", [128, 128])
    x2 = S.dram("x2", [TOK, D], F32, kind="ExternalOutput")
    qkv = S.dram("qkv", [TOK, 1536], F32, kind="ExternalOutput")
    modo = S.dram("modo", [128, 144], F32, kind="ExternalOutput")
    xl0 = S.dram("xl0", [TOK, D], F32, kind="Internal")

    C = alloc_common(S, nc)
    C.idf, C.idb = setup_consts(S, ident[:])
    modT0 = S.sb([128, 72, 2], F32, "modT0")
    S.dma("sync", modT0[:].re("p t j -> p (t j)"), modT0_d[:])
    ngT0 = S.sb([128, 3, 8], F32, "ngT0")
    S.dma("sync", ngT0[:], ngT0_d[:])
    mv0 = mod_derive(S, modT0, ngT0)
    C.gateb = {}
    for j in range(2):
        C.gateb[(2, j)] = gate_bcast(S, mv0["GATE"][:, 2, :, j], C.idf, C.ones, C.pa[j], 0.5, f"gb2{j}")
    for grp in ffn(S, C, xm, xl0, w1a, w2a, mv0, 2, GROUPS):
        pass
    mv1 = mod_vectors(S, cT[:], modw, modbT[:], ngT1_d[:], C.wst, C.pa)
    S.dma("sync", modo[:], mv1["modT"][:].re("p t j -> p (t j)"))
    for j in range(2):
        C.gateb[(0, j)] = gate_bcast(S, mv1["GATE"][:, 0, :, j], C.idf, C.ones, C.pa[2 + j], 0.5, f"gb0{j}")
    cos = S.sb([128, NCH, 32], F32, "cos"); sin = S.sb([128, NCH, 32], F32, "sin")
    S.dma("sync", cos[:], rope_d[0].re("c p f -> p c f"))
    S.dma("sync", sin[:], rope_d[1].re("c p f -> p c f"))
    pst = [S.sb([128, 256], F32, f"pst{i}") for i in range(2)]
    ra = [S.sb([128, 4, 32], F32, f"ra{i}") for i in range(4)]
    psti = [0]

    def after(ci, li, xt):
        j = 0 if ci == 0 else 1
        norm_to_hT(S, C, xt[:], C.hT, li * 128, mv1["G"][:, 1, :, j], mv1["SH"][:, 1, :, j], j)

    for grp in ffn(S, C, xl0, x2, w1b_, w2b_, mv1, 0, GROUPS, after_chunk=after):
        for cb in range(6):
            wst = C.wst[cb % 2]; wb = C.w1b[cb % 2]
            S.dma("sync", wst[:], win[:, cb * 256:(cb + 1) * 256].re("(k p) n -> p k n", p=128))
            S.copy("pool", wb[:], wst[:])
            for li, ci in enumerate(grp):
                pp = C.pa[C.pai % 4]; C.pai += 1
                for k in range(8):
                    S.mm(pp[:, 0:256], C.hT[:, k, li * 128:(li + 1) * 128], wb[:, k, :], start=(k == 0), stop=(k == 7))
                st = pst[psti[0] % 2]; psti[0] += 1
                if cb < 5:
                    pv = pp[:, 0:256].re("p (h two f) -> p h two f", two=2, f=32)
                    sv = st[:].re("p (h two f) -> p h two f", two=2, f=32)
                    cb_ = cos[:, ci, :].re("p (o f) -> p o f", o=1).bc([128, 4, 32])
                    sb_ = sin[:, ci, :].re("p (o f) -> p o f", o=1).bc([128, 4, 32])
                    a, b, c, dd = ra
                    S.tt("dve", a[:], pv[:, :, 0, :], cb_, ALU.mult)
                    S.tt("dve", b[:], pv[:, :, 1, :], sb_, ALU.mult)
                    S.tt("pool", sv[:, :, 0, :], a[:], b[:], ALU.subtract)
                    S.tt("dve", c[:], pv[:, :, 1, :], cb_, ALU.mult)
                    S.tt("dve", dd[:], pv[:, :, 0, :], sb_, ALU.mult)
                    S.tt("pool", sv[:, :, 1, :], c[:], dd[:], ALU.add)
                else:
                    S.copy("act", st[:], pp[:, 0:256])
                S.dma("sync", qkv[ci * 128:(ci + 1) * 128, cb * 256:(cb + 1) * 256], st[:])
    S.finish([x2, qkv, modo])
    return nc, S


NQB = 16


def build_p4a():
    nc = bass.Bass("TRN2", target_bir_lowering=False)
    S = Sched(nc)
    di = lambda n, s: S.dram(n, s, F32, kind="ExternalInput")
    q_d = di("q", [NQB * 128, 1024]); kw_d = di("kw", [18 * 128, 256]); vw_d = di("vw", [18 * 128, 256])
    kc_d = di("kc", [256, 256]); vc_d = di("vc", [256, 256])
    mask_d = di("maskb", [3, 128, 384]); sink_d = di("sinkb", [128, 16])
    x2_d = di("x2", [NQB * 128, 1024]); modT_d = di("modT", [128, 144]); outw_d = di("outw", [D, D]); ident = di("ident", [128, 128])
    x3 = S.dram("x3", [NQB * 128, D], F32, kind="ExternalOutput")

    idf, idb = setup_consts(S, ident[:])
    ones = S.sb([128, 128], F32, "ones"); S.memset("dve", ones[:], 1.0)
    pa = [S.ps([128, 512], F32, f"pa{i}") for i in range(4)]; pai = [0]
    ptb = S.ps([128, 1024], BF16, "ptb")
    po = S.ps([128, 1024], F32, "po")
    modT = S.sb([128, 72, 2], F32, "modT")
    S.dma("sync", modT[:].re("p t j -> p (t j)"), modT_d[:])
    gate5 = gate_bcast(S, modT[:, 5 * 8:6 * 8, 1], idf, ones, pa[0], 1.0, "g5")
    sinkb = S.sb([128, 16], F32, "sinkb"); S.dma("sync", sinkb[:], sink_d[:])
    mt16 = S.sb([128, 16, 3], F32, "mt16")
    S.copy("dve", mt16[:, :, 2], sinkb[:])
    mstage = S.sb([128, 384], F32, "mstage")
    maskb = S.sb([128, 3, 384], BF16, "maskb")
    for i in range(3):
        S.dma("sync", mstage[:], mask_d[i])
        S.copy("dve", maskb[:, i, :], mstage[:])
    outw = S.sb([128, 8, 1024], BF16, "outw")
    wst2 = [S.sb([128, 1024], F32, f"wst2{i}") for i in range(2)]
    for k in range(8):
        S.dma("sync", wst2[k % 2][:], outw_d[k * 128:(k + 1) * 128, :])
        S.copy("pool", outw[:, k, :], wst2[k % 2][:])
    kT = S.sb([64, 4, 18 * 128], BF16, "kT"); kcT = S.sb([64, 4, 256], BF16, "kcT")
    vw = S.sb([128, 18, 256], BF16, "vw"); vc = S.sb([128, 2, 256], BF16, "vc")
    kst = [S.sb([128, 256], F32, f"kst{i}") for i in range(2)]; kb = [S.sb([128, 256], BF16, f"kb{i}") for i in range(2)]
    vst = [S.sb([128, 256], F32, f"vst{i}") for i in range(2)]
    for blk in range(20):
        src_k = kw_d[blk * 128:(blk + 1) * 128, :] if blk < 18 else kc_d[(blk - 18) * 128:(blk - 17) * 128, :]
        src_v = vw_d[blk * 128:(blk + 1) * 128, :] if blk < 18 else vc_d[(blk - 18) * 128:(blk - 17) * 128, :]
        S.dma("sync", kst[blk % 2][:], src_k)
        S.dma("pool", vst[blk % 2][:], src_v)
        S.copy("pool", kb[blk % 2][:], kst[blk % 2][:])
        for kv in range(4):
            S.tr(ptb[0:64, kv * 128:(kv + 1) * 128], kb[blk % 2][:, kv * 64:(kv + 1) * 64], idb[:], sig=(kv == 3))
        dstk = kT[:, :, blk * 128:(blk + 1) * 128] if blk < 18 else kcT[:, :, (blk - 18) * 128:(blk - 17) * 128]
        S.copy("act", dstk, ptb[0:64, 0:512].re("p (a t) -> p a t", t=128))
        dstv = vw[:, blk, :] if blk < 18 else vc[:, blk - 18, :]
        S.copy("dve", dstv, vst[blk % 2][:])
    qst = [S.sb([128, 1024], F32, f"qst{i}") for i in range(2)]
    qb = S.sb([128, 1024], BF16, "qb")
    qT = S.sb([64, 16, 128], BF16, "qT")
    Pm = [S.sb([128, 640], BF16, f"Pm{i}") for i in range(2)]
    PT = [S.sb([128, 5, 128], BF16, f"PT{i}") for i in range(2)]
    rs = [S.sb([128, 4], F32, f"rs{i}") for i in range(2)]
    negm = [S.sb([128, 1], F32, f"negm{i}") for i in range(2)]
    rden = S.sb([128, 16], F32, "rden")
    ob = S.sb([128, 1024], BF16, "ob"); oT = S.sb([128, 8, 128], BF16, "oT")
    xt = [S.sb([128, 1024], F32, f"xt{i}") for i in range(2)]
    tmp = S.sb([128, 1024], F32, "tmp")
    for i in range(NQB):
        rows = slice(i * 128, (i + 1) * 128)
        S.dma("sync", qst[i % 2][:], q_d[rows, :])
        S.dma("pool", xt[i % 2][:], x2_d[rows, :])
        S.act(qb[:], qst[i % 2][:], AF.Copy, scale=0.125)
        for half in range(2):
            for hh in range(8):
                hd = half * 8 + hh
                S.tr(ptb[0:64, hh * 128:(hh + 1) * 128], qb[:, hd * 64:(hd + 1) * 64], idb[:], sig=(hh == 7))
            S.copy("act", qT[:, half * 8:(half + 1) * 8, :], ptb[0:64, :].re("p (a t) -> p a t", t=128))
        mi = 0 if i == 0 else (2 if i == NQB - 1 else 1)
        for hd in range(16):
            kv = hd // 4
            i2 = hd % 2
            pw = pa[pai[0] % 4]; pai[0] += 1
            pcx = pa[pai[0] % 4]; pai[0] += 1
            S.mm(pw[:, 0:384], qT[:, hd, :], kT[:, kv, i * 128:(i + 3) * 128], start=True, stop=False, sig=False)
            S.mm(pw[:, 0:384], idb[:], maskb[:, mi, :], start=False, stop=True)
            S.mm(pcx[:, 0:256], qT[:, hd, :], kcT[:, kv, :])
            S.red("dve", mt16[:, hd, 0:1], pw[:, 0:384], ALU.max)
            S.red("dve", mt16[:, hd, 1:2], pcx[:, 0:256], ALU.max)
            S.red("dve", negm[i2][:], mt16[:, hd, :], ALU.max)
            S.ts("dve", negm[i2][:], negm[i2][:], -1.0, ALU.mult)
            S.act(Pm[i2][:, 0:384], pw[:, 0:384], AF.Exp, bias=negm[i2][:, 0:1], accum=rs[i2][:, 0:1])
            S.act(Pm[i2][:, 384:640], pcx[:, 0:256], AF.Exp, bias=negm[i2][:, 0:1], accum=rs[i2][:, 1:2])
            S.act(rs[i2][:, 2:3], sinkb[:, hd:hd + 1], AF.Exp, bias=negm[i2][:, 0:1])
            S.red("dve", rs[i2][:, 3:4], rs[i2][:, 0:3], ALU.add)
            S.recip(rden[:, hd:hd + 1], rs[i2][:, 3:4])
            for j in range(5):
                S.tr(ptb[:, j * 128:(j + 1) * 128], Pm[i2][:, j * 128:(j + 1) * 128], idb[:], sig=(j == 4))
            S.copy("dve" if hd % 2 == 0 else "act", PT[i2][:], ptb[:, 0:640].re("p (a t) -> p a t", t=128))
            for j in range(5):
                vsrc = vw[:, i + j, kv * 64:(kv + 1) * 64] if j < 3 else vc[:, j - 3, kv * 64:(kv + 1) * 64]
                S.mm(po[:, hd * 64:(hd + 1) * 64], PT[i2][:, j, :], vsrc, start=(j == 0), stop=(j == 4), sig=(j == 4))
        S.tt("dve", ob[:].re("p (h k) -> p h k", k=64), po[:].re("p (h k) -> p h k", k=64),
             rden[:].re("p (h o) -> p h o", o=1).bc([128, 16, 64]), ALU.mult)
        for k in range(8):
            S.tr(ptb[:, k * 128:(k + 1) * 128], ob[:, k * 128:(k + 1) * 128], idb[:], sig=(k == 7))
        S.copy("act", oT[:], ptb[:].re("p (a t) -> p a t", t=128))
        for h in range(2):
            py = pa[pai[0] % 4]; pai[0] += 1
            for k in range(8):
                S.mm(py[:], oT[:, k, :], outw[:, k, h * 512:(h + 1) * 512], start=(k == 0), stop=(k == 7))
            S.tt("dve", tmp[:, h * 512:(h + 1) * 512], py[:], gate5[:, h * 512:(h + 1) * 512], ALU.mult)
        S.tt("pool", xt[i % 2][:], xt[i % 2][:], tmp[:], ALU.add)
        S.dma("pool", x3[rows, :], xt[i % 2][:])
    S.finish([x3])
    return nc, S


GROUPS16 = [list(range(0, 8)), list(range(8, 16))]


def build_p4b():
    nc = bass.Bass("TRN2", target_bir_lowering=False)
    S = Sched(nc)
    di = lambda n, s: S.dram(n, s, F32, kind="ExternalInput")
    x3 = di("x3", [NQB * 128, D]); modT_d = di("modT", [128, 144]); ngT_d = di("ngT", [128, 3, 8])
    w1 = di("w1", [D, 2 * DFF]); w2 = di("w2", [DFF, D]); fing_d = di("fing", [128, 1024]); ident = di("ident", [128, 128])
    out = S.dram("out", [NQB * 128, D], F32, kind="ExternalOutput")
    C = alloc_common(S, nc)
    C.idf, C.idb = setup_consts(S, ident[:])
    modT = S.sb([128, 72, 2], F32, "modT")
    S.dma("sync", modT[:].re("p t j -> p (t j)"), modT_d[:])
    ngT = S.sb([128, 3, 8], F32, "ngT"); S.dma("sync", ngT[:], ngT_d[:])
    mv = mod_derive(S, modT, ngT)
    C.gateb = {(2, 1): gate_bcast(S, mv["GATE"][:, 2, :, 1], C.idf, C.ones, C.pa[0], 0.5, "gb21")}
    fing = S.sb([128, 1024], F32, "fing"); S.dma("sync", fing[:], fing_d[:])
    ot = [S.sb([128, 1024], F32, f"ot{i}") for i in range(2)]
    oi = [0]

    def after(ci, li, xt):
        ss = C.small[C.si % 4]; C.si += 1
        S.act(C.junk[:], xt[:], AF.Square, accum=ss[:, 0:1])
        S.ts("dve", ss[:, 1:2], ss[:, 0:1], 1.0 / D, ALU.mult, 1e-6, ALU.add)
        S.act(ss[:, 3:4], ss[:, 1:2], AF.Sqrt)
        S.recip(ss[:, 2:3], ss[:, 3:4])
        o = ot[oi[0] % 2]; oi[0] += 1
        S.stt("dve", o[:], xt[:], ss[:, 2:3], fing[:], ALU.mult, ALU.mult)
        S.dma("sync", out[ci * 128:(ci + 1) * 128, :], o[:])

    for grp in ffn(S, C, x3, None, w1, w2, mv, 2, GROUPS16, after_chunk=after, jf=lambda ci: 1):
        pass
    S.finish([out])
    return nc, S

LC = 256; NLAT = 4096; L = 4352
GRID_W = 64; ROPE_BASE = 10000.0
def core_tok(seq, h):
    return np.concatenate([seq[h * 128:(h + 1) * 128], seq[256 + h * 2048:256 + (h + 1) * 2048]], 0)
def uncore_tok(parts):
    return np.concatenate([parts[0][:128], parts[1][:128], parts[0][128:], parts[1][128:]], 0)
def colT(v, k=8):
    return np.ascontiguousarray(v.reshape(k, 128).T)
def bc(v):
    return np.ascontiguousarray(np.broadcast_to(v[None, :], (128, v.shape[0])))
def rope_tables(h):
    t = np.arange(h * 2048, (h + 1) * 2048)
    row = (t // GRID_W).astype(np.float32); col = (t % GRID_W).astype(np.float32)
    inv = (ROPE_BASE ** (-np.arange(0, 32, 2, dtype=np.float32) / 32)).astype(np.float32)
    ang = np.concatenate([row[:, None] * inv, col[:, None] * inv], -1).astype(np.float32)
    cos = np.concatenate([np.ones((128, 32), np.float32), np.cos(ang)], 0).reshape(17, 128, 32)
    sin = np.concatenate([np.zeros((128, 32), np.float32), np.sin(ang)], 0).reshape(17, 128, 32)
    return np.stack([cos, sin], 0).astype(np.float32)

ORDER = [np.arange(L), np.concatenate([np.arange(LC - 1, -1, -1), np.arange(L - 1, LC - 1, -1)])]
_EYE = np.eye(128, dtype=np.float32)


def rw_masks():
    m = np.zeros((5, 128, 128), np.float32)
    s = np.arange(128)[:, None]; t = np.arange(128)[None, :]
    same = (s // 64) == (t // 64)
    m[0] = same & (s < t); m[1] = same & (s <= t); m[2] = same & (s > t); m[4] = same & (s >= t)
    m[3] = np.eye(128)
    return m


def shifted(p_nat):
    prev = np.zeros_like(p_nat); nxt = np.zeros_like(p_nat)
    prev[1:LC] = p_nat[0:LC - 1]; prev[LC + 1:] = p_nat[LC:-1]
    nxt[0:LC - 1] = p_nat[1:LC]; nxt[LC:-1] = p_nat[LC + 1:]
    return np.stack([p_nat, prev, nxt], 0)


def rw_inputs(d, p_nat, dr):
    e = 0
    s3 = shifted(p_nat[:, :1792])[:, ORDER[dr]]
    mu = d['rwkv_mu'][e]
    mul = np.zeros((128, 3, 2), np.float32)
    for j in range(2):
        mul[0:64, 0, j] = mu[j, 1536:1600]; mul[0:64, 1, j] = mu[j, 1600:1664]; mul[:, 2, j] = mu[j, 1664:1792]
    T_ = lambda a: np.ascontiguousarray(a.transpose(0, 2, 1))
    return dict(rkv3=np.ascontiguousarray(s3[:, :, :1536]), wl3=T_(s3[:, :, 1536:1600]), al3=T_(s3[:, :, 1600:1664]), gl3=T_(s3[:, :, 1664:1792]),
                mub=np.stack([bc(mu[0, :1536]), bc(mu[1, :1536])], 0), mul=mul,
                kkb=bc(d['rwkv_k_k'][e]), kab=bc(d['rwkv_k_a'][e]),
                w2a=np.ascontiguousarray(np.concatenate([d['rwkv_w2'][e, dr], d['rwkv_w0'][e, dr][None]], 0)),
                a2a=np.ascontiguousarray(np.concatenate([d['rwkv_a2'][e, dr], d['rwkv_a0'][e, dr][None]], 0)),
                g2=d['rwkv_g2'][e], msk=rw_masks())


def s5_inputs(d, u_seq, dr):
    uT = np.ascontiguousarray(u_seq[ORDER[dr]].T)
    uT32 = np.ascontiguousarray(uT.reshape(16, 32, L).transpose(1, 0, 2))

    def st(a):
        return np.ascontiguousarray(a.reshape(16, 128).T)

    def pad(a):
        out = np.zeros((128, 16, 32), np.float32)
        for g in range(32):
            out[(g % 2) * 64:(g % 2) * 64 + 64, g // 2, (g % 2) * 16:(g % 2) * 16 + 16] = a[g]
        return out
    return dict(uT32=uT32, are=st(d['s5_a_re'][0, dr]), aim=st(d['s5_a_im'][0, dr]),
                lst=st(np.repeat(d['s5_log_step'][0, dr][:, None], 64, 1)),
                bre=pad(d['s5_b_re'][0, dr]), bim=pad(d['s5_b_im'][0, dr]),
                cre=pad(d['s5_c_re'][0, dr].transpose(0, 2, 1)), cim=pad(d['s5_c_im'][0, dr].transpose(0, 2, 1)),
                ident=_EYE)


def attn_masks(h):
    qi = np.arange(128)[:, None]; mj = np.arange(384)[None, :] - 128
    valid = np.abs(mj - qi) <= 128
    NEG = -30000.0
    gen = np.where(valid, 0.0, NEG).astype(np.float32)
    left_inv = gen.copy(); left_inv[:, :128] = NEG
    right_inv = gen.copy(); right_inv[:, 256:] = NEG
    return np.stack([left_inv if h == 0 else gen, gen, right_inv if h == 1 else gen], 0)


def p4a_inputs(qkv_halves, x2_halves, modT1, h, d):
    k_lat = np.concatenate([qkv_halves[0][128:, 1024:1280], qkv_halves[1][128:, 1024:1280]], 0)
    v_lat = np.concatenate([qkv_halves[0][128:, 1280:1536], qkv_halves[1][128:, 1280:1536]], 0)
    kc = np.concatenate([qkv_halves[0][:128, 1024:1280], qkv_halves[1][:128, 1024:1280]], 0)
    vc = np.concatenate([qkv_halves[0][:128, 1280:1536], qkv_halves[1][:128, 1280:1536]], 0)
    pad = np.zeros((128, 256), np.float32)
    kp = np.concatenate([pad, k_lat, pad], 0); vp = np.concatenate([pad, v_lat, pad], 0)
    kw = kp[h * 2048:h * 2048 + 18 * 128]; vw = vp[h * 2048:h * 2048 + 18 * 128]
    return dict(q=np.ascontiguousarray(qkv_halves[h][128:, :1024]), kw=np.ascontiguousarray(kw), vw=np.ascontiguousarray(vw),
                kc=np.ascontiguousarray(kc), vc=np.ascontiguousarray(vc),
                maskb=attn_masks(h), sinkb=bc(d['attn_sink'][0]), x2=np.ascontiguousarray(x2_halves[h][128:]), modT=modT1,
                outw=d['attn_out_w'][0], ident=_EYE)


def _run(nc, in_maps):
    res = run_bass_kernel_spmd(nc, in_maps, core_ids=list(range(8)))
    return res.results


def kernel(**inputs):
    d = {k: np.ascontiguousarray(np.asarray(v, dtype=np.float32)) for k, v in inputs.items()}
    B = 4
    ng = lambda l: np.ascontiguousarray(np.stack([colT(d['norm_g'][l, i]) for i in range(3)], 1))
    cores = [(b, h) for b in range(B) for h in range(2)]
    seq = [np.concatenate([d['ctx'][b], d['x'][b]], 0) for b in range(B)]
    cTs = [np.ascontiguousarray(np.stack([colT(d['c_ctx']), colT(d['c'][b])], -1)) for b in range(B)]
    nc, _ = build_p1()
    r1 = _run(nc, [dict(xin=core_tok(seq[b], h), cT=cTs[b], modw=d['mod_w'][0], modbT=colT(d['mod_b'][0], 72), ngT=ng(0),
                        w1=d['ffn_w1'][0, 0], w2=d['ffn_w2'][0, 0], win=d['ab_in_w'][0], ident=_EYE) for (b, h) in cores])
    p_full = [uncore_tok([r1[2 * b]['p'], r1[2 * b + 1]['p']]) for b in range(B)]
    nc, _ = build_p2a(MD=F32)
    r2a = _run(nc, [rw_inputs(d, p_full[b], dr) for (b, dr) in cores])
    nc, _ = build_p2b()
    r2b = _run(nc, [s5_inputs(d, p_full[b][:, 1792:], dr) for (b, dr) in cores])

    def unperm(a, dr):
        o = np.empty_like(a); o[ORDER[dr]] = a
        return o
    in3 = []
    bcs = np.ascontiguousarray(np.stack([bc(d['rwkv_lnx_g'][0]), bc(d['rwkv_lnx_b'][0]), bc(d['rwkv_r_k'][0].reshape(-1)), bc(d['s5_d'][0]),
                                         bc(d['s5_glu_b'][0])], 0))
    for (b, h) in cores:
        ra0, ra1 = r2a[2 * b], r2a[2 * b + 1]
        nat = [unperm(ra0['yd'], 0), unperm(ra1['yd'], 1), unperm(ra0['kdo'], 0), unperm(ra1['kdo'], 1),
               unperm(ra0['rvo'][:, :512], 0), unperm(ra0['rvo'][:, 512:], 0), unperm(ra0['gto'], 0)] if h == 0 else nat
        s5n = [unperm(r2b[2 * b]['yd'], 0), unperm(r2b[2 * b + 1]['yd'], 1), p_full[b][:, 1792:]] if h == 0 else s5n
        in3.append(dict(x1=r1[2 * b + h]['x1'], rw=np.ascontiguousarray(np.stack([core_tok(a, h) for a in nat], 0)),
                        s5=np.ascontiguousarray(np.stack([core_tok(a, h) for a in s5n], 0)), bcs=bcs, gluw=d['s5_glu_w'][0], outw=d['ab_out_w'][0],
                        modT=r1[2 * b + h]['modo'], ident=_EYE))
    nc, _ = build_p3a()
    r3a = _run(nc, in3)
    nc, _ = build_p3b()
    r3b = _run(nc, [dict(xm=r3a[2 * b + h]['xm'], modT0=r1[2 * b + h]['modo'], ngT0=ng(0), w1a=d['ffn_w1'][0, 1], w2a=d['ffn_w2'][0, 1],
                         cT=cTs[b], modw=d['mod_w'][1], modbT=colT(d['mod_b'][1], 72), ngT1=ng(1),
                         w1b=d['ffn_w1'][1, 0], w2b=d['ffn_w2'][1, 0], win=d['attn_in_w'][0], rope=rope_tables(h), ident=_EYE) for (b, h) in cores])
    nc, _ = build_p4a()
    r4a = _run(nc, [p4a_inputs([r3b[2 * b]['qkv'], r3b[2 * b + 1]['qkv']], [r3b[2 * b]['x2'], r3b[2 * b + 1]['x2']], r3b[2 * b + h]['modo'], h, d)
                    for (b, h) in cores])
    nc, _ = build_p4b()
    fing = bc(d['final_g'])
    r4b = _run(nc, [dict(x3=r4a[2 * b + h]['x3'], modT=r3b[2 * b + h]['modo'], ngT=ng(1), w1=d['ffn_w1'][1, 1], w2=d['ffn_w2'][1, 1], fing=fing,
                         ident=_EYE) for (b, h) in cores])
    out = np.stack([np.concatenate([r4b[2 * b]['out'], r4b[2 * b + 1]['out']], 0) for b in range(B)], 0)
    return out.astype(np.float32)
```

```python
import numpy as np
import concourse.bass as bass
import concourse.mybir as mybir
from concourse.bass_utils import run_bass_kernel_spmd

F32 = mybir.dt.float32
BF16 = mybir.dt.bfloat16
ALU = mybir.AluOpType
AF = mybir.ActivationFunctionType
AX = mybir.AxisListType


class T:
    def __init__(self, h, name=""):
        self.h = h
        self.name = name
        self.last_w = None
        self.readers = []

    def __getitem__(self, idx):
        return V(self, self.h[idx])

    def re(self, pat, **kw):
        return self[:].re(pat, **kw)


class V:
    def __init__(self, t, ap):
        self.t = t
        self.ap = ap

    def __getitem__(self, idx):
        return V(self.t, self.ap[idx])

    def re(self, pat, **kw):
        return V(self.t, self.ap.rearrange(pat, **kw))

    def bc(self, shape):
        return V(self.t, self.ap.to_broadcast(shape))


def _ap(x):
    return x.ap if isinstance(x, V) else x


def _ts(xs):
    out = []
    for x in xs:
        if isinstance(x, V):
            out.append(x.t)
        elif isinstance(x, T):
            out.append(x)
    return out


class Sched:
    ENG = ["pe", "act", "dve", "pool", "sync"]

    def __init__(self, nc, n_dma_sems=6, same_engine_sync=True):
        self.nc = nc
        self.q = {e: [] for e in self.ENG}
        self.cnt = {e: 0 for e in self.ENG}
        self.unsig = {e: False for e in self.ENG}
        self.sem = {e: nc.alloc_semaphore(f"s_{e}") for e in ["pe", "act", "dve", "pool"]}
        self.waited = {e: {} for e in self.ENG}
        self.same_engine_sync = same_engine_sync
        self.dsem = {}
        self.dcnt = {}
        self.drr = {}
        for qn in ["sync", "pool", "act"]:
            self.dsem[qn] = [nc.alloc_semaphore(f"d_{qn}{i}") for i in range(n_dma_sems)]
            self.dcnt[qn] = [0] * n_dma_sems
            self.drr[qn] = 0
        self.n_inst = 0
        self.uid = 0

    ARENA_LO = 16640
    ARENA_HI = 229344

    def sb(self, shape, dt=F32, name=None):
        self.uid += 1
        name = name or f"t{self.uid}"
        if not hasattr(self, "off"):
            self.off = self.ARENA_LO
        n = 1
        for x in shape[1:]:
            n *= x
        size = n * (2 if dt == BF16 else 4)
        size = (size + 31) // 32 * 32
        assert self.off + size <= self.ARENA_HI, f"SBUF arena overflow allocating {name} {shape}: off={self.off} size={size}"
        t = T(self.nc.alloc_sbuf_tensor_at(f"{name}_{self.uid}", list(shape), dt, offset=self.off), name)
        self.off += size
        return t

    def mark(self):
        if not hasattr(self, "off"):
            self.off = self.ARENA_LO
        return self.off

    def reset(self, mark):
        self.off = mark

    def barrier(self):
        targets = []
        for e in ("pe", "act", "dve", "pool"):
            assert not self.unsig[e], f"barrier with unsignaled op on {e}"
            if self.cnt[e] > 0:
                targets.append((self.sem[e], self.cnt[e]))
        for qn in self.dsem:
            for sm, c in zip(self.dsem[qn], self.dcnt[qn]):
                if c > 0:
                    targets.append((sm, c))
        for e in self.ENG:
            waits = []
            for (sm, val) in targets:
                if e in self.sem and sm is self.sem[e]:
                    continue
                if self.waited[e].get(id(sm), 0) >= val:
                    continue
                self.waited[e][id(sm)] = val
                waits.append((sm, val))
            if waits:
                self.q[e].append((None, waits, None))

    def ps(self, shape, dt=F32, name=None):
        self.uid += 1
        name = name or f"p{self.uid}"
        return T(self.nc.alloc_psum_tensor(f"{name}_{self.uid}", list(shape), dt), name)

    def dram(self, name, shape, dt=F32, kind="Internal"):
        return T(self.nc.dram_tensor(name, list(shape), dt, kind=kind), name)

    def _collect(self, eng, reads, writes):
        toks = []
        for t in _ts(reads):
            if t.last_w is not None:
                toks.append(t.last_w)
        for t in _ts(writes):
            if t.last_w is not None:
                toks.append(t.last_w)
            toks.extend(t.readers)
        best = {}
        for (kind, key, sem, val) in toks:
            if kind == "eng" and key == eng:
                if eng in ("pe", "sync") or not self.same_engine_sync:
                    continue
            k = id(sem)
            if k not in best or best[k][1] < val:
                best[k] = (sem, val)
        waits = []
        for k, (sem, val) in best.items():
            if self.waited[eng].get(k, 0) >= val:
                continue
            self.waited[eng][k] = val
            waits.append((sem, val))
        return waits

    def _mark(self, tok, reads, writes):
        for t in _ts(reads):
            t.readers.append(tok)
        for t in _ts(writes):
            t.last_w = tok
            t.readers = []

    def op(self, eng, fn, reads, writes, sig=True):
        waits = self._collect(eng, reads, writes)
        if sig:
            self.cnt[eng] += 1
            tok = ("eng", eng, self.sem[eng], self.cnt[eng])
            self.unsig[eng] = False
        else:
            tok = ("eng", eng, self.sem[eng], self.cnt[eng] + 1)
            self.unsig[eng] = True
        self.q[eng].append((fn, waits, (self.sem[eng], 1) if sig else None))
        self._mark(tok, reads, writes)
        self.n_inst += 1

    def dma(self, qn, out, in_, extra_reads=(), extra_writes=(), **kw):
        eng = qn
        i = self.drr[qn]
        self.drr[qn] = (i + 1) % len(self.dsem[qn])
        sem = self.dsem[qn][i]
        reads = [in_] + list(extra_reads)
        writes = [out] + list(extra_writes)
        waits = self._collect(eng, reads, writes)
        prev = self.dcnt[qn][i]
        if prev > 0 and self.waited[eng].get(id(sem), 0) < prev:
            self.waited[eng][id(sem)] = prev
            waits.append((sem, prev))
        self.dcnt[qn][i] += 16
        tok = ("dma", qn, sem, self.dcnt[qn][i])
        o, a = _ap(out), _ap(in_)
        self.q[eng].append((lambda e: e.dma_start(out=o, in_=a, **kw), waits, (sem, 16)))
        self._mark(tok, reads, writes)
        self.n_inst += 1
        return tok

    def mm(self, out, lhsT, rhs, start=True, stop=True, sig=None):
        if sig is None:
            sig = stop
        o, l, r = _ap(out), _ap(lhsT), _ap(rhs)
        self.op("pe", lambda e: e.matmul(o, l, r, start=start, stop=stop), [lhsT, rhs], [out], sig=sig)

    def tr(self, out, in_, ident, sig=True):
        o, i, d = _ap(out), _ap(in_), _ap(ident)
        self.op("pe", lambda e: e.transpose(o, i, d), [in_, ident], [out], sig=sig)

    def act(self, out, in_, func, bias=None, scale=1.0, accum=None, eng="act"):
        o, i = _ap(out), _ap(in_)
        kw = {}
        reads = [in_]
        writes = [out]
        if bias is not None:
            kw["bias"] = _ap(bias)
            reads.append(bias)
        kw["scale"] = _ap(scale)
        if isinstance(scale, V):
            reads.append(scale)
        if accum is not None:
            kw["accum_out"] = _ap(accum)
            writes.append(accum)
        self.op("act", lambda e: e.activation(o, i, func, **kw), reads, writes)

    def tt(self, eng, out, in0, in1, op):
        o, a, b = _ap(out), _ap(in0), _ap(in1)
        self.op(eng, lambda e: e.tensor_tensor(o, a, b, op), [in0, in1], [out])

    def ts(self, eng, out, in0, s1, op0, s2=None, op1=None, accum=None):
        o, a = _ap(out), _ap(in0)
        reads = [in0] + [s for s in (s1, s2) if isinstance(s, V)]
        writes = [out] + ([accum] if accum is not None else [])
        kw = {}
        if op1 is not None:
            kw["op1"] = op1
        if accum is not None:
            kw["accum_out"] = _ap(accum)
        self.op(eng, lambda e: e.tensor_scalar(o, a, _ap(s1), _ap(s2) if s2 is not None else None, op0, **kw), reads, writes)

    def stt(self, eng, out, in0, scalar, in1, op0, op1):
        o, a, b = _ap(out), _ap(in0), _ap(in1)
        reads = [in0, in1] + ([scalar] if isinstance(scalar, V) else [])
        self.op(eng, lambda e: e.scalar_tensor_tensor(o, a, _ap(scalar), b, op0, op1), reads, [out])

    def red(self, eng, out, in_, op, axis=AX.X):
        o, a = _ap(out), _ap(in_)
        self.op(eng, lambda e: e.tensor_reduce(o, a, axis, op), [in_], [out])

    def copy(self, eng, out, in_):
        o, a = _ap(out), _ap(in_)
        if eng == "act":
            self.op(eng, lambda e: e.copy(o, a), [in_], [out])
        else:
            self.op(eng, lambda e: e.tensor_copy(o, a), [in_], [out])

    def memset(self, eng, out, val):
        o = _ap(out)
        self.op(eng, lambda e: e.memset(o, val), [], [out])

    def scan(self, out, d0, d1, init, op0=ALU.mult, op1=ALU.add):
        o, a, b, i = _ap(out), _ap(d0), _ap(d1), _ap(init)
        reads = [d0, d1] + ([init] if isinstance(init, V) else [])
        self.op("dve", lambda e: e.tensor_tensor_scan(o, a, b, i, op0, op1), reads, [out])

    def recip(self, out, in_):
        o, a = _ap(out), _ap(in_)
        self.op("dve", lambda e: e.reciprocal(o, a), [in_], [out])

    def finish(self, final_tiles):
        nc = self.nc
        toks = []
        for t in final_tiles:
            if t.last_w is not None:
                toks.append(t.last_w)
        fin = []
        best = {}
        for (_, _, sem, val) in toks:
            if id(sem) not in best or best[id(sem)][1] < val:
                best[id(sem)] = (sem, val)
        for qn in self.dsem:
            for s, c in zip(self.dsem[qn], self.dcnt[qn]):
                if c > 0:
                    best[id(s)] = (s, max(c, best.get(id(s), (s, 0))[1]))
        for e in ("pe", "act", "dve", "pool"):
            if self.cnt[e] > 0 or self.unsig[e]:
                assert not self.unsig[e], f"engine {e} ends with unsignaled instruction"
                best[id(self.sem[e])] = (self.sem[e], self.cnt[e])
        fin = list(best.values())
        q = self.q
        with nc.Block() as block:
            def replay(lst):
                def f(e):
                    for (fn, waits, inc) in lst:
                        for (sem, val) in waits:
                            e.wait_ge(sem, val)
                        if fn is None:
                            continue
                        ins = fn(e)
                        if inc is not None:
                            ins.then_inc(inc[0], inc[1])
                return f

            @block.tensor
            def _(e):
                replay(q["pe"])(e)

            @block.scalar
            def _(e):
                replay(q["act"])(e)

            @block.vector
            def _(e):
                replay(q["dve"])(e)

            @block.gpsimd
            def _(e):
                replay(q["pool"])(e)

            @block.sync
            def _(e):
                replay(q["sync"])(e)
                for (sem, val) in fin:
                    e.wait_ge(sem, val)
        return nc

import math

NCH = 34
TOK = NCH * 128
LSEQ = 4352
D = 1024
DFF = 2816
NFT = 22
GN_EPS = 64e-5
NEGC = -math.exp(-0.5)
NST = 16
NQB = 32


def prow(n):
    return n * 128 + (1 if n < 2 else 3)


class Ctx:
    pass


def setup_consts(S, ident_d):
    idf = S.sb([128, 128], F32, "idf")
    idb = S.sb([128, 128], BF16, "idb")
    S.dma("sync", idf[:], ident_d)
    S.copy("dve", idb[:], idf[:])
    return idf, idb


def mod_derive(S, modT, ngT):
    G = S.sb([128, 3, 8, 2], F32, "G")
    SH = S.sb([128, 3, 8, 2], F32, "SH")
    GATE = S.sb([128, 3, 8, 2], F32, "GATE")
    for i in range(3):
        for j in range(2):
            S.stt("dve", G[:, i, :, j], modT[:, (3 * i + 1) * 8:(3 * i + 2) * 8, j], 1.0, ngT[:, i, :], ALU.add, ALU.mult)
            S.copy("dve", SH[:, i, :, j], modT[:, (3 * i) * 8:(3 * i + 1) * 8, j])
            S.copy("dve", GATE[:, i, :, j], modT[:, (3 * i + 2) * 8:(3 * i + 3) * 8, j])
    return dict(G=G, SH=SH, GATE=GATE, modT=modT)


def mod_vectors(S, cT_d, modw_d, modbT_d, ngT_d, wst, pm):
    cT = S.sb([128, 8, 2], F32, "cT")
    sc = S.sb([128, 8, 2], F32, "sc")
    S.dma("sync", cT[:], cT_d)
    S.act(sc[:], cT[:], AF.Silu)
    modbT = S.sb([128, 72], F32, "modbT")
    S.dma("sync", modbT[:], modbT_d)
    ngT = S.sb([128, 3, 8], F32, "ngT")
    S.dma("sync", ngT[:], ngT_d)
    modT = S.sb([128, 72, 2], F32, "modT")
    for nb in range(36):
        w = wst[nb % 2]
        S.dma("sync" if nb % 2 == 0 else "pool", w[:], modw_d[:, nb * 256:(nb + 1) * 256].re("(k p) n -> p k n", p=128))
        for ct in range(2):
            t = nb * 2 + ct
            for k in range(8):
                S.mm(pm[:, t * 2:t * 2 + 2], w[:, k, ct * 128:(ct + 1) * 128], sc[:, k, :], start=(k == 0), stop=(k == 7),
                     sig=(k == 7 and t % 2 == 1))
    for j in range(2):
        S.tt("dve", modT[:, :, j], pm[:, 0:144].re("p (t j) -> p t j", j=2)[:, :, j], modbT[:], ALU.add)
    return mod_derive(S, modT, ngT)


def gate_bcast(S, gate_col, idf, ones, ps, scale, name):
    out = S.sb([128, 1024], F32, name)
    dg = S.sb([128, 128], F32, name + "_dg")
    for k in range(8):
        S.ts("dve", dg[:], idf[:], gate_col[:, k:k + 1], ALU.mult)
        S.mm(ps[:, (k % 4) * 128:(k % 4 + 1) * 128], ones[:], dg[:], start=True, stop=True)
        S.ts("dve", out[:, k * 128:(k + 1) * 128], ps[:, (k % 4) * 128:(k % 4 + 1) * 128], float(scale), ALU.mult)
    return out


def norm_to_hT(S, C, xt, hT, col0, G, SH):
    ss = C.small[C.si % 4]; C.si += 1
    S.act(C.junk[:], xt, AF.Square, accum=ss[:, 0:1])
    S.ts("dve", ss[:, 1:2], ss[:, 0:1], 1.0 / D, ALU.mult, 1e-6, ALU.add)
    S.act(ss[:, 3:4], ss[:, 1:2], AF.Sqrt)
    S.recip(ss[:, 2:3], ss[:, 3:4])
    xn = C.xn[C.xi % 2]; C.xi += 1
    S.act(xn[:], xt, AF.Copy, scale=ss[:, 2:3])
    pt = C.pt[C.pti % 2]; C.pti += 1
    for k in range(8):
        S.tr(pt[:, k * 128:(k + 1) * 128], xn[:, k * 128:(k + 1) * 128], C.idb[:], sig=(k == 7))
    for k in range(8):
        S.ts("dve", hT[:, k, col0:col0 + 128], pt[:, k * 128:(k + 1) * 128], G[:, k:k + 1], ALU.mult, SH[:, k:k + 1], ALU.add)


def ffn(S, C, xs, xd, w1_d, w2_d, mv, ni, groups, jf, after_chunk=None):
    G, SH = mv["G"], mv["SH"]
    w2b = C.w2b
    first = True
    for grp in groups:
        nt = len(grp) * 128
        for li, ci in enumerate(grp):
            xt = C.xt[C.xti % 3]; C.xti += 1
            S.dma("sync", xt[:], xs(ci))
            j = jf(ci)
            norm_to_hT(S, C, xt[:], C.hT, li * 128, G[:, ni, :, j], SH[:, ni, :, j])
        for ft in range(NFT):
            wst = C.wst[ft % 2]
            wb = C.w1b[ft % 2]
            S.dma("sync", wst[:, :, 0:128], w1_d[:, ft * 128:(ft + 1) * 128].re("(k p) n -> p k n", p=128))
            S.dma("pool", wst[:, :, 128:256], w1_d[:, DFF + ft * 128:DFF + (ft + 1) * 128].re("(k p) n -> p k n", p=128))
            S.copy("pool", wb[:], wst[:])
            if first:
                w2s = C.w2st[ft % 2]
                S.dma("sync", w2s[:], w2_d[ft * 128:(ft + 1) * 128, :])
                S.copy("pool", w2b[:, ft, :], w2s[:])
            for b0 in range(0, nt, 512):
                bw = min(512, nt - b0)
                pg = C.pa[C.pai % 4]; C.pai += 1
                pu = C.pa[C.pai % 4]; C.pai += 1
                for k in range(8):
                    S.mm(pg[:, 0:bw], wb[:, k, 0:128], C.hT[:, k, b0:b0 + bw], start=(k == 0), stop=(k == 7))
                for k in range(8):
                    S.mm(pu[:, 0:bw], wb[:, k, 128:256], C.hT[:, k, b0:b0 + bw], start=(k == 0), stop=(k == 7))
                sg = C.sg[C.sgi % 2]; C.sgi += 1
                S.act(sg[:, 0:bw], pg[:, 0:bw], AF.Silu)
                S.tt("dve", C.actT[:, ft, b0:b0 + bw], sg[:, 0:bw], pu[:, 0:bw], ALU.mult)
        first = False
        for li, ci in enumerate(grp):
            xt = C.xt[C.xti % 3]; C.xti += 1
            S.dma("sync", xt[:], xs(ci))
            gb = C.gateb[(ni, jf(ci))]
            for h in range(2):
                pc = C.pcs[h]
                for ft in range(NFT):
                    S.mm(pc[:], C.actT[:, ft, li * 128:(li + 1) * 128], w2b[:, ft, h * 512:(h + 1) * 512],
                         start=(ft == 0), stop=(ft == NFT - 1))
                S.tt("dve", C.tmp[:, h * 512:(h + 1) * 512], pc[:], gb[:, h * 512:(h + 1) * 512], ALU.mult)
            S.tt("pool", xt[:], xt[:], C.tmp[:], ALU.add)
            if xd is not None:
                S.dma("pool", xd(ci), xt[:])
            if after_chunk is not None:
                after_chunk(ci, li, xt)
        yield grp


def alloc_common(S, PS):
    C = Ctx()
    C.small = [S.sb([128, 4], F32, f"small{i}") for i in range(4)]; C.si = 0
    C.junk = S.sb([128, 1024], BF16, "junk")
    C.xn = [S.sb([128, 1024], BF16, f"xn{i}") for i in range(2)]; C.xi = 0
    C.pt = PS["b"]; C.pti = 0
    C.pa = PS["g"][0:4]; C.pai = 0
    C.pcs = PS["g"][4:6]
    C.xt = [S.sb([128, 1024], F32, f"xt{i}") for i in range(3)]; C.xti = 0
    C.hT = S.sb([128, 8, 1152], BF16, "hT")
    C.actT = S.sb([128, NFT, 1152], BF16, "actT")
    C.w2b = S.sb([128, NFT, 1024], BF16, "w2b")
    C.wst = [S.sb([128, 8, 256], F32, f"wst{i}") for i in range(2)]
    C.w1b = [S.sb([128, 8, 256], BF16, f"w1b{i}") for i in range(2)]
    C.w2st = [S.sb([128, 1024], F32, f"w2st{i}") for i in range(2)]
    C.sg = [S.sb([128, 512], F32, f"sg{i}") for i in range(2)]; C.sgi = 0
    C.tmp = S.sb([128, 1024], F32, "tmp")
    C.ones = S.sb([128, 128], F32, "ones")
    S.memset("dve", C.ones[:], 1.0)
    return C


GROUPS34 = [list(range(0, 9)), list(range(9, 18)), list(range(18, 26)), list(range(26, 34))]
GROUPS32 = [list(range(0, 8)), list(range(8, 16)), list(range(16, 24)), list(range(24, 32))]
JF34 = lambda ci: 0 if ci < 2 else 1


def phase1(S, PS, IN, SC):
    C = alloc_common(S, PS)
    C.idf, C.idb = setup_consts(S, IN["ident"][:])
    mv = mod_vectors(S, IN["cT"][:], IN["modw"][0], IN["modbT"][0], IN["ngT"][0], C.wst, C.pa[0])
    S.dma("sync", SC["modT0"][:], mv["modT"][:].re("p t j -> p (t j)"))
    C.gateb = {}
    for j in range(2):
        C.gateb[(0, j)] = gate_bcast(S, mv["GATE"][:, 0, :, j], C.idf, C.ones, C.pa[1 + j], 0.5, f"gb0{j}")
    zt = S.sb([2, 2304], F32, "zt")
    S.memset("dve", zt[:], 0.0)
    ppad = SC["ppad"]
    S.dma("sync", ppad[0:1, :], zt[0:1, :]); S.dma("sync", ppad[257:259, :], zt[0:2, :]); S.dma("sync", ppad[4355:4356, :], zt[0:1, :])
    pst = [S.sb([128, 256], F32, f"pst{i}") for i in range(2)]
    psti = [0]

    def xs(ci):
        return IN["ctx"][ci * 128:(ci + 1) * 128, :] if ci < 2 else IN["x"][(ci - 2) * 128:(ci - 1) * 128, :]

    def xd(ci):
        return SC["x1"][ci * 128:(ci + 1) * 128, :]

    def after(ci, li, xt):
        j = JF34(ci)
        norm_to_hT(S, C, xt[:], C.hT, li * 128, mv["G"][:, 1, :, j], mv["SH"][:, 1, :, j])

    win = IN["win_ab"]
    for grp in ffn(S, C, xs, xd, IN["w1"][0, 0], IN["w2"][0, 0], mv, 0, GROUPS34, JF34, after_chunk=after):
        for cb in range(9):
            wst = C.wst[cb % 2]; wb = C.w1b[cb % 2]
            S.dma("sync", wst[:], win[:, cb * 256:(cb + 1) * 256].re("(k p) n -> p k n", p=128))
            S.copy("pool", wb[:], wst[:])
            for li, ci in enumerate(grp):
                pp = C.pa[C.pai % 4]; C.pai += 1
                for k in range(8):
                    S.mm(pp[:, 0:256], C.hT[:, k, li * 128:(li + 1) * 128], wb[:, k, :], start=(k == 0), stop=(k == 7))
                st = pst[psti[0] % 2]; psti[0] += 1
                S.copy("act", st[:], pp[:, 0:256])
                S.dma("sync", ppad[prow(ci):prow(ci) + 128, cb * 256:(cb + 1) * 256], st[:])


def phase2a(S, PS, IN, SC, MD=F32, nblocks=NCH):
    ppad = SC["ppad"]

    def ld(dv, shape, nm, dt=F32):
        t = S.sb(shape, dt, nm)
        S.dma("sync", t[:], dv)
        return t
    mu0 = ld(IN["mub"][0], [128, 1536], "mu0"); mu1 = ld(IN["mub"][1], [128, 1536], "mu1")
    c0 = S.sb([128, 1536], F32, "c0")
    S.tt("dve", c0[:], mu0[:], mu1[:], ALU.add)
    S.ts("dve", c0[:], c0[:], -1.0, ALU.mult, 1.0, ALU.add)
    kkb = ld(IN["kkb"][:], [128, 512], "kkb"); kab = ld(IN["kab"][:], [128, 512], "kab")
    omka = S.sb([128, 512], F32, "omka")
    S.ts("dve", omka[:], kab[:], -1.0, ALU.mult, 1.0, ALU.add)
    g2 = ld(IN["g2"][:], [128, 512], "g2")
    msk = IN["msk"]
    mUs = ld(msk[0], [128, 128], "mUs"); mUi = ld(msk[1], [128, 128], "mUi"); mLs = ld(msk[2], [128, 128], "mLs")
    idf = ld(msk[3], [128, 128], "idf"); mLi = ld(msk[4], [128, 128], "mLi")
    cm = {}
    for nm, m_ in (("Ui", mUi), ("Us", mUs), ("Ls", mLs), ("Li", mLi)):
        cm[nm] = S.sb([128, 128], F32, "c" + nm)
        S.ts("dve", cm[nm][:], m_[:], NEGC, ALU.mult)
    negc = S.sb([128, 1], F32, "negc")
    S.memset("dve", negc[:], NEGC)
    idm = idf
    w2a = [ld(IN["w2a"][dr], [65, 512], f"w2a{dr}") for dr in range(2)]
    a2a = [ld(IN["a2a"][dr], [65, 512], f"a2a{dr}") for dr in range(2)]

    pb = PS["g"]
    pbi = [0]

    def bank():
        b = pb[pbi[0] % 6]; pbi[0] += 1
        return b

    def t512(nm, dt=F32):
        return S.sb([128, 512], dt, nm)

    rc = S.sb([128, 1536], F32, "rc"); rp = S.sb([128, 1536], F32, "rp"); rn_ = S.sb([128, 1536], F32, "rn")
    mix = S.sb([128, 1536], F32, "mix"); mt = S.sb([128, 1536], F32, "mt")
    TW = S.sb([65, 128], F32, "TW"); AL = S.sb([65, 128], F32, "AL"); SG = S.sb([128, 128], F32, "SG")
    S.memset("dve", TW[:], 1.0); S.memset("dve", AL[:], 1.0)
    lmix = S.sb([128, 256], F32, "lmix"); lmt = S.sb([128, 256], F32, "lmt")
    lc = S.sb([128, 256], F32, "lc"); lp = S.sb([128, 256], F32, "lp"); ln_ = S.sb([128, 256], F32, "ln")
    mul0 = ld(IN["mulb"][0], [128, 256], "mul0"); mul1 = ld(IN["mulb"][1], [128, 256], "mul1")
    c0lb = S.sb([128, 256], F32, "c0lb")
    S.tt("dve", c0lb[:], mul0[:], mul1[:], ALU.add)
    S.ts("dve", c0lb[:], c0lb[:], -1.0, ALU.mult, 1.0, ALU.add)
    sig = t512("sig"); a_ = t512("a"); gt = t512("gt")
    kk = t512("kk"); sq = t512("sq"); ss = S.sb([128, 8], F32, "ss"); rn8 = S.sb([128, 8], F32, "rn8")
    kd = t512("kd"); tq = t512("tq"); bq = t512("bq")
    Gc = t512("G"); Gp = t512("Gp"); Gi = t512("Gi"); Ge = t512("Ge")
    A = t512("A", MD); B = t512("B", MD); K = t512("K", MD); Rq = t512("Rq", MD)
    B2m = t512("B2m", MD); K2m = t512("K2m", MD)
    AT = S.sb([64, 8, 128], MD, "AT"); BT = S.sb([64, 8, 128], MD, "BT"); KT = S.sb([64, 8, 128], MD, "KT")
    RT = S.sb([64, 8, 128], MD, "RT")
    mat = lambda nm, dt=MD: [S.sb([128, 4, 128], dt, f"{nm}{g}") for g in range(2)]
    Nm = [mat("Nm0"), mat("Nm1")]; NT = [mat("NT0"), mat("NT1")]
    Mak = mat("Mak"); Mbr = mat("Mbr"); Mkr = mat("Mkr")
    Tf = mat("Tf", F32); Tm = Tf
    WTm = t512("WTm", MD); X1Tm = t512("X1Tm", MD); UlTm = t512("UlTm", MD)
    Rpf = S.sb([64, 8, 128], F32, "Rpf")
    gC = S.sb([64, 16], F32, "gC")
    dgG = S.sb([64, 8, 64], F32, "dgG")
    Pf = [S.sb([64, 8, 64], F32, f"Pf{c}") for c in range(2)]
    QTf = [S.sb([64, 8, 64], F32, f"QTf{c}") for c in range(2)]
    Yloc = t512("Yloc")
    ST = [S.sb([64, 8, 64], F32, f"ST{i}") for i in range(2)]
    yt = [t512(f"yt{i}") for i in range(2)]
    v3 = lambda v: v.re("p (h k) -> p h k", k=64)
    m3 = lambda v: v.re("p (h t) -> p h t", t=128)
    sti = 0
    for dr in range(2):
        if dr == 0:
            m_strict, m_strictT, m_incl = mUs, mLs, mUi
            c_incl, c_strict, c_end = cm["Ui"], cm["Us"], cm["Ls"]
            border = list(range(nblocks)); corder = [0, 1]
        else:
            m_strict, m_strictT, m_incl = mLs, mUs, mLi
            c_incl, c_strict, c_end = cm["Li"], cm["Ls"], cm["Us"]
            border = [1, 0] + list(range(NCH - 1, 1, -1)); corder = [1, 0]
            border = border[:nblocks]
        S.memset("dve", ST[sti % 2][:], 0.0)
        for n in border:
            r0 = prow(n)
            S.dma("sync", rc[:], ppad[r0:r0 + 128, 0:1536])
            S.dma("pool", rp[:], ppad[r0 - 1:r0 + 127, 0:1536])
            S.dma("sync", rn_[:], ppad[r0 + 1:r0 + 129, 0:1536])
            S.dma("pool", lc[:], ppad[r0:r0 + 128, 1536:1792])
            S.dma("pool", lp[:], ppad[r0 - 1:r0 + 127, 1536:1792])
            S.dma("sync", ln_[:], ppad[r0 + 1:r0 + 129, 1536:1792])
            S.tt("dve", mix[:], rc[:], c0[:], ALU.mult)
            S.tt("pool", mt[:], rp[:], mu0[:], ALU.mult)
            S.tt("dve", mix[:], mix[:], mt[:], ALU.add)
            S.tt("pool", mt[:], rn_[:], mu1[:], ALU.mult)
            S.tt("dve", mix[:], mix[:], mt[:], ALU.add)
            r = mix[:, 0:512]; k = mix[:, 512:1024]; v = mix[:, 1024:1536]
            if dr == 0:
                S.dma("pool", SC["rvo"][n * 128:(n + 1) * 128, 0:512], r)
                S.dma("pool", SC["rvo"][n * 128:(n + 1) * 128, 512:1024], v)
            S.tt("dve", lmix[:], lc[:], c0lb[:], ALU.mult)
            S.tt("pool", lmt[:], lp[:], mul0[:], ALU.mult)
            S.tt("dve", lmix[:], lmix[:], lmt[:], ALU.add)
            S.tt("pool", lmt[:], ln_[:], mul1[:], ALU.mult)
            S.tt("dve", lmix[:], lmix[:], lmt[:], ALU.add)
            pl = bank()
            S.mm(pl[0:64, 0:128], lmix[:, 0:64], idf[:], sig=False)
            S.mm(pl[0:64, 128:256], lmix[:, 64:128], idf[:], sig=False)
            S.mm(pl[:, 256:384], lmix[:, 128:256], idf[:])
            S.act(TW[0:64, :], pl[0:64, 0:128], AF.Tanh)
            S.copy("act", AL[0:64, :], pl[0:64, 128:256])
            S.act(SG[:], pl[:, 256:384], AF.Sigmoid)
            pw_ = bank(); S.mm(pw_[:], TW[:], w2a[dr][:])
            S.act(sig[:], pw_[:], AF.Sigmoid)
            pa_ = bank(); S.mm(pa_[:], AL[:], a2a[dr][:])
            S.act(a_[:], pa_[:], AF.Sigmoid)
            if dr == 0:
                pg_ = bank(); S.mm(pg_[:], SG[:], g2[:])
                S.copy("act", gt[:], pg_[:])
                S.dma("pool", SC["gto"][n * 128:(n + 1) * 128, :], gt[:])
            S.tt("dve", kk[:], k, kkb[:], ALU.mult)
            S.tt("pool", sq[:], kk[:], kk[:], ALU.mult)
            S.red("dve", ss[:], v3(sq[:]), ALU.add)
            S.ts("dve", ss[:], ss[:], 1e-12, ALU.max)
            S.act(ss[:], ss[:], AF.Sqrt)
            S.recip(rn8[:], ss[:])
            S.tt("dve", v3(kk[:]), v3(kk[:]), rn8[:].re("p (h o) -> p h o", o=1).bc([128, 8, 64]), ALU.mult)
            S.tt("pool", tq[:], a_[:], kab[:], ALU.mult)
            S.tt("pool", tq[:], tq[:], omka[:], ALU.add)
            S.tt("pool", kd[:], k, tq[:], ALU.mult)
            S.dma("pool", SC["kdo"][dr, n * 128:(n + 1) * 128, :], kd[:])
            S.tt("pool", bq[:], kk[:], a_[:], ALU.mult)
            pc1 = bank(); S.mm(pc1[:], c_incl[:], sig[:])
            pc2 = bank(); S.mm(pc2[:], c_strict[:], sig[:])
            pc3 = bank(); S.mm(pc3[:], c_end[:], sig[:])
            S.act(Gc[:], pc1[:], AF.Exp)
            S.act(Gi[:], pc1[:], AF.Exp, scale=-1.0)
            S.act(Gp[:], pc2[:], AF.Exp)
            S.act(Ge[:], pc3[:], AF.Exp)
            S.stt("dve", A[:], kk[:], -1.0, Gp[:], ALU.mult, ALU.mult)
            S.tt("dve", B[:], bq[:], Gi[:], ALU.mult)
            S.tt("pool", K[:], kd[:], Gi[:], ALU.mult)
            S.tt("pool", Rq[:], r, Gc[:], ALU.mult)
            S.tt("pool", B2m[:], bq[:], Ge[:], ALU.mult)
            S.tt("pool", K2m[:], kd[:], Ge[:], ALU.mult)
            Am_ = A
            Vv = v
            for (src, dst) in ((A, AT), (B, BT), (K, KT), (Rq, RT)):
                for hg in range(2):
                    p_ = bank()
                    for hh in range(4):
                        h = hg * 4 + hh
                        S.mm(p_[0:64, hh * 128:(hh + 1) * 128], src[:, h * 64:(h + 1) * 64], idm[:], sig=(hh == 3))
                    S.copy("act", dst[:, hg * 4:(hg + 1) * 4, :], m3(p_[0:64, :]))
            RTf_ = RT

            def mmat(dst, LT, RTt, mask, hg):
                p_ = bank()
                for hh in range(4):
                    h = hg * 4 + hh
                    S.mm(p_[:, hh * 128:(hh + 1) * 128], LT[:, h, :], RTt[:, h, :], sig=(hh == 3))
                S.tt("dve", dst[hg][:], m3(p_[:]), mask[:].re("p (o t) -> p o t", o=1).bc([128, 4, 128]), ALU.mult)
            for hg in range(2):
                mmat(Nm[0], BT, AT, m_strict, hg)
                mmat(NT[0], AT, BT, m_strictT, hg)
                mmat(Mak, KT, AT, m_strict, hg)
                mmat(Mbr, BT, RT, m_incl, hg)
                mmat(Mkr, KT, RT, m_incl, hg)
            for hg in range(2):
                S.tt("dve", Tf[hg][:], Nm[0][hg][:], idf[:].re("p (o t) -> p o t", o=1).bc([128, 4, 128]), ALU.add)
                cur = 0
                for lev in range(5):
                    nxt = 1 - cur
                    last = lev == 4
                    if not last:
                        p1 = bank()
                        for hh in range(4):
                            S.mm(p1[:, hh * 128:(hh + 1) * 128], NT[cur][hg][:, hh, :], Nm[cur][hg][:, hh, :], sig=(hh == 3))
                        S.copy("act", Nm[nxt][hg][:], m3(p1[:]))
                    p2 = bank()
                    for hh in range(4):
                        S.mm(p2[:, hh * 128:(hh + 1) * 128], Nm[cur][hg][:, hh, :], NT[cur][hg][:, hh, :], sig=(hh == 3))
                    S.copy("act", NT[nxt][hg][:], m3(p2[:]))
                    p3 = bank()
                    for hh in range(4):
                        S.mm(p3[:, hh * 128:(hh + 1) * 128], NT[nxt][hg][:, hh, :], Tm[hg][:, hh, :], sig=(hh == 3))
                    S.tt("dve", Tf[hg][:], Tf[hg][:], m3(p3[:]), ALU.add)
                    cur = nxt
            p_ = bank()
            for h in range(8):
                S.mm(p_[:, h * 64:(h + 1) * 64], Tm[h // 4][:, h % 4, :], Am_[:, h * 64:(h + 1) * 64], sig=(h == 7))
            S.copy("act", WTm[:], p_[:])
            p_ = bank()
            for h in range(8):
                S.mm(p_[:, h * 64:(h + 1) * 64], Mak[h // 4][:, h % 4, :], Vv[:, h * 64:(h + 1) * 64], sig=(h == 7))
            S.copy("dve", X1Tm[:], p_[:])
            p_ = bank()
            for h in range(8):
                S.mm(p_[:, h * 64:(h + 1) * 64], Tm[h // 4][:, h % 4, :], X1Tm[:, h * 64:(h + 1) * 64], sig=(h == 7))
            S.copy("act", UlTm[:], p_[:])
            for hg in range(2):
                p_ = bank()
                for hh in range(4):
                    h = hg * 4 + hh
                    S.mm(p_[0:64, hh * 128:(hh + 1) * 128], WTm[:, h * 64:(h + 1) * 64], Mbr[hg][:, hh, :], sig=(hh == 3))
                S.tt("dve", Rpf[:, hg * 4:(hg + 1) * 4, :], m3(p_[0:64, :]), RTf_[:, hg * 4:(hg + 1) * 4, :], ALU.add)
            pgc = [bank(), bank()]
            for c in range(2):
                for h in range(8):
                    S.mm(pgc[c][0:64, h:h + 1], sig[c * 64:(c + 1) * 64, h * 64:(h + 1) * 64], negc[c * 64:(c + 1) * 64, 0:1], sig=(h == 7))
                S.act(gC[:, c * 8:(c + 1) * 8], pgc[c][0:64, 0:8], AF.Exp)
            for c in range(2):
                pc = slice(c * 64, (c + 1) * 64)
                p_ = bank()
                for h in range(8):
                    S.mm(p_[0:64, h * 64:(h + 1) * 64], WTm[pc, h * 64:(h + 1) * 64], B2m[pc, h * 64:(h + 1) * 64], sig=(h == 7))
                S.tt("pool", dgG[:], idf[0:64, 0:64].re("p (o k) -> p o k", o=1).bc([64, 8, 64]),
                     gC[:, c * 8:(c + 1) * 8].re("p (h o) -> p h o", o=1).bc([64, 8, 64]), ALU.mult)
                S.tt("dve", Pf[c][:], v3(p_[0:64, :]), dgG[:], ALU.add)
                p_ = bank()
                for h in range(8):
                    S.mm(p_[0:64, h * 64:(h + 1) * 64], B2m[pc, h * 64:(h + 1) * 64], UlTm[pc, h * 64:(h + 1) * 64], start=True, stop=False, sig=False)
                    S.mm(p_[0:64, h * 64:(h + 1) * 64], K2m[pc, h * 64:(h + 1) * 64], Vv[pc, h * 64:(h + 1) * 64], start=False, stop=True, sig=(h == 7))
                S.copy("act", QTf[c][:], v3(p_[0:64, :]))
            p_ = bank()
            for h in range(8):
                S.mm(p_[:, h * 64:(h + 1) * 64], Mbr[h // 4][:, h % 4, :], UlTm[:, h * 64:(h + 1) * 64], start=True, stop=False, sig=False)
                S.mm(p_[:, h * 64:(h + 1) * 64], Mkr[h // 4][:, h % 4, :], Vv[:, h * 64:(h + 1) * 64], start=False, stop=True, sig=(h == 7))
            S.copy("act", Yloc[:], p_[:])
            ytile = yt[n % 2]
            for c in corder:
                pc = slice(c * 64, (c + 1) * 64)
                st_cur = ST[sti % 2]; st_nxt = ST[(sti + 1) % 2]; sti += 1
                p_ = bank()
                for h in range(8):
                    S.mm(p_[pc, h * 64:(h + 1) * 64], Rpf[:, h, c * 64:(c + 1) * 64], st_cur[:, h, :], sig=(h == 7))
                S.tt("dve", ytile[pc, :], p_[pc, :], Yloc[pc, :], ALU.add)
                p2 = bank()
                for h in range(8):
                    S.mm(p2[0:64, h * 64:(h + 1) * 64], Pf[c][:, h, :], st_cur[:, h, :], sig=(h == 7))
                S.tt("dve", st_nxt[:], v3(p2[0:64, :]), QTf[c][:], ALU.add)
            S.dma("sync", SC["yd"][dr, n * 128:(n + 1) * 128, :], ytile[:])


def phase2b(S, PS, IN, SC):
    CH = 128
    ppad = SC["ppad"]
    ident = IN["ident"]
    idf = S.sb([128, 128], F32, "idf"); S.dma("sync", idf[:], ident[:])
    Jm = S.sb([128, 128], F32, "Jm"); S.dma("sync", Jm[:], IN["msk"][5])
    pb = PS["g"]
    sm = lambda nm: S.sb([128, NST], F32, nm)
    big = lambda nm: S.sb([128, NST, 32], F32, nm)
    are = sm("are"); aim = sm("aim"); lst = sm("lst")
    bre = big("bre"); bim = big("bim"); cre = big("cre"); cim = big("cim"); ncim = big("ncim")
    lre = sm("lre"); dt_ = sm("dt"); zr = sm("zr"); th = sm("th"); rho = sm("rho")
    sa = sm("sa"); sk = sm("sk"); sr = sm("sr")
    cs = sm("cs"); sn = sm("sn")
    abre = sm("abre"); abim = sm("abim"); den = sm("den"); rden = sm("rden"); t1 = sm("t1"); t2 = sm("t2")
    fre = sm("fre"); fim = sm("fim"); am1 = sm("am1")
    bbre = big("bbre"); bbim = big("bbim"); u1 = big("u1"); u2 = big("u2")
    BBTre = S.sb([32, NST, 128], F32, "BBTre"); BBTim = S.sb([32, NST, 128], F32, "BBTim")
    Ec = S.sb([128, NST, CH], F32, "Ec"); Es = S.sb([128, NST, CH], F32, "Es")
    w1 = S.sb([128, NST, CH // 2], F32, "w1"); w2 = S.sb([128, NST, CH // 2], F32, "w2")
    rhob = S.sb([128, NST, CH], F32, "rhob")
    utok = [S.sb([128, 512], F32, f"utok{i}") for i in range(2)]
    ut = [S.sb([32, NST, CH], F32, f"ut{i}") for i in range(2)]
    Zre = S.sb([128, NST, CH], F32, "Zre"); Zim = S.sb([128, NST, CH], F32, "Zim")
    wre = S.sb([128, NST, CH], F32, "wre"); wim = S.sb([128, NST, CH], F32, "wim")
    xre = [S.sb([128, NST, CH], F32, f"xre{i}") for i in range(2)]
    xim = [S.sb([128, NST, CH], F32, f"xim{i}") for i in range(2)]
    ta = [S.sb([128, 4, CH], F32, f"ta{i}") for i in range(4)]
    tb = [S.sb([128, 4, CH], F32, f"tb{i}") for i in range(4)]
    yst = [S.sb([128, 512], F32, f"yst{i}") for i in range(2)]
    yst2 = [S.sb([128, 512], F32, f"ystb{i}") for i in range(2)]
    MAGIC = 12582912.0
    TWO_PI = 2.0 * math.pi
    pbi = 0
    cc = 0
    for dr in range(2):
        for (t_, nm) in ((are, "are"), (aim, "aim"), (lst, "lst")):
            S.dma("sync", t_[:], IN[nm][dr])
        for (t_, nm) in ((bre, "bre"), (bim, "bim"), (cre, "cre"), (cim, "cim")):
            S.dma("sync", t_[:], IN[nm][dr])
        S.ts("dve", ncim[:], cim[:], -1.0, ALU.mult)
        S.ts("dve", lre[:], are[:], -1e-4, ALU.min)
        S.act(dt_[:], lst[:], AF.Exp)
        S.tt("dve", zr[:], lre[:], dt_[:], ALU.mult)
        S.tt("dve", th[:], aim[:], dt_[:], ALU.mult)
        S.act(rho[:], zr[:], AF.Exp)

        def sin_reduced(out, ang, shift):
            S.ts("dve", sa[:], ang[:], float(shift), ALU.add)
            S.ts("dve", sk[:], sa[:], 1.0 / TWO_PI, ALU.mult, MAGIC, ALU.add)
            S.ts("dve", sk[:], sk[:], MAGIC, ALU.subtract)
            S.stt("dve", sr[:], sk[:], -TWO_PI, sa[:], ALU.mult, ALU.add)
            S.ts("dve", sr[:], sr[:], 3.14159, ALU.min, -3.14159, ALU.max)
            S.act(out, sr[:], AF.Sin)
        sin_reduced(sn[:], th, 0.0)
        sin_reduced(cs[:], th, math.pi / 2)
        S.tt("dve", abre[:], rho[:], cs[:], ALU.mult)
        S.tt("dve", abim[:], rho[:], sn[:], ALU.mult)
        S.tt("dve", t1[:], lre[:], lre[:], ALU.mult)
        S.tt("dve", t2[:], aim[:], aim[:], ALU.mult)
        S.tt("dve", den[:], t1[:], t2[:], ALU.add)
        S.recip(rden[:], den[:])
        S.ts("dve", am1[:], abre[:], -1.0, ALU.add)
        S.tt("dve", t1[:], am1[:], lre[:], ALU.mult)
        S.tt("dve", t2[:], abim[:], aim[:], ALU.mult)
        S.tt("dve", t1[:], t1[:], t2[:], ALU.add)
        S.tt("dve", fre[:], t1[:], rden[:], ALU.mult)
        S.tt("dve", t1[:], abim[:], lre[:], ALU.mult)
        S.tt("dve", t2[:], am1[:], aim[:], ALU.mult)
        S.tt("dve", t1[:], t1[:], t2[:], ALU.subtract)
        S.tt("dve", fim[:], t1[:], rden[:], ALU.mult)
        fre_b = fre[:].re("p (t o) -> p t o", o=1).bc([128, NST, 32])
        fim_b = fim[:].re("p (t o) -> p t o", o=1).bc([128, NST, 32])
        S.tt("dve", u1[:], bre[:], fre_b, ALU.mult)
        S.tt("dve", u2[:], bim[:], fim_b, ALU.mult)
        S.tt("dve", bbre[:], u1[:], u2[:], ALU.subtract)
        S.tt("dve", u1[:], bim[:], fre_b, ALU.mult)
        S.tt("dve", u2[:], bre[:], fim_b, ALU.mult)
        S.tt("dve", bbim[:], u1[:], u2[:], ALU.add)
        for (src, dst) in ((bbre, BBTre), (bbim, BBTim)):
            for g4 in range(4):
                p_ = pb[pbi % 6]; pbi += 1
                for jj in range(4):
                    j = g4 * 4 + jj
                    S.mm(p_[0:32, jj * 128:(jj + 1) * 128], src[:, j, :], idf[:], sig=(jj == 3))
                S.copy("dve", dst[:, g4 * 4:(g4 + 1) * 4, :], p_[0:32, :].re("p (a b) -> p a b", b=128))
        S.copy("dve", Ec[:, :, 0], cs[:])
        S.copy("dve", Es[:, :, 0], sn[:])
        m = 1
        while m < CH:
            cb = Ec[:, :, m - 1:m].bc([128, NST, m]); sb_ = Es[:, :, m - 1:m].bc([128, NST, m])
            S.tt("dve", w1[:, :, 0:m], Ec[:, :, 0:m], cb, ALU.mult)
            S.tt("dve", w2[:, :, 0:m], Es[:, :, 0:m], sb_, ALU.mult)
            S.tt("dve", Ec[:, :, m:2 * m], w1[:, :, 0:m], w2[:, :, 0:m], ALU.subtract)
            S.tt("dve", w1[:, :, 0:m], Ec[:, :, 0:m], sb_, ALU.mult)
            S.tt("dve", w2[:, :, 0:m], Es[:, :, 0:m], cb, ALU.mult)
            S.tt("dve", Es[:, :, m:2 * m], w1[:, :, 0:m], w2[:, :, 0:m], ALU.add)
            m *= 2
        S.copy("dve", rhob[:], rho[:].re("p (t o) -> p t o", o=1).bc([128, NST, CH]))
        Pm = idf if dr == 0 else Jm
        border = list(range(NCH)) if dr == 0 else [1, 0] + list(range(NCH - 1, 1, -1))
        for ci, n in enumerate(border):
            r0 = prow(n)
            utk = utok[cc % 2]
            S.dma("sync", utk[:], ppad[r0:r0 + 128, 1792:2304])
            u = ut[cc % 2]
            for g4 in range(4):
                p_ = pb[pbi % 6]; pbi += 1
                for jj in range(4):
                    j = g4 * 4 + jj
                    S.mm(p_[0:32, jj * 128:(jj + 1) * 128], utk[:, j * 32:(j + 1) * 32], Pm[:], sig=(jj == 3))
                S.copy("act", u[:, g4 * 4:(g4 + 1) * 4, :], p_[0:32, :].re("p (a b) -> p a b", b=128))
            xr, xi = xre[cc % 2], xim[cc % 2]
            xr_prev, xi_prev = xre[(cc + 1) % 2], xim[(cc + 1) % 2]
            for g4 in range(4):
                pr = pb[pbi % 6]; pbi += 1
                pi_ = pb[pbi % 6]; pbi += 1
                for jj in range(4):
                    j = g4 * 4 + jj
                    S.mm(pr[:, jj * CH:(jj + 1) * CH], BBTre[:, j, :], u[:, j, :], sig=False)
                for jj in range(4):
                    j = g4 * 4 + jj
                    S.mm(pi_[:, jj * CH:(jj + 1) * CH], BBTim[:, j, :], u[:, j, :], sig=(jj == 3))
                sl = slice(g4 * 4, (g4 + 1) * 4)
                prv = pr[:, :].re("p (a b) -> p a b", b=CH); piv = pi_[:, :].re("p (a b) -> p a b", b=CH)
                a0, a1, a2, a3 = ta
                S.tt("dve", a0[:], prv, Ec[:, sl, :], ALU.mult)
                S.tt("dve", a1[:], piv, Es[:, sl, :], ALU.mult)
                S.tt("pool", Zre[:, sl, :], a0[:], a1[:], ALU.add)
                S.tt("dve", a2[:], piv, Ec[:, sl, :], ALU.mult)
                S.tt("dve", a3[:], prv, Es[:, sl, :], ALU.mult)
                S.tt("pool", Zim[:, sl, :], a2[:], a3[:], ALU.subtract)
                for jj in range(4):
                    j = g4 * 4 + jj
                    for (wt, zt, xp) in ((wre, Zre, xr_prev), (wim, Zim, xi_prev)):
                        init = 0.0 if ci == 0 else xp[:, j, CH - 1:CH]
                        S.scan(wt[:, j, :], rhob[:, j, :], zt[:, j, :], init)
                b0, b1, b2, b3 = tb
                S.tt("pool", b0[:], wre[:, sl, :], Ec[:, sl, :], ALU.mult)
                S.tt("pool", b1[:], wim[:, sl, :], Es[:, sl, :], ALU.mult)
                S.tt("pool", xr[:, sl, :], b0[:], b1[:], ALU.subtract)
                S.tt("pool", b2[:], wim[:, sl, :], Ec[:, sl, :], ALU.mult)
                S.tt("pool", b3[:], wre[:, sl, :], Es[:, sl, :], ALU.mult)
                S.tt("pool", xi[:, sl, :], b2[:], b3[:], ALU.add)
            py = pb[pbi % 6]; pbi += 1
            for j in range(NST):
                S.mm(py[:, j * 32:(j + 1) * 32], xr[:, j, :], cre[:, j, :], start=True, stop=False, sig=False)
                S.mm(py[:, j * 32:(j + 1) * 32], xi[:, j, :], ncim[:, j, :], start=False, stop=True, sig=(j == NST - 1))
            ys = yst[cc % 2]
            S.copy("act", ys[:], py[:])
            py2 = pb[pbi % 6]; pbi += 1
            S.mm(py2[:], Pm[:], ys[:])
            ys2 = yst2[cc % 2]
            S.copy("act", ys2[:], py2[:])
            S.dma("sync", SC["ys"][dr, n * 128:(n + 1) * 128, :], ys2[:])
            cc += 1


def phase3a(S, PS, IN, SC):
    idf, idb = setup_consts(S, IN["ident"][:])
    ones = S.sb([128, 128], F32, "ones"); S.memset("dve", ones[:], 1.0)
    pa = PS["g"]; pt = PS["b"]
    modT = S.sb([128, 72, 2], F32, "modT")
    S.dma("sync", modT[:].re("p t j -> p (t j)"), SC["modT0"][:])
    gateb = [gate_bcast(S, modT[:, 5 * 8:6 * 8, j], idf, ones, pa[j], 1.0, f"g5{j}") for j in range(2)]
    bcs = [S.sb([128, 512], F32, f"bcs{i}") for i in range(5)]
    for i in range(5):
        S.dma("sync", bcs[i][:], IN["bcs"][i])
    lnxg, lnxb, rk, s5d, glub = bcs
    S.ts("dve", rk[:], rk[:], 0.5, ALU.mult)
    wst = S.sb([128, 4, 512], F32, "wst")
    gluw = S.sb([128, 4, 512], BF16, "gluw")
    S.dma("sync", wst[:], IN["gluw"].re("(k p) n -> p k n", p=128))
    S.copy("pool", gluw[:], wst[:])
    outw = S.sb([128, 8, 1024], BF16, "outw")
    wst2 = [S.sb([128, 1024], F32, f"wst2{i}") for i in range(2)]
    for k in range(8):
        S.dma("sync", wst2[k % 2][:], IN["outw_ab"][k * 128:(k + 1) * 128, :])
        S.copy("pool", outw[:, k, :], wst2[k % 2][:])
    t5 = lambda nm, dt=F32: S.sb([128, 512], dt, nm)
    inr = [[t5(f"inr{i}{j}") for j in range(7)] for i in range(2)]
    ins = [[t5(f"ins{i}{j}") for j in range(3)] for i in range(2)]
    xt = [S.sb([128, 1024], F32, f"xt{i}") for i in range(2)]
    y = t5("y"); yc = t5("yc"); sq = t5("sq"); ks = t5("ks"); tq = t5("tq"); bon = t5("bon")
    s8 = S.sb([128, 8], F32, "s8"); v8 = S.sb([128, 8], F32, "v8"); b8 = S.sb([128, 8], F32, "b8")
    cat = S.sb([128, 1024], BF16, "cat"); ysum = t5("ysum"); z = t5("z"); zb = t5("zb", BF16)
    zT = S.sb([128, 4, 128], BF16, "zT"); gl = t5("gl"); catT = S.sb([128, 8, 128], BF16, "catT")
    tmp = S.sb([128, 1024], F32, "tmp")
    v3 = lambda v: v.re("p (h k) -> p h k", k=64)
    b3 = lambda t: t[:].re("p (h o) -> p h o", o=1).bc([128, 8, 64])
    for ci in range(NCH):
        j = JF34(ci)
        i2 = ci % 2
        rows = slice(ci * 128, (ci + 1) * 128)
        srcs = [SC["yd"][0, rows, :], SC["yd"][1, rows, :], SC["kdo"][0, rows, :], SC["kdo"][1, rows, :],
                SC["rvo"][rows, 0:512], SC["rvo"][rows, 512:1024], SC["gto"][rows, :]]
        for q in range(7):
            S.dma("sync" if q % 2 == 0 else "pool", inr[i2][q][:], srcs[q])
        srcs2 = [SC["ys"][0, rows, :], SC["ys"][1, rows, :], SC["ppad"][prow(ci):prow(ci) + 128, 1792:2304]]
        for q in range(3):
            S.dma("pool" if q % 2 == 0 else "sync", ins[i2][q][:], srcs2[q])
        S.dma("sync", xt[i2][:], SC["x1"][rows, :])
        y0, y1, kd0, kd1, r, v, g = inr[i2]
        S.tt("dve", y[:], y0[:], y1[:], ALU.add)
        S.red("dve", s8[:], v3(y[:]), ALU.add)
        S.ts("dve", s8[:], s8[:], 1.0 / 64, ALU.mult)
        S.tt("dve", v3(yc[:]), v3(y[:]), b3(s8), ALU.subtract)
        S.tt("pool", sq[:], yc[:], yc[:], ALU.mult)
        S.red("dve", v8[:], v3(sq[:]), ALU.add)
        S.ts("dve", v8[:], v8[:], 1.0 / 64, ALU.mult, GN_EPS, ALU.add)
        S.act(v8[:], v8[:], AF.Sqrt)
        S.recip(v8[:], v8[:])
        S.tt("dve", v3(yc[:]), v3(yc[:]), b3(v8), ALU.mult)
        S.tt("pool", yc[:], yc[:], lnxg[:], ALU.mult)
        S.tt("pool", yc[:], yc[:], lnxb[:], ALU.add)
        S.tt("pool", ks[:], kd0[:], kd1[:], ALU.add)
        S.tt("pool", tq[:], r[:], ks[:], ALU.mult)
        S.tt("pool", tq[:], tq[:], rk[:], ALU.mult)
        S.red("dve", b8[:], v3(tq[:]), ALU.add)
        S.tt("dve", v3(bon[:]), v3(v[:]), b3(b8), ALU.mult)
        S.tt("dve", yc[:], yc[:], bon[:], ALU.add)
        S.tt("dve", cat[:, 0:512], yc[:], g[:], ALU.mult)
        ys0, ys1, u = ins[i2]
        S.tt("pool", ysum[:], ys0[:], ys1[:], ALU.add)
        S.tt("pool", tq[:], u[:], s5d[:], ALU.mult)
        S.tt("pool", ysum[:], ysum[:], tq[:], ALU.add)
        S.act(z[:], ysum[:], AF.Gelu)
        S.copy("pool", zb[:], z[:])
        p_ = pt[0]
        for k in range(4):
            S.tr(p_[:, k * 128:(k + 1) * 128], zb[:, k * 128:(k + 1) * 128], idb[:], sig=(k == 3))
        S.copy("dve", zT[:], p_[:, 0:512].re("p (k t) -> p k t", t=128))
        pg = pa[2]
        for k in range(4):
            S.mm(pg[:], zT[:, k, :], gluw[:, k, :], start=(k == 0), stop=(k == 3))
        S.tt("dve", gl[:], pg[:], glub[:], ALU.add)
        S.act(gl[:], gl[:], AF.Sigmoid)
        S.tt("dve", cat[:, 512:1024], z[:], gl[:], ALU.mult)
        p_ = pt[1]
        for k in range(8):
            S.tr(p_[:, k * 128:(k + 1) * 128], cat[:, k * 128:(k + 1) * 128], idb[:], sig=(k == 7))
        S.copy("dve", catT[:], p_[:].re("p (k t) -> p k t", t=128))
        for h in range(2):
            pc = pa[4 + h]
            for k in range(8):
                S.mm(pc[:], catT[:, k, :], outw[:, k, h * 512:(h + 1) * 512], start=(k == 0), stop=(k == 7))
            S.tt("dve", tmp[:, h * 512:(h + 1) * 512], pc[:], gateb[j][:, h * 512:(h + 1) * 512], ALU.mult)
        S.tt("pool", xt[i2][:], xt[i2][:], tmp[:], ALU.add)
        S.dma("pool", SC["xm"][rows, :], xt[i2][:])


def phase3b(S, PS, IN, SC):
    C = alloc_common(S, PS)
    C.idf, C.idb = setup_consts(S, IN["ident"][:])
    modT0 = S.sb([128, 72, 2], F32, "modT0")
    S.dma("sync", modT0[:].re("p t j -> p (t j)"), SC["modT0"][:])
    ngT0 = S.sb([128, 3, 8], F32, "ngT0")
    S.dma("sync", ngT0[:], IN["ngT"][0])
    mv0 = mod_derive(S, modT0, ngT0)
    C.gateb = {}
    for j in range(2):
        C.gateb[(2, j)] = gate_bcast(S, mv0["GATE"][:, 2, :, j], C.idf, C.ones, C.pa[j], 0.5, f"gb2{j}")
    rows = lambda t: (lambda ci: t[ci * 128:(ci + 1) * 128, :])
    for grp in ffn(S, C, rows(SC["xm"]), rows(SC["xl0"]), IN["w1"][0, 1], IN["w2"][0, 1], mv0, 2, GROUPS34, JF34):
        pass
    mv1 = mod_vectors(S, IN["cT"][:], IN["modw"][1], IN["modbT"][1], IN["ngT"][1], C.wst, C.pa[0])
    S.dma("sync", SC["modT1"][:], mv1["modT"][:].re("p t j -> p (t j)"))
    for j in range(2):
        C.gateb[(0, j)] = gate_bcast(S, mv1["GATE"][:, 0, :, j], C.idf, C.ones, C.pa[2 + j], 0.5, f"gb0{j}")
    cos = S.sb([128, NCH, 32], F32, "cos"); sin = S.sb([128, NCH, 32], F32, "sin")
    S.dma("sync", cos[:], IN["rope"][0].re("c p f -> p c f"))
    S.dma("sync", sin[:], IN["rope"][1].re("c p f -> p c f"))
    pst = [S.sb([128, 256], F32, f"pst{i}") for i in range(2)]
    ra = [S.sb([128, 4, 32], F32, f"ra{i}") for i in range(4)]
    psti = [0]

    def after(ci, li, xt):
        j = JF34(ci)
        norm_to_hT(S, C, xt[:], C.hT, li * 128, mv1["G"][:, 1, :, j], mv1["SH"][:, 1, :, j])
    win = IN["win_at"]
    qkv = SC["qkv"]
    for grp in ffn(S, C, rows(SC["xl0"]), rows(SC["x2"]), IN["w1"][1, 0], IN["w2"][1, 0], mv1, 0, GROUPS34, JF34, after_chunk=after):
        for cb in range(6):
            wst = C.wst[cb % 2]; wb = C.w1b[cb % 2]
            S.dma("sync", wst[:], win[:, cb * 256:(cb + 1) * 256].re("(k p) n -> p k n", p=128))
            S.copy("pool", wb[:], wst[:])
            for li, ci in enumerate(grp):
                pp = C.pa[C.pai % 4]; C.pai += 1
                for k in range(8):
                    S.mm(pp[:, 0:256], C.hT[:, k, li * 128:(li + 1) * 128], wb[:, k, :], start=(k == 0), stop=(k == 7))
                st = pst[psti[0] % 2]; psti[0] += 1
                if cb < 5:
                    pv = pp[:, 0:256].re("p (h two f) -> p h two f", two=2, f=32)
                    sv = st[:].re("p (h two f) -> p h two f", two=2, f=32)
                    cb_ = cos[:, ci, :].re("p (o f) -> p o f", o=1).bc([128, 4, 32])
                    sb_ = sin[:, ci, :].re("p (o f) -> p o f", o=1).bc([128, 4, 32])
                    a, b, c, dd = ra
                    S.tt("dve", a[:], pv[:, :, 0, :], cb_, ALU.mult)
                    S.tt("dve", b[:], pv[:, :, 1, :], sb_, ALU.mult)
                    S.tt("pool", sv[:, :, 0, :], a[:], b[:], ALU.subtract)
                    S.tt("dve", c[:], pv[:, :, 1, :], cb_, ALU.mult)
                    S.tt("dve", dd[:], pv[:, :, 0, :], sb_, ALU.mult)
                    S.tt("pool", sv[:, :, 1, :], c[:], dd[:], ALU.add)
                else:
                    S.copy("act", st[:], pp[:, 0:256])
                S.dma("sync", qkv[ci * 128:(ci + 1) * 128, cb * 256:(cb + 1) * 256], st[:])


def phase4a(S, PS, IN, SC):
    idf, idb = setup_consts(S, IN["ident"][:])
    ones = S.sb([128, 128], F32, "ones"); S.memset("dve", ones[:], 1.0)
    pa = PS["g"][0:4]; pai = [0]
    ptb = PS["b"][0]
    pos = PS["g"][4:6]
    qkv = SC["qkv"]
    modT = S.sb([128, 72, 2], F32, "modT")
    S.dma("sync", modT[:].re("p t j -> p (t j)"), SC["modT1"][:])
    gate5 = gate_bcast(S, modT[:, 5 * 8:6 * 8, 1], idf, ones, pa[0], 1.0, "g5")
    sinkb = S.sb([128, 16], F32, "sinkb"); S.dma("sync", sinkb[:], IN["sinkb"][:])
    mt16 = S.sb([128, 16, 3], F32, "mt16")
    S.copy("dve", mt16[:, :, 2], sinkb[:])
    mstage = S.sb([128, 384], F32, "mstage")
    maskb = S.sb([128, 3, 384], BF16, "maskb")
    for i in range(3):
        S.dma("sync", mstage[:], IN["maskb"][i])
        S.copy("dve", maskb[:, i, :], mstage[:])
    outw = S.sb([128, 8, 1024], BF16, "outw")
    wst2 = [S.sb([128, 1024], F32, f"wst2{i}") for i in range(2)]
    for k in range(8):
        S.dma("sync", wst2[k % 2][:], IN["outw_at"][k * 128:(k + 1) * 128, :])
        S.copy("pool", outw[:, k, :], wst2[k % 2][:])
    NKB = NQB + 2
    kT = S.sb([64, 4, NKB * 128], BF16, "kT"); kcT = S.sb([64, 4, 256], BF16, "kcT")
    vw = S.sb([128, NKB, 256], BF16, "vw"); vc = S.sb([128, 2, 256], BF16, "vc")
    for blk in (0, NKB - 1):
        S.memset("dve", kT[:, :, blk * 128:(blk + 1) * 128], 0.0)
        S.memset("dve", vw[:, blk, :], 0.0)
    kst = [S.sb([128, 512], F32, f"kst{i}") for i in range(2)]; kb = [S.sb([128, 256], BF16, f"kb{i}") for i in range(2)]
    for c in range(NCH):
        S.dma("sync", kst[c % 2][:], qkv[c * 128:(c + 1) * 128, 1024:1536])
        S.copy("pool", kb[c % 2][:], kst[c % 2][:, 0:256])
        for kv in range(4):
            S.tr(ptb[0:64, kv * 128:(kv + 1) * 128], kb[c % 2][:, kv * 64:(kv + 1) * 64], idb[:], sig=(kv == 3))
        blk = c - 1
        dstk = kcT[:, :, c * 128:(c + 1) * 128] if c < 2 else kT[:, :, blk * 128:(blk + 1) * 128]
        S.copy("act", dstk, ptb[0:64, 0:512].re("p (a t) -> p a t", t=128))
        dstv = vc[:, c, :] if c < 2 else vw[:, blk, :]
        S.copy("dve", dstv, kst[c % 2][:, 256:512])
    qst = [S.sb([128, 1024], F32, f"qst{i}") for i in range(2)]
    qb = S.sb([128, 1024], BF16, "qb")
    qT = S.sb([64, 16, 128], BF16, "qT")
    Pm = [S.sb([128, 640], BF16, f"Pm{i}") for i in range(2)]
    PT = [S.sb([128, 5, 128], BF16, f"PT{i}") for i in range(2)]
    rs = [S.sb([128, 4], F32, f"rs{i}") for i in range(2)]
    negm = [S.sb([128, 1], F32, f"negm{i}") for i in range(2)]
    rden = S.sb([128, 16], F32, "rden")
    ob = S.sb([128, 1024], BF16, "ob"); oT = S.sb([128, 8, 128], BF16, "oT")
    xt = [S.sb([128, 1024], F32, f"xt{i}") for i in range(2)]
    tmp = S.sb([128, 1024], F32, "tmp")
    for i in range(NQB):
        rows = slice((i + 2) * 128, (i + 3) * 128)
        S.dma("sync", qst[i % 2][:], qkv[rows, 0:1024])
        S.dma("pool", xt[i % 2][:], SC["x2"][rows, :])
        S.act(qb[:], qst[i % 2][:], AF.Copy, scale=0.125)
        for half in range(2):
            for hh in range(8):
                hd = half * 8 + hh
                S.tr(ptb[0:64, hh * 128:(hh + 1) * 128], qb[:, hd * 64:(hd + 1) * 64], idb[:], sig=(hh == 7))
            S.copy("act", qT[:, half * 8:(half + 1) * 8, :], ptb[0:64, :].re("p (a t) -> p a t", t=128))
        mi = 0 if i == 0 else (2 if i == NQB - 1 else 1)
        for hd in range(16):
            kv = hd // 4
            i2 = hd % 2
            pw = pa[pai[0] % 4]; pai[0] += 1
            pcx = pa[pai[0] % 4]; pai[0] += 1
            S.mm(pw[:, 0:384], qT[:, hd, :], kT[:, kv, i * 128:(i + 3) * 128], start=True, stop=False, sig=False)
            S.mm(pw[:, 0:384], idb[:], maskb[:, mi, :], start=False, stop=True)
            S.mm(pcx[:, 0:256], qT[:, hd, :], kcT[:, kv, :])
            S.red("dve", mt16[:, hd, 0:1], pw[:, 0:384], ALU.max)
            S.red("dve", mt16[:, hd, 1:2], pcx[:, 0:256], ALU.max)
            S.red("dve", negm[i2][:], mt16[:, hd, :], ALU.max)
            S.ts("dve", negm[i2][:], negm[i2][:], -1.0, ALU.mult)
            S.act(Pm[i2][:, 0:384], pw[:, 0:384], AF.Exp, bias=negm[i2][:, 0:1], accum=rs[i2][:, 0:1])
            S.act(Pm[i2][:, 384:640], pcx[:, 0:256], AF.Exp, bias=negm[i2][:, 0:1], accum=rs[i2][:, 1:2])
            S.act(rs[i2][:, 2:3], sinkb[:, hd:hd + 1], AF.Exp, bias=negm[i2][:, 0:1])
            S.red("dve", rs[i2][:, 3:4], rs[i2][:, 0:3], ALU.add)
            S.recip(rden[:, hd:hd + 1], rs[i2][:, 3:4])
            for j in range(5):
                S.tr(ptb[:, j * 128:(j + 1) * 128], Pm[i2][:, j * 128:(j + 1) * 128], idb[:], sig=(j == 4))
            S.copy("dve" if hd % 2 == 0 else "act", PT[i2][:], ptb[:, 0:640].re("p (a t) -> p a t", t=128))
            po = pos[hd // 8]
            for j in range(5):
                vsrc = vw[:, i + j, kv * 64:(kv + 1) * 64] if j < 3 else vc[:, j - 3, kv * 64:(kv + 1) * 64]
                S.mm(po[:, (hd % 8) * 64:(hd % 8 + 1) * 64], PT[i2][:, j, :], vsrc, start=(j == 0), stop=(j == 4), sig=(j == 4))
        for h2 in range(2):
            S.tt("dve", ob[:, h2 * 512:(h2 + 1) * 512].re("p (h k) -> p h k", k=64), pos[h2][:].re("p (h k) -> p h k", k=64),
                 rden[:, h2 * 8:(h2 + 1) * 8].re("p (h o) -> p h o", o=1).bc([128, 8, 64]), ALU.mult)
        for k in range(8):
            S.tr(ptb[:, k * 128:(k + 1) * 128], ob[:, k * 128:(k + 1) * 128], idb[:], sig=(k == 7))
        S.copy("act", oT[:], ptb[:].re("p (a t) -> p a t", t=128))
        for h in range(2):
            py = pa[pai[0] % 4]; pai[0] += 1
            for k in range(8):
                S.mm(py[:], oT[:, k, :], outw[:, k, h * 512:(h + 1) * 512], start=(k == 0), stop=(k == 7))
            S.tt("dve", tmp[:, h * 512:(h + 1) * 512], py[:], gate5[:, h * 512:(h + 1) * 512], ALU.mult)
        S.tt("pool", xt[i % 2][:], xt[i % 2][:], tmp[:], ALU.add)
        S.dma("pool", SC["x3"][i * 128:(i + 1) * 128, :], xt[i % 2][:])


def phase4b(S, PS, IN, SC, OUT):
    C = alloc_common(S, PS)
    C.idf, C.idb = setup_consts(S, IN["ident"][:])
    modT = S.sb([128, 72, 2], F32, "modT")
    S.dma("sync", modT[:].re("p t j -> p (t j)"), SC["modT1"][:])
    ngT = S.sb([128, 3, 8], F32, "ngT"); S.dma("sync", ngT[:], IN["ngT"][1])
    mv = mod_derive(S, modT, ngT)
    C.gateb = {(2, 1): gate_bcast(S, mv["GATE"][:, 2, :, 1], C.idf, C.ones, C.pa[0], 0.5, "gb21")}
    fing = S.sb([128, 1024], F32, "fing"); S.dma("sync", fing[:], IN["fing"][:])
    ot = [S.sb([128, 1024], F32, f"ot{i}") for i in range(2)]
    oi = [0]

    def after(ci, li, xt):
        ss = C.small[C.si % 4]; C.si += 1
        S.act(C.junk[:], xt[:], AF.Square, accum=ss[:, 0:1])
        S.ts("dve", ss[:, 1:2], ss[:, 0:1], 1.0 / D, ALU.mult, 1e-6, ALU.add)
        S.act(ss[:, 3:4], ss[:, 1:2], AF.Sqrt)
        S.recip(ss[:, 2:3], ss[:, 3:4])
        o = ot[oi[0] % 2]; oi[0] += 1
        S.stt("dve", o[:], xt[:], ss[:, 2:3], fing[:], ALU.mult, ALU.mult)
        S.dma("sync", OUT[ci * 128:(ci + 1) * 128, :], o[:])
    rows = lambda t: (lambda ci: t[ci * 128:(ci + 1) * 128, :])
    for grp in ffn(S, C, rows(SC["x3"]), None, IN["w1"][1, 1], IN["w2"][1, 1], mv, 2, GROUPS32, lambda ci: 1, after_chunk=after):
        pass


IN_SPECS = dict(
    x=[4096, D], ctx=[256, D], cT=[128, 8, 2], modw=[2, D, 9216], modbT=[2, 128, 72], ngT=[2, 128, 3, 8],
    w1=[2, 2, D, 2 * DFF], w2=[2, 2, DFF, D], win_ab=[D, 2304], ident=[128, 128],
    mub=[2, 128, 1536], mulb=[2, 128, 256], kkb=[128, 512], kab=[128, 512], w2a=[2, 65, 512], a2a=[2, 65, 512], g2=[128, 512], msk=[6, 128, 128],
    are=[2, 128, NST], aim=[2, 128, NST], lst=[2, 128, NST], bre=[2, 128, NST, 32], bim=[2, 128, NST, 32], cre=[2, 128, NST, 32], cim=[2, 128, NST, 32],
    bcs=[5, 128, 512], gluw=[512, 512], outw_ab=[D, D], win_at=[D, 1536], rope=[2, NCH, 128, 32],
    maskb=[3, 128, 384], sinkb=[128, 16], outw_at=[D, D], fing=[128, D])

SC_SPECS = dict(x1=[TOK, D], ppad=[4356, 2304], yd=[2, TOK, 512], kdo=[2, TOK, 512], rvo=[TOK, 1024], gto=[TOK, 512], ys=[2, TOK, 512],
                xm=[TOK, D], xl0=[TOK, D], x2=[TOK, D], qkv=[TOK, 1536], x3=[4096, D], modT0=[128, 144], modT1=[128, 144])


def build_fused(upto=99, debug=()):
    nc = bass.Bass("TRN2", target_bir_lowering=False)
    S = Sched(nc)
    IN = {k: S.dram(k, v, F32, kind="ExternalInput") for k, v in IN_SPECS.items()}
    SC = {k: S.dram("sc_" + k, v, F32, kind=("ExternalOutput" if k in debug else "Internal")) for k, v in SC_SPECS.items()}
    OUT = S.dram("out", [4096, D], F32, kind="ExternalOutput")
    PS = dict(g=[S.ps([128, 512], F32, f"g{i}") for i in range(6)], b=[S.ps([128, 1024], BF16, f"b{i}") for i in range(2)])
    base = S.mark()
    phases = [lambda: phase1(S, PS, IN, SC), lambda: phase2a(S, PS, IN, SC), lambda: phase2b(S, PS, IN, SC), lambda: phase3a(S, PS, IN, SC),
              lambda: phase3b(S, PS, IN, SC), lambda: phase4a(S, PS, IN, SC), lambda: phase4b(S, PS, IN, SC, OUT)]
    for i, ph in enumerate(phases):
        if i > upto:
            break
        S.reset(base)
        ph()
        S.barrier()
    finals = [OUT] + [SC[k] for k in debug]
    S.finish(finals)
    return nc, S

import numpy as np
LC = 256; NLAT = 4096; L = 4352
GRID_W = 64; ROPE_BASE = 10000.0
def core_tok(seq, h):
    return np.concatenate([seq[h * 128:(h + 1) * 128], seq[256 + h * 2048:256 + (h + 1) * 2048]], 0)
def uncore_tok(parts):
    return np.concatenate([parts[0][:128], parts[1][:128], parts[0][128:], parts[1][128:]], 0)
def colT(v, k=8):
    return np.ascontiguousarray(v.reshape(k, 128).T)
def bc(v):
    return np.ascontiguousarray(np.broadcast_to(v[None, :], (128, v.shape[0])))
def rope_tables(h):
    t = np.arange(h * 2048, (h + 1) * 2048)
    row = (t // GRID_W).astype(np.float32); col = (t % GRID_W).astype(np.float32)
    inv = (ROPE_BASE ** (-np.arange(0, 32, 2, dtype=np.float32) / 32)).astype(np.float32)
    ang = np.concatenate([row[:, None] * inv, col[:, None] * inv], -1).astype(np.float32)
    cos = np.concatenate([np.ones((128, 32), np.float32), np.cos(ang)], 0).reshape(17, 128, 32)
    sin = np.concatenate([np.zeros((128, 32), np.float32), np.sin(ang)], 0).reshape(17, 128, 32)
    return np.stack([cos, sin], 0).astype(np.float32)

import numpy as np
LC = 256

def f_masks():
    m = np.zeros((6, 128, 128), np.float32)
    s = np.arange(128)[:, None]; t = np.arange(128)[None, :]
    same = (s // 64) == (t // 64)
    m[0] = same & (s < t); m[1] = same & (s <= t); m[2] = same & (s > t); m[4] = same & (s >= t)
    m[3] = np.eye(128)
    m[5] = np.eye(128)[::-1]
    return m

def f_rope():
    GRID_W = 64
    t = np.arange(4096)
    row = (t // GRID_W).astype(np.float32); col = (t % GRID_W).astype(np.float32)
    inv = (10000.0 ** (-np.arange(0, 32, 2, dtype=np.float32) / 32)).astype(np.float32)
    ang = np.concatenate([row[:, None] * inv, col[:, None] * inv], -1).astype(np.float32)
    cos = np.concatenate([np.ones((256, 32), np.float32), np.cos(ang)], 0).reshape(34, 128, 32)
    sin = np.concatenate([np.zeros((256, 32), np.float32), np.sin(ang)], 0).reshape(34, 128, 32)
    return np.ascontiguousarray(np.stack([cos, sin], 0).astype(np.float32))

def f_attn_masks():
    qi = np.arange(128)[:, None]; mj = np.arange(384)[None, :] - 128
    valid = np.abs(mj - qi) <= 128
    NEG = -30000.0
    gen = np.where(valid, 0.0, NEG).astype(np.float32)
    left_inv = gen.copy(); left_inv[:, :128] = NEG
    right_inv = gen.copy(); right_inv[:, 256:] = NEG
    return np.ascontiguousarray(np.stack([left_inv, gen, right_inv], 0))

def f_shared(d):
    e = 0
    st = lambda a: np.ascontiguousarray(a.reshape(16, 128).T)
    def pad(a):
        out = np.zeros((128, 16, 32), np.float32)
        for g in range(32):
            out[(g % 2) * 64:(g % 2) * 64 + 64, g // 2, (g % 2) * 16:(g % 2) * 16 + 16] = a[g]
        return out
    mu = d['rwkv_mu'][e]
    sh = dict(
        modw=d['mod_w'], modbT=np.ascontiguousarray(np.stack([colT(d['mod_b'][l], 72) for l in range(2)], 0)),
        ngT=np.ascontiguousarray(np.stack([np.stack([colT(d['norm_g'][l, i]) for i in range(3)], 1) for l in range(2)], 0)),
        w1=d['ffn_w1'], w2=d['ffn_w2'], win_ab=d['ab_in_w'][0], ident=np.eye(128, dtype=np.float32),
        mub=np.ascontiguousarray(np.stack([bc(mu[0, :1536]), bc(mu[1, :1536])], 0)),
        mulb=np.ascontiguousarray(np.stack([bc(mu[0, 1536:1792]), bc(mu[1, 1536:1792])], 0)),
        kkb=bc(d['rwkv_k_k'][e]), kab=bc(d['rwkv_k_a'][e]),
        w2a=np.ascontiguousarray(np.stack([np.concatenate([d['rwkv_w2'][e, dr], d['rwkv_w0'][e, dr][None]], 0) for dr in range(2)], 0)),
        a2a=np.ascontiguousarray(np.stack([np.concatenate([d['rwkv_a2'][e, dr], d['rwkv_a0'][e, dr][None]], 0) for dr in range(2)], 0)),
        g2=d['rwkv_g2'][e], msk=f_masks(),
        are=np.stack([st(d['s5_a_re'][0, dr]) for dr in range(2)], 0), aim=np.stack([st(d['s5_a_im'][0, dr]) for dr in range(2)], 0),
        lst=np.stack([st(np.repeat(d['s5_log_step'][0, dr][:, None], 64, 1)) for dr in range(2)], 0),
        bre=np.stack([pad(d['s5_b_re'][0, dr]) for dr in range(2)], 0), bim=np.stack([pad(d['s5_b_im'][0, dr]) for dr in range(2)], 0),
        cre=np.stack([pad(d['s5_c_re'][0, dr].transpose(0, 2, 1)) for dr in range(2)], 0),
        cim=np.stack([pad(d['s5_c_im'][0, dr].transpose(0, 2, 1)) for dr in range(2)], 0),
        bcs=np.ascontiguousarray(np.stack([bc(d['rwkv_lnx_g'][0]), bc(d['rwkv_lnx_b'][0]), bc(d['rwkv_r_k'][0].reshape(-1)), bc(d['s5_d'][0]),
                                           bc(d['s5_glu_b'][0])], 0)),
        gluw=d['s5_glu_w'][0], outw_ab=d['ab_out_w'][0], win_at=d['attn_in_w'][0], rope=f_rope(),
        maskb=f_attn_masks(), sinkb=bc(d['attn_sink'][0]), outw_at=d['attn_out_w'][0], fing=bc(d['final_g']))
    return {k: np.ascontiguousarray(v, dtype=np.float32) for k, v in sh.items()}

def f_core(d, b):
    return dict(x=np.ascontiguousarray(d['x'][b]), ctx=np.ascontiguousarray(d['ctx'][b]),
                cT=np.ascontiguousarray(np.stack([colT(d['c_ctx']), colT(d['c'][b])], -1)))


def kernel(**inputs):
    d = {k: np.ascontiguousarray(np.asarray(v, dtype=np.float32)) for k, v in inputs.items()}
    nc, _ = build_fused()
    sh = f_shared(d)
    in_maps = [dict(sh, **f_core(d, c % 4)) for c in range(8)]
    res = run_bass_kernel_spmd(nc, in_maps, core_ids=list(range(8)))
    out = np.stack([res.results[b]['out'] for b in range(4)], 0)
    return out.astype(np.float32)
```

```python
import numpy as np
import concourse.bass as bass
import concourse.mybir as mybir
from concourse.bass_utils import run_bass_kernel_spmd

F32 = mybir.dt.float32
BF16 = mybir.dt.bfloat16
F32R = mybir.dt.float32r
ALU = mybir.AluOpType
AF = mybir.ActivationFunctionType
AX = mybir.AxisListType


class T:
    def __init__(self, h, name=""):
        self.h = h
        self.name = name
        self.last_w = None
        self.readers = []

    def __getitem__(self, idx):
        return V(self, self.h[idx])

    def re(self, pat, **kw):
        return self[:].re(pat, **kw)


class V:
    def __init__(self, t, ap):
        self.t = t
        self.ap = ap

    def __getitem__(self, idx):
        return V(self.t, self.ap[idx])

    def re(self, pat, **kw):
        return V(self.t, self.ap.rearrange(pat, **kw))

    def bc(self, shape):
        return V(self.t, self.ap.to_broadcast(shape))


def _ap(x):
    return x.ap if isinstance(x, V) else x


def _ts(xs):
    out = []
    for x in xs:
        if isinstance(x, V):
            out.append(x.t)
        elif isinstance(x, T):
            out.append(x)
    return out


class Sched:
    ENG = ["pe", "act", "dve", "pool", "sync"]

    def __init__(self, nc, n_dma_sems=6, same_engine_sync=True):
        self.nc = nc
        self.q = {e: [] for e in self.ENG}
        self.cnt = {e: 0 for e in self.ENG}
        self.unsig = {e: False for e in self.ENG}
        self.sem = {e: nc.alloc_semaphore(f"s_{e}") for e in ["pe", "act", "dve", "pool"]}
        self.waited = {e: {} for e in self.ENG}
        self.same_engine_sync = same_engine_sync
        self.dsem = {}
        self.dcnt = {}
        self.drr = {}
        for qn in ["sync", "pool", "act"]:
            self.dsem[qn] = [nc.alloc_semaphore(f"d_{qn}{i}") for i in range(n_dma_sems)]
            self.dcnt[qn] = [0] * n_dma_sems
            self.drr[qn] = 0
        self.n_inst = 0
        self.uid = 0

    ARENA_LO = 16640
    ARENA_HI = 229344

    def sb(self, shape, dt=F32, name=None):
        self.uid += 1
        name = name or f"t{self.uid}"
        if not hasattr(self, "off"):
            self.off = self.ARENA_LO
        n = 1
        for x in shape[1:]:
            n *= x
        size = n * (2 if dt == BF16 else 4)
        size = (size + 31) // 32 * 32
        assert self.off + size <= self.ARENA_HI, f"SBUF arena overflow allocating {name} {shape}: off={self.off} size={size}"
        t = T(self.nc.alloc_sbuf_tensor_at(f"{name}_{self.uid}", list(shape), dt, offset=self.off), name)
        self.off += size
        return t

    def mark(self):
        if not hasattr(self, "off"):
            self.off = self.ARENA_LO
        return self.off

    def reset(self, mark):
        self.off = mark

    def barrier(self):
        targets = []
        for e in ("pe", "act", "dve", "pool"):
            assert not self.unsig[e], f"barrier with unsignaled op on {e}"
            if self.cnt[e] > 0:
                targets.append((self.sem[e], self.cnt[e]))
        for qn in self.dsem:
            for sm, c in zip(self.dsem[qn], self.dcnt[qn]):
                if c > 0:
                    targets.append((sm, c))
        for e in self.ENG:
            waits = []
            for (sm, val) in targets:
                if e in self.sem and sm is self.sem[e]:
                    continue
                if self.waited[e].get(id(sm), 0) >= val:
                    continue
                self.waited[e][id(sm)] = val
                waits.append((sm, val))
            if waits:
                self.q[e].append((None, waits, None))

    def ps(self, shape, dt=F32, name=None):
        self.uid += 1
        name = name or f"p{self.uid}"
        return T(self.nc.alloc_psum_tensor(f"{name}_{self.uid}", list(shape), dt), name)

    def dram(self, name, shape, dt=F32, kind="Internal"):
        return T(self.nc.dram_tensor(name, list(shape), dt, kind=kind), name)

    def _collect(self, eng, reads, writes):
        toks = []
        for t in _ts(reads):
            if t.last_w is not None:
                toks.append(t.last_w)
        for t in _ts(writes):
            if t.last_w is not None:
                toks.append(t.last_w)
            toks.extend(t.readers)
        best = {}
        for (kind, key, sem, val) in toks:
            if kind == "eng" and key == eng:
                if eng in ("pe", "sync") or not self.same_engine_sync:
                    continue
            k = id(sem)
            if k not in best or best[k][1] < val:
                best[k] = (sem, val)
        waits = []
        for k, (sem, val) in best.items():
            if self.waited[eng].get(k, 0) >= val:
                continue
            self.waited[eng][k] = val
            waits.append((sem, val))
        return waits

    def _mark(self, tok, reads, writes):
        for t in _ts(reads):
            t.readers.append(tok)
        for t in _ts(writes):
            t.last_w = tok
            t.readers = []

    def op(self, eng, fn, reads, writes, sig=True):
        waits = self._collect(eng, reads, writes)
        if sig:
            self.cnt[eng] += 1
            tok = ("eng", eng, self.sem[eng], self.cnt[eng])
            self.unsig[eng] = False
        else:
            tok = ("eng", eng, self.sem[eng], self.cnt[eng] + 1)
            self.unsig[eng] = True
        self.q[eng].append((fn, waits, (self.sem[eng], 1) if sig else None))
        self._mark(tok, reads, writes)
        self.n_inst += 1

    def dma(self, qn, out, in_, extra_reads=(), extra_writes=(), **kw):
        eng = qn
        i = self.drr[qn]
        self.drr[qn] = (i + 1) % len(self.dsem[qn])
        sem = self.dsem[qn][i]
        reads = [in_] + list(extra_reads)
        writes = [out] + list(extra_writes)
        waits = self._collect(eng, reads, writes)
        prev = self.dcnt[qn][i]
        if prev > 0 and self.waited[eng].get(id(sem), 0) < prev:
            self.waited[eng][id(sem)] = prev
            waits.append((sem, prev))
        self.dcnt[qn][i] += 16
        tok = ("dma", qn, sem, self.dcnt[qn][i])
        o, a = _ap(out), _ap(in_)
        self.q[eng].append((lambda e: e.dma_start(out=o, in_=a, **kw), waits, (sem, 16)))
        self._mark(tok, reads, writes)
        self.n_inst += 1
        return tok

    def mm(self, out, lhsT, rhs, start=True, stop=True, sig=None, f32=False):
        if sig is None:
            sig = stop
        o, l, r = _ap(out), _ap(lhsT), _ap(rhs)
        if f32:
            if l.dtype == F32R:
                l = l.bitcast(F32)
            if r.dtype == F32R:
                r = r.bitcast(F32)
        self.op("pe", lambda e: e.matmul(o, l, r, start=start, stop=stop), [lhsT, rhs], [out], sig=sig)

    def tr(self, out, in_, ident, sig=True):
        o, i, d = _ap(out), _ap(in_), _ap(ident)
        self.op("pe", lambda e: e.transpose(o, i, d), [in_, ident], [out], sig=sig)

    def act(self, out, in_, func, bias=None, scale=1.0, accum=None, eng="act"):
        o, i = _ap(out), _ap(in_)
        kw = {}
        reads = [in_]
        writes = [out]
        if bias is not None:
            kw["bias"] = _ap(bias)
            reads.append(bias)
        kw["scale"] = _ap(scale)
        if isinstance(scale, V):
            reads.append(scale)
        if accum is not None:
            kw["accum_out"] = _ap(accum)
            writes.append(accum)
        self.op("act", lambda e: e.activation(o, i, func, **kw), reads, writes)

    def tt(self, eng, out, in0, in1, op):
        o, a, b = _ap(out), _ap(in0), _ap(in1)
        self.op(eng, lambda e: e.tensor_tensor(o, a, b, op), [in0, in1], [out])

    def ts(self, eng, out, in0, s1, op0, s2=None, op1=None, accum=None):
        o, a = _ap(out), _ap(in0)
        reads = [in0] + [s for s in (s1, s2) if isinstance(s, V)]
        writes = [out] + ([accum] if accum is not None else [])
        kw = {}
        if op1 is not None:
            kw["op1"] = op1
        if accum is not None:
            kw["accum_out"] = _ap(accum)
        self.op(eng, lambda e: e.tensor_scalar(o, a, _ap(s1), _ap(s2) if s2 is not None else None, op0, **kw), reads, writes)

    def stt(self, eng, out, in0, scalar, in1, op0, op1):
        o, a, b = _ap(out), _ap(in0), _ap(in1)
        reads = [in0, in1] + ([scalar] if isinstance(scalar, V) else [])
        self.op(eng, lambda e: e.scalar_tensor_tensor(o, a, _ap(scalar), b, op0, op1), reads, [out])

    def red(self, eng, out, in_, op, axis=AX.X):
        o, a = _ap(out), _ap(in_)
        self.op(eng, lambda e: e.tensor_reduce(o, a, axis, op), [in_], [out])

    def copy(self, eng, out, in_):
        o, a = _ap(out), _ap(in_)
        if eng == "act":
            self.op(eng, lambda e: e.copy(o, a), [in_], [out])
        else:
            self.op(eng, lambda e: e.tensor_copy(o, a), [in_], [out])

    def memset(self, eng, out, val):
        o = _ap(out)
        self.op(eng, lambda e: e.memset(o, val), [], [out])

    def scan(self, out, d0, d1, init, op0=ALU.mult, op1=ALU.add):
        o, a, b, i = _ap(out), _ap(d0), _ap(d1), _ap(init)
        reads = [d0, d1] + ([init] if isinstance(init, V) else [])
        self.op("dve", lambda e: e.tensor_tensor_scan(o, a, b, i, op0, op1), reads, [out])

    def recip(self, out, in_):
        o, a = _ap(out), _ap(in_)
        self.op("dve", lambda e: e.reciprocal(o, a), [in_], [out])

    def finish(self, final_tiles):
        nc = self.nc
        toks = []
        for t in final_tiles:
            if t.last_w is not None:
                toks.append(t.last_w)
        fin = []
        best = {}
        for (_, _, sem, val) in toks:
            if id(sem) not in best or best[id(sem)][1] < val:
                best[id(sem)] = (sem, val)
        for qn in self.dsem:
            for s, c in zip(self.dsem[qn], self.dcnt[qn]):
                if c > 0:
                    best[id(s)] = (s, max(c, best.get(id(s), (s, 0))[1]))
        for e in ("pe", "act", "dve", "pool"):
            if self.cnt[e] > 0 or self.unsig[e]:
                assert not self.unsig[e], f"engine {e} ends with unsignaled instruction"
                best[id(self.sem[e])] = (self.sem[e], self.cnt[e])
        fin = list(best.values())
        q = self.q
        with nc.Block() as block:
            def replay(lst):
                def f(e):
                    for (fn, waits, inc) in lst:
                        for (sem, val) in waits:
                            e.wait_ge(sem, val)
                        if fn is None:
                            continue
                        ins = fn(e)
                        if inc is not None:
                            ins.then_inc(inc[0], inc[1])
                return f

            @block.tensor
            def _(e):
                replay(q["pe"])(e)

            @block.scalar
            def _(e):
                replay(q["act"])(e)

            @block.vector
            def _(e):
                replay(q["dve"])(e)

            @block.gpsimd
            def _(e):
                replay(q["pool"])(e)

            @block.sync
            def _(e):
                replay(q["sync"])(e)
                for (sem, val) in fin:
                    e.wait_ge(sem, val)
        return nc

import math

NCH = 34
TOK = NCH * 128
LSEQ = 4352
D = 1024
DFF = 2816
NFT = 22
GN_EPS = 64e-5
NEGC = -math.exp(-0.5)
NST = 16
NQB = 32


def prow(n):
    return n * 128 + (1 if n < 2 else 3)


class Ctx:
    pass


def setup_consts(S, ident_d):
    idf = S.sb([128, 128], F32, "idf")
    idb = S.sb([128, 128], BF16, "idb")
    S.dma("sync", idf[:], ident_d)
    S.copy("dve", idb[:], idf[:])
    return idf, idb


def mod_derive(S, modT, ngT):
    G = S.sb([128, 3, 8, 2], F32, "G")
    SH = S.sb([128, 3, 8, 2], F32, "SH")
    GATE = S.sb([128, 3, 8, 2], F32, "GATE")
    for i in range(3):
        for j in range(2):
            S.stt("dve", G[:, i, :, j], modT[:, (3 * i + 1) * 8:(3 * i + 2) * 8, j], 1.0, ngT[:, i, :], ALU.add, ALU.mult)
            S.copy("dve", SH[:, i, :, j], modT[:, (3 * i) * 8:(3 * i + 1) * 8, j])
            S.copy("dve", GATE[:, i, :, j], modT[:, (3 * i + 2) * 8:(3 * i + 3) * 8, j])
    return dict(G=G, SH=SH, GATE=GATE, modT=modT)


def mod_vectors(S, cT_d, modw_d, modbT_d, ngT_d, wst, pm):
    cT = S.sb([128, 8, 2], F32, "cT")
    sc = S.sb([128, 8, 2], F32, "sc")
    S.dma("sync", cT[:], cT_d)
    S.act(sc[:], cT[:], AF.Silu)
    modbT = S.sb([128, 72], F32, "modbT")
    S.dma("sync", modbT[:], modbT_d)
    ngT = S.sb([128, 3, 8], F32, "ngT")
    S.dma("sync", ngT[:], ngT_d)
    modT = S.sb([128, 72, 2], F32, "modT")
    for nb in range(36):
        w = wst[nb % 2]
        S.dma("sync", w[:, 0:4, :], modw_d[nb, :, 0:4, :])
        S.dma("pool", w[:, 4:8, :], modw_d[nb, :, 4:8, :])
        for ct in range(2):
            t = nb * 2 + ct
            for k in range(8):
                S.mm(pm[:, t * 2:t * 2 + 2], w[:, k, ct * 128:(ct + 1) * 128], sc[:, k, :], start=(k == 0), stop=(k == 7),
                     sig=(k == 7 and t % 2 == 1))
    for j in range(2):
        S.tt("dve", modT[:, :, j], pm[:, 0:144].re("p (t j) -> p t j", j=2)[:, :, j], modbT[:], ALU.add)
    return mod_derive(S, modT, ngT)


def gate_bcast(S, gate_col, idf, ones, ps, scale, name):
    out = S.sb([128, 1024], F32, name)
    dg = S.sb([128, 128], F32, name + "_dg")
    for k in range(8):
        S.ts("dve", dg[:], idf[:], gate_col[:, k:k + 1], ALU.mult)
        S.mm(ps[:, (k % 4) * 128:(k % 4 + 1) * 128], ones[:], dg[:], start=True, stop=True)
        S.ts("dve", out[:, k * 128:(k + 1) * 128], ps[:, (k % 4) * 128:(k % 4 + 1) * 128], float(scale), ALU.mult)
    return out


def norm_to_hT(S, C, xt, hT, col0, G, SH):
    ss = C.small[C.si % 4]; C.si += 1
    S.act(C.junk[:], xt, AF.Square, accum=ss[:, 0:1])
    S.ts("dve", ss[:, 1:2], ss[:, 0:1], 1.0 / D, ALU.mult, 1e-6, ALU.add)
    S.act(ss[:, 3:4], ss[:, 1:2], AF.Sqrt)
    S.recip(ss[:, 2:3], ss[:, 3:4])
    xn = C.xn[C.xi % 2]; C.xi += 1
    S.act(xn[:], xt, AF.Copy, scale=ss[:, 2:3])
    pt = C.pt[C.pti % 2]; C.pti += 1
    for k in range(8):
        S.tr(pt[:, k * 128:(k + 1) * 128], xn[:, k * 128:(k + 1) * 128], C.idb[:], sig=(k == 7))
    for k in range(8):
        S.ts("dve", hT[:, k, col0:col0 + 128], pt[:, k * 128:(k + 1) * 128], G[:, k:k + 1], ALU.mult, SH[:, k:k + 1], ALU.add)


def ffn(S, C, xs, xd, w1_d, w2_d, mv, ni, groups, jf, after_chunk=None):
    G, SH = mv["G"], mv["SH"]
    w2b = C.w2b
    first = True
    for grp in groups:
        nt = len(grp) * 128
        for li, ci in enumerate(grp):
            xt = C.xt[C.xti % 3]; C.xti += 1
            S.dma("sync", xt[:], xs(ci))
            j = jf(ci)
            norm_to_hT(S, C, xt[:], C.hT, li * 128, G[:, ni, :, j], SH[:, ni, :, j])
        for ft in range(NFT):
            wst = C.wst[ft % 2]
            wb = C.w1b[ft % 2]
            S.dma("sync", wst[:, 0:4, :], w1_d[ft, :, 0:4, :])
            S.dma("pool", wst[:, 4:8, :], w1_d[ft, :, 4:8, :])
            S.copy("pool", wb[:], wst[:])
            if first:
                w2s = C.w2st[ft % 2]
                S.dma("sync", w2s[:], w2_d[ft * 128:(ft + 1) * 128, :])
                S.copy("pool", w2b[:, ft, :], w2s[:])
            for b0 in range(0, nt, 512):
                bw = min(512, nt - b0)
                pg = C.pa[C.pai % 4]; C.pai += 1
                pu = C.pa[C.pai % 4]; C.pai += 1
                for k in range(8):
                    S.mm(pg[:, 0:bw], wb[:, k, 0:128], C.hT[:, k, b0:b0 + bw], start=(k == 0), stop=(k == 7))
                for k in range(8):
                    S.mm(pu[:, 0:bw], wb[:, k, 128:256], C.hT[:, k, b0:b0 + bw], start=(k == 0), stop=(k == 7))
                sg = C.sg[C.sgi % 2]; C.sgi += 1
                S.act(sg[:, 0:bw], pg[:, 0:bw], AF.Silu)
                S.tt("dve", C.actT[:, ft, b0:b0 + bw], sg[:, 0:bw], pu[:, 0:bw], ALU.mult)
        first = False
        for li, ci in enumerate(grp):
            xt = C.xt[C.xti % 3]; C.xti += 1
            S.dma("sync", xt[:], xs(ci))
            gb = C.gateb[(ni, jf(ci))]
            for h in range(2):
                pc = C.pcs[h]
                for ft in range(NFT):
                    S.mm(pc[:], C.actT[:, ft, li * 128:(li + 1) * 128], w2b[:, ft, h * 512:(h + 1) * 512],
                         start=(ft == 0), stop=(ft == NFT - 1))
                S.tt("dve", C.tmp[:, h * 512:(h + 1) * 512], pc[:], gb[:, h * 512:(h + 1) * 512], ALU.mult)
            S.tt("pool", xt[:], xt[:], C.tmp[:], ALU.add)
            if xd is not None:
                S.dma("pool", xd(ci), xt[:])
            if after_chunk is not None:
                after_chunk(ci, li, xt)
        yield grp


def alloc_common(S, PS):
    C = Ctx()
    C.small = [S.sb([128, 4], F32, f"small{i}") for i in range(4)]; C.si = 0
    C.junk = S.sb([128, 1024], BF16, "junk")
    C.xn = [S.sb([128, 1024], BF16, f"xn{i}") for i in range(2)]; C.xi = 0
    C.pt = PS["b"]; C.pti = 0
    C.pa = PS["g"][0:4]; C.pai = 0
    C.pcs = PS["g"][4:6]
    C.xt = [S.sb([128, 1024], F32, f"xt{i}") for i in range(3)]; C.xti = 0
    C.hT = S.sb([128, 8, 1152], BF16, "hT")
    C.actT = S.sb([128, NFT, 1152], BF16, "actT")
    C.w2b = S.sb([128, NFT, 1024], BF16, "w2b")
    C.wst = [S.sb([128, 8, 256], F32, f"wst{i}") for i in range(2)]
    C.w1b = [S.sb([128, 8, 256], BF16, f"w1b{i}") for i in range(2)]
    C.w2st = [S.sb([128, 1024], F32, f"w2st{i}") for i in range(2)]
    C.sg = [S.sb([128, 512], F32, f"sg{i}") for i in range(2)]; C.sgi = 0
    C.tmp = S.sb([128, 1024], F32, "tmp")
    C.ones = S.sb([128, 128], F32, "ones")
    S.memset("dve", C.ones[:], 1.0)
    return C


GROUPS34 = [list(range(0, 9)), list(range(9, 18)), list(range(18, 26)), list(range(26, 34))]
GROUPS32 = [list(range(0, 8)), list(range(8, 16)), list(range(16, 24)), list(range(24, 32))]
JF34 = lambda ci: 0 if ci < 2 else 1


def phase1(S, PS, IN, SC):
    C = alloc_common(S, PS)
    C.idf, C.idb = setup_consts(S, IN["ident"][:])
    mv = mod_vectors(S, IN["cT"][:], IN["modw"][0], IN["modbT"][0], IN["ngT"][0], C.wst, C.pa[0])
    S.dma("sync", SC["modT0"][:], mv["modT"][:].re("p t j -> p (t j)"))
    C.gateb = {}
    for j in range(2):
        C.gateb[(0, j)] = gate_bcast(S, mv["GATE"][:, 0, :, j], C.idf, C.ones, C.pa[1 + j], 0.5, f"gb0{j}")
    zt = S.sb([2, 2304], F32, "zt")
    S.memset("dve", zt[:], 0.0)
    ppad = SC["ppad"]
    S.dma("sync", ppad[0:1, :], zt[0:1, :]); S.dma("sync", ppad[257:259, :], zt[0:2, :]); S.dma("sync", ppad[4355:4356, :], zt[0:1, :])
    pst = [S.sb([128, 256], F32, f"pst{i}") for i in range(2)]
    psti = [0]

    def xs(ci):
        return IN["ctx"][ci * 128:(ci + 1) * 128, :] if ci < 2 else IN["x"][(ci - 2) * 128:(ci - 1) * 128, :]

    def xd(ci):
        return SC["x1"][ci * 128:(ci + 1) * 128, :]

    def after(ci, li, xt):
        j = JF34(ci)
        norm_to_hT(S, C, xt[:], C.hT, li * 128, mv["G"][:, 1, :, j], mv["SH"][:, 1, :, j])

    win = IN["win_ab"]
    for grp in ffn(S, C, xs, xd, IN["w1"][0, 0], IN["w2"][0, 0], mv, 0, GROUPS34, JF34, after_chunk=after):
        for cb in range(9):
            wst = C.wst[cb % 2]; wb = C.w1b[cb % 2]
            S.dma("sync", wst[:, 0:4, :], win[cb, :, 0:4, :])
            S.dma("pool", wst[:, 4:8, :], win[cb, :, 4:8, :])
            S.copy("pool", wb[:], wst[:])
            for li, ci in enumerate(grp):
                pp = C.pa[C.pai % 4]; C.pai += 1
                for k in range(8):
                    S.mm(pp[:, 0:256], C.hT[:, k, li * 128:(li + 1) * 128], wb[:, k, :], start=(k == 0), stop=(k == 7))
                st = pst[psti[0] % 2]; psti[0] += 1
                S.copy("act", st[:], pp[:, 0:256])
                S.dma("sync", ppad[prow(ci):prow(ci) + 128, cb * 256:(cb + 1) * 256], st[:])


def phase2a(S, PS, IN, SC, MD=F32R, nblocks=NCH):
    ppad = SC["ppad"]

    def ld(dv, shape, nm, dt=F32):
        t = S.sb(shape, dt, nm)
        S.dma("sync", t[:], dv)
        return t
    mu0 = ld(IN["mub"][0], [128, 1536], "mu0"); mu1 = ld(IN["mub"][1], [128, 1536], "mu1")
    c0 = S.sb([128, 1536], F32, "c0")
    S.tt("dve", c0[:], mu0[:], mu1[:], ALU.add)
    S.ts("dve", c0[:], c0[:], -1.0, ALU.mult, 1.0, ALU.add)
    kkb = ld(IN["kkb"][:], [128, 512], "kkb"); kab = ld(IN["kab"][:], [128, 512], "kab")
    omka = S.sb([128, 512], F32, "omka")
    S.ts("dve", omka[:], kab[:], -1.0, ALU.mult, 1.0, ALU.add)
    g2 = ld(IN["g2"][:], [128, 512], "g2")
    msk = IN["msk"]
    mUs = ld(msk[0], [128, 128], "mUs"); mUi = ld(msk[1], [128, 128], "mUi"); mLs = ld(msk[2], [128, 128], "mLs")
    idf = ld(msk[3], [128, 128], "idf"); mLi = ld(msk[4], [128, 128], "mLi")
    cm = {}
    for nm, m_ in (("Ui", mUi), ("Us", mUs), ("Ls", mLs), ("Li", mLi)):
        cm[nm] = S.sb([128, 128], MD, "c" + nm)
        S.ts("dve", cm[nm][:], m_[:], NEGC, ALU.mult)
    negc = S.sb([128, 2], MD, "negc")
    S.ts("dve", negc[:], mUi[:, 0:2], 0.0, ALU.mult, NEGC, ALU.add)
    idm = S.sb([128, 128], MD, "idm")
    S.copy("dve", idm[:], idf[:])
    w2a = []; a2a = []
    for dr in range(2):
        t_ = ld(IN["w2a"][dr], [65, 512], f"w2a{dr}"); t2_ = S.sb([65, 512], MD, f"w2ar{dr}"); S.copy("dve", t2_[:], t_[:]); w2a.append(t2_)
        t_ = ld(IN["a2a"][dr], [65, 512], f"a2a{dr}"); t2_ = S.sb([65, 512], MD, f"a2ar{dr}"); S.copy("dve", t2_[:], t_[:]); a2a.append(t2_)
    g2r = S.sb([128, 512], MD, "g2r"); S.copy("dve", g2r[:], g2[:]); g2 = g2r

    pb = PS["g"]
    pbi = [0]

    def bank():
        b = pb[pbi[0] % 6]; pbi[0] += 1
        return b

    def t512(nm, dt=F32):
        return S.sb([128, 512], dt, nm)

    rc = S.sb([128, 1536], F32, "rc"); rp = S.sb([128, 1536], F32, "rp"); rn_ = S.sb([128, 1536], F32, "rn")
    mix = S.sb([128, 1536], MD, "mix"); mt = S.sb([128, 1536], F32, "mt")
    TW = S.sb([65, 128], MD, "TW"); AL = S.sb([65, 128], MD, "AL"); SG = S.sb([128, 128], MD, "SG")
    S.ts("dve", TW[:], mUi[0:65, :], 0.0, ALU.mult, 1.0, ALU.add); S.ts("dve", AL[:], mUi[0:65, :], 0.0, ALU.mult, 1.0, ALU.add)
    lmix = S.sb([128, 256], MD, "lmix"); lmt = S.sb([128, 256], F32, "lmt")
    lc = S.sb([128, 256], F32, "lc"); lp = S.sb([128, 256], F32, "lp"); ln_ = S.sb([128, 256], F32, "ln")
    mul0 = ld(IN["mulb"][0], [128, 256], "mul0"); mul1 = ld(IN["mulb"][1], [128, 256], "mul1")
    c0lb = S.sb([128, 256], F32, "c0lb")
    S.tt("dve", c0lb[:], mul0[:], mul1[:], ALU.add)
    S.ts("dve", c0lb[:], c0lb[:], -1.0, ALU.mult, 1.0, ALU.add)
    sig = t512("sig", MD); a_ = t512("a"); gt = t512("gt")
    kk = t512("kk"); sq = t512("sq"); ss = S.sb([128, 8], F32, "ss"); rn8 = S.sb([128, 8], F32, "rn8")
    kd = t512("kd"); tq = t512("tq"); bq = t512("bq")
    Gc = t512("G"); Gp = t512("Gp"); Gi = t512("Gi"); Ge = t512("Ge")
    A = t512("A", MD); B = t512("B", MD); K = t512("K", MD); Rq = t512("Rq", MD)
    B2m = t512("B2m", MD); K2m = t512("K2m", MD)
    AT = S.sb([64, 8, 128], MD, "AT"); BT = S.sb([64, 8, 128], MD, "BT"); KT = S.sb([64, 8, 128], MD, "KT")
    RT = S.sb([64, 8, 128], MD, "RT")
    mat = lambda nm, dt=MD: [S.sb([128, 4, 128], dt, f"{nm}{g}") for g in range(2)]
    Nm = [mat("Nm0"), mat("Nm1")]; NT = [mat("NT0"), mat("NT1")]
    Mak = mat("Mak"); Mbr = mat("Mbr"); Mkr = mat("Mkr")
    Tf = mat("Tf", MD); Tm = Tf
    WTm = t512("WTm", MD); X1Tm = t512("X1Tm", MD); UlTm = t512("UlTm", MD)
    Rpf = S.sb([64, 8, 128], MD, "Rpf")
    gC = S.sb([64, 16], F32, "gC")
    dgG = S.sb([64, 8, 64], F32, "dgG")
    Pf = [S.sb([64, 8, 64], MD, f"Pf{c}") for c in range(2)]
    QTf = [S.sb([64, 8, 64], F32, f"QTf{c}") for c in range(2)]
    Yloc = t512("Yloc")
    ST = [S.sb([64, 8, 64], MD, f"ST{i}") for i in range(2)]
    yt = [t512(f"yt{i}") for i in range(2)]
    v3 = lambda v: v.re("p (h k) -> p h k", k=64)
    m3 = lambda v: v.re("p (h t) -> p h t", t=128)
    sti = 0
    for dr in range(2):
        if dr == 0:
            m_strict, m_strictT, m_incl = mUs, mLs, mUi
            c_incl, c_strict, c_end = cm["Ui"], cm["Us"], cm["Ls"]
            border = list(range(nblocks)); corder = [0, 1]
        else:
            m_strict, m_strictT, m_incl = mLs, mUs, mLi
            c_incl, c_strict, c_end = cm["Li"], cm["Ls"], cm["Us"]
            border = [1, 0] + list(range(NCH - 1, 1, -1)); corder = [1, 0]
            border = border[:nblocks]
        S.ts("dve", ST[sti % 2][:].re("p h k -> p (h k)"), kkb[0:64, :], 0.0, ALU.mult)
        for n in border:
            r0 = prow(n)
            S.dma("sync", rc[:], ppad[r0:r0 + 128, 0:1536])
            S.dma("pool", rp[:], ppad[r0 - 1:r0 + 127, 0:1536])
            S.dma("sync", rn_[:], ppad[r0 + 1:r0 + 129, 0:1536])
            S.dma("pool", lc[:], ppad[r0:r0 + 128, 1536:1792])
            S.dma("pool", lp[:], ppad[r0 - 1:r0 + 127, 1536:1792])
            S.dma("sync", ln_[:], ppad[r0 + 1:r0 + 129, 1536:1792])
            S.tt("dve", mix[:], rc[:], c0[:], ALU.mult)
            S.tt("pool", mt[:], rp[:], mu0[:], ALU.mult)
            S.tt("dve", mix[:], mix[:], mt[:], ALU.add)
            S.tt("pool", mt[:], rn_[:], mu1[:], ALU.mult)
            S.tt("dve", mix[:], mix[:], mt[:], ALU.add)
            r = mix[:, 0:512]; k = mix[:, 512:1024]; v = mix[:, 1024:1536]
            if dr == 0:
                S.dma("pool", SC["rvo"][n * 128:(n + 1) * 128, 0:512], r)
                S.dma("pool", SC["rvo"][n * 128:(n + 1) * 128, 512:1024], v)
            S.tt("dve", lmix[:], lc[:], c0lb[:], ALU.mult)
            S.tt("pool", lmt[:], lp[:], mul0[:], ALU.mult)
            S.tt("dve", lmix[:], lmix[:], lmt[:], ALU.add)
            S.tt("pool", lmt[:], ln_[:], mul1[:], ALU.mult)
            S.tt("dve", lmix[:], lmix[:], lmt[:], ALU.add)
            pl = bank()
            S.mm(pl[0:64, 0:128], lmix[:, 0:64], idm[:], sig=False, f32=True)
            S.mm(pl[0:64, 128:256], lmix[:, 64:128], idm[:], sig=False, f32=True)
            S.mm(pl[:, 256:384], lmix[:, 128:256], idm[:])
            S.act(TW[0:64, :], pl[0:64, 0:128], AF.Tanh)
            S.copy("act", AL[0:64, :], pl[0:64, 128:256])
            S.act(SG[:], pl[:, 256:384], AF.Sigmoid)
            pw_ = bank(); S.mm(pw_[:], TW[:], w2a[dr][:])
            S.act(sig[:], pw_[:], AF.Sigmoid)
            pa_ = bank(); S.mm(pa_[:], AL[:], a2a[dr][:])
            S.act(a_[:], pa_[:], AF.Sigmoid)
            if dr == 0:
                pg_ = bank(); S.mm(pg_[:], SG[:], g2[:])
                S.copy("act", gt[:], pg_[:])
                S.dma("pool", SC["gto"][n * 128:(n + 1) * 128, :], gt[:])
            S.tt("dve", kk[:], k, kkb[:], ALU.mult)
            S.tt("pool", sq[:], kk[:], kk[:], ALU.mult)
            S.red("dve", ss[:], v3(sq[:]), ALU.add)
            S.ts("dve", ss[:], ss[:], 1e-12, ALU.max)
            S.act(ss[:], ss[:], AF.Sqrt)
            S.recip(rn8[:], ss[:])
            S.tt("dve", v3(kk[:]), v3(kk[:]), rn8[:].re("p (h o) -> p h o", o=1).bc([128, 8, 64]), ALU.mult)
            S.tt("pool", tq[:], a_[:], kab[:], ALU.mult)
            S.tt("pool", tq[:], tq[:], omka[:], ALU.add)
            S.tt("pool", kd[:], k, tq[:], ALU.mult)
            S.dma("pool", SC["kdo"][dr, n * 128:(n + 1) * 128, :], kd[:])
            S.tt("pool", bq[:], kk[:], a_[:], ALU.mult)
            pc1 = bank(); S.mm(pc1[:], c_incl[:], sig[:])
            pc2 = bank(); S.mm(pc2[:], c_strict[:], sig[:])
            pc3 = bank(); S.mm(pc3[:], c_end[:], sig[:])
            S.act(Gc[:], pc1[:], AF.Exp)
            S.act(Gi[:], pc1[:], AF.Exp, scale=-1.0)
            S.act(Gp[:], pc2[:], AF.Exp)
            S.act(Ge[:], pc3[:], AF.Exp)
            S.stt("dve", A[:], kk[:], -1.0, Gp[:], ALU.mult, ALU.mult)
            S.tt("dve", B[:], bq[:], Gi[:], ALU.mult)
            S.tt("pool", K[:], kd[:], Gi[:], ALU.mult)
            S.tt("pool", Rq[:], r, Gc[:], ALU.mult)
            S.tt("pool", B2m[:], bq[:], Ge[:], ALU.mult)
            S.tt("pool", K2m[:], kd[:], Ge[:], ALU.mult)
            Am_ = A
            Vv = v
            for (src, dst) in ((A, AT), (B, BT), (K, KT), (Rq, RT)):
                for hg in range(2):
                    p_ = bank()
                    for hh in range(4):
                        h = hg * 4 + hh
                        S.mm(p_[0:64, hh * 128:(hh + 1) * 128], src[:, h * 64:(h + 1) * 64], idm[:], sig=(hh == 3), f32=True)
                    S.copy("act", dst[:, hg * 4:(hg + 1) * 4, :], m3(p_[0:64, :]))
            RTf_ = RT

            def mmat(dst, LT, RTt, mask, hg):
                p_ = bank()
                for hh in range(4):
                    h = hg * 4 + hh
                    S.mm(p_[:, hh * 128:(hh + 1) * 128], LT[:, h, :], RTt[:, h, :], sig=(hh == 3))
                S.tt("dve", dst[hg][:], m3(p_[:]), mask[:].re("p (o t) -> p o t", o=1).bc([128, 4, 128]), ALU.mult)
            for hg in range(2):
                mmat(Nm[0], BT, AT, m_strict, hg)
                mmat(NT[0], AT, BT, m_strictT, hg)
                mmat(Mak, KT, AT, m_strict, hg)
                mmat(Mbr, BT, RT, m_incl, hg)
                mmat(Mkr, KT, RT, m_incl, hg)
            for hg in range(2):
                S.tt("dve", Tf[hg][:], Nm[0][hg][:], idf[:].re("p (o t) -> p o t", o=1).bc([128, 4, 128]), ALU.add)
            cur = 0
            for lev in range(5):
                nxt = 1 - cur
                last = lev == 4
                for hg in range(2):
                    if not last:
                        p1 = bank()
                        for hh in range(4):
                            S.mm(p1[:, hh * 128:(hh + 1) * 128], NT[cur][hg][:, hh, :], Nm[cur][hg][:, hh, :], sig=(hh == 3))
                        S.copy("act", Nm[nxt][hg][:], m3(p1[:]))
                    p2 = bank()
                    for hh in range(4):
                        S.mm(p2[:, hh * 128:(hh + 1) * 128], Nm[cur][hg][:, hh, :], NT[cur][hg][:, hh, :], sig=(hh == 3))
                    S.copy("dve" if hg == 0 else "act", NT[nxt][hg][:], m3(p2[:]))
                for hg in range(2):
                    p3 = bank()
                    for hh in range(4):
                        S.mm(p3[:, hh * 128:(hh + 1) * 128], NT[nxt][hg][:, hh, :], Tm[hg][:, hh, :], sig=(hh == 3))
                    S.tt("dve", Tf[hg][:], Tf[hg][:], m3(p3[:]), ALU.add)
                cur = nxt
            p_ = bank()
            for h in range(8):
                S.mm(p_[:, h * 64:(h + 1) * 64], Tm[h // 4][:, h % 4, :], Am_[:, h * 64:(h + 1) * 64], sig=(h == 7))
            S.copy("act", WTm[:], p_[:])
            p_ = bank()
            for h in range(8):
                S.mm(p_[:, h * 64:(h + 1) * 64], Mak[h // 4][:, h % 4, :], Vv[:, h * 64:(h + 1) * 64], sig=(h == 7))
            S.copy("dve", X1Tm[:], p_[:])
            p_ = bank()
            for h in range(8):
                S.mm(p_[:, h * 64:(h + 1) * 64], Tm[h // 4][:, h % 4, :], X1Tm[:, h * 64:(h + 1) * 64], sig=(h == 7))
            S.copy("act", UlTm[:], p_[:])
            for hg in range(2):
                p_ = bank()
                for hh in range(4):
                    h = hg * 4 + hh
                    S.mm(p_[0:64, hh * 128:(hh + 1) * 128], WTm[:, h * 64:(h + 1) * 64], Mbr[hg][:, hh, :], sig=(hh == 3), f32=True)
                S.tt("dve", Rpf[:, hg * 4:(hg + 1) * 4, :], m3(p_[0:64, :]), RTf_[:, hg * 4:(hg + 1) * 4, :], ALU.add)
            pgc = [bank(), bank()]
            for c in range(2):
                for h in range(8):
                    S.mm(pgc[c][0:64, h:h + 1], sig[c * 64:(c + 1) * 64, h * 64:(h + 1) * 64], negc[c * 64:(c + 1) * 64, 0:1], sig=(h == 7), f32=True)
                S.act(gC[:, c * 8:(c + 1) * 8], pgc[c][0:64, 0:8], AF.Exp)
            for c in range(2):
                pc = slice(c * 64, (c + 1) * 64)
                p_ = bank()
                for h in range(8):
                    S.mm(p_[0:64, h * 64:(h + 1) * 64], WTm[pc, h * 64:(h + 1) * 64], B2m[pc, h * 64:(h + 1) * 64], sig=(h == 7), f32=True)
                S.tt("pool", dgG[:], idf[0:64, 0:64].re("p (o k) -> p o k", o=1).bc([64, 8, 64]),
                     gC[:, c * 8:(c + 1) * 8].re("p (h o) -> p h o", o=1).bc([64, 8, 64]), ALU.mult)
                S.tt("dve", Pf[c][:], v3(p_[0:64, :]), dgG[:], ALU.add)
                p_ = bank()
                for h in range(8):
                    S.mm(p_[0:64, h * 64:(h + 1) * 64], B2m[pc, h * 64:(h + 1) * 64], UlTm[pc, h * 64:(h + 1) * 64], start=True, stop=False, sig=False, f32=True)
                    S.mm(p_[0:64, h * 64:(h + 1) * 64], K2m[pc, h * 64:(h + 1) * 64], Vv[pc, h * 64:(h + 1) * 64], start=False, stop=True, sig=(h == 7), f32=True)
                S.copy("act", QTf[c][:], v3(p_[0:64, :]))
            p_ = bank()
            for h in range(8):
                S.mm(p_[:, h * 64:(h + 1) * 64], Mbr[h // 4][:, h % 4, :], UlTm[:, h * 64:(h + 1) * 64], start=True, stop=False, sig=False)
                S.mm(p_[:, h * 64:(h + 1) * 64], Mkr[h // 4][:, h % 4, :], Vv[:, h * 64:(h + 1) * 64], start=False, stop=True, sig=(h == 7))
            S.copy("act", Yloc[:], p_[:])
            ytile = yt[n % 2]
            for c in corder:
                pc = slice(c * 64, (c + 1) * 64)
                st_cur = ST[sti % 2]; st_nxt = ST[(sti + 1) % 2]; sti += 1
                p_ = bank()
                for h in range(8):
                    S.mm(p_[:, h * 64:(h + 1) * 64], Rpf[:, h, :], st_cur[:, h, :], sig=(h == 7))
                S.tt("dve", ytile[pc, :], p_[pc, :], Yloc[pc, :], ALU.add)
                p2 = bank()
                for h in range(8):
                    S.mm(p2[0:64, h * 64:(h + 1) * 64], Pf[c][:, h, :], st_cur[:, h, :], sig=(h == 7), f32=True)
                S.tt("dve", st_nxt[:], v3(p2[0:64, :]), QTf[c][:], ALU.add)
            S.dma("sync", SC["yd"][dr, n * 128:(n + 1) * 128, :], ytile[:])


def phase2b(S, PS, IN, SC):
    CH = 128
    ppad = SC["ppad"]
    ident = IN["ident"]
    idf0 = S.sb([128, 128], F32, "idf0"); S.dma("sync", idf0[:], ident[:])
    Jm0 = S.sb([128, 128], F32, "Jm0"); S.dma("sync", Jm0[:], IN["msk"][5])
    idf = S.sb([128, 128], F32R, "idf"); S.copy("dve", idf[:], idf0[:])
    Jm = S.sb([128, 128], F32R, "Jm"); S.copy("dve", Jm[:], Jm0[:])
    pb = PS["g"]
    sm = lambda nm: S.sb([128, NST], F32, nm)
    big = lambda nm, dt=F32: S.sb([128, NST, 32], dt, nm)
    are = sm("are"); aim = sm("aim"); lst = sm("lst")
    bre = big("bre"); bim = big("bim"); cre0 = big("cre0"); cim = big("cim"); ncim = big("ncim", F32R); cre = big("cre", F32R)
    lre = sm("lre"); dt_ = sm("dt"); zr = sm("zr"); th = sm("th"); rho = sm("rho")
    sa = sm("sa"); sk = sm("sk"); sr = sm("sr")
    cs = sm("cs"); sn = sm("sn")
    abre = sm("abre"); abim = sm("abim"); den = sm("den"); rden = sm("rden"); t1 = sm("t1"); t2 = sm("t2")
    fre = sm("fre"); fim = sm("fim"); am1 = sm("am1")
    bbre = big("bbre", F32R); bbim = big("bbim", F32R); u1 = big("u1"); u2 = big("u2")
    BBTre = S.sb([32, NST, 128], F32R, "BBTre"); BBTim = S.sb([32, NST, 128], F32R, "BBTim")
    Ec = S.sb([128, NST, CH], F32, "Ec"); Es = S.sb([128, NST, CH], F32, "Es")
    w1 = S.sb([128, NST, CH // 2], F32, "w1"); w2 = S.sb([128, NST, CH // 2], F32, "w2")
    rhob = S.sb([128, NST, CH], F32, "rhob")
    utok = [S.sb([128, 512], F32, f"utok{i}") for i in range(2)]
    utokr = [S.sb([128, 512], F32R, f"utokr{i}") for i in range(2)]
    ut = [S.sb([32, NST, CH], F32R, f"ut{i}") for i in range(2)]
    Zre_ = [S.sb([128, NST, CH], F32, f"Zre{i}") for i in range(2)]; Zim_ = [S.sb([128, NST, CH], F32, f"Zim{i}") for i in range(2)]
    wre_ = [S.sb([128, NST, CH], F32, "wre0")] * 2; wim_ = [S.sb([128, NST, CH], F32, "wim0")] * 2
    xre = [S.sb([128, NST, CH], F32R, f"xre{i}") for i in range(2)]
    xim = [S.sb([128, NST, CH], F32R, f"xim{i}") for i in range(2)]
    ta_ = [[S.sb([128, 4, CH], F32, f"ta{q}{i}") for i in range(4)] for q in range(2)]
    tb_ = [[S.sb([128, 4, CH], F32, f"tb0{i}") for i in range(4)]] * 2
    tai = 0
    yst = [S.sb([128, 512], F32R, f"yst{i}") for i in range(2)]
    yst2 = [S.sb([128, 512], F32, f"ystb{i}") for i in range(2)]
    MAGIC = 12582912.0
    TWO_PI = 2.0 * math.pi
    pbi = 0
    cc = 0
    for dr in range(2):
        for (t_, nm) in ((are, "are"), (aim, "aim"), (lst, "lst")):
            S.dma("sync", t_[:], IN[nm][dr])
        for (t_, nm) in ((bre, "bre"), (bim, "bim"), (cre0, "cre"), (cim, "cim")):
            S.dma("sync", t_[:], IN[nm][dr])
        S.ts("dve", ncim[:], cim[:], -1.0, ALU.mult)
        S.copy("dve", cre[:], cre0[:])
        S.ts("dve", lre[:], are[:], -1e-4, ALU.min)
        S.act(dt_[:], lst[:], AF.Exp)
        S.tt("dve", zr[:], lre[:], dt_[:], ALU.mult)
        S.tt("dve", th[:], aim[:], dt_[:], ALU.mult)
        S.act(rho[:], zr[:], AF.Exp)

        def sin_reduced(out, ang, shift):
            S.ts("dve", sa[:], ang[:], float(shift), ALU.add)
            S.ts("dve", sk[:], sa[:], 1.0 / TWO_PI, ALU.mult, MAGIC, ALU.add)
            S.ts("dve", sk[:], sk[:], MAGIC, ALU.subtract)
            S.stt("dve", sr[:], sk[:], -TWO_PI, sa[:], ALU.mult, ALU.add)
            S.ts("dve", sr[:], sr[:], 3.14159, ALU.min, -3.14159, ALU.max)
            S.act(out, sr[:], AF.Sin)
        sin_reduced(sn[:], th, 0.0)
        sin_reduced(cs[:], th, math.pi / 2)
        S.tt("dve", abre[:], rho[:], cs[:], ALU.mult)
        S.tt("dve", abim[:], rho[:], sn[:], ALU.mult)
        S.tt("dve", t1[:], lre[:], lre[:], ALU.mult)
        S.tt("dve", t2[:], aim[:], aim[:], ALU.mult)
        S.tt("dve", den[:], t1[:], t2[:], ALU.add)
        S.recip(rden[:], den[:])
        S.ts("dve", am1[:], abre[:], -1.0, ALU.add)
        S.tt("dve", t1[:], am1[:], lre[:], ALU.mult)
        S.tt("dve", t2[:], abim[:], aim[:], ALU.mult)
        S.tt("dve", t1[:], t1[:], t2[:], ALU.add)
        S.tt("dve", fre[:], t1[:], rden[:], ALU.mult)
        S.tt("dve", t1[:], abim[:], lre[:], ALU.mult)
        S.tt("dve", t2[:], am1[:], aim[:], ALU.mult)
        S.tt("dve", t1[:], t1[:], t2[:], ALU.subtract)
        S.tt("dve", fim[:], t1[:], rden[:], ALU.mult)
        fre_b = fre[:].re("p (t o) -> p t o", o=1).bc([128, NST, 32])
        fim_b = fim[:].re("p (t o) -> p t o", o=1).bc([128, NST, 32])
        S.tt("dve", u1[:], bre[:], fre_b, ALU.mult)
        S.tt("dve", u2[:], bim[:], fim_b, ALU.mult)
        S.tt("dve", bbre[:], u1[:], u2[:], ALU.subtract)
        S.tt("dve", u1[:], bim[:], fre_b, ALU.mult)
        S.tt("dve", u2[:], bre[:], fim_b, ALU.mult)
        S.tt("dve", bbim[:], u1[:], u2[:], ALU.add)
        for (src, dst) in ((bbre, BBTre), (bbim, BBTim)):
            for g4 in range(4):
                p_ = pb[pbi % 6]; pbi += 1
                for jj in range(4):
                    j = g4 * 4 + jj
                    S.mm(p_[0:32, jj * 128:(jj + 1) * 128], src[:, j, :], idf[:], sig=(jj == 3), f32=True)
                S.copy("dve", dst[:, g4 * 4:(g4 + 1) * 4, :], p_[0:32, :].re("p (a b) -> p a b", b=128))
        S.copy("dve", Ec[:, :, 0], cs[:])
        S.copy("dve", Es[:, :, 0], sn[:])
        m = 1
        while m < CH:
            cb = Ec[:, :, m - 1:m].bc([128, NST, m]); sb_ = Es[:, :, m - 1:m].bc([128, NST, m])
            S.tt("dve", w1[:, :, 0:m], Ec[:, :, 0:m], cb, ALU.mult)
            S.tt("dve", w2[:, :, 0:m], Es[:, :, 0:m], sb_, ALU.mult)
            S.tt("dve", Ec[:, :, m:2 * m], w1[:, :, 0:m], w2[:, :, 0:m], ALU.subtract)
            S.tt("dve", w1[:, :, 0:m], Ec[:, :, 0:m], sb_, ALU.mult)
            S.tt("dve", w2[:, :, 0:m], Es[:, :, 0:m], cb, ALU.mult)
            S.tt("dve", Es[:, :, m:2 * m], w1[:, :, 0:m], w2[:, :, 0:m], ALU.add)
            m *= 2
        S.copy("dve", rhob[:], rho[:].re("p (t o) -> p t o", o=1).bc([128, NST, CH]))
        Pm = idf if dr == 0 else Jm
        border = list(range(NCH)) if dr == 0 else [1, 0] + list(range(NCH - 1, 1, -1))
        for ci, n in enumerate(border):
            r0 = prow(n)
            utk0 = utok[cc % 2]
            S.dma("sync", utk0[:], ppad[r0:r0 + 128, 1792:2304])
            utk = utokr[cc % 2]
            S.copy("act", utk[:], utk0[:])
            u = ut[cc % 2]
            for g4 in range(4):
                p_ = pb[pbi % 6]; pbi += 1
                for jj in range(4):
                    j = g4 * 4 + jj
                    S.mm(p_[0:32, jj * 128:(jj + 1) * 128], utk[:, j * 32:(j + 1) * 32], Pm[:], sig=(jj == 3), f32=True)
                S.copy("act", u[:, g4 * 4:(g4 + 1) * 4, :], p_[0:32, :].re("p (a b) -> p a b", b=128))
            xr, xi = xre[cc % 2], xim[cc % 2]
            xr_prev, xi_prev = xre[(cc + 1) % 2], xim[(cc + 1) % 2]
            Zre, Zim, wre, wim = Zre_[cc % 2], Zim_[cc % 2], wre_[cc % 2], wim_[cc % 2]
            for g4 in range(4):
                ta = ta_[tai % 2]; tb = tb_[tai % 2]; tai += 1
                pr = pb[pbi % 6]; pbi += 1
                pi_ = pb[pbi % 6]; pbi += 1
                for jj in range(4):
                    j = g4 * 4 + jj
                    S.mm(pr[:, jj * CH:(jj + 1) * CH], BBTre[:, j, :], u[:, j, :], sig=False)
                for jj in range(4):
                    j = g4 * 4 + jj
                    S.mm(pi_[:, jj * CH:(jj + 1) * CH], BBTim[:, j, :], u[:, j, :], sig=(jj == 3))
                sl = slice(g4 * 4, (g4 + 1) * 4)
                prv = pr[:, :].re("p (a b) -> p a b", b=CH); piv = pi_[:, :].re("p (a b) -> p a b", b=CH)
                a0, a1, a2, a3 = ta
                S.tt("dve", a0[:], prv, Ec[:, sl, :], ALU.mult)
                S.tt("dve", a1[:], piv, Es[:, sl, :], ALU.mult)
                S.tt("pool", Zre[:, sl, :], a0[:], a1[:], ALU.add)
                S.tt("dve", a2[:], piv, Ec[:, sl, :], ALU.mult)
                S.tt("dve", a3[:], prv, Es[:, sl, :], ALU.mult)
                S.tt("pool", Zim[:, sl, :], a2[:], a3[:], ALU.subtract)
                for jj in range(4):
                    j = g4 * 4 + jj
                    for (wt, zt, xp) in ((wre, Zre, xr_prev), (wim, Zim, xi_prev)):
                        init = 0.0 if ci == 0 else xp[:, j, CH - 1:CH]
                        S.scan(wt[:, j, :], rhob[:, j, :], zt[:, j, :], init)
                b0, b1, b2, b3 = tb
                S.tt("pool", b0[:], wre[:, sl, :], Ec[:, sl, :], ALU.mult)
                S.tt("pool", b1[:], wim[:, sl, :], Es[:, sl, :], ALU.mult)
                S.tt("pool", xr[:, sl, :], b0[:], b1[:], ALU.subtract)
                S.tt("pool", b2[:], wim[:, sl, :], Ec[:, sl, :], ALU.mult)
                S.tt("pool", b3[:], wre[:, sl, :], Es[:, sl, :], ALU.mult)
                S.tt("pool", xi[:, sl, :], b2[:], b3[:], ALU.add)
            py = pb[pbi % 6]; pbi += 1
            for j in range(NST):
                S.mm(py[:, j * 32:(j + 1) * 32], xr[:, j, :], cre[:, j, :], start=True, stop=False, sig=False)
                S.mm(py[:, j * 32:(j + 1) * 32], xi[:, j, :], ncim[:, j, :], start=False, stop=True, sig=(j == NST - 1))
            ys = yst[cc % 2]
            S.copy("act", ys[:], py[:])
            py2 = pb[pbi % 6]; pbi += 1
            S.mm(py2[:], Pm[:], ys[:])
            ys2 = yst2[cc % 2]
            S.copy("act", ys2[:], py2[:])
            S.dma("sync", SC["ys"][dr, n * 128:(n + 1) * 128, :], ys2[:])
            cc += 1


def phase3a(S, PS, IN, SC):
    idf, idb = setup_consts(S, IN["ident"][:])
    ones = S.sb([128, 128], F32, "ones"); S.memset("dve", ones[:], 1.0)
    pa = PS["g"]; pt = PS["b"]
    modT = S.sb([128, 72, 2], F32, "modT")
    S.dma("sync", modT[:].re("p t j -> p (t j)"), SC["modT0"][:])
    gateb = [gate_bcast(S, modT[:, 5 * 8:6 * 8, j], idf, ones, pa[j], 1.0, f"g5{j}") for j in range(2)]
    bcs = [S.sb([128, 512], F32, f"bcs{i}") for i in range(5)]
    for i in range(5):
        S.dma("sync", bcs[i][:], IN["bcs"][i])
    lnxg, lnxb, rk, s5d, glub = bcs
    S.ts("dve", rk[:], rk[:], 0.5, ALU.mult)
    wst = S.sb([128, 4, 512], F32, "wst")
    gluw = S.sb([128, 4, 512], BF16, "gluw")
    S.dma("sync", wst[:], IN["gluw"].re("(k p) n -> p k n", p=128))
    S.copy("pool", gluw[:], wst[:])
    outw = S.sb([128, 8, 1024], BF16, "outw")
    wst2 = [S.sb([128, 1024], F32, f"wst2{i}") for i in range(2)]
    for k in range(8):
        S.dma("sync", wst2[k % 2][:], IN["outw_ab"][k * 128:(k + 1) * 128, :])
        S.copy("pool", outw[:, k, :], wst2[k % 2][:])
    t5 = lambda nm, dt=F32: S.sb([128, 512], dt, nm)
    inr = [[t5(f"inr{i}{j}") for j in range(7)] for i in range(2)]
    ins = [[t5(f"ins{i}{j}") for j in range(3)] for i in range(2)]
    xt = [S.sb([128, 1024], F32, f"xt{i}") for i in range(2)]
    y = t5("y"); yc = t5("yc"); sq = t5("sq"); ks = t5("ks"); tq = t5("tq"); bon = t5("bon")
    s8 = S.sb([128, 8], F32, "s8"); v8 = S.sb([128, 8], F32, "v8"); b8 = S.sb([128, 8], F32, "b8")
    cat = S.sb([128, 1024], BF16, "cat"); ysum = t5("ysum"); z = t5("z"); zb = t5("zb", BF16)
    zT = S.sb([128, 4, 128], BF16, "zT"); gl = t5("gl"); catT = S.sb([128, 8, 128], BF16, "catT")
    tmp = S.sb([128, 1024], F32, "tmp")
    v3 = lambda v: v.re("p (h k) -> p h k", k=64)
    b3 = lambda t: t[:].re("p (h o) -> p h o", o=1).bc([128, 8, 64])
    for ci in range(NCH):
        j = JF34(ci)
        i2 = ci % 2
        rows = slice(ci * 128, (ci + 1) * 128)
        srcs = [SC["yd"][0, rows, :], SC["yd"][1, rows, :], SC["kdo"][0, rows, :], SC["kdo"][1, rows, :],
                SC["rvo"][rows, 0:512], SC["rvo"][rows, 512:1024], SC["gto"][rows, :]]
        for q in range(7):
            S.dma("sync" if q % 2 == 0 else "pool", inr[i2][q][:], srcs[q])
        srcs2 = [SC["ys"][0, rows, :], SC["ys"][1, rows, :], SC["ppad"][prow(ci):prow(ci) + 128, 1792:2304]]
        for q in range(3):
            S.dma("pool" if q % 2 == 0 else "sync", ins[i2][q][:], srcs2[q])
        S.dma("sync", xt[i2][:], SC["x1"][rows, :])
        y0, y1, kd0, kd1, r, v, g = inr[i2]
        S.tt("dve", y[:], y0[:], y1[:], ALU.add)
        S.red("dve", s8[:], v3(y[:]), ALU.add)
        S.ts("dve", s8[:], s8[:], 1.0 / 64, ALU.mult)
        S.tt("dve", v3(yc[:]), v3(y[:]), b3(s8), ALU.subtract)
        S.tt("pool", sq[:], yc[:], yc[:], ALU.mult)
        S.red("dve", v8[:], v3(sq[:]), ALU.add)
        S.ts("dve", v8[:], v8[:], 1.0 / 64, ALU.mult, GN_EPS, ALU.add)
        S.act(v8[:], v8[:], AF.Sqrt)
        S.recip(v8[:], v8[:])
        S.tt("dve", v3(yc[:]), v3(yc[:]), b3(v8), ALU.mult)
        S.tt("pool", yc[:], yc[:], lnxg[:], ALU.mult)
        S.tt("pool", yc[:], yc[:], lnxb[:], ALU.add)
        S.tt("pool", ks[:], kd0[:], kd1[:], ALU.add)
        S.tt("pool", tq[:], r[:], ks[:], ALU.mult)
        S.tt("pool", tq[:], tq[:], rk[:], ALU.mult)
        S.red("dve", b8[:], v3(tq[:]), ALU.add)
        S.tt("dve", v3(bon[:]), v3(v[:]), b3(b8), ALU.mult)
        S.tt("dve", yc[:], yc[:], bon[:], ALU.add)
        S.tt("dve", cat[:, 0:512], yc[:], g[:], ALU.mult)
        ys0, ys1, u = ins[i2]
        S.tt("pool", ysum[:], ys0[:], ys1[:], ALU.add)
        S.tt("pool", tq[:], u[:], s5d[:], ALU.mult)
        S.tt("pool", ysum[:], ysum[:], tq[:], ALU.add)
        S.act(z[:], ysum[:], AF.Gelu)
        S.copy("pool", zb[:], z[:])
        p_ = pt[0]
        for k in range(4):
            S.tr(p_[:, k * 128:(k + 1) * 128], zb[:, k * 128:(k + 1) * 128], idb[:], sig=(k == 3))
        S.copy("dve", zT[:], p_[:, 0:512].re("p (k t) -> p k t", t=128))
        pg = pa[2]
        for k in range(4):
            S.mm(pg[:], zT[:, k, :], gluw[:, k, :], start=(k == 0), stop=(k == 3))
        S.tt("dve", gl[:], pg[:], glub[:], ALU.add)
        S.act(gl[:], gl[:], AF.Sigmoid)
        S.tt("dve", cat[:, 512:1024], z[:], gl[:], ALU.mult)
        p_ = pt[1]
        for k in range(8):
            S.tr(p_[:, k * 128:(k + 1) * 128], cat[:, k * 128:(k + 1) * 128], idb[:], sig=(k == 7))
        S.copy("dve", catT[:], p_[:].re("p (k t) -> p k t", t=128))
        for h in range(2):
            pc = pa[4 + h]
            for k in range(8):
                S.mm(pc[:], catT[:, k, :], outw[:, k, h * 512:(h + 1) * 512], start=(k == 0), stop=(k == 7))
            S.tt("dve", tmp[:, h * 512:(h + 1) * 512], pc[:], gateb[j][:, h * 512:(h + 1) * 512], ALU.mult)
        S.tt("pool", xt[i2][:], xt[i2][:], tmp[:], ALU.add)
        S.dma("pool", SC["xm"][rows, :], xt[i2][:])


def phase3b(S, PS, IN, SC):
    C = alloc_common(S, PS)
    C.idf, C.idb = setup_consts(S, IN["ident"][:])
    modT0 = S.sb([128, 72, 2], F32, "modT0")
    S.dma("sync", modT0[:].re("p t j -> p (t j)"), SC["modT0"][:])
    ngT0 = S.sb([128, 3, 8], F32, "ngT0")
    S.dma("sync", ngT0[:], IN["ngT"][0])
    mv0 = mod_derive(S, modT0, ngT0)
    C.gateb = {}
    for j in range(2):
        C.gateb[(2, j)] = gate_bcast(S, mv0["GATE"][:, 2, :, j], C.idf, C.ones, C.pa[j], 0.5, f"gb2{j}")
    rows = lambda t: (lambda ci: t[ci * 128:(ci + 1) * 128, :])
    for grp in ffn(S, C, rows(SC["xm"]), rows(SC["xl0"]), IN["w1"][0, 1], IN["w2"][0, 1], mv0, 2, GROUPS34, JF34):
        pass
    mv1 = mod_vectors(S, IN["cT"][:], IN["modw"][1], IN["modbT"][1], IN["ngT"][1], C.wst, C.pa[0])
    S.dma("sync", SC["modT1"][:], mv1["modT"][:].re("p t j -> p (t j)"))
    for j in range(2):
        C.gateb[(0, j)] = gate_bcast(S, mv1["GATE"][:, 0, :, j], C.idf, C.ones, C.pa[2 + j], 0.5, f"gb0{j}")
    cos = S.sb([128, NCH, 32], F32, "cos"); sin = S.sb([128, NCH, 32], F32, "sin")
    S.dma("sync", cos[:], IN["rope"][0].re("c p f -> p c f"))
    S.dma("sync", sin[:], IN["rope"][1].re("c p f -> p c f"))
    pst = [S.sb([128, 256], F32, f"pst{i}") for i in range(2)]
    ra = [S.sb([128, 4, 32], F32, f"ra{i}") for i in range(4)]
    psti = [0]

    def after(ci, li, xt):
        j = JF34(ci)
        norm_to_hT(S, C, xt[:], C.hT, li * 128, mv1["G"][:, 1, :, j], mv1["SH"][:, 1, :, j])
    win = IN["win_at"]
    qkv = SC["qkv"]
    for grp in ffn(S, C, rows(SC["xl0"]), rows(SC["x2"]), IN["w1"][1, 0], IN["w2"][1, 0], mv1, 0, GROUPS34, JF34, after_chunk=after):
        for cb in range(6):
            wst = C.wst[cb % 2]; wb = C.w1b[cb % 2]
            S.dma("sync", wst[:, 0:4, :], win[cb, :, 0:4, :])
            S.dma("pool", wst[:, 4:8, :], win[cb, :, 4:8, :])
            S.copy("pool", wb[:], wst[:])
            for li, ci in enumerate(grp):
                pp = C.pa[C.pai % 4]; C.pai += 1
                for k in range(8):
                    S.mm(pp[:, 0:256], C.hT[:, k, li * 128:(li + 1) * 128], wb[:, k, :], start=(k == 0), stop=(k == 7))
                st = pst[psti[0] % 2]; psti[0] += 1
                if cb < 5:
                    pv = pp[:, 0:256].re("p (h two f) -> p h two f", two=2, f=32)
                    sv = st[:].re("p (h two f) -> p h two f", two=2, f=32)
                    cb_ = cos[:, ci, :].re("p (o f) -> p o f", o=1).bc([128, 4, 32])
                    sb_ = sin[:, ci, :].re("p (o f) -> p o f", o=1).bc([128, 4, 32])
                    a, b, c, dd = ra
                    S.tt("dve", a[:], pv[:, :, 0, :], cb_, ALU.mult)
                    S.tt("dve", b[:], pv[:, :, 1, :], sb_, ALU.mult)
                    S.tt("pool", sv[:, :, 0, :], a[:], b[:], ALU.subtract)
                    S.tt("dve", c[:], pv[:, :, 1, :], cb_, ALU.mult)
                    S.tt("dve", dd[:], pv[:, :, 0, :], sb_, ALU.mult)
                    S.tt("pool", sv[:, :, 1, :], c[:], dd[:], ALU.add)
                else:
                    S.copy("act", st[:], pp[:, 0:256])
                S.dma("sync", qkv[ci * 128:(ci + 1) * 128, cb * 256:(cb + 1) * 256], st[:])


def phase4a(S, PS, IN, SC):
    idf, idb = setup_consts(S, IN["ident"][:])
    ones = S.sb([128, 128], F32, "ones"); S.memset("dve", ones[:], 1.0)
    pa = PS["g"][0:4]; pai = [0]
    ptb = PS["b"][0]
    pos = PS["g"][4:6]
    qkv = SC["qkv"]
    modT = S.sb([128, 72, 2], F32, "modT")
    S.dma("sync", modT[:].re("p t j -> p (t j)"), SC["modT1"][:])
    gate5 = gate_bcast(S, modT[:, 5 * 8:6 * 8, 1], idf, ones, pa[0], 1.0, "g5")
    sinkb = S.sb([128, 16], F32, "sinkb"); S.dma("sync", sinkb[:], IN["sinkb"][:])
    mt16 = S.sb([128, 16, 3], F32, "mt16")
    S.copy("dve", mt16[:, :, 2], sinkb[:])
    mstage = S.sb([128, 384], F32, "mstage")
    maskb = S.sb([128, 3, 384], BF16, "maskb")
    for i in range(3):
        S.dma("sync", mstage[:], IN["maskb"][i])
        S.copy("dve", maskb[:, i, :], mstage[:])
    outw = S.sb([128, 8, 1024], BF16, "outw")
    wst2 = [S.sb([128, 1024], F32, f"wst2{i}") for i in range(2)]
    for k in range(8):
        S.dma("sync", wst2[k % 2][:], IN["outw_at"][k * 128:(k + 1) * 128, :])
        S.copy("pool", outw[:, k, :], wst2[k % 2][:])
    NKB = NQB + 2
    kT = S.sb([64, 4, NKB * 128], BF16, "kT"); kcT = S.sb([64, 4, 256], BF16, "kcT")
    vw = S.sb([128, NKB, 256], BF16, "vw"); vc = S.sb([128, 2, 256], BF16, "vc")
    for blk in (0, NKB - 1):
        S.memset("dve", kT[:, :, blk * 128:(blk + 1) * 128], 0.0)
        S.memset("dve", vw[:, blk, :], 0.0)
    kst = [S.sb([128, 512], F32, f"kst{i}") for i in range(2)]; kb = [S.sb([128, 256], BF16, f"kb{i}") for i in range(2)]
    for c in range(NCH):
        S.dma("sync", kst[c % 2][:], qkv[c * 128:(c + 1) * 128, 1024:1536])
        S.copy("pool", kb[c % 2][:], kst[c % 2][:, 0:256])
        for kv in range(4):
            S.tr(ptb[0:64, kv * 128:(kv + 1) * 128], kb[c % 2][:, kv * 64:(kv + 1) * 64], idb[:], sig=(kv == 3))
        blk = c - 1
        dstk = kcT[:, :, c * 128:(c + 1) * 128] if c < 2 else kT[:, :, blk * 128:(blk + 1) * 128]
        S.copy("act", dstk, ptb[0:64, 0:512].re("p (a t) -> p a t", t=128))
        dstv = vc[:, c, :] if c < 2 else vw[:, blk, :]
        S.copy("dve", dstv, kst[c % 2][:, 256:512])
    qst = [S.sb([128, 1024], F32, f"qst{i}") for i in range(2)]
    qb = S.sb([128, 1024], BF16, "qb")
    qT = S.sb([64, 16, 128], BF16, "qT")
    Pm = [S.sb([128, 640], BF16, f"Pm{i}") for i in range(2)]
    PT = [S.sb([128, 5, 128], BF16, f"PT{i}") for i in range(2)]
    rs = [S.sb([128, 4], F32, f"rs{i}") for i in range(2)]
    negm = [S.sb([128, 1], F32, f"negm{i}") for i in range(2)]
    rden = S.sb([128, 16], F32, "rden")
    ob = S.sb([128, 1024], BF16, "ob"); oT = S.sb([128, 8, 128], BF16, "oT")
    xt = [S.sb([128, 1024], F32, f"xt{i}") for i in range(2)]
    tmp = S.sb([128, 1024], F32, "tmp")
    for i in range(NQB):
        rows = slice((i + 2) * 128, (i + 3) * 128)
        S.dma("sync", qst[i % 2][:], qkv[rows, 0:1024])
        S.dma("pool", xt[i % 2][:], SC["x2"][rows, :])
        S.act(qb[:], qst[i % 2][:], AF.Copy, scale=0.125)
        for half in range(2):
            for hh in range(8):
                hd = half * 8 + hh
                S.tr(ptb[0:64, hh * 128:(hh + 1) * 128], qb[:, hd * 64:(hd + 1) * 64], idb[:], sig=(hh == 7))
            S.copy("act", qT[:, half * 8:(half + 1) * 8, :], ptb[0:64, :].re("p (a t) -> p a t", t=128))
        mi = 0 if i == 0 else (2 if i == NQB - 1 else 1)
        for hd in range(16):
            kv = hd // 4
            i2 = hd % 2
            pw = pa[pai[0] % 4]; pai[0] += 1
            pcx = pa[pai[0] % 4]; pai[0] += 1
            S.mm(pw[:, 0:384], qT[:, hd, :], kT[:, kv, i * 128:(i + 3) * 128], start=True, stop=False, sig=False)
            S.mm(pw[:, 0:384], idb[:], maskb[:, mi, :], start=False, stop=True)
            S.mm(pcx[:, 0:256], qT[:, hd, :], kcT[:, kv, :])
            S.red("dve", mt16[:, hd, 0:1], pw[:, 0:384], ALU.max)
            S.red("dve", mt16[:, hd, 1:2], pcx[:, 0:256], ALU.max)
            S.red("dve", negm[i2][:], mt16[:, hd, :], ALU.max)
            S.ts("dve", negm[i2][:], negm[i2][:], -1.0, ALU.mult)
            S.act(Pm[i2][:, 0:384], pw[:, 0:384], AF.Exp, bias=negm[i2][:, 0:1], accum=rs[i2][:, 0:1])
            S.act(Pm[i2][:, 384:640], pcx[:, 0:256], AF.Exp, bias=negm[i2][:, 0:1], accum=rs[i2][:, 1:2])
            S.act(rs[i2][:, 2:3], sinkb[:, hd:hd + 1], AF.Exp, bias=negm[i2][:, 0:1])
            S.red("dve", rs[i2][:, 3:4], rs[i2][:, 0:3], ALU.add)
            S.recip(rden[:, hd:hd + 1], rs[i2][:, 3:4])
            for j in range(5):
                S.tr(ptb[:, j * 128:(j + 1) * 128], Pm[i2][:, j * 128:(j + 1) * 128], idb[:], sig=(j == 4))
            S.copy("dve" if hd % 2 == 0 else "act", PT[i2][:], ptb[:, 0:640].re("p (a t) -> p a t", t=128))
            po = pos[hd // 8]
            for j in range(5):
                vsrc = vw[:, i + j, kv * 64:(kv + 1) * 64] if j < 3 else vc[:, j - 3, kv * 64:(kv + 1) * 64]
                S.mm(po[:, (hd % 8) * 64:(hd % 8 + 1) * 64], PT[i2][:, j, :], vsrc, start=(j == 0), stop=(j == 4), sig=(j == 4))
        for h2 in range(2):
            S.tt("dve", ob[:, h2 * 512:(h2 + 1) * 512].re("p (h k) -> p h k", k=64), pos[h2][:].re("p (h k) -> p h k", k=64),
                 rden[:, h2 * 8:(h2 + 1) * 8].re("p (h o) -> p h o", o=1).bc([128, 8, 64]), ALU.mult)
        for k in range(8):
            S.tr(ptb[:, k * 128:(k + 1) * 128], ob[:, k * 128:(k + 1) * 128], idb[:], sig=(k == 7))
        S.copy("act", oT[:], ptb[:].re("p (a t) -> p a t", t=128))
        for h in range(2):
            py = pa[pai[0] % 4]; pai[0] += 1
            for k in range(8):
                S.mm(py[:], oT[:, k, :], outw[:, k, h * 512:(h + 1) * 512], start=(k == 0), stop=(k == 7))
            S.tt("dve", tmp[:, h * 512:(h + 1) * 512], py[:], gate5[:, h * 512:(h + 1) * 512], ALU.mult)
        S.tt("pool", xt[i % 2][:], xt[i % 2][:], tmp[:], ALU.add)
        S.dma("pool", SC["x3"][i * 128:(i + 1) * 128, :], xt[i % 2][:])


def phase4b(S, PS, IN, SC, OUT):
    C = alloc_common(S, PS)
    C.idf, C.idb = setup_consts(S, IN["ident"][:])
    modT = S.sb([128, 72, 2], F32, "modT")
    S.dma("sync", modT[:].re("p t j -> p (t j)"), SC["modT1"][:])
    ngT = S.sb([128, 3, 8], F32, "ngT"); S.dma("sync", ngT[:], IN["ngT"][1])
    mv = mod_derive(S, modT, ngT)
    C.gateb = {(2, 1): gate_bcast(S, mv["GATE"][:, 2, :, 1], C.idf, C.ones, C.pa[0], 0.5, "gb21")}
    fing = S.sb([128, 1024], F32, "fing"); S.dma("sync", fing[:], IN["fing"][:])
    ot = [S.sb([128, 1024], F32, f"ot{i}") for i in range(2)]
    oi = [0]

    def after(ci, li, xt):
        ss = C.small[C.si % 4]; C.si += 1
        S.act(C.junk[:], xt[:], AF.Square, accum=ss[:, 0:1])
        S.ts("dve", ss[:, 1:2], ss[:, 0:1], 1.0 / D, ALU.mult, 1e-6, ALU.add)
        S.act(ss[:, 3:4], ss[:, 1:2], AF.Sqrt)
        S.recip(ss[:, 2:3], ss[:, 3:4])
        o = ot[oi[0] % 2]; oi[0] += 1
        S.stt("dve", o[:], xt[:], ss[:, 2:3], fing[:], ALU.mult, ALU.mult)
        S.dma("sync", OUT[ci * 128:(ci + 1) * 128, :], o[:])
    rows = lambda t: (lambda ci: t[ci * 128:(ci + 1) * 128, :])
    for grp in ffn(S, C, rows(SC["x3"]), None, IN["w1"][1, 1], IN["w2"][1, 1], mv, 2, GROUPS32, lambda ci: 1, after_chunk=after):
        pass


IN_SPECS = dict(
    x=[4096, D], ctx=[256, D], cT=[128, 8, 2], modw=[2, 36, 128, 8, 256], modbT=[2, 128, 72], ngT=[2, 128, 3, 8],
    w1=[2, 2, NFT, 128, 8, 256], w2=[2, 2, DFF, D], win_ab=[9, 128, 8, 256], ident=[128, 128],
    mub=[2, 128, 1536], mulb=[2, 128, 256], kkb=[128, 512], kab=[128, 512], w2a=[2, 65, 512], a2a=[2, 65, 512], g2=[128, 512], msk=[6, 128, 128],
    are=[2, 128, NST], aim=[2, 128, NST], lst=[2, 128, NST], bre=[2, 128, NST, 32], bim=[2, 128, NST, 32], cre=[2, 128, NST, 32], cim=[2, 128, NST, 32],
    bcs=[5, 128, 512], gluw=[512, 512], outw_ab=[D, D], win_at=[6, 128, 8, 256], rope=[2, NCH, 128, 32],
    maskb=[3, 128, 384], sinkb=[128, 16], outw_at=[D, D], fing=[128, D])

SC_SPECS = dict(x1=[TOK, D], ppad=[4356, 2304], yd=[2, TOK, 512], kdo=[2, TOK, 512], rvo=[TOK, 1024], gto=[TOK, 512], ys=[2, TOK, 512],
                xm=[TOK, D], xl0=[TOK, D], x2=[TOK, D], qkv=[TOK, 1536], x3=[4096, D], modT0=[128, 144], modT1=[128, 144])


def build_fused(upto=99, debug=(), ses=True):
    nc = bass.Bass("TRN2", target_bir_lowering=False)
    S = Sched(nc, same_engine_sync=ses)
    IN = {k: S.dram(k, v, F32, kind="ExternalInput") for k, v in IN_SPECS.items()}
    SC = {k: S.dram("sc_" + k, v, F32, kind=("ExternalOutput" if k in debug else "Internal")) for k, v in SC_SPECS.items()}
    OUT = S.dram("out", [4096, D], F32, kind="ExternalOutput")
    PS = dict(g=[S.ps([128, 512], F32, f"g{i}") for i in range(6)], b=[S.ps([128, 1024], BF16, f"b{i}") for i in range(2)])
    base = S.mark()
    phases = [lambda: phase1(S, PS, IN, SC), lambda: phase2a(S, PS, IN, SC), lambda: phase2b(S, PS, IN, SC), lambda: phase3a(S, PS, IN, SC),
              lambda: phase3b(S, PS, IN, SC), lambda: phase4a(S, PS, IN, SC), lambda: phase4b(S, PS, IN, SC, OUT)]
    for i, ph in enumerate(phases):
        if i > upto:
            break
        S.reset(base)
        ph()
        S.barrier()
    finals = [OUT] + [SC[k] for k in debug]
    S.finish(finals)
    return nc, S

import numpy as np
LC = 256; NLAT = 4096; L = 4352
GRID_W = 64; ROPE_BASE = 10000.0
def core_tok(seq, h):
    return np.concatenate([seq[h * 128:(h + 1) * 128], seq[256 + h * 2048:256 + (h + 1) * 2048]], 0)
def uncore_tok(parts):
    return np.concatenate([parts[0][:128], parts[1][:128], parts[0][128:], parts[1][128:]], 0)
def colT(v, k=8):
    return np.ascontiguousarray(v.reshape(k, 128).T)
def bc(v):
    return np.ascontiguousarray(np.broadcast_to(v[None, :], (128, v.shape[0])))
def rope_tables(h):
    t = np.arange(h * 2048, (h + 1) * 2048)
    row = (t // GRID_W).astype(np.float32); col = (t % GRID_W).astype(np.float32)
    inv = (ROPE_BASE ** (-np.arange(0, 32, 2, dtype=np.float32) / 32)).astype(np.float32)
    ang = np.concatenate([row[:, None] * inv, col[:, None] * inv], -1).astype(np.float32)
    cos = np.concatenate([np.ones((128, 32), np.float32), np.cos(ang)], 0).reshape(17, 128, 32)
    sin = np.concatenate([np.zeros((128, 32), np.float32), np.sin(ang)], 0).reshape(17, 128, 32)
    return np.stack([cos, sin], 0).astype(np.float32)

import numpy as np
LC = 256

def f_masks():
    m = np.zeros((6, 128, 128), np.float32)
    s = np.arange(128)[:, None]; t = np.arange(128)[None, :]
    same = (s // 64) == (t // 64)
    m[0] = same & (s < t); m[1] = same & (s <= t); m[2] = same & (s > t); m[4] = same & (s >= t)
    m[3] = np.eye(128)
    m[5] = np.eye(128)[::-1]
    return m

def f_rope():
    GRID_W = 64
    t = np.arange(4096)
    row = (t // GRID_W).astype(np.float32); col = (t % GRID_W).astype(np.float32)
    inv = (10000.0 ** (-np.arange(0, 32, 2, dtype=np.float32) / 32)).astype(np.float32)
    ang = np.concatenate([row[:, None] * inv, col[:, None] * inv], -1).astype(np.float32)
    cos = np.concatenate([np.ones((256, 32), np.float32), np.cos(ang)], 0).reshape(34, 128, 32)
    sin = np.concatenate([np.zeros((256, 32), np.float32), np.sin(ang)], 0).reshape(34, 128, 32)
    return np.ascontiguousarray(np.stack([cos, sin], 0).astype(np.float32))

def f_attn_masks():
    qi = np.arange(128)[:, None]; mj = np.arange(384)[None, :] - 128
    valid = np.abs(mj - qi) <= 128
    NEG = -30000.0
    gen = np.where(valid, 0.0, NEG).astype(np.float32)
    left_inv = gen.copy(); left_inv[:, :128] = NEG
    right_inv = gen.copy(); right_inv[:, 256:] = NEG
    return np.ascontiguousarray(np.stack([left_inv, gen, right_inv], 0))

def wblk(w):
    n = w.shape[1] // 256
    return np.ascontiguousarray(w.reshape(8, 128, n, 256).transpose(2, 1, 0, 3))

def w1blk(w):
    a = w.reshape(8, 128, 2, 22, 128).transpose(3, 1, 0, 2, 4)
    return np.ascontiguousarray(a.reshape(22, 128, 8, 256))

def f_shared(d):
    e = 0
    st = lambda a: np.ascontiguousarray(a.reshape(16, 128).T)
    def pad(a):
        out = np.zeros((128, 16, 32), np.float32)
        for g in range(32):
            out[(g % 2) * 64:(g % 2) * 64 + 64, g // 2, (g % 2) * 16:(g % 2) * 16 + 16] = a[g]
        return out
    mu = d['rwkv_mu'][e]
    sh = dict(
        modw=np.stack([wblk(d['mod_w'][l]) for l in range(2)], 0), modbT=np.ascontiguousarray(np.stack([colT(d['mod_b'][l], 72) for l in range(2)], 0)),
        ngT=np.ascontiguousarray(np.stack([np.stack([colT(d['norm_g'][l, i]) for i in range(3)], 1) for l in range(2)], 0)),
        w1=np.stack([np.stack([w1blk(d['ffn_w1'][l, j]) for j in range(2)], 0) for l in range(2)], 0), w2=d['ffn_w2'], win_ab=wblk(d['ab_in_w'][0]), ident=np.eye(128, dtype=np.float32),
        mub=np.ascontiguousarray(np.stack([bc(mu[0, :1536]), bc(mu[1, :1536])], 0)),
        mulb=np.ascontiguousarray(np.stack([bc(mu[0, 1536:1792]), bc(mu[1, 1536:1792])], 0)),
        kkb=bc(d['rwkv_k_k'][e]), kab=bc(d['rwkv_k_a'][e]),
        w2a=np.ascontiguousarray(np.stack([np.concatenate([d['rwkv_w2'][e, dr], d['rwkv_w0'][e, dr][None]], 0) for dr in range(2)], 0)),
        a2a=np.ascontiguousarray(np.stack([np.concatenate([d['rwkv_a2'][e, dr], d['rwkv_a0'][e, dr][None]], 0) for dr in range(2)], 0)),
        g2=d['rwkv_g2'][e], msk=f_masks(),
        are=np.stack([st(d['s5_a_re'][0, dr]) for dr in range(2)], 0), aim=np.stack([st(d['s5_a_im'][0, dr]) for dr in range(2)], 0),
        lst=np.stack([st(np.repeat(d['s5_log_step'][0, dr][:, None], 64, 1)) for dr in range(2)], 0),
        bre=np.stack([pad(d['s5_b_re'][0, dr]) for dr in range(2)], 0), bim=np.stack([pad(d['s5_b_im'][0, dr]) for dr in range(2)], 0),
        cre=np.stack([pad(d['s5_c_re'][0, dr].transpose(0, 2, 1)) for dr in range(2)], 0),
        cim=np.stack([pad(d['s5_c_im'][0, dr].transpose(0, 2, 1)) for dr in range(2)], 0),
        bcs=np.ascontiguousarray(np.stack([bc(d['rwkv_lnx_g'][0]), bc(d['rwkv_lnx_b'][0]), bc(d['rwkv_r_k'][0].reshape(-1)), bc(d['s5_d'][0]),
                                           bc(d['s5_glu_b'][0])], 0)),
        gluw=d['s5_glu_w'][0], outw_ab=d['ab_out_w'][0], win_at=wblk(d['attn_in_w'][0]), rope=f_rope(),
        maskb=f_attn_masks(), sinkb=bc(d['attn_sink'][0]), outw_at=d['attn_out_w'][0], fing=bc(d['final_g']))
    return {k: np.ascontiguousarray(v, dtype=np.float32) for k, v in sh.items()}

def f_core(d, b):
    return dict(x=np.ascontiguousarray(d['x'][b]), ctx=np.ascontiguousarray(d['ctx'][b]),
                cT=np.ascontiguousarray(np.stack([colT(d['c_ctx']), colT(d['c'][b])], -1)))


def kernel(**inputs):
    d = {k: np.ascontiguousarray(np.asarray(v, dtype=np.float32)) for k, v in inputs.items()}
    nc, _ = build_fused()
    sh = f_shared(d)
    in_maps = [dict(sh, **f_core(d, c % 4)) for c in range(8)]
    res = run_bass_kernel_spmd(nc, in_maps, core_ids=list(range(8)))
    out = np.stack([res.results[b]['out'] for b in range(4)], 0)
    return out.astype(np.float32)
```

```python
import numpy as np
import concourse.bass as bass
import concourse.mybir as mybir
from concourse.bass_utils import run_bass_kernel_spmd

F32 = mybir.dt.float32
BF16 = mybir.dt.bfloat16
F32R = mybir.dt.float32r
ALU = mybir.AluOpType
AF = mybir.ActivationFunctionType
AX = mybir.AxisListType


class T:
    def __init__(self, h, name=""):
        self.h = h
        self.name = name
        self.last_w = None
        self.readers = []

    def __getitem__(self, idx):
        return V(self, self.h[idx])

    def re(self, pat, **kw):
        return self[:].re(pat, **kw)


class V:
    def __init__(self, t, ap):
        self.t = t
        self.ap = ap

    def __getitem__(self, idx):
        return V(self.t, self.ap[idx])

    def re(self, pat, **kw):
        return V(self.t, self.ap.rearrange(pat, **kw))

    def bc(self, shape):
        return V(self.t, self.ap.to_broadcast(shape))


def _ap(x):
    return x.ap if isinstance(x, V) else x


def _ts(xs):
    out = []
    for x in xs:
        if isinstance(x, V):
            out.append(x.t)
        elif isinstance(x, T):
            out.append(x)
    return out


class Sched:
    ENG = ["pe", "act", "dve", "pool", "sync"]

    def __init__(self, nc, n_dma_sems=6, same_engine_sync=True):
        self.nc = nc
        self.q = {e: [] for e in self.ENG}
        self.cnt = {e: 0 for e in self.ENG}
        self.unsig = {e: False for e in self.ENG}
        self.sem = {e: nc.alloc_semaphore(f"s_{e}") for e in ["pe", "act", "dve", "pool"]}
        self.waited = {e: {} for e in self.ENG}
        self.same_engine_sync = same_engine_sync
        self.dsem = {}
        self.dcnt = {}
        self.drr = {}
        for qn in ["sync", "pool", "act"]:
            self.dsem[qn] = [nc.alloc_semaphore(f"d_{qn}{i}") for i in range(n_dma_sems)]
            self.dcnt[qn] = [0] * n_dma_sems
            self.drr[qn] = 0
        self.n_inst = 0
        self.uid = 0

    ARENA_LO = 16640
    ARENA_HI = 229344

    def sb(self, shape, dt=F32, name=None):
        self.uid += 1
        name = name or f"t{self.uid}"
        if not hasattr(self, "off"):
            self.off = self.ARENA_LO
        n = 1
        for x in shape[1:]:
            n *= x
        size = n * (2 if dt == BF16 else 4)
        size = (size + 31) // 32 * 32
        assert self.off + size <= self.ARENA_HI, f"SBUF arena overflow allocating {name} {shape}: off={self.off} size={size}"
        t = T(self.nc.alloc_sbuf_tensor_at(f"{name}_{self.uid}", list(shape), dt, offset=self.off), name)
        self.off += size
        return t

    def mark(self):
        if not hasattr(self, "off"):
            self.off = self.ARENA_LO
        return self.off

    def reset(self, mark):
        self.off = mark

    def barrier(self):
        targets = []
        for e in ("pe", "act", "dve", "pool"):
            assert not self.unsig[e], f"barrier with unsignaled op on {e}"
            if self.cnt[e] > 0:
                targets.append((self.sem[e], self.cnt[e]))
        for qn in self.dsem:
            for sm, c in zip(self.dsem[qn], self.dcnt[qn]):
                if c > 0:
                    targets.append((sm, c))
        for e in self.ENG:
            waits = []
            for (sm, val) in targets:
                if e in self.sem and sm is self.sem[e]:
                    continue
                if self.waited[e].get(id(sm), 0) >= val:
                    continue
                self.waited[e][id(sm)] = val
                waits.append((sm, val))
            if waits:
                self.q[e].append((None, waits, None))

    def ps(self, shape, dt=F32, name=None):
        self.uid += 1
        name = name or f"p{self.uid}"
        return T(self.nc.alloc_psum_tensor(f"{name}_{self.uid}", list(shape), dt), name)

    def dram(self, name, shape, dt=F32, kind="Internal"):
        return T(self.nc.dram_tensor(name, list(shape), dt, kind=kind), name)

    def _collect(self, eng, reads, writes):
        toks = []
        for t in _ts(reads):
            if t.last_w is not None:
                toks.append(t.last_w)
        for t in _ts(writes):
            if t.last_w is not None:
                toks.append(t.last_w)
            toks.extend(t.readers)
        best = {}
        for (kind, key, sem, val) in toks:
            if kind == "eng" and key == eng:
                if eng in ("pe", "sync") or not self.same_engine_sync:
                    continue
            k = id(sem)
            if k not in best or best[k][1] < val:
                best[k] = (sem, val)
        waits = []
        for k, (sem, val) in best.items():
            if self.waited[eng].get(k, 0) >= val:
                continue
            self.waited[eng][k] = val
            waits.append((sem, val))
        return waits

    def _mark(self, tok, reads, writes):
        for t in _ts(reads):
            t.readers.append(tok)
        for t in _ts(writes):
            t.last_w = tok
            t.readers = []

    def op(self, eng, fn, reads, writes, sig=True):
        waits = self._collect(eng, reads, writes)
        if sig:
            self.cnt[eng] += 1
            tok = ("eng", eng, self.sem[eng], self.cnt[eng])
            self.unsig[eng] = False
        else:
            tok = ("eng", eng, self.sem[eng], self.cnt[eng] + 1)
            self.unsig[eng] = True
        self.q[eng].append((fn, waits, (self.sem[eng], 1) if sig else None))
        self._mark(tok, reads, writes)
        self.n_inst += 1

    def dma(self, qn, out, in_, extra_reads=(), extra_writes=(), **kw):
        eng = qn
        i = self.drr[qn]
        self.drr[qn] = (i + 1) % len(self.dsem[qn])
        sem = self.dsem[qn][i]
        reads = [in_] + list(extra_reads)
        writes = [out] + list(extra_writes)
        waits = self._collect(eng, reads, writes)
        prev = self.dcnt[qn][i]
        if prev > 0 and self.waited[eng].get(id(sem), 0) < prev:
            self.waited[eng][id(sem)] = prev
            waits.append((sem, prev))
        self.dcnt[qn][i] += 16
        tok = ("dma", qn, sem, self.dcnt[qn][i])
        o, a = _ap(out), _ap(in_)
        self.q[eng].append((lambda e: e.dma_start(out=o, in_=a, **kw), waits, (sem, 16)))
        self._mark(tok, reads, writes)
        self.n_inst += 1
        return tok

    def mm(self, out, lhsT, rhs, start=True, stop=True, sig=None, f32=False):
        if sig is None:
            sig = stop
        o, l, r = _ap(out), _ap(lhsT), _ap(rhs)
        if f32:
            if l.dtype == F32R:
                l = l.bitcast(F32)
            if r.dtype == F32R:
                r = r.bitcast(F32)
        self.op("pe", lambda e: e.matmul(o, l, r, start=start, stop=stop), [lhsT, rhs], [out], sig=sig)

    def tr(self, out, in_, ident, sig=True):
        o, i, d = _ap(out), _ap(in_), _ap(ident)
        self.op("pe", lambda e: e.transpose(o, i, d), [in_, ident], [out], sig=sig)

    def act(self, out, in_, func, bias=None, scale=1.0, accum=None, eng="act"):
        o, i = _ap(out), _ap(in_)
        kw = {}
        reads = [in_]
        writes = [out]
        if bias is not None:
            kw["bias"] = _ap(bias)
            reads.append(bias)
        kw["scale"] = _ap(scale)
        if isinstance(scale, V):
            reads.append(scale)
        if accum is not None:
            kw["accum_out"] = _ap(accum)
            writes.append(accum)
        self.op("act", lambda e: e.activation(o, i, func, **kw), reads, writes)

    def tt(self, eng, out, in0, in1, op):
        o, a, b = _ap(out), _ap(in0), _ap(in1)
        self.op(eng, lambda e: e.tensor_tensor(o, a, b, op), [in0, in1], [out])

    def ts(self, eng, out, in0, s1, op0, s2=None, op1=None, accum=None):
        o, a = _ap(out), _ap(in0)
        reads = [in0] + [s for s in (s1, s2) if isinstance(s, V)]
        writes = [out] + ([accum] if accum is not None else [])
        kw = {}
        if op1 is not None:
            kw["op1"] = op1
        if accum is not None:
            kw["accum_out"] = _ap(accum)
        self.op(eng, lambda e: e.tensor_scalar(o, a, _ap(s1), _ap(s2) if s2 is not None else None, op0, **kw), reads, writes)

    def stt(self, eng, out, in0, scalar, in1, op0, op1):
        o, a, b = _ap(out), _ap(in0), _ap(in1)
        reads = [in0, in1] + ([scalar] if isinstance(scalar, V) else [])
        self.op(eng, lambda e: e.scalar_tensor_tensor(o, a, _ap(scalar), b, op0, op1), reads, [out])

    def red(self, eng, out, in_, op, axis=AX.X):
        o, a = _ap(out), _ap(in_)
        self.op(eng, lambda e: e.tensor_reduce(o, a, axis, op), [in_], [out])

    def copy(self, eng, out, in_):
        o, a = _ap(out), _ap(in_)
        if eng == "act":
            self.op(eng, lambda e: e.copy(o, a), [in_], [out])
        else:
            self.op(eng, lambda e: e.tensor_copy(o, a), [in_], [out])

    def memset(self, eng, out, val):
        o = _ap(out)
        self.op(eng, lambda e: e.memset(o, val), [], [out])

    def scan(self, out, d0, d1, init, op0=ALU.mult, op1=ALU.add):
        o, a, b, i = _ap(out), _ap(d0), _ap(d1), _ap(init)
        reads = [d0, d1] + ([init] if isinstance(init, V) else [])
        self.op("dve", lambda e: e.tensor_tensor_scan(o, a, b, i, op0, op1), reads, [out])

    def recip(self, out, in_):
        o, a = _ap(out), _ap(in_)
        self.op("dve", lambda e: e.reciprocal(o, a), [in_], [out])

    def finish(self, final_tiles):
        nc = self.nc
        toks = []
        for t in final_tiles:
            if t.last_w is not None:
                toks.append(t.last_w)
        fin = []
        best = {}
        for (_, _, sem, val) in toks:
            if id(sem) not in best or best[id(sem)][1] < val:
                best[id(sem)] = (sem, val)
        for qn in self.dsem:
            for s, c in zip(self.dsem[qn], self.dcnt[qn]):
                if c > 0:
                    best[id(s)] = (s, max(c, best.get(id(s), (s, 0))[1]))
        for e in ("pe", "act", "dve", "pool"):
            if self.cnt[e] > 0 or self.unsig[e]:
                assert not self.unsig[e], f"engine {e} ends with unsignaled instruction"
                best[id(self.sem[e])] = (self.sem[e], self.cnt[e])
        fin = list(best.values())
        q = self.q
        with nc.Block() as block:
            def replay(lst):
                def f(e):
                    for (fn, waits, inc) in lst:
                        for (sem, val) in waits:
                            e.wait_ge(sem, val)
                        if fn is None:
                            continue
                        ins = fn(e)
                        if inc is not None:
                            ins.then_inc(inc[0], inc[1])
                return f

            @block.tensor
            def _(e):
                replay(q["pe"])(e)

            @block.scalar
            def _(e):
                replay(q["act"])(e)

            @block.vector
            def _(e):
                replay(q["dve"])(e)

            @block.gpsimd
            def _(e):
                replay(q["pool"])(e)

            @block.sync
            def _(e):
                replay(q["sync"])(e)
                for (sem, val) in fin:
                    e.wait_ge(sem, val)
        return nc

import math

NCH = 34
TOK = NCH * 128
LSEQ = 4352
D = 1024
DFF = 2816
NFT = 22
GN_EPS = 64e-5
NEGC = -math.exp(-0.5)
NST = 16
NQB = 32


def prow(n):
    return n * 128 + (1 if n < 2 else 3)


class Ctx:
    pass


def setup_consts(S, ident_d):
    idf = S.sb([128, 128], F32, "idf")
    idb = S.sb([128, 128], BF16, "idb")
    S.dma("sync", idf[:], ident_d)
    S.copy("dve", idb[:], idf[:])
    return idf, idb


def mod_derive(S, modT, ngT):
    G = S.sb([128, 3, 8, 2], F32, "G")
    SH = S.sb([128, 3, 8, 2], F32, "SH")
    GATE = S.sb([128, 3, 8, 2], F32, "GATE")
    for i in range(3):
        for j in range(2):
            S.stt("dve", G[:, i, :, j], modT[:, (3 * i + 1) * 8:(3 * i + 2) * 8, j], 1.0, ngT[:, i, :], ALU.add, ALU.mult)
            S.copy("dve", SH[:, i, :, j], modT[:, (3 * i) * 8:(3 * i + 1) * 8, j])
            S.copy("dve", GATE[:, i, :, j], modT[:, (3 * i + 2) * 8:(3 * i + 3) * 8, j])
    return dict(G=G, SH=SH, GATE=GATE, modT=modT)


def mod_vectors(S, cT_d, modw_d, modbT_d, ngT_d, wst, pm):
    cT = S.sb([128, 8, 2], F32, "cT")
    sc = S.sb([128, 8, 2], F32, "sc")
    S.dma("sync", cT[:], cT_d)
    S.act(sc[:], cT[:], AF.Silu)
    modbT = S.sb([128, 72], F32, "modbT")
    S.dma("sync", modbT[:], modbT_d)
    ngT = S.sb([128, 3, 8], F32, "ngT")
    S.dma("sync", ngT[:], ngT_d)
    modT = S.sb([128, 72, 2], F32, "modT")
    for nb in range(36):
        w = wst[nb % 2]
        S.dma("sync", w[:, 0:4, :], modw_d[nb, :, 0:4, :])
        S.dma("pool", w[:, 4:8, :], modw_d[nb, :, 4:8, :])
        for ct in range(2):
            t = nb * 2 + ct
            for k in range(8):
                S.mm(pm[:, t * 2:t * 2 + 2], w[:, k, ct * 128:(ct + 1) * 128], sc[:, k, :], start=(k == 0), stop=(k == 7),
                     sig=(k == 7 and t % 2 == 1))
    for j in range(2):
        S.tt("dve", modT[:, :, j], pm[:, 0:144].re("p (t j) -> p t j", j=2)[:, :, j], modbT[:], ALU.add)
    return mod_derive(S, modT, ngT)


def gate_bcast(S, gate_col, idf, ones, ps, scale, name):
    out = S.sb([128, 1024], F32, name)
    dg = S.sb([128, 128], F32, name + "_dg")
    for k in range(8):
        S.ts("dve", dg[:], idf[:], gate_col[:, k:k + 1], ALU.mult)
        S.mm(ps[:, (k % 4) * 128:(k % 4 + 1) * 128], ones[:], dg[:], start=True, stop=True)
        S.ts("dve", out[:, k * 128:(k + 1) * 128], ps[:, (k % 4) * 128:(k % 4 + 1) * 128], float(scale), ALU.mult)
    return out


def norm_to_hT(S, C, xt, hT, col0, G, SH):
    ss = C.small[C.si % 4]; C.si += 1
    S.act(C.junk[:], xt, AF.Square, accum=ss[:, 0:1])
    S.ts("dve", ss[:, 1:2], ss[:, 0:1], 1.0 / D, ALU.mult, 1e-6, ALU.add)
    S.act(ss[:, 3:4], ss[:, 1:2], AF.Sqrt)
    S.recip(ss[:, 2:3], ss[:, 3:4])
    xn = C.xn[C.xi % 2]; C.xi += 1
    S.act(xn[:], xt, AF.Copy, scale=ss[:, 2:3])
    pt = C.pt[C.pti % 2]; C.pti += 1
    for k in range(8):
        S.tr(pt[:, k * 128:(k + 1) * 128], xn[:, k * 128:(k + 1) * 128], C.idb[:], sig=(k == 7))
    for k in range(8):
        S.ts("dve", hT[:, k, col0:col0 + 128], pt[:, k * 128:(k + 1) * 128], G[:, k:k + 1], ALU.mult, SH[:, k:k + 1], ALU.add)


def ffn(S, C, xs, xd, w1_d, w2_d, mv, ni, groups, jf, after_chunk=None):
    G, SH = mv["G"], mv["SH"]
    w2b = C.w2b
    first = True
    for grp in groups:
        nt = len(grp) * 128
        for li, ci in enumerate(grp):
            xt = C.xt[C.xti % 3]; C.xti += 1
            S.dma("sync", xt[:], xs(ci))
            j = jf(ci)
            norm_to_hT(S, C, xt[:], C.hT, li * 128, G[:, ni, :, j], SH[:, ni, :, j])
        for ft in range(NFT):
            wst = C.wst[ft % 2]
            wb = C.w1b[ft % 2]
            S.dma("sync", wst[:, 0:4, :], w1_d[ft, :, 0:4, :])
            S.dma("pool", wst[:, 4:8, :], w1_d[ft, :, 4:8, :])
            S.copy("pool", wb[:], wst[:])
            if first:
                w2s = C.w2st[ft % 2]
                S.dma("sync", w2s[:], w2_d[ft * 128:(ft + 1) * 128, :])
                S.copy("pool", w2b[:, ft, :], w2s[:])
            for b0 in range(0, nt, 512):
                bw = min(512, nt - b0)
                pg = C.pa[C.pai % 4]; C.pai += 1
                pu = C.pa[C.pai % 4]; C.pai += 1
                for k in range(8):
                    S.mm(pg[:, 0:bw], wb[:, k, 0:128], C.hT[:, k, b0:b0 + bw], start=(k == 0), stop=(k == 7))
                for k in range(8):
                    S.mm(pu[:, 0:bw], wb[:, k, 128:256], C.hT[:, k, b0:b0 + bw], start=(k == 0), stop=(k == 7))
                sg = C.sg[C.sgi % 2]; C.sgi += 1
                S.act(sg[:, 0:bw], pg[:, 0:bw], AF.Silu)
                S.tt("dve", C.actT[:, ft, b0:b0 + bw], sg[:, 0:bw], pu[:, 0:bw], ALU.mult)
        first = False
        for li, ci in enumerate(grp):
            xt = C.xt[C.xti % 3]; C.xti += 1
            S.dma("sync", xt[:], xs(ci))
            gb = C.gateb[(ni, jf(ci))]
            for h in range(2):
                pc = C.pcs[h]
                for ft in range(NFT):
                    S.mm(pc[:], C.actT[:, ft, li * 128:(li + 1) * 128], w2b[:, ft, h * 512:(h + 1) * 512],
                         start=(ft == 0), stop=(ft == NFT - 1))
                S.tt("dve", C.tmp[:, h * 512:(h + 1) * 512], pc[:], gb[:, h * 512:(h + 1) * 512], ALU.mult)
            S.tt("pool", xt[:], xt[:], C.tmp[:], ALU.add)
            if xd is not None:
                S.dma("pool", xd(ci), xt[:])
            if after_chunk is not None:
                after_chunk(ci, li, xt)
        yield grp


def alloc_common(S, PS):
    C = Ctx()
    C.small = [S.sb([128, 4], F32, f"small{i}") for i in range(4)]; C.si = 0
    C.junk = S.sb([128, 1024], BF16, "junk")
    C.xn = [S.sb([128, 1024], BF16, f"xn{i}") for i in range(2)]; C.xi = 0
    C.pt = PS["b"]; C.pti = 0
    C.pa = PS["g"][0:4]; C.pai = 0
    C.pcs = PS["g"][4:6]
    C.xt = [S.sb([128, 1024], F32, f"xt{i}") for i in range(3)]; C.xti = 0
    C.hT = S.sb([128, 8, 1152], BF16, "hT")
    C.actT = S.sb([128, NFT, 1152], BF16, "actT")
    C.w2b = S.sb([128, NFT, 1024], BF16, "w2b")
    C.wst = [S.sb([128, 8, 256], F32, f"wst{i}") for i in range(2)]
    C.w1b = [S.sb([128, 8, 256], BF16, f"w1b{i}") for i in range(2)]
    C.w2st = [S.sb([128, 1024], F32, f"w2st{i}") for i in range(2)]
    C.sg = [S.sb([128, 512], F32, f"sg{i}") for i in range(2)]; C.sgi = 0
    C.tmp = S.sb([128, 1024], F32, "tmp")
    C.ones = S.sb([128, 128], F32, "ones")
    S.memset("dve", C.ones[:], 1.0)
    return C


GROUPS34 = [list(range(0, 9)), list(range(9, 18)), list(range(18, 26)), list(range(26, 34))]
GROUPS32 = [list(range(0, 8)), list(range(8, 16)), list(range(16, 24)), list(range(24, 32))]
JF34 = lambda ci: 0 if ci < 2 else 1


def phase1(S, PS, IN, SC):
    C = alloc_common(S, PS)
    C.idf, C.idb = setup_consts(S, IN["ident"][:])
    mv = mod_vectors(S, IN["cT"][:], IN["modw"][0], IN["modbT"][0], IN["ngT"][0], C.wst, C.pa[0])
    S.dma("sync", SC["modT0"][:], mv["modT"][:].re("p t j -> p (t j)"))
    C.gateb = {}
    for j in range(2):
        C.gateb[(0, j)] = gate_bcast(S, mv["GATE"][:, 0, :, j], C.idf, C.ones, C.pa[1 + j], 0.5, f"gb0{j}")
    zt = S.sb([2, 2304], F32, "zt")
    S.memset("dve", zt[:], 0.0)
    ppad = SC["ppad"]
    S.dma("sync", ppad[0:1, :], zt[0:1, :]); S.dma("sync", ppad[257:259, :], zt[0:2, :]); S.dma("sync", ppad[4355:4356, :], zt[0:1, :])
    pst = [S.sb([128, 256], F32, f"pst{i}") for i in range(2)]
    psti = [0]

    def xs(ci):
        return IN["ctx"][ci * 128:(ci + 1) * 128, :] if ci < 2 else IN["x"][(ci - 2) * 128:(ci - 1) * 128, :]

    def xd(ci):
        return SC["x1"][ci * 128:(ci + 1) * 128, :]

    def after(ci, li, xt):
        j = JF34(ci)
        norm_to_hT(S, C, xt[:], C.hT, li * 128, mv["G"][:, 1, :, j], mv["SH"][:, 1, :, j])

    win = IN["win_ab"]
    for grp in ffn(S, C, xs, xd, IN["w1"][0, 0], IN["w2"][0, 0], mv, 0, GROUPS34, JF34, after_chunk=after):
        for cb in range(9):
            wst = C.wst[cb % 2]; wb = C.w1b[cb % 2]
            S.dma("sync", wst[:, 0:4, :], win[cb, :, 0:4, :])
            S.dma("pool", wst[:, 4:8, :], win[cb, :, 4:8, :])
            S.copy("pool", wb[:], wst[:])
            for li, ci in enumerate(grp):
                pp = C.pa[C.pai % 4]; C.pai += 1
                for k in range(8):
                    S.mm(pp[:, 0:256], C.hT[:, k, li * 128:(li + 1) * 128], wb[:, k, :], start=(k == 0), stop=(k == 7))
                st = pst[psti[0] % 2]; psti[0] += 1
                S.copy("act", st[:], pp[:, 0:256])
                S.dma("sync", ppad[prow(ci):prow(ci) + 128, cb * 256:(cb + 1) * 256], st[:])


def phase2a(S, PS, IN, SC, MD=F32R, nblocks=NCH):
    ppad = SC["ppad"]

    def ld(dv, shape, nm, dt=F32):
        t = S.sb(shape, dt, nm)
        S.dma("sync", t[:], dv)
        return t
    mu0 = ld(IN["mub"][0], [128, 1536], "mu0"); mu1 = ld(IN["mub"][1], [128, 1536], "mu1")
    c0 = S.sb([128, 1536], F32, "c0")
    S.tt("dve", c0[:], mu0[:], mu1[:], ALU.add)
    S.ts("dve", c0[:], c0[:], -1.0, ALU.mult, 1.0, ALU.add)
    kkb = ld(IN["kkb"][:], [128, 512], "kkb"); kab = ld(IN["kab"][:], [128, 512], "kab")
    omka = S.sb([128, 512], F32, "omka")
    S.ts("dve", omka[:], kab[:], -1.0, ALU.mult, 1.0, ALU.add)
    g2 = ld(IN["g2"][:], [128, 512], "g2")
    msk = IN["msk"]
    mUs = ld(msk[0], [128, 128], "mUs"); mUi = ld(msk[1], [128, 128], "mUi"); mLs = ld(msk[2], [128, 128], "mLs")
    idf = ld(msk[3], [128, 128], "idf"); mLi = ld(msk[4], [128, 128], "mLi")
    cm = {}
    for nm, m_ in (("Ui", mUi), ("Us", mUs), ("Ls", mLs), ("Li", mLi)):
        cm[nm] = S.sb([128, 128], MD, "c" + nm)
        S.ts("dve", cm[nm][:], m_[:], NEGC, ALU.mult)
    negc = S.sb([128, 2], MD, "negc")
    S.ts("dve", negc[:], mUi[:, 0:2], 0.0, ALU.mult, NEGC, ALU.add)
    idm = S.sb([128, 128], MD, "idm")
    S.copy("dve", idm[:], idf[:])
    w2a = []; a2a = []
    for dr in range(2):
        t_ = ld(IN["w2a"][dr], [65, 512], f"w2a{dr}"); t2_ = S.sb([65, 512], MD, f"w2ar{dr}"); S.copy("dve", t2_[:], t_[:]); w2a.append(t2_)
        t_ = ld(IN["a2a"][dr], [65, 512], f"a2a{dr}"); t2_ = S.sb([65, 512], MD, f"a2ar{dr}"); S.copy("dve", t2_[:], t_[:]); a2a.append(t2_)
    g2r = S.sb([128, 512], MD, "g2r"); S.copy("dve", g2r[:], g2[:]); g2 = g2r

    pb = PS["g"]
    pbi = [0]

    def bank():
        b = pb[pbi[0] % 6]; pbi[0] += 1
        return b

    def t512(nm, dt=F32):
        return S.sb([128, 512], dt, nm)

    rc = S.sb([128, 1536], F32, "rc"); rp = S.sb([128, 1536], F32, "rp"); rn_ = S.sb([128, 1536], F32, "rn")
    mix = S.sb([128, 1536], MD, "mix"); mt = S.sb([128, 1536], F32, "mt")
    TW = S.sb([65, 128], MD, "TW"); AL = S.sb([65, 128], MD, "AL"); SG = S.sb([128, 128], MD, "SG")
    S.ts("dve", TW[:], mUi[0:65, :], 0.0, ALU.mult, 1.0, ALU.add); S.ts("dve", AL[:], mUi[0:65, :], 0.0, ALU.mult, 1.0, ALU.add)
    lmix = S.sb([128, 256], MD, "lmix"); lmt = S.sb([128, 256], F32, "lmt")
    lc = S.sb([128, 256], F32, "lc"); lp = S.sb([128, 256], F32, "lp"); ln_ = S.sb([128, 256], F32, "ln")
    mul0 = ld(IN["mulb"][0], [128, 256], "mul0"); mul1 = ld(IN["mulb"][1], [128, 256], "mul1")
    c0lb = S.sb([128, 256], F32, "c0lb")
    S.tt("dve", c0lb[:], mul0[:], mul1[:], ALU.add)
    S.ts("dve", c0lb[:], c0lb[:], -1.0, ALU.mult, 1.0, ALU.add)
    sig = t512("sig", MD); a_ = t512("a"); gt = t512("gt")
    kk = t512("kk"); sq = t512("sq"); ss = S.sb([128, 8], F32, "ss"); rn8 = S.sb([128, 8], F32, "rn8")
    kd = t512("kd"); tq = t512("tq"); bq = t512("bq")
    Gc = t512("G"); Gp = t512("Gp"); Gi = t512("Gi"); Ge = t512("Ge")
    A = t512("A", MD); B = t512("B", MD); K = t512("K", MD); Rq = t512("Rq", MD)
    B2m = t512("B2m", MD); K2m = t512("K2m", MD)
    AT = S.sb([64, 8, 128], MD, "AT"); BT = S.sb([64, 8, 128], MD, "BT"); KT = S.sb([64, 8, 128], MD, "KT")
    RT = S.sb([64, 8, 128], MD, "RT")
    mat = lambda nm, dt=MD: [S.sb([128, 4, 128], dt, f"{nm}{g}") for g in range(2)]
    Nm = [mat("Nm0"), mat("Nm1")]; NT = [mat("NT0"), mat("NT1")]
    Mak = mat("Mak"); Mbr = mat("Mbr"); Mkr = mat("Mkr")
    Tf = mat("Tf", MD); Tm = Tf
    WTm = t512("WTm", MD); X1Tm = t512("X1Tm", MD); UlTm = t512("UlTm", MD)
    Rpf = S.sb([64, 8, 128], MD, "Rpf")
    gC = S.sb([64, 16], F32, "gC")
    dgG = S.sb([64, 8, 64], F32, "dgG")
    Pf = [S.sb([64, 8, 64], MD, f"Pf{c}") for c in range(2)]
    QTf = [S.sb([64, 8, 64], F32, f"QTf{c}") for c in range(2)]
    Yloc = t512("Yloc")
    ST = [S.sb([64, 8, 64], MD, f"ST{i}") for i in range(2)]
    yt = [t512(f"yt{i}") for i in range(2)]
    v3 = lambda v: v.re("p (h k) -> p h k", k=64)
    m3 = lambda v: v.re("p (h t) -> p h t", t=128)
    sti = 0
    for dr in range(2):
        if dr == 0:
            m_strict, m_strictT, m_incl = mUs, mLs, mUi
            c_incl, c_strict, c_end = cm["Ui"], cm["Us"], cm["Ls"]
            border = list(range(nblocks)); corder = [0, 1]
        else:
            m_strict, m_strictT, m_incl = mLs, mUs, mLi
            c_incl, c_strict, c_end = cm["Li"], cm["Ls"], cm["Us"]
            border = [1, 0] + list(range(NCH - 1, 1, -1)); corder = [1, 0]
            border = border[:nblocks]
        S.ts("dve", ST[sti % 2][:].re("p h k -> p (h k)"), kkb[0:64, :], 0.0, ALU.mult)
        for n in border:
            r0 = prow(n)
            S.dma("sync", rc[:], ppad[r0:r0 + 128, 0:1536])
            S.dma("pool", rp[:], ppad[r0 - 1:r0 + 127, 0:1536])
            S.dma("sync", rn_[:], ppad[r0 + 1:r0 + 129, 0:1536])
            S.dma("pool", lc[:], ppad[r0:r0 + 128, 1536:1792])
            S.dma("pool", lp[:], ppad[r0 - 1:r0 + 127, 1536:1792])
            S.dma("sync", ln_[:], ppad[r0 + 1:r0 + 129, 1536:1792])
            S.tt("dve", mix[:], rc[:], c0[:], ALU.mult)
            S.tt("pool", mt[:], rp[:], mu0[:], ALU.mult)
            S.tt("dve", mix[:], mix[:], mt[:], ALU.add)
            S.tt("pool", mt[:], rn_[:], mu1[:], ALU.mult)
            S.tt("dve", mix[:], mix[:], mt[:], ALU.add)
            r = mix[:, 0:512]; k = mix[:, 512:1024]; v = mix[:, 1024:1536]
            if dr == 0:
                S.dma("pool", SC["rvo"][n * 128:(n + 1) * 128, 0:512], r)
                S.dma("pool", SC["rvo"][n * 128:(n + 1) * 128, 512:1024], v)
            S.tt("dve", lmix[:], lc[:], c0lb[:], ALU.mult)
            S.tt("pool", lmt[:], lp[:], mul0[:], ALU.mult)
            S.tt("dve", lmix[:], lmix[:], lmt[:], ALU.add)
            S.tt("pool", lmt[:], ln_[:], mul1[:], ALU.mult)
            S.tt("dve", lmix[:], lmix[:], lmt[:], ALU.add)
            pl = bank()
            S.mm(pl[0:64, 0:128], lmix[:, 0:64], idm[:], sig=False, f32=True)
            S.mm(pl[0:64, 128:256], lmix[:, 64:128], idm[:], sig=False, f32=True)
            S.mm(pl[:, 256:384], lmix[:, 128:256], idm[:])
            S.act(TW[0:64, :], pl[0:64, 0:128], AF.Tanh)
            S.copy("act", AL[0:64, :], pl[0:64, 128:256])
            S.act(SG[:], pl[:, 256:384], AF.Sigmoid)
            pw_ = bank(); S.mm(pw_[:], TW[:], w2a[dr][:])
            S.act(sig[:], pw_[:], AF.Sigmoid)
            pa_ = bank(); S.mm(pa_[:], AL[:], a2a[dr][:])
            S.act(a_[:], pa_[:], AF.Sigmoid)
            if dr == 0:
                pg_ = bank(); S.mm(pg_[:], SG[:], g2[:])
                S.copy("act", gt[:], pg_[:])
                S.dma("pool", SC["gto"][n * 128:(n + 1) * 128, :], gt[:])
            S.tt("dve", kk[:], k, kkb[:], ALU.mult)
            S.tt("pool", sq[:], kk[:], kk[:], ALU.mult)
            S.red("dve", ss[:], v3(sq[:]), ALU.add)
            S.ts("dve", ss[:], ss[:], 1e-12, ALU.max)
            S.act(ss[:], ss[:], AF.Sqrt)
            S.recip(rn8[:], ss[:])
            S.tt("dve", v3(kk[:]), v3(kk[:]), rn8[:].re("p (h o) -> p h o", o=1).bc([128, 8, 64]), ALU.mult)
            S.tt("pool", tq[:], a_[:], kab[:], ALU.mult)
            S.tt("pool", tq[:], tq[:], omka[:], ALU.add)
            S.tt("pool", kd[:], k, tq[:], ALU.mult)
            S.dma("pool", SC["kdo"][dr, n * 128:(n + 1) * 128, :], kd[:])
            S.tt("pool", bq[:], kk[:], a_[:], ALU.mult)
            pc1 = bank(); S.mm(pc1[:], c_incl[:], sig[:])
            pc2 = bank(); S.mm(pc2[:], c_strict[:], sig[:])
            pc3 = bank(); S.mm(pc3[:], c_end[:], sig[:])
            S.act(Gc[:], pc1[:], AF.Exp)
            S.act(Gi[:], pc1[:], AF.Exp, scale=-1.0)
            S.act(Gp[:], pc2[:], AF.Exp)
            S.act(Ge[:], pc3[:], AF.Exp)
            S.stt("dve", A[:], kk[:], -1.0, Gp[:], ALU.mult, ALU.mult)
            S.tt("dve", B[:], bq[:], Gi[:], ALU.mult)
            S.tt("pool", K[:], kd[:], Gi[:], ALU.mult)
            S.tt("pool", Rq[:], r, Gc[:], ALU.mult)
            S.tt("pool", B2m[:], bq[:], Ge[:], ALU.mult)
            S.tt("pool", K2m[:], kd[:], Ge[:], ALU.mult)
            Am_ = A
            Vv = v
            for (src, dst) in ((A, AT), (B, BT), (K, KT), (Rq, RT)):
                for hg in range(2):
                    p_ = bank()
                    for hh in range(4):
                        h = hg * 4 + hh
                        S.mm(p_[0:64, hh * 128:(hh + 1) * 128], src[:, h * 64:(h + 1) * 64], idm[:], sig=(hh == 3), f32=True)
                    S.copy("act", dst[:, hg * 4:(hg + 1) * 4, :], m3(p_[0:64, :]))
            RTf_ = RT

            def mmat(dst, LT, RTt, mask, hg):
                p_ = bank()
                for hh in range(4):
                    h = hg * 4 + hh
                    S.mm(p_[:, hh * 128:(hh + 1) * 128], LT[:, h, :], RTt[:, h, :], sig=(hh == 3))
                S.tt("dve", dst[hg][:], m3(p_[:]), mask[:].re("p (o t) -> p o t", o=1).bc([128, 4, 128]), ALU.mult)
            for hg in range(2):
                mmat(Nm[0], BT, AT, m_strict, hg)
                mmat(NT[0], AT, BT, m_strictT, hg)
                mmat(Mak, KT, AT, m_strict, hg)
                mmat(Mbr, BT, RT, m_incl, hg)
                mmat(Mkr, KT, RT, m_incl, hg)
            for hg in range(2):
                S.tt("dve", Tf[hg][:], Nm[0][hg][:], idf[:].re("p (o t) -> p o t", o=1).bc([128, 4, 128]), ALU.add)
            cur = 0
            for lev in range(5):
                nxt = 1 - cur
                last = lev == 4
                for hg in range(2):
                    if not last:
                        p1 = bank()
                        for hh in range(4):
                            S.mm(p1[:, hh * 128:(hh + 1) * 128], NT[cur][hg][:, hh, :], Nm[cur][hg][:, hh, :], sig=(hh == 3))
                        S.copy("act", Nm[nxt][hg][:], m3(p1[:]))
                    p2 = bank()
                    for hh in range(4):
                        S.mm(p2[:, hh * 128:(hh + 1) * 128], Nm[cur][hg][:, hh, :], NT[cur][hg][:, hh, :], sig=(hh == 3))
                    S.copy("dve" if hg == 0 else "act", NT[nxt][hg][:], m3(p2[:]))
                for hg in range(2):
                    p3 = bank()
                    for hh in range(4):
                        S.mm(p3[:, hh * 128:(hh + 1) * 128], NT[nxt][hg][:, hh, :], Tm[hg][:, hh, :], sig=(hh == 3))
                    S.tt("dve", Tf[hg][:], Tf[hg][:], m3(p3[:]), ALU.add)
                cur = nxt
            p_ = bank()
            for h in range(8):
                S.mm(p_[:, h * 64:(h + 1) * 64], Tm[h // 4][:, h % 4, :], Am_[:, h * 64:(h + 1) * 64], sig=(h == 7))
            S.copy("act", WTm[:], p_[:])
            p_ = bank()
            for h in range(8):
                S.mm(p_[:, h * 64:(h + 1) * 64], Mak[h // 4][:, h % 4, :], Vv[:, h * 64:(h + 1) * 64], sig=(h == 7))
            S.copy("dve", X1Tm[:], p_[:])
            p_ = bank()
            for h in range(8):
                S.mm(p_[:, h * 64:(h + 1) * 64], Tm[h // 4][:, h % 4, :], X1Tm[:, h * 64:(h + 1) * 64], sig=(h == 7))
            S.copy("act", UlTm[:], p_[:])
            for hg in range(2):
                p_ = bank()
                for hh in range(4):
                    h = hg * 4 + hh
                    S.mm(p_[0:64, hh * 128:(hh + 1) * 128], WTm[:, h * 64:(h + 1) * 64], Mbr[hg][:, hh, :], sig=(hh == 3), f32=True)
                S.tt("dve", Rpf[:, hg * 4:(hg + 1) * 4, :], m3(p_[0:64, :]), RTf_[:, hg * 4:(hg + 1) * 4, :], ALU.add)
            pgc = [bank(), bank()]
            for c in range(2):
                for h in range(8):
                    S.mm(pgc[c][0:64, h:h + 1], sig[c * 64:(c + 1) * 64, h * 64:(h + 1) * 64], negc[c * 64:(c + 1) * 64, 0:1], sig=(h == 7), f32=True)
                S.act(gC[:, c * 8:(c + 1) * 8], pgc[c][0:64, 0:8], AF.Exp)
            for c in range(2):
                pc = slice(c * 64, (c + 1) * 64)
                p_ = bank()
                for h in range(8):
                    S.mm(p_[0:64, h * 64:(h + 1) * 64], WTm[pc, h * 64:(h + 1) * 64], B2m[pc, h * 64:(h + 1) * 64], sig=(h == 7), f32=True)
                S.tt("pool", dgG[:], idf[0:64, 0:64].re("p (o k) -> p o k", o=1).bc([64, 8, 64]),
                     gC[:, c * 8:(c + 1) * 8].re("p (h o) -> p h o", o=1).bc([64, 8, 64]), ALU.mult)
                S.tt("dve", Pf[c][:], v3(p_[0:64, :]), dgG[:], ALU.add)
                p_ = bank()
                for h in range(8):
                    S.mm(p_[0:64, h * 64:(h + 1) * 64], B2m[pc, h * 64:(h + 1) * 64], UlTm[pc, h * 64:(h + 1) * 64], start=True, stop=False, sig=False, f32=True)
                    S.mm(p_[0:64, h * 64:(h + 1) * 64], K2m[pc, h * 64:(h + 1) * 64], Vv[pc, h * 64:(h + 1) * 64], start=False, stop=True, sig=(h == 7), f32=True)
                S.copy("act", QTf[c][:], v3(p_[0:64, :]))
            p_ = bank()
            for h in range(8):
                S.mm(p_[:, h * 64:(h + 1) * 64], Mbr[h // 4][:, h % 4, :], UlTm[:, h * 64:(h + 1) * 64], start=True, stop=False, sig=False)
                S.mm(p_[:, h * 64:(h + 1) * 64], Mkr[h // 4][:, h % 4, :], Vv[:, h * 64:(h + 1) * 64], start=False, stop=True, sig=(h == 7))
            S.copy("act", Yloc[:], p_[:])
            ytile = yt[n % 2]
            for c in corder:
                pc = slice(c * 64, (c + 1) * 64)
                st_cur = ST[sti % 2]; st_nxt = ST[(sti + 1) % 2]; sti += 1
                p_ = bank()
                for h in range(8):
                    S.mm(p_[:, h * 64:(h + 1) * 64], Rpf[:, h, :], st_cur[:, h, :], sig=(h == 7))
                S.tt("dve", ytile[pc, :], p_[pc, :], Yloc[pc, :], ALU.add)
                p2 = bank()
                for h in range(8):
                    S.mm(p2[0:64, h * 64:(h + 1) * 64], Pf[c][:, h, :], st_cur[:, h, :], sig=(h == 7), f32=True)
                S.tt("dve", st_nxt[:], v3(p2[0:64, :]), QTf[c][:], ALU.add)
            S.dma("sync", SC["yd"][dr, n * 128:(n + 1) * 128, :], ytile[:])


def phase2b(S, PS, IN, SC):
    CH = 128
    ppad = SC["ppad"]
    ident = IN["ident"]
    idf0 = S.sb([128, 128], F32, "idf0"); S.dma("sync", idf0[:], ident[:])
    Jm0 = S.sb([128, 128], F32, "Jm0"); S.dma("sync", Jm0[:], IN["msk"][5])
    idf = S.sb([128, 128], F32R, "idf"); S.copy("dve", idf[:], idf0[:])
    Jm = S.sb([128, 128], F32R, "Jm"); S.copy("dve", Jm[:], Jm0[:])
    pb = PS["g"]
    sm = lambda nm: S.sb([128, NST], F32, nm)
    big = lambda nm, dt=F32: S.sb([128, NST, 32], dt, nm)
    are = sm("are"); aim = sm("aim"); lst = sm("lst")
    bre = big("bre"); bim = big("bim"); cre0 = big("cre0"); cim = big("cim"); ncim = big("ncim", F32R); cre = big("cre", F32R)
    lre = sm("lre"); dt_ = sm("dt"); zr = sm("zr"); th = sm("th"); rho = sm("rho")
    sa = sm("sa"); sk = sm("sk"); sr = sm("sr")
    cs = sm("cs"); sn = sm("sn")
    abre = sm("abre"); abim = sm("abim"); den = sm("den"); rden = sm("rden"); t1 = sm("t1"); t2 = sm("t2")
    fre = sm("fre"); fim = sm("fim"); am1 = sm("am1")
    bbre = big("bbre", F32R); bbim = big("bbim", F32R); u1 = big("u1"); u2 = big("u2")
    BBTre = S.sb([32, NST, 128], F32R, "BBTre"); BBTim = S.sb([32, NST, 128], F32R, "BBTim")
    Ec = S.sb([128, NST, CH], F32, "Ec"); Es = S.sb([128, NST, CH], F32, "Es")
    w1 = S.sb([128, NST, CH // 2], F32, "w1"); w2 = S.sb([128, NST, CH // 2], F32, "w2")
    rhob = S.sb([128, NST, CH], F32, "rhob")
    utok = [S.sb([128, 512], F32, f"utok{i}") for i in range(2)]
    utokr = [S.sb([128, 512], F32R, f"utokr{i}") for i in range(2)]
    ut = [S.sb([32, NST, CH], F32R, f"ut{i}") for i in range(2)]
    Zre_ = [S.sb([128, NST, CH], F32, f"Zre{i}") for i in range(2)]; Zim_ = [S.sb([128, NST, CH], F32, f"Zim{i}") for i in range(2)]
    wre_ = [S.sb([128, NST, CH], F32, "wre0")] * 2; wim_ = [S.sb([128, NST, CH], F32, "wim0")] * 2
    xre = [S.sb([128, NST, CH], F32R, f"xre{i}") for i in range(2)]
    xim = [S.sb([128, NST, CH], F32R, f"xim{i}") for i in range(2)]
    ta_ = [[S.sb([128, 4, CH], F32, f"ta{q}{i}") for i in range(4)] for q in range(2)]
    tb_ = [[S.sb([128, 4, CH], F32, f"tb0{i}") for i in range(4)]] * 2
    pbi = 0
    tai = 0
    yst = [S.sb([128, 512], F32R, f"yst{i}") for i in range(2)]
    yst2 = [S.sb([128, 512], F32, f"ystb{i}") for i in range(2)]
    MAGIC = 12582912.0
    TWO_PI = 2.0 * math.pi
    pbi = 0
    cc = 0
    for dr in range(2):
        for (t_, nm) in ((are, "are"), (aim, "aim"), (lst, "lst")):
            S.dma("sync", t_[:], IN[nm][dr])
        for (t_, nm) in ((bre, "bre"), (bim, "bim"), (cre0, "cre"), (cim, "cim")):
            S.dma("sync", t_[:], IN[nm][dr])
        S.ts("dve", ncim[:], cim[:], -1.0, ALU.mult)
        S.copy("dve", cre[:], cre0[:])
        S.ts("dve", lre[:], are[:], -1e-4, ALU.min)
        S.act(dt_[:], lst[:], AF.Exp)
        S.tt("dve", zr[:], lre[:], dt_[:], ALU.mult)
        S.tt("dve", th[:], aim[:], dt_[:], ALU.mult)
        S.act(rho[:], zr[:], AF.Exp)

        def sin_reduced(out, ang, shift):
            S.ts("dve", sa[:], ang[:], float(shift), ALU.add)
            S.ts("dve", sk[:], sa[:], 1.0 / TWO_PI, ALU.mult, MAGIC, ALU.add)
            S.ts("dve", sk[:], sk[:], MAGIC, ALU.subtract)
            S.stt("dve", sr[:], sk[:], -TWO_PI, sa[:], ALU.mult, ALU.add)
            S.ts("dve", sr[:], sr[:], 3.14159, ALU.min, -3.14159, ALU.max)
            S.act(out, sr[:], AF.Sin)
        sin_reduced(sn[:], th, 0.0)
        sin_reduced(cs[:], th, math.pi / 2)
        S.tt("dve", abre[:], rho[:], cs[:], ALU.mult)
        S.tt("dve", abim[:], rho[:], sn[:], ALU.mult)
        S.tt("dve", t1[:], lre[:], lre[:], ALU.mult)
        S.tt("dve", t2[:], aim[:], aim[:], ALU.mult)
        S.tt("dve", den[:], t1[:], t2[:], ALU.add)
        S.recip(rden[:], den[:])
        S.ts("dve", am1[:], abre[:], -1.0, ALU.add)
        S.tt("dve", t1[:], am1[:], lre[:], ALU.mult)
        S.tt("dve", t2[:], abim[:], aim[:], ALU.mult)
        S.tt("dve", t1[:], t1[:], t2[:], ALU.add)
        S.tt("dve", fre[:], t1[:], rden[:], ALU.mult)
        S.tt("dve", t1[:], abim[:], lre[:], ALU.mult)
        S.tt("dve", t2[:], am1[:], aim[:], ALU.mult)
        S.tt("dve", t1[:], t1[:], t2[:], ALU.subtract)
        S.tt("dve", fim[:], t1[:], rden[:], ALU.mult)
        fre_b = fre[:].re("p (t o) -> p t o", o=1).bc([128, NST, 32])
        fim_b = fim[:].re("p (t o) -> p t o", o=1).bc([128, NST, 32])
        S.tt("dve", u1[:], bre[:], fre_b, ALU.mult)
        S.tt("dve", u2[:], bim[:], fim_b, ALU.mult)
        S.tt("dve", bbre[:], u1[:], u2[:], ALU.subtract)
        S.tt("dve", u1[:], bim[:], fre_b, ALU.mult)
        S.tt("dve", u2[:], bre[:], fim_b, ALU.mult)
        S.tt("dve", bbim[:], u1[:], u2[:], ALU.add)
        for (src, dst) in ((bbre, BBTre), (bbim, BBTim)):
            for g4 in range(4):
                p_ = pb[pbi % 6]; pbi += 1
                for jj in range(4):
                    j = g4 * 4 + jj
                    S.mm(p_[0:32, jj * 128:(jj + 1) * 128], src[:, j, :], idf[:], sig=(jj == 3), f32=True)
                S.copy("dve", dst[:, g4 * 4:(g4 + 1) * 4, :], p_[0:32, :].re("p (a b) -> p a b", b=128))
        S.copy("dve", Ec[:, :, 0], cs[:])
        S.copy("dve", Es[:, :, 0], sn[:])
        m = 1
        while m < CH:
            cb = Ec[:, :, m - 1:m].bc([128, NST, m]); sb_ = Es[:, :, m - 1:m].bc([128, NST, m])
            S.tt("dve", w1[:, :, 0:m], Ec[:, :, 0:m], cb, ALU.mult)
            S.tt("dve", w2[:, :, 0:m], Es[:, :, 0:m], sb_, ALU.mult)
            S.tt("dve", Ec[:, :, m:2 * m], w1[:, :, 0:m], w2[:, :, 0:m], ALU.subtract)
            S.tt("dve", w1[:, :, 0:m], Ec[:, :, 0:m], sb_, ALU.mult)
            S.tt("dve", w2[:, :, 0:m], Es[:, :, 0:m], cb, ALU.mult)
            S.tt("dve", Es[:, :, m:2 * m], w1[:, :, 0:m], w2[:, :, 0:m], ALU.add)
            m *= 2
        S.copy("dve", rhob[:], rho[:].re("p (t o) -> p t o", o=1).bc([128, NST, CH]))
        Pm = idf if dr == 0 else Jm
        border = list(range(NCH)) if dr == 0 else [1, 0] + list(range(NCH - 1, 1, -1))
        def stageA(ci, n, cc):
            nonlocal pbi, tai
            r0 = prow(n)
            utk0 = utok[cc % 2]
            S.dma("sync", utk0[:], ppad[r0:r0 + 128, 1792:2304])
            utk = utokr[cc % 2]
            S.copy("act", utk[:], utk0[:])
            u = ut[cc % 2]
            for g4 in range(4):
                p_ = pb[pbi % 6]; pbi += 1
                for jj in range(4):
                    j = g4 * 4 + jj
                    S.mm(p_[0:32, jj * 128:(jj + 1) * 128], utk[:, j * 32:(j + 1) * 32], Pm[:], sig=(jj == 3), f32=True)
                S.copy("act", u[:, g4 * 4:(g4 + 1) * 4, :], p_[0:32, :].re("p (a b) -> p a b", b=128))
            Zre, Zim = Zre_[cc % 2], Zim_[cc % 2]
            for g4 in range(4):
                ta = ta_[tai % 2]; tai += 1
                pr = pb[pbi % 6]; pbi += 1
                pi_ = pb[pbi % 6]; pbi += 1
                for jj in range(4):
                    j = g4 * 4 + jj
                    S.mm(pr[:, jj * CH:(jj + 1) * CH], BBTre[:, j, :], u[:, j, :], sig=False)
                for jj in range(4):
                    j = g4 * 4 + jj
                    S.mm(pi_[:, jj * CH:(jj + 1) * CH], BBTim[:, j, :], u[:, j, :], sig=(jj == 3))
                sl = slice(g4 * 4, (g4 + 1) * 4)
                prv = pr[:, :].re("p (a b) -> p a b", b=CH); piv = pi_[:, :].re("p (a b) -> p a b", b=CH)
                a0, a1, a2, a3 = ta
                S.tt("dve", a0[:], prv, Ec[:, sl, :], ALU.mult)
                S.tt("dve", a1[:], piv, Es[:, sl, :], ALU.mult)
                S.tt("dve", Zre[:, sl, :], a0[:], a1[:], ALU.add)
                S.tt("dve", a2[:], piv, Ec[:, sl, :], ALU.mult)
                S.tt("dve", a3[:], prv, Es[:, sl, :], ALU.mult)
                S.tt("dve", Zim[:, sl, :], a2[:], a3[:], ALU.subtract)

        def stageSc(ci, cc):
            Zre, Zim = Zre_[cc % 2], Zim_[cc % 2]
            xr_prev, xi_prev = xre[(cc + 1) % 2], xim[(cc + 1) % 2]
            for j in range(NST):
                for (wt, zt, xp) in ((wre_[0], Zre, xr_prev), (wim_[0], Zim, xi_prev)):
                    init = 0.0 if ci == 0 else xp[:, j, CH - 1:CH]
                    S.scan(wt[:, j, :], rhob[:, j, :], zt[:, j, :], init)

        def stageB(ci, n, cc):
            nonlocal pbi
            xr, xi = xre[cc % 2], xim[cc % 2]
            wre, wim = wre_[0], wim_[0]
            tb = tb_[0]
            for g4 in range(4):
                sl = slice(g4 * 4, (g4 + 1) * 4)
                b0, b1, b2, b3 = tb
                S.tt("pool", b0[:], wre[:, sl, :], Ec[:, sl, :], ALU.mult)
                S.tt("pool", b1[:], wim[:, sl, :], Es[:, sl, :], ALU.mult)
                S.tt("dve", xr[:, sl, :], b0[:], b1[:], ALU.subtract)
                S.tt("pool", b2[:], wim[:, sl, :], Ec[:, sl, :], ALU.mult)
                S.tt("pool", b3[:], wre[:, sl, :], Es[:, sl, :], ALU.mult)
                S.tt("dve", xi[:, sl, :], b2[:], b3[:], ALU.add)
            py = pb[pbi % 6]; pbi += 1
            for j in range(NST):
                S.mm(py[:, j * 32:(j + 1) * 32], xr[:, j, :], cre[:, j, :], start=True, stop=False, sig=False)
                S.mm(py[:, j * 32:(j + 1) * 32], xi[:, j, :], ncim[:, j, :], start=False, stop=True, sig=(j == NST - 1))
            ys = yst[cc % 2]
            S.copy("act", ys[:], py[:])
            py2 = pb[pbi % 6]; pbi += 1
            S.mm(py2[:], Pm[:], ys[:])
            ys2 = yst2[cc % 2]
            S.copy("act", ys2[:], py2[:])
            S.dma("sync", SC["ys"][dr, n * 128:(n + 1) * 128, :], ys2[:])

        stageA(0, border[0], cc)
        for ci, n in enumerate(border):
            stageSc(ci, cc)
            if ci + 1 < len(border):
                stageA(ci + 1, border[ci + 1], cc + 1)
            stageB(ci, n, cc)
            cc += 1


def phase3a(S, PS, IN, SC):
    idf, idb = setup_consts(S, IN["ident"][:])
    ones = S.sb([128, 128], F32, "ones"); S.memset("dve", ones[:], 1.0)
    pa = PS["g"]; pt = PS["b"]
    modT = S.sb([128, 72, 2], F32, "modT")
    S.dma("sync", modT[:].re("p t j -> p (t j)"), SC["modT0"][:])
    gateb = [gate_bcast(S, modT[:, 5 * 8:6 * 8, j], idf, ones, pa[j], 1.0, f"g5{j}") for j in range(2)]
    bcs = [S.sb([128, 512], F32, f"bcs{i}") for i in range(5)]
    for i in range(5):
        S.dma("sync", bcs[i][:], IN["bcs"][i])
    lnxg, lnxb, rk, s5d, glub = bcs
    S.ts("dve", rk[:], rk[:], 0.5, ALU.mult)
    wst = S.sb([128, 4, 512], F32, "wst")
    gluw = S.sb([128, 4, 512], BF16, "gluw")
    S.dma("sync", wst[:], IN["gluw"].re("(k p) n -> p k n", p=128))
    S.copy("pool", gluw[:], wst[:])
    outw = S.sb([128, 8, 1024], BF16, "outw")
    wst2 = [S.sb([128, 1024], F32, f"wst2{i}") for i in range(2)]
    for k in range(8):
        S.dma("sync", wst2[k % 2][:], IN["outw_ab"][k * 128:(k + 1) * 128, :])
        S.copy("pool", outw[:, k, :], wst2[k % 2][:])
    t5 = lambda nm, dt=F32: S.sb([128, 512], dt, nm)
    inr = [[t5(f"inr{i}{j}") for j in range(7)] for i in range(2)]
    ins = [[t5(f"ins{i}{j}") for j in range(3)] for i in range(2)]
    xt = [S.sb([128, 1024], F32, f"xt{i}") for i in range(2)]
    y = t5("y"); yc = t5("yc"); sq = t5("sq"); ks = t5("ks"); tq = t5("tq"); bon = t5("bon")
    s8 = S.sb([128, 8], F32, "s8"); v8 = S.sb([128, 8], F32, "v8"); b8 = S.sb([128, 8], F32, "b8")
    cat = S.sb([128, 1024], BF16, "cat"); ysum = t5("ysum"); z = t5("z"); zb = t5("zb", BF16)
    zT = S.sb([128, 4, 128], BF16, "zT"); gl = t5("gl"); catT = S.sb([128, 8, 128], BF16, "catT")
    tmp = S.sb([128, 1024], F32, "tmp")
    v3 = lambda v: v.re("p (h k) -> p h k", k=64)
    b3 = lambda t: t[:].re("p (h o) -> p h o", o=1).bc([128, 8, 64])
    for ci in range(NCH):
        j = JF34(ci)
        i2 = ci % 2
        rows = slice(ci * 128, (ci + 1) * 128)
        srcs = [SC["yd"][0, rows, :], SC["yd"][1, rows, :], SC["kdo"][0, rows, :], SC["kdo"][1, rows, :],
                SC["rvo"][rows, 0:512], SC["rvo"][rows, 512:1024], SC["gto"][rows, :]]
        for q in range(7):
            S.dma("sync" if q % 2 == 0 else "pool", inr[i2][q][:], srcs[q])
        srcs2 = [SC["ys"][0, rows, :], SC["ys"][1, rows, :], SC["ppad"][prow(ci):prow(ci) + 128, 1792:2304]]
        for q in range(3):
            S.dma("pool" if q % 2 == 0 else "sync", ins[i2][q][:], srcs2[q])
        S.dma("sync", xt[i2][:], SC["x1"][rows, :])
        y0, y1, kd0, kd1, r, v, g = inr[i2]
        S.tt("dve", y[:], y0[:], y1[:], ALU.add)
        S.red("dve", s8[:], v3(y[:]), ALU.add)
        S.ts("dve", s8[:], s8[:], 1.0 / 64, ALU.mult)
        S.tt("dve", v3(yc[:]), v3(y[:]), b3(s8), ALU.subtract)
        S.tt("pool", sq[:], yc[:], yc[:], ALU.mult)
        S.red("dve", v8[:], v3(sq[:]), ALU.add)
        S.ts("dve", v8[:], v8[:], 1.0 / 64, ALU.mult, GN_EPS, ALU.add)
        S.act(v8[:], v8[:], AF.Sqrt)
        S.recip(v8[:], v8[:])
        S.tt("dve", v3(yc[:]), v3(yc[:]), b3(v8), ALU.mult)
        S.tt("pool", yc[:], yc[:], lnxg[:], ALU.mult)
        S.tt("pool", yc[:], yc[:], lnxb[:], ALU.add)
        S.tt("pool", ks[:], kd0[:], kd1[:], ALU.add)
        S.tt("pool", tq[:], r[:], ks[:], ALU.mult)
        S.tt("pool", tq[:], tq[:], rk[:], ALU.mult)
        S.red("dve", b8[:], v3(tq[:]), ALU.add)
        S.tt("dve", v3(bon[:]), v3(v[:]), b3(b8), ALU.mult)
        S.tt("dve", yc[:], yc[:], bon[:], ALU.add)
        S.tt("dve", cat[:, 0:512], yc[:], g[:], ALU.mult)
        ys0, ys1, u = ins[i2]
        S.tt("pool", ysum[:], ys0[:], ys1[:], ALU.add)
        S.tt("pool", tq[:], u[:], s5d[:], ALU.mult)
        S.tt("pool", ysum[:], ysum[:], tq[:], ALU.add)
        S.act(z[:], ysum[:], AF.Gelu)
        S.copy("pool", zb[:], z[:])
        p_ = pt[0]
        for k in range(4):
            S.tr(p_[:, k * 128:(k + 1) * 128], zb[:, k * 128:(k + 1) * 128], idb[:], sig=(k == 3))
        S.copy("dve", zT[:], p_[:, 0:512].re("p (k t) -> p k t", t=128))
        pg = pa[2]
        for k in range(4):
            S.mm(pg[:], zT[:, k, :], gluw[:, k, :], start=(k == 0), stop=(k == 3))
        S.tt("dve", gl[:], pg[:], glub[:], ALU.add)
        S.act(gl[:], gl[:], AF.Sigmoid)
        S.tt("dve", cat[:, 512:1024], z[:], gl[:], ALU.mult)
        p_ = pt[1]
        for k in range(8):
            S.tr(p_[:, k * 128:(k + 1) * 128], cat[:, k * 128:(k + 1) * 128], idb[:], sig=(k == 7))
        S.copy("dve", catT[:], p_[:].re("p (k t) -> p k t", t=128))
        for h in range(2):
            pc = pa[4 + h]
            for k in range(8):
                S.mm(pc[:], catT[:, k, :], outw[:, k, h * 512:(h + 1) * 512], start=(k == 0), stop=(k == 7))
            S.tt("dve", tmp[:, h * 512:(h + 1) * 512], pc[:], gateb[j][:, h * 512:(h + 1) * 512], ALU.mult)
        S.tt("pool", xt[i2][:], xt[i2][:], tmp[:], ALU.add)
        S.dma("pool", SC["xm"][rows, :], xt[i2][:])


def phase3b(S, PS, IN, SC):
    C = alloc_common(S, PS)
    C.idf, C.idb = setup_consts(S, IN["ident"][:])
    modT0 = S.sb([128, 72, 2], F32, "modT0")
    S.dma("sync", modT0[:].re("p t j -> p (t j)"), SC["modT0"][:])
    ngT0 = S.sb([128, 3, 8], F32, "ngT0")
    S.dma("sync", ngT0[:], IN["ngT"][0])
    mv0 = mod_derive(S, modT0, ngT0)
    C.gateb = {}
    for j in range(2):
        C.gateb[(2, j)] = gate_bcast(S, mv0["GATE"][:, 2, :, j], C.idf, C.ones, C.pa[j], 0.5, f"gb2{j}")
    rows = lambda t: (lambda ci: t[ci * 128:(ci + 1) * 128, :])
    for grp in ffn(S, C, rows(SC["xm"]), rows(SC["xl0"]), IN["w1"][0, 1], IN["w2"][0, 1], mv0, 2, GROUPS34, JF34):
        pass
    mv1 = mod_vectors(S, IN["cT"][:], IN["modw"][1], IN["modbT"][1], IN["ngT"][1], C.wst, C.pa[0])
    S.dma("sync", SC["modT1"][:], mv1["modT"][:].re("p t j -> p (t j)"))
    for j in range(2):
        C.gateb[(0, j)] = gate_bcast(S, mv1["GATE"][:, 0, :, j], C.idf, C.ones, C.pa[2 + j], 0.5, f"gb0{j}")
    cos = S.sb([128, NCH, 32], F32, "cos"); sin = S.sb([128, NCH, 32], F32, "sin")
    S.dma("sync", cos[:], IN["rope"][0].re("c p f -> p c f"))
    S.dma("sync", sin[:], IN["rope"][1].re("c p f -> p c f"))
    pst = [S.sb([128, 256], F32, f"pst{i}") for i in range(2)]
    ra = [S.sb([128, 4, 32], F32, f"ra{i}") for i in range(4)]
    psti = [0]

    def after(ci, li, xt):
        j = JF34(ci)
        norm_to_hT(S, C, xt[:], C.hT, li * 128, mv1["G"][:, 1, :, j], mv1["SH"][:, 1, :, j])
    win = IN["win_at"]
    qkv = SC["qkv"]
    for grp in ffn(S, C, rows(SC["xl0"]), rows(SC["x2"]), IN["w1"][1, 0], IN["w2"][1, 0], mv1, 0, GROUPS34, JF34, after_chunk=after):
        for cb in range(6):
            wst = C.wst[cb % 2]; wb = C.w1b[cb % 2]
            S.dma("sync", wst[:, 0:4, :], win[cb, :, 0:4, :])
            S.dma("pool", wst[:, 4:8, :], win[cb, :, 4:8, :])
            S.copy("pool", wb[:], wst[:])
            for li, ci in enumerate(grp):
                pp = C.pa[C.pai % 4]; C.pai += 1
                for k in range(8):
                    S.mm(pp[:, 0:256], C.hT[:, k, li * 128:(li + 1) * 128], wb[:, k, :], start=(k == 0), stop=(k == 7))
                st = pst[psti[0] % 2]; psti[0] += 1
                if cb < 5:
                    pv = pp[:, 0:256].re("p (h two f) -> p h two f", two=2, f=32)
                    sv = st[:].re("p (h two f) -> p h two f", two=2, f=32)
                    cb_ = cos[:, ci, :].re("p (o f) -> p o f", o=1).bc([128, 4, 32])
                    sb_ = sin[:, ci, :].re("p (o f) -> p o f", o=1).bc([128, 4, 32])
                    a, b, c, dd = ra
                    S.tt("dve", a[:], pv[:, :, 0, :], cb_, ALU.mult)
                    S.tt("dve", b[:], pv[:, :, 1, :], sb_, ALU.mult)
                    S.tt("pool", sv[:, :, 0, :], a[:], b[:], ALU.subtract)
                    S.tt("dve", c[:], pv[:, :, 1, :], cb_, ALU.mult)
                    S.tt("dve", dd[:], pv[:, :, 0, :], sb_, ALU.mult)
                    S.tt("pool", sv[:, :, 1, :], c[:], dd[:], ALU.add)
                else:
                    S.copy("act", st[:], pp[:, 0:256])
                S.dma("sync", qkv[ci * 128:(ci + 1) * 128, cb * 256:(cb + 1) * 256], st[:])


def phase4a(S, PS, IN, SC):
    idf, idb = setup_consts(S, IN["ident"][:])
    ones = S.sb([128, 128], F32, "ones"); S.memset("dve", ones[:], 1.0)
    pa = PS["g"][0:4]; pai = [0]
    ptb = PS["b"][0]
    pos = PS["g"][4:6]
    qkv = SC["qkv"]
    modT = S.sb([128, 72, 2], F32, "modT")
    S.dma("sync", modT[:].re("p t j -> p (t j)"), SC["modT1"][:])
    gate5 = gate_bcast(S, modT[:, 5 * 8:6 * 8, 1], idf, ones, pa[0], 1.0, "g5")
    sinkb = S.sb([128, 16], F32, "sinkb"); S.dma("sync", sinkb[:], IN["sinkb"][:])
    mt16 = S.sb([128, 16, 3], F32, "mt16")
    S.copy("dve", mt16[:, :, 2], sinkb[:])
    mstage = S.sb([128, 384], F32, "mstage")
    maskb = S.sb([128, 3, 384], BF16, "maskb")
    for i in range(3):
        S.dma("sync", mstage[:], IN["maskb"][i])
        S.copy("dve", maskb[:, i, :], mstage[:])
    outw = S.sb([128, 8, 1024], BF16, "outw")
    wst2 = [S.sb([128, 1024], F32, f"wst2{i}") for i in range(2)]
    for k in range(8):
        S.dma("sync", wst2[k % 2][:], IN["outw_at"][k * 128:(k + 1) * 128, :])
        S.copy("pool", outw[:, k, :], wst2[k % 2][:])
    NKB = NQB + 2
    kT = S.sb([64, 4, NKB * 128], BF16, "kT"); kcT = S.sb([64, 4, 256], BF16, "kcT")
    vw = S.sb([128, NKB, 256], BF16, "vw"); vc = S.sb([128, 2, 256], BF16, "vc")
    for blk in (0, NKB - 1):
        S.memset("dve", kT[:, :, blk * 128:(blk + 1) * 128], 0.0)
        S.memset("dve", vw[:, blk, :], 0.0)
    kst = [S.sb([128, 512], F32, f"kst{i}") for i in range(2)]; kb = [S.sb([128, 256], BF16, f"kb{i}") for i in range(2)]
    for c in range(NCH):
        S.dma("sync", kst[c % 2][:], qkv[c * 128:(c + 1) * 128, 1024:1536])
        S.copy("pool", kb[c % 2][:], kst[c % 2][:, 0:256])
        for kv in range(4):
            S.tr(ptb[0:64, kv * 128:(kv + 1) * 128], kb[c % 2][:, kv * 64:(kv + 1) * 64], idb[:], sig=(kv == 3))
        blk = c - 1
        dstk = kcT[:, :, c * 128:(c + 1) * 128] if c < 2 else kT[:, :, blk * 128:(blk + 1) * 128]
        S.copy("act", dstk, ptb[0:64, 0:512].re("p (a t) -> p a t", t=128))
        dstv = vc[:, c, :] if c < 2 else vw[:, blk, :]
        S.copy("dve", dstv, kst[c % 2][:, 256:512])
    qst = [S.sb([128, 1024], F32, f"qst{i}") for i in range(2)]
    qb = S.sb([128, 1024], BF16, "qb")
    qT = S.sb([64, 16, 128], BF16, "qT")
    Pm = [S.sb([128, 640], BF16, f"Pm{i}") for i in range(2)]
    PT = [S.sb([128, 5, 128], BF16, f"PT{i}") for i in range(2)]
    rs = [S.sb([128, 4], F32, f"rs{i}") for i in range(2)]
    negm = [S.sb([128, 1], F32, f"negm{i}") for i in range(2)]
    rden = S.sb([128, 16], F32, "rden")
    ob = S.sb([128, 1024], BF16, "ob"); oT = S.sb([128, 8, 128], BF16, "oT")
    xt = [S.sb([128, 1024], F32, f"xt{i}") for i in range(2)]
    tmp = S.sb([128, 1024], F32, "tmp")
    for i in range(NQB):
        rows = slice((i + 2) * 128, (i + 3) * 128)
        S.dma("sync", qst[i % 2][:], qkv[rows, 0:1024])
        S.dma("pool", xt[i % 2][:], SC["x2"][rows, :])
        S.act(qb[:], qst[i % 2][:], AF.Copy, scale=0.125)
        for half in range(2):
            for hh in range(8):
                hd = half * 8 + hh
                S.tr(ptb[0:64, hh * 128:(hh + 1) * 128], qb[:, hd * 64:(hd + 1) * 64], idb[:], sig=(hh == 7))
            S.copy("act", qT[:, half * 8:(half + 1) * 8, :], ptb[0:64, :].re("p (a t) -> p a t", t=128))
        mi = 0 if i == 0 else (2 if i == NQB - 1 else 1)
        def scores(hd):
            kv = hd // 4
            pw = pa[pai[0] % 4]; pai[0] += 1
            pcx = pa[pai[0] % 4]; pai[0] += 1
            S.mm(pw[:, 0:384], qT[:, hd, :], kT[:, kv, i * 128:(i + 3) * 128], start=True, stop=False, sig=False)
            S.mm(pw[:, 0:384], idb[:], maskb[:, mi, :], start=False, stop=True)
            S.mm(pcx[:, 0:256], qT[:, hd, :], kcT[:, kv, :])
            return pw, pcx
        nxt_sc = scores(0)
        for hd in range(16):
            kv = hd // 4
            i2 = hd % 2
            pw, pcx = nxt_sc
            if hd + 1 < 16:
                nxt_sc = scores(hd + 1)
            S.red("dve", mt16[:, hd, 0:1], pw[:, 0:384], ALU.max)
            S.red("dve", mt16[:, hd, 1:2], pcx[:, 0:256], ALU.max)
            S.red("dve", negm[i2][:], mt16[:, hd, :], ALU.max)
            S.ts("dve", negm[i2][:], negm[i2][:], -1.0, ALU.mult)
            S.act(Pm[i2][:, 0:384], pw[:, 0:384], AF.Exp, bias=negm[i2][:, 0:1], accum=rs[i2][:, 0:1])
            S.act(Pm[i2][:, 384:640], pcx[:, 0:256], AF.Exp, bias=negm[i2][:, 0:1], accum=rs[i2][:, 1:2])
            S.act(rs[i2][:, 2:3], sinkb[:, hd:hd + 1], AF.Exp, bias=negm[i2][:, 0:1])
            S.red("dve", rs[i2][:, 3:4], rs[i2][:, 0:3], ALU.add)
            S.recip(rden[:, hd:hd + 1], rs[i2][:, 3:4])
            for j in range(5):
                S.tr(ptb[:, j * 128:(j + 1) * 128], Pm[i2][:, j * 128:(j + 1) * 128], idb[:], sig=(j == 4))
            S.copy("dve" if hd % 2 == 0 else "act", PT[i2][:], ptb[:, 0:640].re("p (a t) -> p a t", t=128))
            po = pos[hd // 8]
            for j in range(5):
                vsrc = vw[:, i + j, kv * 64:(kv + 1) * 64] if j < 3 else vc[:, j - 3, kv * 64:(kv + 1) * 64]
                S.mm(po[:, (hd % 8) * 64:(hd % 8 + 1) * 64], PT[i2][:, j, :], vsrc, start=(j == 0), stop=(j == 4), sig=(j == 4))
        for h2 in range(2):
            S.tt("dve", ob[:, h2 * 512:(h2 + 1) * 512].re("p (h k) -> p h k", k=64), pos[h2][:].re("p (h k) -> p h k", k=64),
                 rden[:, h2 * 8:(h2 + 1) * 8].re("p (h o) -> p h o", o=1).bc([128, 8, 64]), ALU.mult)
        for k in range(8):
            S.tr(ptb[:, k * 128:(k + 1) * 128], ob[:, k * 128:(k + 1) * 128], idb[:], sig=(k == 7))
        S.copy("act", oT[:], ptb[:].re("p (a t) -> p a t", t=128))
        for h in range(2):
            py = pa[pai[0] % 4]; pai[0] += 1
            for k in range(8):
                S.mm(py[:], oT[:, k, :], outw[:, k, h * 512:(h + 1) * 512], start=(k == 0), stop=(k == 7))
            S.tt("dve", tmp[:, h * 512:(h + 1) * 512], py[:], gate5[:, h * 512:(h + 1) * 512], ALU.mult)
        S.tt("pool", xt[i % 2][:], xt[i % 2][:], tmp[:], ALU.add)
        S.dma("pool", SC["x3"][i * 128:(i + 1) * 128, :], xt[i % 2][:])


def phase4b(S, PS, IN, SC, OUT):
    C = alloc_common(S, PS)
    C.idf, C.idb = setup_consts(S, IN["ident"][:])
    modT = S.sb([128, 72, 2], F32, "modT")
    S.dma("sync", modT[:].re("p t j -> p (t j)"), SC["modT1"][:])
    ngT = S.sb([128, 3, 8], F32, "ngT"); S.dma("sync", ngT[:], IN["ngT"][1])
    mv = mod_derive(S, modT, ngT)
    C.gateb = {(2, 1): gate_bcast(S, mv["GATE"][:, 2, :, 1], C.idf, C.ones, C.pa[0], 0.5, "gb21")}
    fing = S.sb([128, 1024], F32, "fing"); S.dma("sync", fing[:], IN["fing"][:])
    ot = [S.sb([128, 1024], F32, f"ot{i}") for i in range(2)]
    oi = [0]

    def after(ci, li, xt):
        ss = C.small[C.si % 4]; C.si += 1
        S.act(C.junk[:], xt[:], AF.Square, accum=ss[:, 0:1])
        S.ts("dve", ss[:, 1:2], ss[:, 0:1], 1.0 / D, ALU.mult, 1e-6, ALU.add)
        S.act(ss[:, 3:4], ss[:, 1:2], AF.Sqrt)
        S.recip(ss[:, 2:3], ss[:, 3:4])
        o = ot[oi[0] % 2]; oi[0] += 1
        S.stt("dve", o[:], xt[:], ss[:, 2:3], fing[:], ALU.mult, ALU.mult)
        S.dma("sync", OUT[ci * 128:(ci + 1) * 128, :], o[:])
    rows = lambda t: (lambda ci: t[ci * 128:(ci + 1) * 128, :])
    for grp in ffn(S, C, rows(SC["x3"]), None, IN["w1"][1, 1], IN["w2"][1, 1], mv, 2, GROUPS32, lambda ci: 1, after_chunk=after):
        pass


IN_SPECS = dict(
    x=[4096, D], ctx=[256, D], cT=[128, 8, 2], modw=[2, 36, 128, 8, 256], modbT=[2, 128, 72], ngT=[2, 128, 3, 8],
    w1=[2, 2, NFT, 128, 8, 256], w2=[2, 2, DFF, D], win_ab=[9, 128, 8, 256], ident=[128, 128],
    mub=[2, 128, 1536], mulb=[2, 128, 256], kkb=[128, 512], kab=[128, 512], w2a=[2, 65, 512], a2a=[2, 65, 512], g2=[128, 512], msk=[6, 128, 128],
    are=[2, 128, NST], aim=[2, 128, NST], lst=[2, 128, NST], bre=[2, 128, NST, 32], bim=[2, 128, NST, 32], cre=[2, 128, NST, 32], cim=[2, 128, NST, 32],
    bcs=[5, 128, 512], gluw=[512, 512], outw_ab=[D, D], win_at=[6, 128, 8, 256], rope=[2, NCH, 128, 32],
    maskb=[3, 128, 384], sinkb=[128, 16], outw_at=[D, D], fing=[128, D])

SC_SPECS = dict(x1=[TOK, D], ppad=[4356, 2304], yd=[2, TOK, 512], kdo=[2, TOK, 512], rvo=[TOK, 1024], gto=[TOK, 512], ys=[2, TOK, 512],
                xm=[TOK, D], xl0=[TOK, D], x2=[TOK, D], qkv=[TOK, 1536], x3=[4096, D], modT0=[128, 144], modT1=[128, 144])


def build_fused(upto=99, debug=(), ses=True):
    nc = bass.Bass("TRN2", target_bir_lowering=False)
    S = Sched(nc, same_engine_sync=ses)
    IN = {k: S.dram(k, v, F32, kind="ExternalInput") for k, v in IN_SPECS.items()}
    SC = {k: S.dram("sc_" + k, v, F32, kind=("ExternalOutput" if k in debug else "Internal")) for k, v in SC_SPECS.items()}
    OUT = S.dram("out", [4096, D], F32, kind="ExternalOutput")
    PS = dict(g=[S.ps([128, 512], F32, f"g{i}") for i in range(6)], b=[S.ps([128, 1024], BF16, f"b{i}") for i in range(2)])
    base = S.mark()
    phases = [lambda: phase1(S, PS, IN, SC), lambda: phase2a(S, PS, IN, SC), lambda: phase2b(S, PS, IN, SC), lambda: phase3a(S, PS, IN, SC),
              lambda: phase3b(S, PS, IN, SC), lambda: phase4a(S, PS, IN, SC), lambda: phase4b(S, PS, IN, SC, OUT)]
    for i, ph in enumerate(phases):
        if i > upto:
            break
        S.reset(base)
        ph()
        S.barrier()
    finals = [OUT] + [SC[k] for k in debug]
    S.finish(finals)
    return nc, S

import numpy as np
LC = 256; NLAT = 4096; L = 4352
GRID_W = 64; ROPE_BASE = 10000.0
def core_tok(seq, h):
    return np.concatenate([seq[h * 128:(h + 1) * 128], seq[256 + h * 2048:256 + (h + 1) * 2048]], 0)
def uncore_tok(parts):
    return np.concatenate([parts[0][:128], parts[1][:128], parts[0][128:], parts[1][128:]], 0)
def colT(v, k=8):
    return np.ascontiguousarray(v.reshape(k, 128).T)
def bc(v):
    return np.ascontiguousarray(np.broadcast_to(v[None, :], (128, v.shape[0])))
def rope_tables(h):
    t = np.arange(h * 2048, (h + 1) * 2048)
    row = (t // GRID_W).astype(np.float32); col = (t % GRID_W).astype(np.float32)
    inv = (ROPE_BASE ** (-np.arange(0, 32, 2, dtype=np.float32) / 32)).astype(np.float32)
    ang = np.concatenate([row[:, None] * inv, col[:, None] * inv], -1).astype(np.float32)
    cos = np.concatenate([np.ones((128, 32), np.float32), np.cos(ang)], 0).reshape(17, 128, 32)
    sin = np.concatenate([np.zeros((128, 32), np.float32), np.sin(ang)], 0).reshape(17, 128, 32)
    return np.stack([cos, sin], 0).astype(np.float32)

import numpy as np
LC = 256

def f_masks():
    m = np.zeros((6, 128, 128), np.float32)
    s = np.arange(128)[:, None]; t = np.arange(128)[None, :]
    same = (s // 64) == (t // 64)
    m[0] = same & (s < t); m[1] = same & (s <= t); m[2] = same & (s > t); m[4] = same & (s >= t)
    m[3] = np.eye(128)
    m[5] = np.eye(128)[::-1]
    return m

def f_rope():
    GRID_W = 64
    t = np.arange(4096)
    row = (t // GRID_W).astype(np.float32); col = (t % GRID_W).astype(np.float32)
    inv = (10000.0 ** (-np.arange(0, 32, 2, dtype=np.float32) / 32)).astype(np.float32)
    ang = np.concatenate([row[:, None] * inv, col[:, None] * inv], -1).astype(np.float32)
    cos = np.concatenate([np.ones((256, 32), np.float32), np.cos(ang)], 0).reshape(34, 128, 32)
    sin = np.concatenate([np.zeros((256, 32), np.float32), np.sin(ang)], 0).reshape(34, 128, 32)
    return np.ascontiguousarray(np.stack([cos, sin], 0).astype(np.float32))

def f_attn_masks():
    qi = np.arange(128)[:, None]; mj = np.arange(384)[None, :] - 128
    valid = np.abs(mj - qi) <= 128
    NEG = -30000.0
    gen = np.where(valid, 0.0, NEG).astype(np.float32)
    left_inv = gen.copy(); left_inv[:, :128] = NEG
    right_inv = gen.copy(); right_inv[:, 256:] = NEG
    return np.ascontiguousarray(np.stack([left_inv, gen, right_inv], 0))

def wblk(w):
    n = w.shape[1] // 256
    return np.ascontiguousarray(w.reshape(8, 128, n, 256).transpose(2, 1, 0, 3))

def w1blk(w):
    a = w.reshape(8, 128, 2, 22, 128).transpose(3, 1, 0, 2, 4)
    return np.ascontiguousarray(a.reshape(22, 128, 8, 256))

def f_shared(d):
    e = 0
    st = lambda a: np.ascontiguousarray(a.reshape(16, 128).T)
    def pad(a):
        out = np.zeros((128, 16, 32), np.float32)
        for g in range(32):
            out[(g % 2) * 64:(g % 2) * 64 + 64, g // 2, (g % 2) * 16:(g % 2) * 16 + 16] = a[g]
        return out
    mu = d['rwkv_mu'][e]
    sh = dict(
        modw=np.stack([wblk(d['mod_w'][l]) for l in range(2)], 0), modbT=np.ascontiguousarray(np.stack([colT(d['mod_b'][l], 72) for l in range(2)], 0)),
        ngT=np.ascontiguousarray(np.stack([np.stack([colT(d['norm_g'][l, i]) for i in range(3)], 1) for l in range(2)], 0)),
        w1=np.stack([np.stack([w1blk(d['ffn_w1'][l, j]) for j in range(2)], 0) for l in range(2)], 0), w2=d['ffn_w2'], win_ab=wblk(d['ab_in_w'][0]), ident=np.eye(128, dtype=np.float32),
        mub=np.ascontiguousarray(np.stack([bc(mu[0, :1536]), bc(mu[1, :1536])], 0)),
        mulb=np.ascontiguousarray(np.stack([bc(mu[0, 1536:1792]), bc(mu[1, 1536:1792])], 0)),
        kkb=bc(d['rwkv_k_k'][e]), kab=bc(d['rwkv_k_a'][e]),
        w2a=np.ascontiguousarray(np.stack([np.concatenate([d['rwkv_w2'][e, dr], d['rwkv_w0'][e, dr][None]], 0) for dr in range(2)], 0)),
        a2a=np.ascontiguousarray(np.stack([np.concatenate([d['rwkv_a2'][e, dr], d['rwkv_a0'][e, dr][None]], 0) for dr in range(2)], 0)),
        g2=d['rwkv_g2'][e], msk=f_masks(),
        are=np.stack([st(d['s5_a_re'][0, dr]) for dr in range(2)], 0), aim=np.stack([st(d['s5_a_im'][0, dr]) for dr in range(2)], 0),
        lst=np.stack([st(np.repeat(d['s5_log_step'][0, dr][:, None], 64, 1)) for dr in range(2)], 0),
        bre=np.stack([pad(d['s5_b_re'][0, dr]) for dr in range(2)], 0), bim=np.stack([pad(d['s5_b_im'][0, dr]) for dr in range(2)], 0),
        cre=np.stack([pad(d['s5_c_re'][0, dr].transpose(0, 2, 1)) for dr in range(2)], 0),
        cim=np.stack([pad(d['s5_c_im'][0, dr].transpose(0, 2, 1)) for dr in range(2)], 0),
        bcs=np.ascontiguousarray(np.stack([bc(d['rwkv_lnx_g'][0]), bc(d['rwkv_lnx_b'][0]), bc(d['rwkv_r_k'][0].reshape(-1)), bc(d['s5_d'][0]),
                                           bc(d['s5_glu_b'][0])], 0)),
        gluw=d['s5_glu_w'][0], outw_ab=d['ab_out_w'][0], win_at=wblk(d['attn_in_w'][0]), rope=f_rope(),
        maskb=f_attn_masks(), sinkb=bc(d['attn_sink'][0]), outw_at=d['attn_out_w'][0], fing=bc(d['final_g']))
    return {k: np.ascontiguousarray(v, dtype=np.float32) for k, v in sh.items()}

def f_core(d, b):
    return dict(x=np.ascontiguousarray(d['x'][b]), ctx=np.ascontiguousarray(d['ctx'][b]),
                cT=np.ascontiguousarray(np.stack([colT(d['c_ctx']), colT(d['c'][b])], -1)))


def kernel(**inputs):
    d = {k: np.ascontiguousarray(np.asarray(v, dtype=np.float32)) for k, v in inputs.items()}
    nc, _ = build_fused()
    sh = f_shared(d)
    in_maps = [dict(sh, **f_core(d, c % 4)) for c in range(8)]
    res = run_bass_kernel_spmd(nc, in_maps, core_ids=list(range(8)))
    out = np.stack([res.results[b]['out'] for b in range(4)], 0)
    return out.astype(np.float32)
```

```python
import numpy as np
import concourse.bass as bass
import concourse.mybir as mybir
from concourse.bass_utils import run_bass_kernel_spmd

F32 = mybir.dt.float32
BF16 = mybir.dt.bfloat16
F32R = mybir.dt.float32r
ALU = mybir.AluOpType
AF = mybir.ActivationFunctionType
AX = mybir.AxisListType


class T:
    def __init__(self, h, name=""):
        self.h = h
        self.name = name
        self.last_w = None
        self.readers = []

    def __getitem__(self, idx):
        return V(self, self.h[idx])

    def re(self, pat, **kw):
        return self[:].re(pat, **kw)


class V:
    def __init__(self, t, ap):
        self.t = t
        self.ap = ap

    def __getitem__(self, idx):
        return V(self.t, self.ap[idx])

    def re(self, pat, **kw):
        return V(self.t, self.ap.rearrange(pat, **kw))

    def bc(self, shape):
        return V(self.t, self.ap.to_broadcast(shape))


def _ap(x):
    return x.ap if isinstance(x, V) else x


def _ts(xs):
    out = []
    for x in xs:
        if isinstance(x, V):
            out.append(x.t)
        elif isinstance(x, T):
            out.append(x)
    return out


class Sched:
    ENG = ["pe", "act", "dve", "pool", "sync"]

    def __init__(self, nc, n_dma_sems=6, same_engine_sync=True):
        self.nc = nc
        self.q = {e: [] for e in self.ENG}
        self.cnt = {e: 0 for e in self.ENG}
        self.unsig = {e: False for e in self.ENG}
        self.sem = {e: nc.alloc_semaphore(f"s_{e}") for e in ["pe", "act", "dve", "pool"]}
        self.waited = {e: {} for e in self.ENG}
        self.same_engine_sync = same_engine_sync
        self.dsem = {}
        self.dcnt = {}
        self.drr = {}
        for qn in ["sync", "pool", "act"]:
            self.dsem[qn] = [nc.alloc_semaphore(f"d_{qn}{i}") for i in range(n_dma_sems)]
            self.dcnt[qn] = [0] * n_dma_sems
            self.drr[qn] = 0
        self.n_inst = 0
        self.uid = 0

    ARENA_LO = 16640
    ARENA_HI = 229344

    def sb(self, shape, dt=F32, name=None):
        self.uid += 1
        name = name or f"t{self.uid}"
        if not hasattr(self, "off"):
            self.off = self.ARENA_LO
        n = 1
        for x in shape[1:]:
            n *= x
        size = n * (2 if dt == BF16 else 4)
        size = (size + 31) // 32 * 32
        assert self.off + size <= self.ARENA_HI, f"SBUF arena overflow allocating {name} {shape}: off={self.off} size={size}"
        t = T(self.nc.alloc_sbuf_tensor_at(f"{name}_{self.uid}", list(shape), dt, offset=self.off), name)
        self.off += size
        return t

    def mark(self):
        if not hasattr(self, "off"):
            self.off = self.ARENA_LO
        return self.off

    def reset(self, mark):
        self.off = mark

    def barrier(self):
        targets = []
        for e in ("pe", "act", "dve", "pool"):
            assert not self.unsig[e], f"barrier with unsignaled op on {e}"
            if self.cnt[e] > 0:
                targets.append((self.sem[e], self.cnt[e]))
        for qn in self.dsem:
            for sm, c in zip(self.dsem[qn], self.dcnt[qn]):
                if c > 0:
                    targets.append((sm, c))
        for e in self.ENG:
            waits = []
            for (sm, val) in targets:
                if e in self.sem and sm is self.sem[e]:
                    continue
                if self.waited[e].get(id(sm), 0) >= val:
                    continue
                self.waited[e][id(sm)] = val
                waits.append((sm, val))
            if waits:
                self.q[e].append((None, waits, None))

    def ps(self, shape, dt=F32, name=None):
        self.uid += 1
        name = name or f"p{self.uid}"
        return T(self.nc.alloc_psum_tensor(f"{name}_{self.uid}", list(shape), dt), name)

    def dram(self, name, shape, dt=F32, kind="Internal"):
        return T(self.nc.dram_tensor(name, list(shape), dt, kind=kind), name)

    def _collect(self, eng, reads, writes):
        toks = []
        for t in _ts(reads):
            if t.last_w is not None:
                toks.append(t.last_w)
        for t in _ts(writes):
            if t.last_w is not None:
                toks.append(t.last_w)
            toks.extend(t.readers)
        best = {}
        for (kind, key, sem, val) in toks:
            if kind == "eng" and key == eng:
                if eng in ("pe", "sync") or not self.same_engine_sync:
                    continue
            k = id(sem)
            if k not in best or best[k][1] < val:
                best[k] = (sem, val)
        waits = []
        for k, (sem, val) in best.items():
            if self.waited[eng].get(k, 0) >= val:
                continue
            self.waited[eng][k] = val
            waits.append((sem, val))
        return waits

    def _mark(self, tok, reads, writes):
        for t in _ts(reads):
            t.readers.append(tok)
        for t in _ts(writes):
            t.last_w = tok
            t.readers = []

    def op(self, eng, fn, reads, writes, sig=True):
        waits = self._collect(eng, reads, writes)
        if sig:
            self.cnt[eng] += 1
            tok = ("eng", eng, self.sem[eng], self.cnt[eng])
            self.unsig[eng] = False
        else:
            tok = ("eng", eng, self.sem[eng], self.cnt[eng] + 1)
            self.unsig[eng] = True
        self.q[eng].append((fn, waits, (self.sem[eng], 1) if sig else None))
        self._mark(tok, reads, writes)
        self.n_inst += 1

    def dma(self, qn, out, in_, extra_reads=(), extra_writes=(), **kw):
        eng = qn
        i = self.drr[qn]
        self.drr[qn] = (i + 1) % len(self.dsem[qn])
        sem = self.dsem[qn][i]
        reads = [in_] + list(extra_reads)
        writes = [out] + list(extra_writes)
        waits = self._collect(eng, reads, writes)
        prev = self.dcnt[qn][i]
        if prev > 0 and self.waited[eng].get(id(sem), 0) < prev:
            self.waited[eng][id(sem)] = prev
            waits.append((sem, prev))
        self.dcnt[qn][i] += 16
        tok = ("dma", qn, sem, self.dcnt[qn][i])
        o, a = _ap(out), _ap(in_)
        self.q[eng].append((lambda e: e.dma_start(out=o, in_=a, **kw), waits, (sem, 16)))
        self._mark(tok, reads, writes)
        self.n_inst += 1
        return tok

    def mm(self, out, lhsT, rhs, start=True, stop=True, sig=None, f32=False):
        if sig is None:
            sig = stop
        o, l, r = _ap(out), _ap(lhsT), _ap(rhs)
        if f32:
            if l.dtype == F32R:
                l = l.bitcast(F32)
            if r.dtype == F32R:
                r = r.bitcast(F32)
        self.op("pe", lambda e: e.matmul(o, l, r, start=start, stop=stop), [lhsT, rhs], [out], sig=sig)

    def tr(self, out, in_, ident, sig=True):
        o, i, d = _ap(out), _ap(in_), _ap(ident)
        self.op("pe", lambda e: e.transpose(o, i, d), [in_, ident], [out], sig=sig)

    def act(self, out, in_, func, bias=None, scale=1.0, accum=None, eng="act"):
        o, i = _ap(out), _ap(in_)
        kw = {}
        reads = [in_]
        writes = [out]
        if bias is not None:
            kw["bias"] = _ap(bias)
            reads.append(bias)
        kw["scale"] = _ap(scale)
        if isinstance(scale, V):
            reads.append(scale)
        if accum is not None:
            kw["accum_out"] = _ap(accum)
            writes.append(accum)
        self.op("act", lambda e: e.activation(o, i, func, **kw), reads, writes)

    def tt(self, eng, out, in0, in1, op):
        o, a, b = _ap(out), _ap(in0), _ap(in1)
        self.op(eng, lambda e: e.tensor_tensor(o, a, b, op), [in0, in1], [out])

    def ts(self, eng, out, in0, s1, op0, s2=None, op1=None, accum=None):
        o, a = _ap(out), _ap(in0)
        reads = [in0] + [s for s in (s1, s2) if isinstance(s, V)]
        writes = [out] + ([accum] if accum is not None else [])
        kw = {}
        if op1 is not None:
            kw["op1"] = op1
        if accum is not None:
            kw["accum_out"] = _ap(accum)
        self.op(eng, lambda e: e.tensor_scalar(o, a, _ap(s1), _ap(s2) if s2 is not None else None, op0, **kw), reads, writes)

    def stt(self, eng, out, in0, scalar, in1, op0, op1):
        o, a, b = _ap(out), _ap(in0), _ap(in1)
        reads = [in0, in1] + ([scalar] if isinstance(scalar, V) else [])
        self.op(eng, lambda e: e.scalar_tensor_tensor(o, a, _ap(scalar), b, op0, op1), reads, [out])

    def red(self, eng, out, in_, op, axis=AX.X):
        o, a = _ap(out), _ap(in_)
        self.op(eng, lambda e: e.tensor_reduce(o, a, axis, op), [in_], [out])

    def copy(self, eng, out, in_):
        o, a = _ap(out), _ap(in_)
        if eng == "act":
            self.op(eng, lambda e: e.copy(o, a), [in_], [out])
        else:
            self.op(eng, lambda e: e.tensor_copy(o, a), [in_], [out])

    def memset(self, eng, out, val):
        o = _ap(out)
        self.op(eng, lambda e: e.memset(o, val), [], [out])

    def scan(self, out, d0, d1, init, op0=ALU.mult, op1=ALU.add):
        o, a, b, i = _ap(out), _ap(d0), _ap(d1), _ap(init)
        reads = [d0, d1] + ([init] if isinstance(init, V) else [])
        self.op("dve", lambda e: e.tensor_tensor_scan(o, a, b, i, op0, op1), reads, [out])

    def recip(self, out, in_):
        o, a = _ap(out), _ap(in_)
        self.op("dve", lambda e: e.reciprocal(o, a), [in_], [out])

    def finish(self, final_tiles):
        nc = self.nc
        toks = []
        for t in final_tiles:
            if t.last_w is not None:
                toks.append(t.last_w)
        fin = []
        best = {}
        for (_, _, sem, val) in toks:
            if id(sem) not in best or best[id(sem)][1] < val:
                best[id(sem)] = (sem, val)
        for qn in self.dsem:
            for s, c in zip(self.dsem[qn], self.dcnt[qn]):
                if c > 0:
                    best[id(s)] = (s, max(c, best.get(id(s), (s, 0))[1]))
        for e in ("pe", "act", "dve", "pool"):
            if self.cnt[e] > 0 or self.unsig[e]:
                assert not self.unsig[e], f"engine {e} ends with unsignaled instruction"
                best[id(self.sem[e])] = (self.sem[e], self.cnt[e])
        fin = list(best.values())
        q = self.q
        with nc.Block() as block:
            def replay(lst):
                def f(e):
                    for (fn, waits, inc) in lst:
                        for (sem, val) in waits:
                            e.wait_ge(sem, val)
                        if fn is None:
                            continue
                        ins = fn(e)
                        if inc is not None:
                            ins.then_inc(inc[0], inc[1])
                return f

            @block.tensor
            def _(e):
                replay(q["pe"])(e)

            @block.scalar
            def _(e):
                replay(q["act"])(e)

            @block.vector
            def _(e):
                replay(q["dve"])(e)

            @block.gpsimd
            def _(e):
                replay(q["pool"])(e)

            @block.sync
            def _(e):
                replay(q["sync"])(e)
                for (sem, val) in fin:
                    e.wait_ge(sem, val)
        return nc

import math

NCH = 34
TOK = NCH * 128
LSEQ = 4352
D = 1024
DFF = 2816
NFT = 22
GN_EPS = 64e-5
NEGC = -math.exp(-0.5)
NST = 16
NQB = 32


def prow(n):
    return n * 128 + (1 if n < 2 else 3)


class Ctx:
    pass


def setup_consts(S, ident_d):
    idf = S.sb([128, 128], F32, "idf")
    idb = S.sb([128, 128], BF16, "idb")
    S.dma("sync", idf[:], ident_d)
    S.copy("dve", idb[:], idf[:])
    return idf, idb


def mod_derive(S, modT, ngT):
    G = S.sb([128, 3, 8, 2], F32, "G")
    SH = S.sb([128, 3, 8, 2], F32, "SH")
    GATE = S.sb([128, 3, 8, 2], F32, "GATE")
    for i in range(3):
        for j in range(2):
            S.stt("dve", G[:, i, :, j], modT[:, (3 * i + 1) * 8:(3 * i + 2) * 8, j], 1.0, ngT[:, i, :], ALU.add, ALU.mult)
            S.copy("dve", SH[:, i, :, j], modT[:, (3 * i) * 8:(3 * i + 1) * 8, j])
            S.copy("dve", GATE[:, i, :, j], modT[:, (3 * i + 2) * 8:(3 * i + 3) * 8, j])
    return dict(G=G, SH=SH, GATE=GATE, modT=modT)


def mod_vectors(S, cT_d, modw_d, modbT_d, ngT_d, wst, pm):
    cT = S.sb([128, 8, 2], F32, "cT")
    sc = S.sb([128, 8, 2], F32, "sc")
    S.dma("sync", cT[:], cT_d)
    S.act(sc[:], cT[:], AF.Silu)
    modbT = S.sb([128, 72], F32, "modbT")
    S.dma("sync", modbT[:], modbT_d)
    ngT = S.sb([128, 3, 8], F32, "ngT")
    S.dma("sync", ngT[:], ngT_d)
    modT = S.sb([128, 72, 2], F32, "modT")
    for nb in range(36):
        w = wst[nb % 2]
        S.dma("sync", w[:, 0:4, :], modw_d[nb, :, 0:4, :])
        S.dma("pool", w[:, 4:8, :], modw_d[nb, :, 4:8, :])
        for ct in range(2):
            t = nb * 2 + ct
            for k in range(8):
                S.mm(pm[:, t * 2:t * 2 + 2], w[:, k, ct * 128:(ct + 1) * 128], sc[:, k, :], start=(k == 0), stop=(k == 7),
                     sig=(k == 7 and t % 2 == 1))
    for j in range(2):
        S.tt("dve", modT[:, :, j], pm[:, 0:144].re("p (t j) -> p t j", j=2)[:, :, j], modbT[:], ALU.add)
    return mod_derive(S, modT, ngT)


def gate_bcast(S, gate_col, idf, ones, ps, scale, name):
    out = S.sb([128, 1024], F32, name)
    dg = S.sb([128, 128], F32, name + "_dg")
    for k in range(8):
        S.ts("dve", dg[:], idf[:], gate_col[:, k:k + 1], ALU.mult)
        S.mm(ps[:, (k % 4) * 128:(k % 4 + 1) * 128], ones[:], dg[:], start=True, stop=True)
        S.ts("dve", out[:, k * 128:(k + 1) * 128], ps[:, (k % 4) * 128:(k % 4 + 1) * 128], float(scale), ALU.mult)
    return out


def norm_to_hT(S, C, xt, hT, col0, G, SH):
    ss = C.small[C.si % 4]; C.si += 1
    S.act(C.junk[:], xt, AF.Square, accum=ss[:, 0:1])
    S.ts("dve", ss[:, 1:2], ss[:, 0:1], 1.0 / D, ALU.mult, 1e-6, ALU.add)
    S.act(ss[:, 3:4], ss[:, 1:2], AF.Sqrt)
    S.recip(ss[:, 2:3], ss[:, 3:4])
    xn = C.xn[C.xi % 2]; C.xi += 1
    S.act(xn[:], xt, AF.Copy, scale=ss[:, 2:3])
    pt = C.pt[C.pti % 2]; C.pti += 1
    for k in range(8):
        S.tr(pt[:, k * 128:(k + 1) * 128], xn[:, k * 128:(k + 1) * 128], C.idb[:], sig=(k == 7))
    for k in range(8):
        S.ts("dve", hT[:, k, col0:col0 + 128], pt[:, k * 128:(k + 1) * 128], G[:, k:k + 1], ALU.mult, SH[:, k:k + 1], ALU.add)


def ffn(S, C, xs, xd, w1_d, w2_d, mv, ni, groups, jf, after_chunk=None):
    G, SH = mv["G"], mv["SH"]
    w2b = C.w2b
    first = True
    for grp in groups:
        nt = len(grp) * 128
        for li, ci in enumerate(grp):
            xt = C.xt[C.xti % 3]; C.xti += 1
            S.dma("sync", xt[:], xs(ci))
            j = jf(ci)
            norm_to_hT(S, C, xt[:], C.hT, li * 128, G[:, ni, :, j], SH[:, ni, :, j])
        for ft in range(NFT):
            wst = C.wst[ft % 2]
            wb = C.w1b[ft % 2]
            S.dma("sync", wst[:, 0:4, :], w1_d[ft, :, 0:4, :])
            S.dma("sync", wst[:, 4:8, :], w1_d[ft, :, 4:8, :])
            S.copy("act", wb[:, 0:4, :], wst[:, 0:4, :]); S.copy("dve", wb[:, 4:8, :], wst[:, 4:8, :])
            if first:
                w2s = C.w2st[ft % 2]
                S.dma("sync", w2s[:], w2_d[ft * 128:(ft + 1) * 128, :])
                S.copy("act", w2b[:, ft, :], w2s[:])
            for b0 in range(0, nt, 512):
                bw = min(512, nt - b0)
                pg = C.pa[C.pai % 4]; C.pai += 1
                pu = C.pa[C.pai % 4]; C.pai += 1
                for k in range(8):
                    S.mm(pg[:, 0:bw], wb[:, k, 0:128], C.hT[:, k, b0:b0 + bw], start=(k == 0), stop=(k == 7))
                for k in range(8):
                    S.mm(pu[:, 0:bw], wb[:, k, 128:256], C.hT[:, k, b0:b0 + bw], start=(k == 0), stop=(k == 7))
                sg = C.sg[C.sgi % 2]; C.sgi += 1
                S.act(sg[:, 0:bw], pg[:, 0:bw], AF.Silu)
                S.tt("dve", C.actT[:, ft, b0:b0 + bw], sg[:, 0:bw], pu[:, 0:bw], ALU.mult)
        first = False
        for li, ci in enumerate(grp):
            xt = C.xt[C.xti % 3]; C.xti += 1
            S.dma("sync", xt[:], xs(ci))
            gb = C.gateb[(ni, jf(ci))]
            for h in range(2):
                pc = C.pcs[h]
                for ft in range(NFT):
                    S.mm(pc[:], C.actT[:, ft, li * 128:(li + 1) * 128], w2b[:, ft, h * 512:(h + 1) * 512],
                         start=(ft == 0), stop=(ft == NFT - 1))
                S.tt("dve", C.tmp[:, h * 512:(h + 1) * 512], pc[:], gb[:, h * 512:(h + 1) * 512], ALU.mult)
            S.tt("pool", xt[:], xt[:], C.tmp[:], ALU.add)
            if xd is not None:
                S.dma("pool", xd(ci), xt[:])
            if after_chunk is not None:
                after_chunk(ci, li, xt)
        yield grp


def alloc_common(S, PS):
    C = Ctx()
    C.small = [S.sb([128, 4], F32, f"small{i}") for i in range(4)]; C.si = 0
    C.junk = S.sb([128, 1024], BF16, "junk")
    C.xn = [S.sb([128, 1024], BF16, f"xn{i}") for i in range(2)]; C.xi = 0
    C.pt = PS["b"]; C.pti = 0
    C.pa = PS["g"][0:4]; C.pai = 0
    C.pcs = PS["g"][4:6]
    C.xt = [S.sb([128, 1024], F32, f"xt{i}") for i in range(3)]; C.xti = 0
    C.hT = S.sb([128, 8, 1152], BF16, "hT")
    C.actT = S.sb([128, NFT, 1152], BF16, "actT")
    C.w2b = S.sb([128, NFT, 1024], BF16, "w2b")
    C.wst = [S.sb([128, 8, 256], F32, f"wst{i}") for i in range(2)]
    C.w1b = [S.sb([128, 8, 256], BF16, f"w1b{i}") for i in range(2)]
    C.w2st = [S.sb([128, 1024], F32, f"w2st{i}") for i in range(2)]
    C.sg = [S.sb([128, 512], F32, f"sg{i}") for i in range(2)]; C.sgi = 0
    C.tmp = S.sb([128, 1024], F32, "tmp")
    C.ones = S.sb([128, 128], F32, "ones")
    S.memset("dve", C.ones[:], 1.0)
    return C


GROUPS34 = [list(range(0, 9)), list(range(9, 18)), list(range(18, 26)), list(range(26, 34))]
GROUPS32 = [list(range(0, 8)), list(range(8, 16)), list(range(16, 24)), list(range(24, 32))]
JF34 = lambda ci: 0 if ci < 2 else 1


def phase1(S, PS, IN, SC):
    C = alloc_common(S, PS)
    C.idf, C.idb = setup_consts(S, IN["ident"][:])
    mv = mod_vectors(S, IN["cT"][:], IN["modw"][0], IN["modbT"][0], IN["ngT"][0], C.wst, C.pa[0])
    S.dma("sync", SC["modT0"][:], mv["modT"][:].re("p t j -> p (t j)"))
    C.gateb = {}
    for j in range(2):
        C.gateb[(0, j)] = gate_bcast(S, mv["GATE"][:, 0, :, j], C.idf, C.ones, C.pa[1 + j], 0.5, f"gb0{j}")
    zt = S.sb([2, 2304], F32, "zt")
    S.memset("dve", zt[:], 0.0)
    ppad = SC["ppad"]
    S.dma("sync", ppad[0:1, :], zt[0:1, :]); S.dma("sync", ppad[257:259, :], zt[0:2, :]); S.dma("sync", ppad[4355:4356, :], zt[0:1, :])
    pst = [S.sb([128, 256], F32, f"pst{i}") for i in range(2)]
    psti = [0]

    def xs(ci):
        return IN["ctx"][ci * 128:(ci + 1) * 128, :] if ci < 2 else IN["x"][(ci - 2) * 128:(ci - 1) * 128, :]

    def xd(ci):
        return SC["x1"][ci * 128:(ci + 1) * 128, :]

    def after(ci, li, xt):
        j = JF34(ci)
        norm_to_hT(S, C, xt[:], C.hT, li * 128, mv["G"][:, 1, :, j], mv["SH"][:, 1, :, j])

    win = IN["win_ab"]
    for grp in ffn(S, C, xs, xd, IN["w1"][0, 0], IN["w2"][0, 0], mv, 0, GROUPS34, JF34, after_chunk=after):
        for cb in range(9):
            wst = C.wst[cb % 2]; wb = C.w1b[cb % 2]
            S.dma("sync", wst[:, 0:4, :], win[cb, :, 0:4, :])
            S.dma("sync", wst[:, 4:8, :], win[cb, :, 4:8, :])
            S.copy("act", wb[:, 0:4, :], wst[:, 0:4, :]); S.copy("dve", wb[:, 4:8, :], wst[:, 4:8, :])
            for li, ci in enumerate(grp):
                pp = C.pa[C.pai % 4]; C.pai += 1
                for k in range(8):
                    S.mm(pp[:, 0:256], C.hT[:, k, li * 128:(li + 1) * 128], wb[:, k, :], start=(k == 0), stop=(k == 7))
                st = pst[psti[0] % 2]; psti[0] += 1
                S.copy("act", st[:], pp[:, 0:256])
                S.dma("sync", ppad[prow(ci):prow(ci) + 128, cb * 256:(cb + 1) * 256], st[:])


def phase2a(S, PS, IN, SC, MD=F32R, nblocks=NCH):
    ppad = SC["ppad"]

    def ld(dv, shape, nm, dt=F32):
        t = S.sb(shape, dt, nm)
        S.dma("sync", t[:], dv)
        return t
    mu0 = ld(IN["mub"][0], [128, 1536], "mu0"); mu1 = ld(IN["mub"][1], [128, 1536], "mu1")
    c0 = S.sb([128, 1536], F32, "c0")
    S.tt("dve", c0[:], mu0[:], mu1[:], ALU.add)
    S.ts("dve", c0[:], c0[:], -1.0, ALU.mult, 1.0, ALU.add)
    kkb = ld(IN["kkb"][:], [128, 512], "kkb"); kab = ld(IN["kab"][:], [128, 512], "kab")
    omka = S.sb([128, 512], F32, "omka")
    S.ts("dve", omka[:], kab[:], -1.0, ALU.mult, 1.0, ALU.add)
    g2 = ld(IN["g2"][:], [128, 512], "g2")
    msk = IN["msk"]
    mUs = ld(msk[0], [128, 128], "mUs"); mUi = ld(msk[1], [128, 128], "mUi"); mLs = ld(msk[2], [128, 128], "mLs")
    idf = ld(msk[3], [128, 128], "idf"); mLi = ld(msk[4], [128, 128], "mLi")
    cm = {}
    for nm, m_ in (("Ui", mUi), ("Us", mUs), ("Ls", mLs), ("Li", mLi)):
        cm[nm] = S.sb([128, 128], MD, "c" + nm)
        S.ts("dve", cm[nm][:], m_[:], NEGC, ALU.mult)
    negc = S.sb([128, 2], MD, "negc")
    S.ts("dve", negc[:], mUi[:, 0:2], 0.0, ALU.mult, NEGC, ALU.add)
    idm = S.sb([128, 128], MD, "idm")
    S.copy("dve", idm[:], idf[:])
    w2a = []; a2a = []
    for dr in range(2):
        t_ = ld(IN["w2a"][dr], [65, 512], f"w2a{dr}"); t2_ = S.sb([65, 512], MD, f"w2ar{dr}"); S.copy("dve", t2_[:], t_[:]); w2a.append(t2_)
        t_ = ld(IN["a2a"][dr], [65, 512], f"a2a{dr}"); t2_ = S.sb([65, 512], MD, f"a2ar{dr}"); S.copy("dve", t2_[:], t_[:]); a2a.append(t2_)
    g2r = S.sb([128, 512], MD, "g2r"); S.copy("dve", g2r[:], g2[:]); g2 = g2r

    pb = PS["g"]
    pbi = [0]

    def bank():
        b = pb[pbi[0] % 6]; pbi[0] += 1
        return b

    def t512(nm, dt=F32):
        return S.sb([128, 512], dt, nm)

    rc = S.sb([128, 1536], F32, "rc"); rp = S.sb([128, 1536], F32, "rp"); rn_ = S.sb([128, 1536], F32, "rn")
    mix = S.sb([128, 1536], MD, "mix"); mt = S.sb([128, 1536], F32, "mt")
    TW = S.sb([65, 128], MD, "TW"); AL = S.sb([65, 128], MD, "AL"); SG = S.sb([128, 128], MD, "SG")
    S.ts("dve", TW[:], mUi[0:65, :], 0.0, ALU.mult, 1.0, ALU.add); S.ts("dve", AL[:], mUi[0:65, :], 0.0, ALU.mult, 1.0, ALU.add)
    lmix = S.sb([128, 256], MD, "lmix"); lmt = S.sb([128, 256], F32, "lmt")
    lc = S.sb([128, 256], F32, "lc"); lp = S.sb([128, 256], F32, "lp"); ln_ = S.sb([128, 256], F32, "ln")
    mul0 = ld(IN["mulb"][0], [128, 256], "mul0"); mul1 = ld(IN["mulb"][1], [128, 256], "mul1")
    c0lb = S.sb([128, 256], F32, "c0lb")
    S.tt("dve", c0lb[:], mul0[:], mul1[:], ALU.add)
    S.ts("dve", c0lb[:], c0lb[:], -1.0, ALU.mult, 1.0, ALU.add)
    sig = t512("sig", MD); a_ = t512("a"); gt = t512("gt")
    kk = t512("kk"); sq = t512("sq"); ss = S.sb([128, 8], F32, "ss"); rn8 = S.sb([128, 8], F32, "rn8")
    kd = t512("kd"); tq = t512("tq"); bq = t512("bq")
    Gc = t512("G"); Gp = t512("Gp"); Gi = t512("Gi"); Ge = t512("Ge")
    A = t512("A", MD); B = t512("B", MD); K = t512("K", MD); Rq = t512("Rq", MD)
    B2m = t512("B2m", MD); K2m = t512("K2m", MD)
    AT = S.sb([64, 8, 128], MD, "AT"); BT = S.sb([64, 8, 128], MD, "BT"); KT = S.sb([64, 8, 128], MD, "KT")
    RT = S.sb([64, 8, 128], MD, "RT")
    mat = lambda nm, dt=MD: [S.sb([128, 4, 128], dt, f"{nm}{g}") for g in range(2)]
    Nm = [mat("Nm0"), mat("Nm1")]; NT = [mat("NT0"), mat("NT1")]
    Mak = mat("Mak"); Mbr = mat("Mbr"); Mkr = mat("Mkr")
    Tf = mat("Tf", MD); Tm = Tf
    WTm = t512("WTm", MD); X1Tm = t512("X1Tm", MD); UlTm = t512("UlTm", MD)
    Rpf = S.sb([64, 8, 128], MD, "Rpf")
    gC = S.sb([64, 16], F32, "gC")
    dgG = S.sb([64, 8, 64], F32, "dgG")
    Pf = [S.sb([64, 8, 64], MD, f"Pf{c}") for c in range(2)]
    QTf = [S.sb([64, 8, 64], F32, f"QTf{c}") for c in range(2)]
    Yloc = t512("Yloc")
    ST = [S.sb([64, 8, 64], MD, f"ST{i}") for i in range(2)]
    yt = [t512(f"yt{i}") for i in range(2)]
    v3 = lambda v: v.re("p (h k) -> p h k", k=64)
    m3 = lambda v: v.re("p (h t) -> p h t", t=128)
    sti = 0
    for dr in range(2):
        if dr == 0:
            m_strict, m_strictT, m_incl = mUs, mLs, mUi
            c_incl, c_strict, c_end = cm["Ui"], cm["Us"], cm["Ls"]
            border = list(range(nblocks)); corder = [0, 1]
        else:
            m_strict, m_strictT, m_incl = mLs, mUs, mLi
            c_incl, c_strict, c_end = cm["Li"], cm["Ls"], cm["Us"]
            border = [1, 0] + list(range(NCH - 1, 1, -1)); corder = [1, 0]
            border = border[:nblocks]
        S.ts("dve", ST[sti % 2][:].re("p h k -> p (h k)"), kkb[0:64, :], 0.0, ALU.mult)
        for n in border:
            r0 = prow(n)
            S.dma("sync", rc[:], ppad[r0:r0 + 128, 0:1536])
            S.dma("pool", rp[:], ppad[r0 - 1:r0 + 127, 0:1536])
            S.dma("sync", rn_[:], ppad[r0 + 1:r0 + 129, 0:1536])
            S.dma("pool", lc[:], ppad[r0:r0 + 128, 1536:1792])
            S.dma("pool", lp[:], ppad[r0 - 1:r0 + 127, 1536:1792])
            S.dma("sync", ln_[:], ppad[r0 + 1:r0 + 129, 1536:1792])
            S.tt("dve", mix[:], rc[:], c0[:], ALU.mult)
            S.tt("pool", mt[:], rp[:], mu0[:], ALU.mult)
            S.tt("dve", mix[:], mix[:], mt[:], ALU.add)
            S.tt("pool", mt[:], rn_[:], mu1[:], ALU.mult)
            S.tt("dve", mix[:], mix[:], mt[:], ALU.add)
            r = mix[:, 0:512]; k = mix[:, 512:1024]; v = mix[:, 1024:1536]
            if dr == 0:
                S.dma("pool", SC["rvo"][n * 128:(n + 1) * 128, 0:512], r)
                S.dma("pool", SC["rvo"][n * 128:(n + 1) * 128, 512:1024], v)
            S.tt("dve", lmix[:], lc[:], c0lb[:], ALU.mult)
            S.tt("pool", lmt[:], lp[:], mul0[:], ALU.mult)
            S.tt("dve", lmix[:], lmix[:], lmt[:], ALU.add)
            S.tt("pool", lmt[:], ln_[:], mul1[:], ALU.mult)
            S.tt("dve", lmix[:], lmix[:], lmt[:], ALU.add)
            pl = bank()
            S.mm(pl[0:64, 0:128], lmix[:, 0:64], idm[:], sig=False, f32=True)
            S.mm(pl[0:64, 128:256], lmix[:, 64:128], idm[:], sig=False, f32=True)
            S.mm(pl[:, 256:384], lmix[:, 128:256], idm[:])
            S.act(TW[0:64, :], pl[0:64, 0:128], AF.Tanh)
            S.copy("act", AL[0:64, :], pl[0:64, 128:256])
            S.act(SG[:], pl[:, 256:384], AF.Sigmoid)
            pw_ = bank(); S.mm(pw_[:], TW[:], w2a[dr][:])
            S.act(sig[:], pw_[:], AF.Sigmoid)
            pa_ = bank(); S.mm(pa_[:], AL[:], a2a[dr][:])
            S.act(a_[:], pa_[:], AF.Sigmoid)
            if dr == 0:
                pg_ = bank(); S.mm(pg_[:], SG[:], g2[:])
                S.copy("act", gt[:], pg_[:])
                S.dma("pool", SC["gto"][n * 128:(n + 1) * 128, :], gt[:])
            S.tt("dve", kk[:], k, kkb[:], ALU.mult)
            S.tt("pool", sq[:], kk[:], kk[:], ALU.mult)
            S.red("dve", ss[:], v3(sq[:]), ALU.add)
            S.ts("dve", ss[:], ss[:], 1e-12, ALU.max)
            S.act(ss[:], ss[:], AF.Sqrt)
            S.recip(rn8[:], ss[:])
            S.tt("dve", v3(kk[:]), v3(kk[:]), rn8[:].re("p (h o) -> p h o", o=1).bc([128, 8, 64]), ALU.mult)
            S.tt("pool", tq[:], a_[:], kab[:], ALU.mult)
            S.tt("pool", tq[:], tq[:], omka[:], ALU.add)
            S.tt("pool", kd[:], k, tq[:], ALU.mult)
            S.dma("pool", SC["kdo"][dr, n * 128:(n + 1) * 128, :], kd[:])
            S.tt("pool", bq[:], kk[:], a_[:], ALU.mult)
            pc1 = bank(); S.mm(pc1[:], c_incl[:], sig[:])
            pc2 = bank(); S.mm(pc2[:], c_strict[:], sig[:])
            pc3 = bank(); S.mm(pc3[:], c_end[:], sig[:])
            S.act(Gc[:], pc1[:], AF.Exp)
            S.act(Gi[:], pc1[:], AF.Exp, scale=-1.0)
            S.act(Gp[:], pc2[:], AF.Exp)
            S.act(Ge[:], pc3[:], AF.Exp)
            S.stt("dve", A[:], kk[:], -1.0, Gp[:], ALU.mult, ALU.mult)
            S.tt("dve", B[:], bq[:], Gi[:], ALU.mult)
            S.tt("pool", K[:], kd[:], Gi[:], ALU.mult)
            S.tt("pool", Rq[:], r, Gc[:], ALU.mult)
            S.tt("pool", B2m[:], bq[:], Ge[:], ALU.mult)
            S.tt("pool", K2m[:], kd[:], Ge[:], ALU.mult)
            Am_ = A
            Vv = v
            for (src, dst) in ((A, AT), (B, BT), (K, KT), (Rq, RT)):
                for hg in range(2):
                    p_ = bank()
                    for hh in range(4):
                        h = hg * 4 + hh
                        S.mm(p_[0:64, hh * 128:(hh + 1) * 128], src[:, h * 64:(h + 1) * 64], idm[:], sig=(hh == 3), f32=True)
                    S.copy("act", dst[:, hg * 4:(hg + 1) * 4, :], m3(p_[0:64, :]))
            RTf_ = RT

            def mmat(dst, LT, RTt, mask, hg):
                p_ = bank()
                for hh in range(4):
                    h = hg * 4 + hh
                    S.mm(p_[:, hh * 128:(hh + 1) * 128], LT[:, h, :], RTt[:, h, :], sig=(hh == 3))
                S.tt("dve", dst[hg][:], m3(p_[:]), mask[:].re("p (o t) -> p o t", o=1).bc([128, 4, 128]), ALU.mult)
            for hg in range(2):
                mmat(Nm[0], BT, AT, m_strict, hg)
                mmat(NT[0], AT, BT, m_strictT, hg)
                mmat(Mak, KT, AT, m_strict, hg)
                mmat(Mbr, BT, RT, m_incl, hg)
                mmat(Mkr, KT, RT, m_incl, hg)
            for hg in range(2):
                S.tt("dve", Tf[hg][:], Nm[0][hg][:], idf[:].re("p (o t) -> p o t", o=1).bc([128, 4, 128]), ALU.add)
            cur = 0
            for lev in range(5):
                nxt = 1 - cur
                last = lev == 4
                for hg in range(2):
                    if not last:
                        p1 = bank()
                        for hh in range(4):
                            S.mm(p1[:, hh * 128:(hh + 1) * 128], NT[cur][hg][:, hh, :], Nm[cur][hg][:, hh, :], sig=(hh == 3))
                        S.copy("act", Nm[nxt][hg][:], m3(p1[:]))
                    p2 = bank()
                    for hh in range(4):
                        S.mm(p2[:, hh * 128:(hh + 1) * 128], Nm[cur][hg][:, hh, :], NT[cur][hg][:, hh, :], sig=(hh == 3))
                    S.copy("dve" if hg == 0 else "act", NT[nxt][hg][:], m3(p2[:]))
                for hg in range(2):
                    p3 = bank()
                    for hh in range(4):
                        S.mm(p3[:, hh * 128:(hh + 1) * 128], NT[nxt][hg][:, hh, :], Tm[hg][:, hh, :], sig=(hh == 3))
                    S.tt("dve", Tf[hg][:], Tf[hg][:], m3(p3[:]), ALU.add)
                cur = nxt
            p_ = bank()
            for h in range(8):
                S.mm(p_[:, h * 64:(h + 1) * 64], Tm[h // 4][:, h % 4, :], Am_[:, h * 64:(h + 1) * 64], sig=(h == 7))
            S.copy("act", WTm[:], p_[:])
            p_ = bank()
            for h in range(8):
                S.mm(p_[:, h * 64:(h + 1) * 64], Mak[h // 4][:, h % 4, :], Vv[:, h * 64:(h + 1) * 64], sig=(h == 7))
            S.copy("dve", X1Tm[:], p_[:])
            p_ = bank()
            for h in range(8):
                S.mm(p_[:, h * 64:(h + 1) * 64], Tm[h // 4][:, h % 4, :], X1Tm[:, h * 64:(h + 1) * 64], sig=(h == 7))
            S.copy("act", UlTm[:], p_[:])
            for hg in range(2):
                p_ = bank()
                for hh in range(4):
                    h = hg * 4 + hh
                    S.mm(p_[0:64, hh * 128:(hh + 1) * 128], WTm[:, h * 64:(h + 1) * 64], Mbr[hg][:, hh, :], sig=(hh == 3), f32=True)
                S.tt("dve", Rpf[:, hg * 4:(hg + 1) * 4, :], m3(p_[0:64, :]), RTf_[:, hg * 4:(hg + 1) * 4, :], ALU.add)
            pgc = [bank(), bank()]
            for c in range(2):
                for h in range(8):
                    S.mm(pgc[c][0:64, h:h + 1], sig[c * 64:(c + 1) * 64, h * 64:(h + 1) * 64], negc[c * 64:(c + 1) * 64, 0:1], sig=(h == 7), f32=True)
                S.act(gC[:, c * 8:(c + 1) * 8], pgc[c][0:64, 0:8], AF.Exp)
            for c in range(2):
                pc = slice(c * 64, (c + 1) * 64)
                p_ = bank()
                for h in range(8):
                    S.mm(p_[0:64, h * 64:(h + 1) * 64], WTm[pc, h * 64:(h + 1) * 64], B2m[pc, h * 64:(h + 1) * 64], sig=(h == 7), f32=True)
                S.tt("pool", dgG[:], idf[0:64, 0:64].re("p (o k) -> p o k", o=1).bc([64, 8, 64]),
                     gC[:, c * 8:(c + 1) * 8].re("p (h o) -> p h o", o=1).bc([64, 8, 64]), ALU.mult)
                S.tt("dve", Pf[c][:], v3(p_[0:64, :]), dgG[:], ALU.add)
                p_ = bank()
                for h in range(8):
                    S.mm(p_[0:64, h * 64:(h + 1) * 64], B2m[pc, h * 64:(h + 1) * 64], UlTm[pc, h * 64:(h + 1) * 64], start=True, stop=False, sig=False, f32=True)
                    S.mm(p_[0:64, h * 64:(h + 1) * 64], K2m[pc, h * 64:(h + 1) * 64], Vv[pc, h * 64:(h + 1) * 64], start=False, stop=True, sig=(h == 7), f32=True)
                S.copy("act", QTf[c][:], v3(p_[0:64, :]))
            p_ = bank()
            for h in range(8):
                S.mm(p_[:, h * 64:(h + 1) * 64], Mbr[h // 4][:, h % 4, :], UlTm[:, h * 64:(h + 1) * 64], start=True, stop=False, sig=False)
                S.mm(p_[:, h * 64:(h + 1) * 64], Mkr[h // 4][:, h % 4, :], Vv[:, h * 64:(h + 1) * 64], start=False, stop=True, sig=(h == 7))
            S.copy("act", Yloc[:], p_[:])
            ytile = yt[n % 2]
            for c in corder:
                pc = slice(c * 64, (c + 1) * 64)
                st_cur = ST[sti % 2]; st_nxt = ST[(sti + 1) % 2]; sti += 1
                p_ = bank()
                for h in range(8):
                    S.mm(p_[:, h * 64:(h + 1) * 64], Rpf[:, h, :], st_cur[:, h, :], sig=(h == 7))
                S.tt("dve", ytile[pc, :], p_[pc, :], Yloc[pc, :], ALU.add)
                p2 = bank()
                for h in range(8):
                    S.mm(p2[0:64, h * 64:(h + 1) * 64], Pf[c][:, h, :], st_cur[:, h, :], sig=(h == 7), f32=True)
                S.tt("dve", st_nxt[:], v3(p2[0:64, :]), QTf[c][:], ALU.add)
            S.dma("sync", SC["yd"][dr, n * 128:(n + 1) * 128, :], ytile[:])


def phase2b(S, PS, IN, SC):
    CH = 128
    ppad = SC["ppad"]
    ident = IN["ident"]
    idf0 = S.sb([128, 128], F32, "idf0"); S.dma("sync", idf0[:], ident[:])
    Jm0 = S.sb([128, 128], F32, "Jm0"); S.dma("sync", Jm0[:], IN["msk"][5])
    idf = S.sb([128, 128], F32R, "idf"); S.copy("dve", idf[:], idf0[:])
    Jm = S.sb([128, 128], F32R, "Jm"); S.copy("dve", Jm[:], Jm0[:])
    pb = PS["g"]
    sm = lambda nm: S.sb([128, NST], F32, nm)
    big = lambda nm, dt=F32: S.sb([128, NST, 32], dt, nm)
    are = sm("are"); aim = sm("aim"); lst = sm("lst")
    bre = big("bre"); bim = big("bim"); cre0 = big("cre0"); cim = big("cim"); ncim = big("ncim", F32R); cre = big("cre", F32R)
    lre = sm("lre"); dt_ = sm("dt"); zr = sm("zr"); th = sm("th"); rho = sm("rho")
    sa = sm("sa"); sk = sm("sk"); sr = sm("sr")
    cs = sm("cs"); sn = sm("sn")
    abre = sm("abre"); abim = sm("abim"); den = sm("den"); rden = sm("rden"); t1 = sm("t1"); t2 = sm("t2")
    fre = sm("fre"); fim = sm("fim"); am1 = sm("am1")
    bbre = big("bbre", F32R); bbim = big("bbim", F32R); u1 = big("u1"); u2 = big("u2")
    BBTre = S.sb([32, NST, 128], F32R, "BBTre"); BBTim = S.sb([32, NST, 128], F32R, "BBTim")
    Ec = S.sb([128, NST, CH], F32, "Ec"); Es = S.sb([128, NST, CH], F32, "Es")
    w1 = S.sb([128, NST, CH // 2], F32, "w1"); w2 = S.sb([128, NST, CH // 2], F32, "w2")
    rhob = S.sb([128, NST, CH], F32, "rhob")
    utok = [S.sb([128, 512], F32, f"utok{i}") for i in range(2)]
    utokr = [S.sb([128, 512], F32R, f"utokr{i}") for i in range(2)]
    ut = [S.sb([32, NST, CH], F32R, f"ut{i}") for i in range(2)]
    Zre_ = [S.sb([128, NST, CH], F32, f"Zre{i}") for i in range(2)]; Zim_ = [S.sb([128, NST, CH], F32, f"Zim{i}") for i in range(2)]
    wre_ = [S.sb([128, NST, CH], F32, "wre0")] * 2; wim_ = [S.sb([128, NST, CH], F32, "wim0")] * 2
    xre = [S.sb([128, NST, CH], F32R, f"xre{i}") for i in range(2)]
    xim = [S.sb([128, NST, CH], F32R, f"xim{i}") for i in range(2)]
    ta_ = [[S.sb([128, 4, CH], F32, f"ta{q}{i}") for i in range(4)] for q in range(2)]
    tb_ = [[S.sb([128, 4, CH], F32, f"tb0{i}") for i in range(4)]] * 2
    pbi = 0
    tai = 0
    yst = [S.sb([128, 512], F32R, f"yst{i}") for i in range(2)]
    yst2 = [S.sb([128, 512], F32, f"ystb{i}") for i in range(2)]
    MAGIC = 12582912.0
    TWO_PI = 2.0 * math.pi
    pbi = 0
    cc = 0
    for dr in range(2):
        for (t_, nm) in ((are, "are"), (aim, "aim"), (lst, "lst")):
            S.dma("sync", t_[:], IN[nm][dr])
        for (t_, nm) in ((bre, "bre"), (bim, "bim"), (cre0, "cre"), (cim, "cim")):
            S.dma("sync", t_[:], IN[nm][dr])
        S.ts("dve", ncim[:], cim[:], -1.0, ALU.mult)
        S.copy("dve", cre[:], cre0[:])
        S.ts("dve", lre[:], are[:], -1e-4, ALU.min)
        S.act(dt_[:], lst[:], AF.Exp)
        S.tt("dve", zr[:], lre[:], dt_[:], ALU.mult)
        S.tt("dve", th[:], aim[:], dt_[:], ALU.mult)
        S.act(rho[:], zr[:], AF.Exp)

        def sin_reduced(out, ang, shift):
            S.ts("dve", sa[:], ang[:], float(shift), ALU.add)
            S.ts("dve", sk[:], sa[:], 1.0 / TWO_PI, ALU.mult, MAGIC, ALU.add)
            S.ts("dve", sk[:], sk[:], MAGIC, ALU.subtract)
            S.stt("dve", sr[:], sk[:], -TWO_PI, sa[:], ALU.mult, ALU.add)
            S.ts("dve", sr[:], sr[:], 3.14159, ALU.min, -3.14159, ALU.max)
            S.act(out, sr[:], AF.Sin)
        sin_reduced(sn[:], th, 0.0)
        sin_reduced(cs[:], th, math.pi / 2)
        S.tt("dve", abre[:], rho[:], cs[:], ALU.mult)
        S.tt("dve", abim[:], rho[:], sn[:], ALU.mult)
        S.tt("dve", t1[:], lre[:], lre[:], ALU.mult)
        S.tt("dve", t2[:], aim[:], aim[:], ALU.mult)
        S.tt("dve", den[:], t1[:], t2[:], ALU.add)
        S.recip(rden[:], den[:])
        S.ts("dve", am1[:], abre[:], -1.0, ALU.add)
        S.tt("dve", t1[:], am1[:], lre[:], ALU.mult)
        S.tt("dve", t2[:], abim[:], aim[:], ALU.mult)
        S.tt("dve", t1[:], t1[:], t2[:], ALU.add)
        S.tt("dve", fre[:], t1[:], rden[:], ALU.mult)
        S.tt("dve", t1[:], abim[:], lre[:], ALU.mult)
        S.tt("dve", t2[:], am1[:], aim[:], ALU.mult)
        S.tt("dve", t1[:], t1[:], t2[:], ALU.subtract)
        S.tt("dve", fim[:], t1[:], rden[:], ALU.mult)
        fre_b = fre[:].re("p (t o) -> p t o", o=1).bc([128, NST, 32])
        fim_b = fim[:].re("p (t o) -> p t o", o=1).bc([128, NST, 32])
        S.tt("dve", u1[:], bre[:], fre_b, ALU.mult)
        S.tt("dve", u2[:], bim[:], fim_b, ALU.mult)
        S.tt("dve", bbre[:], u1[:], u2[:], ALU.subtract)
        S.tt("dve", u1[:], bim[:], fre_b, ALU.mult)
        S.tt("dve", u2[:], bre[:], fim_b, ALU.mult)
        S.tt("dve", bbim[:], u1[:], u2[:], ALU.add)
        for (src, dst) in ((bbre, BBTre), (bbim, BBTim)):
            for g4 in range(4):
                p_ = pb[pbi % 6]; pbi += 1
                for jj in range(4):
                    j = g4 * 4 + jj
                    S.mm(p_[0:32, jj * 128:(jj + 1) * 128], src[:, j, :], idf[:], sig=(jj == 3), f32=True)
                S.copy("dve", dst[:, g4 * 4:(g4 + 1) * 4, :], p_[0:32, :].re("p (a b) -> p a b", b=128))
        S.copy("dve", Ec[:, :, 0], cs[:])
        S.copy("dve", Es[:, :, 0], sn[:])
        m = 1
        while m < CH:
            cb = Ec[:, :, m - 1:m].bc([128, NST, m]); sb_ = Es[:, :, m - 1:m].bc([128, NST, m])
            S.tt("dve", w1[:, :, 0:m], Ec[:, :, 0:m], cb, ALU.mult)
            S.tt("dve", w2[:, :, 0:m], Es[:, :, 0:m], sb_, ALU.mult)
            S.tt("dve", Ec[:, :, m:2 * m], w1[:, :, 0:m], w2[:, :, 0:m], ALU.subtract)
            S.tt("dve", w1[:, :, 0:m], Ec[:, :, 0:m], sb_, ALU.mult)
            S.tt("dve", w2[:, :, 0:m], Es[:, :, 0:m], cb, ALU.mult)
            S.tt("dve", Es[:, :, m:2 * m], w1[:, :, 0:m], w2[:, :, 0:m], ALU.add)
            m *= 2
        S.copy("dve", rhob[:], rho[:].re("p (t o) -> p t o", o=1).bc([128, NST, CH]))
        Pm = idf if dr == 0 else Jm
        border = list(range(NCH)) if dr == 0 else [1, 0] + list(range(NCH - 1, 1, -1))
        def stageA(ci, n, cc):
            nonlocal pbi, tai
            r0 = prow(n)
            utk0 = utok[cc % 2]
            S.dma("sync", utk0[:], ppad[r0:r0 + 128, 1792:2304])
            utk = utokr[cc % 2]
            S.copy("act", utk[:], utk0[:])
            u = ut[cc % 2]
            for g4 in range(4):
                p_ = pb[pbi % 6]; pbi += 1
                for jj in range(4):
                    j = g4 * 4 + jj
                    S.mm(p_[0:32, jj * 128:(jj + 1) * 128], utk[:, j * 32:(j + 1) * 32], Pm[:], sig=(jj == 3), f32=True)
                S.copy("act", u[:, g4 * 4:(g4 + 1) * 4, :], p_[0:32, :].re("p (a b) -> p a b", b=128))
            Zre, Zim = Zre_[cc % 2], Zim_[cc % 2]
            for g4 in range(4):
                ta = ta_[tai % 2]; tai += 1
                pr = pb[pbi % 6]; pbi += 1
                pi_ = pb[pbi % 6]; pbi += 1
                for jj in range(4):
                    j = g4 * 4 + jj
                    S.mm(pr[:, jj * CH:(jj + 1) * CH], BBTre[:, j, :], u[:, j, :], sig=False)
                for jj in range(4):
                    j = g4 * 4 + jj
                    S.mm(pi_[:, jj * CH:(jj + 1) * CH], BBTim[:, j, :], u[:, j, :], sig=(jj == 3))
                sl = slice(g4 * 4, (g4 + 1) * 4)
                prv = pr[:, :].re("p (a b) -> p a b", b=CH); piv = pi_[:, :].re("p (a b) -> p a b", b=CH)
                a0, a1, a2, a3 = ta
                S.tt("dve", a0[:], prv, Ec[:, sl, :], ALU.mult)
                S.tt("dve", a1[:], piv, Es[:, sl, :], ALU.mult)
                S.tt("dve", Zre[:, sl, :], a0[:], a1[:], ALU.add)
                S.tt("dve", a2[:], piv, Ec[:, sl, :], ALU.mult)
                S.tt("dve", a3[:], prv, Es[:, sl, :], ALU.mult)
                S.tt("dve", Zim[:, sl, :], a2[:], a3[:], ALU.subtract)

        def stageSc(ci, cc):
            Zre, Zim = Zre_[cc % 2], Zim_[cc % 2]
            xr_prev, xi_prev = xre[(cc + 1) % 2], xim[(cc + 1) % 2]
            for j in range(NST):
                for (wt, zt, xp) in ((wre_[0], Zre, xr_prev), (wim_[0], Zim, xi_prev)):
                    init = 0.0 if ci == 0 else xp[:, j, CH - 1:CH]
                    S.scan(wt[:, j, :], rhob[:, j, :], zt[:, j, :], init)

        def stageB(ci, n, cc):
            nonlocal pbi
            xr, xi = xre[cc % 2], xim[cc % 2]
            wre, wim = wre_[0], wim_[0]
            tb = tb_[0]
            for g4 in range(4):
                sl = slice(g4 * 4, (g4 + 1) * 4)
                b0, b1, b2, b3 = tb
                S.tt("pool", b0[:], wre[:, sl, :], Ec[:, sl, :], ALU.mult)
                S.tt("pool", b1[:], wim[:, sl, :], Es[:, sl, :], ALU.mult)
                S.tt("dve", xr[:, sl, :], b0[:], b1[:], ALU.subtract)
                S.tt("pool", b2[:], wim[:, sl, :], Ec[:, sl, :], ALU.mult)
                S.tt("pool", b3[:], wre[:, sl, :], Es[:, sl, :], ALU.mult)
                S.tt("dve", xi[:, sl, :], b2[:], b3[:], ALU.add)
            py = pb[pbi % 6]; pbi += 1
            for j in range(NST):
                S.mm(py[:, j * 32:(j + 1) * 32], xr[:, j, :], cre[:, j, :], start=True, stop=False, sig=False)
                S.mm(py[:, j * 32:(j + 1) * 32], xi[:, j, :], ncim[:, j, :], start=False, stop=True, sig=(j == NST - 1))
            ys = yst[cc % 2]
            S.copy("act", ys[:], py[:])
            py2 = pb[pbi % 6]; pbi += 1
            S.mm(py2[:], Pm[:], ys[:])
            ys2 = yst2[cc % 2]
            S.copy("act", ys2[:], py2[:])
            S.dma("sync", SC["ys"][dr, n * 128:(n + 1) * 128, :], ys2[:])

        stageA(0, border[0], cc)
        for ci, n in enumerate(border):
            stageSc(ci, cc)
            if ci + 1 < len(border):
                stageA(ci + 1, border[ci + 1], cc + 1)
            stageB(ci, n, cc)
            cc += 1


def phase3a(S, PS, IN, SC):
    idf, idb = setup_consts(S, IN["ident"][:])
    ones = S.sb([128, 128], F32, "ones"); S.memset("dve", ones[:], 1.0)
    pa = PS["g"]; pt = PS["b"]
    modT = S.sb([128, 72, 2], F32, "modT")
    S.dma("sync", modT[:].re("p t j -> p (t j)"), SC["modT0"][:])
    gateb = [gate_bcast(S, modT[:, 5 * 8:6 * 8, j], idf, ones, pa[j], 1.0, f"g5{j}") for j in range(2)]
    bcs = [S.sb([128, 512], F32, f"bcs{i}") for i in range(5)]
    for i in range(5):
        S.dma("sync", bcs[i][:], IN["bcs"][i])
    lnxg, lnxb, rk, s5d, glub = bcs
    S.ts("dve", rk[:], rk[:], 0.5, ALU.mult)
    wst = S.sb([128, 4, 512], F32, "wst")
    gluw = S.sb([128, 4, 512], BF16, "gluw")
    S.dma("sync", wst[:], IN["gluw"].re("(k p) n -> p k n", p=128))
    S.copy("pool", gluw[:], wst[:])
    outw = S.sb([128, 8, 1024], BF16, "outw")
    wst2 = [S.sb([128, 1024], F32, f"wst2{i}") for i in range(2)]
    for k in range(8):
        S.dma("sync", wst2[k % 2][:], IN["outw_ab"][k * 128:(k + 1) * 128, :])
        S.copy("act", outw[:, k, :], wst2[k % 2][:])
    t5 = lambda nm, dt=F32: S.sb([128, 512], dt, nm)
    inr = [[t5(f"inr{i}{j}") for j in range(7)] for i in range(2)]
    ins = [[t5(f"ins{i}{j}") for j in range(3)] for i in range(2)]
    xt = [S.sb([128, 1024], F32, f"xt{i}") for i in range(2)]
    y = t5("y"); yc = t5("yc"); sq = t5("sq"); ks = t5("ks"); tq = t5("tq"); bon = t5("bon")
    s8 = S.sb([128, 8], F32, "s8"); v8 = S.sb([128, 8], F32, "v8"); b8 = S.sb([128, 8], F32, "b8")
    cat = S.sb([128, 1024], BF16, "cat"); ysum = t5("ysum"); z = t5("z"); zb = t5("zb", BF16)
    zT = S.sb([128, 4, 128], BF16, "zT"); gl = t5("gl"); catT = S.sb([128, 8, 128], BF16, "catT")
    tmp = S.sb([128, 1024], F32, "tmp")
    v3 = lambda v: v.re("p (h k) -> p h k", k=64)
    b3 = lambda t: t[:].re("p (h o) -> p h o", o=1).bc([128, 8, 64])
    for ci in range(NCH):
        j = JF34(ci)
        i2 = ci % 2
        rows = slice(ci * 128, (ci + 1) * 128)
        srcs = [SC["yd"][0, rows, :], SC["yd"][1, rows, :], SC["kdo"][0, rows, :], SC["kdo"][1, rows, :],
                SC["rvo"][rows, 0:512], SC["rvo"][rows, 512:1024], SC["gto"][rows, :]]
        for q in range(7):
            S.dma("sync" if q % 2 == 0 else "pool", inr[i2][q][:], srcs[q])
        srcs2 = [SC["ys"][0, rows, :], SC["ys"][1, rows, :], SC["ppad"][prow(ci):prow(ci) + 128, 1792:2304]]
        for q in range(3):
            S.dma("pool" if q % 2 == 0 else "sync", ins[i2][q][:], srcs2[q])
        S.dma("sync", xt[i2][:], SC["x1"][rows, :])
        y0, y1, kd0, kd1, r, v, g = inr[i2]
        S.tt("dve", y[:], y0[:], y1[:], ALU.add)
        S.red("dve", s8[:], v3(y[:]), ALU.add)
        S.ts("dve", s8[:], s8[:], 1.0 / 64, ALU.mult)
        S.tt("dve", v3(yc[:]), v3(y[:]), b3(s8), ALU.subtract)
        S.tt("pool", sq[:], yc[:], yc[:], ALU.mult)
        S.red("dve", v8[:], v3(sq[:]), ALU.add)
        S.ts("dve", v8[:], v8[:], 1.0 / 64, ALU.mult, GN_EPS, ALU.add)
        S.act(v8[:], v8[:], AF.Sqrt)
        S.recip(v8[:], v8[:])
        S.tt("dve", v3(yc[:]), v3(yc[:]), b3(v8), ALU.mult)
        S.tt("pool", yc[:], yc[:], lnxg[:], ALU.mult)
        S.tt("pool", yc[:], yc[:], lnxb[:], ALU.add)
        S.tt("pool", ks[:], kd0[:], kd1[:], ALU.add)
        S.tt("pool", tq[:], r[:], ks[:], ALU.mult)
        S.tt("pool", tq[:], tq[:], rk[:], ALU.mult)
        S.red("dve", b8[:], v3(tq[:]), ALU.add)
        S.tt("dve", v3(bon[:]), v3(v[:]), b3(b8), ALU.mult)
        S.tt("dve", yc[:], yc[:], bon[:], ALU.add)
        S.tt("dve", cat[:, 0:512], yc[:], g[:], ALU.mult)
        ys0, ys1, u = ins[i2]
        S.tt("pool", ysum[:], ys0[:], ys1[:], ALU.add)
        S.tt("pool", tq[:], u[:], s5d[:], ALU.mult)
        S.tt("pool", ysum[:], ysum[:], tq[:], ALU.add)
        S.act(z[:], ysum[:], AF.Gelu)
        S.copy("pool", zb[:], z[:])
        p_ = pt[0]
        for k in range(4):
            S.tr(p_[:, k * 128:(k + 1) * 128], zb[:, k * 128:(k + 1) * 128], idb[:], sig=(k == 3))
        S.copy("dve", zT[:], p_[:, 0:512].re("p (k t) -> p k t", t=128))
        pg = pa[2]
        for k in range(4):
            S.mm(pg[:], zT[:, k, :], gluw[:, k, :], start=(k == 0), stop=(k == 3))
        S.tt("dve", gl[:], pg[:], glub[:], ALU.add)
        S.act(gl[:], gl[:], AF.Sigmoid)
        S.tt("dve", cat[:, 512:1024], z[:], gl[:], ALU.mult)
        p_ = pt[1]
        for k in range(8):
            S.tr(p_[:, k * 128:(k + 1) * 128], cat[:, k * 128:(k + 1) * 128], idb[:], sig=(k == 7))
        S.copy("dve", catT[:], p_[:].re("p (k t) -> p k t", t=128))
        for h in range(2):
            pc = pa[4 + h]
            for k in range(8):
                S.mm(pc[:], catT[:, k, :], outw[:, k, h * 512:(h + 1) * 512], start=(k == 0), stop=(k == 7))
            S.tt("dve", tmp[:, h * 512:(h + 1) * 512], pc[:], gateb[j][:, h * 512:(h + 1) * 512], ALU.mult)
        S.tt("pool", xt[i2][:], xt[i2][:], tmp[:], ALU.add)
        S.dma("pool", SC["xm"][rows, :], xt[i2][:])


def phase3b(S, PS, IN, SC):
    C = alloc_common(S, PS)
    C.idf, C.idb = setup_consts(S, IN["ident"][:])
    modT0 = S.sb([128, 72, 2], F32, "modT0")
    S.dma("sync", modT0[:].re("p t j -> p (t j)"), SC["modT0"][:])
    ngT0 = S.sb([128, 3, 8], F32, "ngT0")
    S.dma("sync", ngT0[:], IN["ngT"][0])
    mv0 = mod_derive(S, modT0, ngT0)
    C.gateb = {}
    for j in range(2):
        C.gateb[(2, j)] = gate_bcast(S, mv0["GATE"][:, 2, :, j], C.idf, C.ones, C.pa[j], 0.5, f"gb2{j}")
    rows = lambda t: (lambda ci: t[ci * 128:(ci + 1) * 128, :])
    for grp in ffn(S, C, rows(SC["xm"]), rows(SC["xl0"]), IN["w1"][0, 1], IN["w2"][0, 1], mv0, 2, GROUPS34, JF34):
        pass
    mv1 = mod_vectors(S, IN["cT"][:], IN["modw"][1], IN["modbT"][1], IN["ngT"][1], C.wst, C.pa[0])
    S.dma("sync", SC["modT1"][:], mv1["modT"][:].re("p t j -> p (t j)"))
    for j in range(2):
        C.gateb[(0, j)] = gate_bcast(S, mv1["GATE"][:, 0, :, j], C.idf, C.ones, C.pa[2 + j], 0.5, f"gb0{j}")
    cos = S.sb([128, NCH, 32], F32, "cos"); sin = S.sb([128, NCH, 32], F32, "sin")
    S.dma("sync", cos[:], IN["rope"][0].re("c p f -> p c f"))
    S.dma("sync", sin[:], IN["rope"][1].re("c p f -> p c f"))
    pst = [S.sb([128, 256], F32, f"pst{i}") for i in range(2)]
    ra = [S.sb([128, 4, 32], F32, f"ra{i}") for i in range(4)]
    psti = [0]

    def after(ci, li, xt):
        j = JF34(ci)
        norm_to_hT(S, C, xt[:], C.hT, li * 128, mv1["G"][:, 1, :, j], mv1["SH"][:, 1, :, j])
    win = IN["win_at"]
    qkv = SC["qkv"]
    for grp in ffn(S, C, rows(SC["xl0"]), rows(SC["x2"]), IN["w1"][1, 0], IN["w2"][1, 0], mv1, 0, GROUPS34, JF34, after_chunk=after):
        for cb in range(6):
            wst = C.wst[cb % 2]; wb = C.w1b[cb % 2]
            S.dma("sync", wst[:, 0:4, :], win[cb, :, 0:4, :])
            S.dma("sync", wst[:, 4:8, :], win[cb, :, 4:8, :])
            S.copy("act", wb[:, 0:4, :], wst[:, 0:4, :]); S.copy("dve", wb[:, 4:8, :], wst[:, 4:8, :])
            for li, ci in enumerate(grp):
                pp = C.pa[C.pai % 4]; C.pai += 1
                for k in range(8):
                    S.mm(pp[:, 0:256], C.hT[:, k, li * 128:(li + 1) * 128], wb[:, k, :], start=(k == 0), stop=(k == 7))
                st = pst[psti[0] % 2]; psti[0] += 1
                if cb < 5:
                    pv = pp[:, 0:256].re("p (h two f) -> p h two f", two=2, f=32)
                    sv = st[:].re("p (h two f) -> p h two f", two=2, f=32)
                    cb_ = cos[:, ci, :].re("p (o f) -> p o f", o=1).bc([128, 4, 32])
                    sb_ = sin[:, ci, :].re("p (o f) -> p o f", o=1).bc([128, 4, 32])
                    a, b, c, dd = ra
                    S.tt("dve", a[:], pv[:, :, 0, :], cb_, ALU.mult)
                    S.tt("dve", b[:], pv[:, :, 1, :], sb_, ALU.mult)
                    S.tt("pool", sv[:, :, 0, :], a[:], b[:], ALU.subtract)
                    S.tt("dve", c[:], pv[:, :, 1, :], cb_, ALU.mult)
                    S.tt("dve", dd[:], pv[:, :, 0, :], sb_, ALU.mult)
                    S.tt("pool", sv[:, :, 1, :], c[:], dd[:], ALU.add)
                else:
                    S.copy("act", st[:], pp[:, 0:256])
                S.dma("sync", qkv[ci * 128:(ci + 1) * 128, cb * 256:(cb + 1) * 256], st[:])


def phase4a(S, PS, IN, SC):
    idf, idb = setup_consts(S, IN["ident"][:])
    ones = S.sb([128, 128], F32, "ones"); S.memset("dve", ones[:], 1.0)
    pa = PS["g"][0:4]; pai = [0]
    ptb = PS["b"][0]
    pos = PS["g"][4:6]
    qkv = SC["qkv"]
    modT = S.sb([128, 72, 2], F32, "modT")
    S.dma("sync", modT[:].re("p t j -> p (t j)"), SC["modT1"][:])
    gate5 = gate_bcast(S, modT[:, 5 * 8:6 * 8, 1], idf, ones, pa[0], 1.0, "g5")
    sinkb = S.sb([128, 16], F32, "sinkb"); S.dma("sync", sinkb[:], IN["sinkb"][:])
    mt16 = S.sb([128, 16, 3], F32, "mt16")
    S.copy("dve", mt16[:, :, 2], sinkb[:])
    mstage = S.sb([128, 384], F32, "mstage")
    maskb = S.sb([128, 3, 384], BF16, "maskb")
    for i in range(3):
        S.dma("sync", mstage[:], IN["maskb"][i])
        S.copy("dve", maskb[:, i, :], mstage[:])
    outw = S.sb([128, 8, 1024], BF16, "outw")
    wst2 = [S.sb([128, 1024], F32, f"wst2{i}") for i in range(2)]
    for k in range(8):
        S.dma("sync", wst2[k % 2][:], IN["outw_at"][k * 128:(k + 1) * 128, :])
        S.copy("act", outw[:, k, :], wst2[k % 2][:])
    NKB = NQB + 2
    kT = S.sb([64, 4, NKB * 128], BF16, "kT"); kcT = S.sb([64, 4, 256], BF16, "kcT")
    vw = S.sb([128, NKB, 256], BF16, "vw"); vc = S.sb([128, 2, 256], BF16, "vc")
    for blk in (0, NKB - 1):
        S.memset("dve", kT[:, :, blk * 128:(blk + 1) * 128], 0.0)
        S.memset("dve", vw[:, blk, :], 0.0)
    kst = [S.sb([128, 512], F32, f"kst{i}") for i in range(2)]; kb = [S.sb([128, 256], BF16, f"kb{i}") for i in range(2)]
    for c in range(NCH):
        S.dma("sync", kst[c % 2][:], qkv[c * 128:(c + 1) * 128, 1024:1536])
        S.copy("pool", kb[c % 2][:], kst[c % 2][:, 0:256])
        for kv in range(4):
            S.tr(ptb[0:64, kv * 128:(kv + 1) * 128], kb[c % 2][:, kv * 64:(kv + 1) * 64], idb[:], sig=(kv == 3))
        blk = c - 1
        dstk = kcT[:, :, c * 128:(c + 1) * 128] if c < 2 else kT[:, :, blk * 128:(blk + 1) * 128]
        S.copy("act", dstk, ptb[0:64, 0:512].re("p (a t) -> p a t", t=128))
        dstv = vc[:, c, :] if c < 2 else vw[:, blk, :]
        S.copy("dve", dstv, kst[c % 2][:, 256:512])
    qst = [S.sb([128, 1024], F32, f"qst{i}") for i in range(2)]
    qb = S.sb([128, 1024], BF16, "qb")
    qT = S.sb([64, 16, 128], BF16, "qT")
    Pm = [S.sb([128, 640], BF16, f"Pm{i}") for i in range(2)]
    PT = [S.sb([128, 5, 128], BF16, f"PT{i}") for i in range(2)]
    rs = [S.sb([128, 4], F32, f"rs{i}") for i in range(2)]
    negm = [S.sb([128, 1], F32, f"negm{i}") for i in range(2)]
    rden = S.sb([128, 16], F32, "rden")
    ob = S.sb([128, 1024], BF16, "ob"); oT = S.sb([128, 8, 128], BF16, "oT")
    xt = [S.sb([128, 1024], F32, f"xt{i}") for i in range(2)]
    tmp = S.sb([128, 1024], F32, "tmp")
    for i in range(NQB):
        rows = slice((i + 2) * 128, (i + 3) * 128)
        S.dma("sync", qst[i % 2][:], qkv[rows, 0:1024])
        S.dma("pool", xt[i % 2][:], SC["x2"][rows, :])
        S.act(qb[:], qst[i % 2][:], AF.Copy, scale=0.125)
        for half in range(2):
            for hh in range(8):
                hd = half * 8 + hh
                S.tr(ptb[0:64, hh * 128:(hh + 1) * 128], qb[:, hd * 64:(hd + 1) * 64], idb[:], sig=(hh == 7))
            S.copy("act", qT[:, half * 8:(half + 1) * 8, :], ptb[0:64, :].re("p (a t) -> p a t", t=128))
        mi = 0 if i == 0 else (2 if i == NQB - 1 else 1)
        def scores(hd):
            kv = hd // 4
            pw = pa[pai[0] % 4]; pai[0] += 1
            pcx = pa[pai[0] % 4]; pai[0] += 1
            S.mm(pw[:, 0:384], qT[:, hd, :], kT[:, kv, i * 128:(i + 3) * 128], start=True, stop=False, sig=False)
            S.mm(pw[:, 0:384], idb[:], maskb[:, mi, :], start=False, stop=True)
            S.mm(pcx[:, 0:256], qT[:, hd, :], kcT[:, kv, :])
            return pw, pcx
        nxt_sc = scores(0)
        for hd in range(16):
            kv = hd // 4
            i2 = hd % 2
            pw, pcx = nxt_sc
            if hd + 1 < 16:
                nxt_sc = scores(hd + 1)
            S.red("dve", mt16[:, hd, 0:1], pw[:, 0:384], ALU.max)
            S.red("dve", mt16[:, hd, 1:2], pcx[:, 0:256], ALU.max)
            S.red("dve", negm[i2][:], mt16[:, hd, :], ALU.max)
            S.ts("dve", negm[i2][:], negm[i2][:], -1.0, ALU.mult)
            S.act(Pm[i2][:, 0:384], pw[:, 0:384], AF.Exp, bias=negm[i2][:, 0:1], accum=rs[i2][:, 0:1])
            S.act(Pm[i2][:, 384:640], pcx[:, 0:256], AF.Exp, bias=negm[i2][:, 0:1], accum=rs[i2][:, 1:2])
            S.act(rs[i2][:, 2:3], sinkb[:, hd:hd + 1], AF.Exp, bias=negm[i2][:, 0:1])
            S.red("dve", rs[i2][:, 3:4], rs[i2][:, 0:3], ALU.add)
            S.recip(rden[:, hd:hd + 1], rs[i2][:, 3:4])
            for j in range(5):
                S.tr(ptb[:, j * 128:(j + 1) * 128], Pm[i2][:, j * 128:(j + 1) * 128], idb[:], sig=(j == 4))
            S.copy("dve" if hd % 2 == 0 else "act", PT[i2][:], ptb[:, 0:640].re("p (a t) -> p a t", t=128))
            po = pos[hd // 8]
            for j in range(5):
                vsrc = vw[:, i + j, kv * 64:(kv + 1) * 64] if j < 3 else vc[:, j - 3, kv * 64:(kv + 1) * 64]
                S.mm(po[:, (hd % 8) * 64:(hd % 8 + 1) * 64], PT[i2][:, j, :], vsrc, start=(j == 0), stop=(j == 4), sig=(j == 4))
        for h2 in range(2):
            S.tt("dve", ob[:, h2 * 512:(h2 + 1) * 512].re("p (h k) -> p h k", k=64), pos[h2][:].re("p (h k) -> p h k", k=64),
                 rden[:, h2 * 8:(h2 + 1) * 8].re("p (h o) -> p h o", o=1).bc([128, 8, 64]), ALU.mult)
        for k in range(8):
            S.tr(ptb[:, k * 128:(k + 1) * 128], ob[:, k * 128:(k + 1) * 128], idb[:], sig=(k == 7))
        S.copy("act", oT[:], ptb[:].re("p (a t) -> p a t", t=128))
        for h in range(2):
            py = pa[pai[0] % 4]; pai[0] += 1
            for k in range(8):
                S.mm(py[:], oT[:, k, :], outw[:, k, h * 512:(h + 1) * 512], start=(k == 0), stop=(k == 7))
            S.tt("dve", tmp[:, h * 512:(h + 1) * 512], py[:], gate5[:, h * 512:(h + 1) * 512], ALU.mult)
        S.tt("pool", xt[i % 2][:], xt[i % 2][:], tmp[:], ALU.add)
        S.dma("pool", SC["x3"][i * 128:(i + 1) * 128, :], xt[i % 2][:])


def phase4b(S, PS, IN, SC, OUT):
    C = alloc_common(S, PS)
    C.idf, C.idb = setup_consts(S, IN["ident"][:])
    modT = S.sb([128, 72, 2], F32, "modT")
    S.dma("sync", modT[:].re("p t j -> p (t j)"), SC["modT1"][:])
    ngT = S.sb([128, 3, 8], F32, "ngT"); S.dma("sync", ngT[:], IN["ngT"][1])
    mv = mod_derive(S, modT, ngT)
    C.gateb = {(2, 1): gate_bcast(S, mv["GATE"][:, 2, :, 1], C.idf, C.ones, C.pa[0], 0.5, "gb21")}
    fing = S.sb([128, 1024], F32, "fing"); S.dma("sync", fing[:], IN["fing"][:])
    ot = [S.sb([128, 1024], F32, f"ot{i}") for i in range(2)]
    oi = [0]

    def after(ci, li, xt):
        ss = C.small[C.si % 4]; C.si += 1
        S.act(C.junk[:], xt[:], AF.Square, accum=ss[:, 0:1])
        S.ts("dve", ss[:, 1:2], ss[:, 0:1], 1.0 / D, ALU.mult, 1e-6, ALU.add)
        S.act(ss[:, 3:4], ss[:, 1:2], AF.Sqrt)
        S.recip(ss[:, 2:3], ss[:, 3:4])
        o = ot[oi[0] % 2]; oi[0] += 1
        S.stt("dve", o[:], xt[:], ss[:, 2:3], fing[:], ALU.mult, ALU.mult)
        S.dma("sync", OUT[ci * 128:(ci + 1) * 128, :], o[:])
    rows = lambda t: (lambda ci: t[ci * 128:(ci + 1) * 128, :])
    for grp in ffn(S, C, rows(SC["x3"]), None, IN["w1"][1, 1], IN["w2"][1, 1], mv, 2, GROUPS32, lambda ci: 1, after_chunk=after):
        pass


IN_SPECS = dict(
    x=[4096, D], ctx=[256, D], cT=[128, 8, 2], modw=[2, 36, 128, 8, 256], modbT=[2, 128, 72], ngT=[2, 128, 3, 8],
    w1=[2, 2, NFT, 128, 8, 256], w2=[2, 2, DFF, D], win_ab=[9, 128, 8, 256], ident=[128, 128],
    mub=[2, 128, 1536], mulb=[2, 128, 256], kkb=[128, 512], kab=[128, 512], w2a=[2, 65, 512], a2a=[2, 65, 512], g2=[128, 512], msk=[6, 128, 128],
    are=[2, 128, NST], aim=[2, 128, NST], lst=[2, 128, NST], bre=[2, 128, NST, 32], bim=[2, 128, NST, 32], cre=[2, 128, NST, 32], cim=[2, 128, NST, 32],
    bcs=[5, 128, 512], gluw=[512, 512], outw_ab=[D, D], win_at=[6, 128, 8, 256], rope=[2, NCH, 128, 32],
    maskb=[3, 128, 384], sinkb=[128, 16], outw_at=[D, D], fing=[128, D])

SC_SPECS = dict(x1=[TOK, D], ppad=[4356, 2304], yd=[2, TOK, 512], kdo=[2, TOK, 512], rvo=[TOK, 1024], gto=[TOK, 512], ys=[2, TOK, 512],
                xm=[TOK, D], xl0=[TOK, D], x2=[TOK, D], qkv=[TOK, 1536], x3=[4096, D], modT0=[128, 144], modT1=[128, 144])


def build_fused(upto=99, debug=(), ses=True):
    nc = bass.Bass("TRN2", target_bir_lowering=False)
    S = Sched(nc, same_engine_sync=ses)
    IN = {k: S.dram(k, v, F32, kind="ExternalInput") for k, v in IN_SPECS.items()}
    SC = {k: S.dram("sc_" + k, v, F32, kind=("ExternalOutput" if k in debug else "Internal")) for k, v in SC_SPECS.items()}
    OUT = S.dram("out", [4096, D], F32, kind="ExternalOutput")
    PS = dict(g=[S.ps([128, 512], F32, f"g{i}") for i in range(6)], b=[S.ps([128, 1024], BF16, f"b{i}") for i in range(2)])
    base = S.mark()
    phases = [lambda: phase1(S, PS, IN, SC), lambda: phase2a(S, PS, IN, SC), lambda: phase2b(S, PS, IN, SC), lambda: phase3a(S, PS, IN, SC),
              lambda: phase3b(S, PS, IN, SC), lambda: phase4a(S, PS, IN, SC), lambda: phase4b(S, PS, IN, SC, OUT)]
    for i, ph in enumerate(phases):
        if i > upto:
            break
        S.reset(base)
        ph()
        S.barrier()
    finals = [OUT] + [SC[k] for k in debug]
    S.finish(finals)
    return nc, S

import numpy as np
LC = 256; NLAT = 4096; L = 4352
GRID_W = 64; ROPE_BASE = 10000.0
def core_tok(seq, h):
    return np.concatenate([seq[h * 128:(h + 1) * 128], seq[256 + h * 2048:256 + (h + 1) * 2048]], 0)
def uncore_tok(parts):
    return np.concatenate([parts[0][:128], parts[1][:128], parts[0][128:], parts[1][128:]], 0)
def colT(v, k=8):
    return np.ascontiguousarray(v.reshape(k, 128).T)
def bc(v):
    return np.ascontiguousarray(np.broadcast_to(v[None, :], (128, v.shape[0])))
def rope_tables(h):
    t = np.arange(h * 2048, (h + 1) * 2048)
    row = (t // GRID_W).astype(np.float32); col = (t % GRID_W).astype(np.float32)
    inv = (ROPE_BASE ** (-np.arange(0, 32, 2, dtype=np.float32) / 32)).astype(np.float32)
    ang = np.concatenate([row[:, None] * inv, col[:, None] * inv], -1).astype(np.float32)
    cos = np.concatenate([np.ones((128, 32), np.float32), np.cos(ang)], 0).reshape(17, 128, 32)
    sin = np.concatenate([np.zeros((128, 32), np.float32), np.sin(ang)], 0).reshape(17, 128, 32)
    return np.stack([cos, sin], 0).astype(np.float32)

import numpy as np
LC = 256

def f_masks():
    m = np.zeros((6, 128, 128), np.float32)
    s = np.arange(128)[:, None]; t = np.arange(128)[None, :]
    same = (s // 64) == (t // 64)
    m[0] = same & (s < t); m[1] = same & (s <= t); m[2] = same & (s > t); m[4] = same & (s >= t)
    m[3] = np.eye(128)
    m[5] = np.eye(128)[::-1]
    return m

def f_rope():
    GRID_W = 64
    t = np.arange(4096)
    row = (t // GRID_W).astype(np.float32); col = (t % GRID_W).astype(np.float32)
    inv = (10000.0 ** (-np.arange(0, 32, 2, dtype=np.float32) / 32)).astype(np.float32)
    ang = np.concatenate([row[:, None] * inv, col[:, None] * inv], -1).astype(np.float32)
    cos = np.concatenate([np.ones((256, 32), np.float32), np.cos(ang)], 0).reshape(34, 128, 32)
    sin = np.concatenate([np.zeros((256, 32), np.float32), np.sin(ang)], 0).reshape(34, 128, 32)
    return np.ascontiguousarray(np.stack([cos, sin], 0).astype(np.float32))

def f_attn_masks():
    qi = np.arange(128)[:, None]; mj = np.arange(384)[None, :] - 128
    valid = np.abs(mj - qi) <= 128
    NEG = -30000.0
    gen = np.where(valid, 0.0, NEG).astype(np.float32)
    left_inv = gen.copy(); left_inv[:, :128] = NEG
    right_inv = gen.copy(); right_inv[:, 256:] = NEG
    return np.ascontiguousarray(np.stack([left_inv, gen, right_inv], 0))

def wblk(w):
    n = w.shape[1] // 256
    return np.ascontiguousarray(w.reshape(8, 128, n, 256).transpose(2, 1, 0, 3))

def w1blk(w):
    a = w.reshape(8, 128, 2, 22, 128).transpose(3, 1, 0, 2, 4)
    return np.ascontiguousarray(a.reshape(22, 128, 8, 256))

def f_shared(d):
    e = 0
    st = lambda a: np.ascontiguousarray(a.reshape(16, 128).T)
    def pad(a):
        out = np.zeros((128, 16, 32), np.float32)
        for g in range(32):
            out[(g % 2) * 64:(g % 2) * 64 + 64, g // 2, (g % 2) * 16:(g % 2) * 16 + 16] = a[g]
        return out
    mu = d['rwkv_mu'][e]
    sh = dict(
        modw=np.stack([wblk(d['mod_w'][l]) for l in range(2)], 0), modbT=np.ascontiguousarray(np.stack([colT(d['mod_b'][l], 72) for l in range(2)], 0)),
        ngT=np.ascontiguousarray(np.stack([np.stack([colT(d['norm_g'][l, i]) for i in range(3)], 1) for l in range(2)], 0)),
        w1=np.stack([np.stack([w1blk(d['ffn_w1'][l, j]) for j in range(2)], 0) for l in range(2)], 0), w2=d['ffn_w2'], win_ab=wblk(d['ab_in_w'][0]), ident=np.eye(128, dtype=np.float32),
        mub=np.ascontiguousarray(np.stack([bc(mu[0, :1536]), bc(mu[1, :1536])], 0)),
        mulb=np.ascontiguousarray(np.stack([bc(mu[0, 1536:1792]), bc(mu[1, 1536:1792])], 0)),
        kkb=bc(d['rwkv_k_k'][e]), kab=bc(d['rwkv_k_a'][e]),
        w2a=np.ascontiguousarray(np.stack([np.concatenate([d['rwkv_w2'][e, dr], d['rwkv_w0'][e, dr][None]], 0) for dr in range(2)], 0)),
        a2a=np.ascontiguousarray(np.stack([np.concatenate([d['rwkv_a2'][e, dr], d['rwkv_a0'][e, dr][None]], 0) for dr in range(2)], 0)),
        g2=d['rwkv_g2'][e], msk=f_masks(),
        are=np.stack([st(d['s5_a_re'][0, dr]) for dr in range(2)], 0), aim=np.stack([st(d['s5_a_im'][0, dr]) for dr in range(2)], 0),
        lst=np.stack([st(np.repeat(d['s5_log_step'][0, dr][:, None], 64, 1)) for dr in range(2)], 0),
        bre=np.stack([pad(d['s5_b_re'][0, dr]) for dr in range(2)], 0), bim=np.stack([pad(d['s5_b_im'][0, dr]) for dr in range(2)], 0),
        cre=np.stack([pad(d['s5_c_re'][0, dr].transpose(0, 2, 1)) for dr in range(2)], 0),
        cim=np.stack([pad(d['s5_c_im'][0, dr].transpose(0, 2, 1)) for dr in range(2)], 0),
        bcs=np.ascontiguousarray(np.stack([bc(d['rwkv_lnx_g'][0]), bc(d['rwkv_lnx_b'][0]), bc(d['rwkv_r_k'][0].reshape(-1)), bc(d['s5_d'][0]),
                                           bc(d['s5_glu_b'][0])], 0)),
        gluw=d['s5_glu_w'][0], outw_ab=d['ab_out_w'][0], win_at=wblk(d['attn_in_w'][0]), rope=f_rope(),
        maskb=f_attn_masks(), sinkb=bc(d['attn_sink'][0]), outw_at=d['attn_out_w'][0], fing=bc(d['final_g']))
    return {k: np.ascontiguousarray(v, dtype=np.float32) for k, v in sh.items()}

def f_core(d, b):
    return dict(x=np.ascontiguousarray(d['x'][b]), ctx=np.ascontiguousarray(d['ctx'][b]),
                cT=np.ascontiguousarray(np.stack([colT(d['c_ctx']), colT(d['c'][b])], -1)))


def kernel(**inputs):
    d = {k: np.ascontiguousarray(np.asarray(v, dtype=np.float32)) for k, v in inputs.items()}
    nc, _ = build_fused()
    sh = f_shared(d)
    in_maps = [dict(sh, **f_core(d, c % 4)) for c in range(8)]
    res = run_bass_kernel_spmd(nc, in_maps, core_ids=list(range(8)))
    out = np.stack([res.results[b]['out'] for b in range(4)], 0)
    return out.astype(np.float32)
```

```python
import numpy as np
import concourse.bass as bass
import concourse.mybir as mybir
from concourse.bass_utils import run_bass_kernel_spmd

F32 = mybir.dt.float32
BF16 = mybir.dt.bfloat16
F32R = mybir.dt.float32r
ALU = mybir.AluOpType
AF = mybir.ActivationFunctionType
AX = mybir.AxisListType


class T:
    def __init__(self, h, name=""):
        self.h = h
        self.name = name
        self.last_w = None
        self.readers = []

    def __getitem__(self, idx):
        return V(self, self.h[idx])

    def re(self, pat, **kw):
        return self[:].re(pat, **kw)


class V:
    def __init__(self, t, ap):
        self.t = t
        self.ap = ap

    def __getitem__(self, idx):
        return V(self.t, self.ap[idx])

    def re(self, pat, **kw):
        return V(self.t, self.ap.rearrange(pat, **kw))

    def bc(self, shape):
        return V(self.t, self.ap.to_broadcast(shape))


def _ap(x):
    return x.ap if isinstance(x, V) else x


def _ts(xs):
    out = []
    for x in xs:
        if isinstance(x, V):
            out.append(x.t)
        elif isinstance(x, T):
            out.append(x)
    return out


class Sched:
    ENG = ["pe", "act", "dve", "pool", "sync"]

    def __init__(self, nc, n_dma_sems=6, same_engine_sync=True):
        self.nc = nc
        self.q = {e: [] for e in self.ENG}
        self.cnt = {e: 0 for e in self.ENG}
        self.unsig = {e: False for e in self.ENG}
        self.sem = {e: nc.alloc_semaphore(f"s_{e}") for e in ["pe", "act", "dve", "pool"]}
        self.waited = {e: {} for e in self.ENG}
        self.same_engine_sync = same_engine_sync
        self.dsem = {}
        self.dcnt = {}
        self.drr = {}
        for qn in ["sync", "pool", "act"]:
            self.dsem[qn] = [nc.alloc_semaphore(f"d_{qn}{i}") for i in range(n_dma_sems)]
            self.dcnt[qn] = [0] * n_dma_sems
            self.drr[qn] = 0
        self.n_inst = 0
        self.uid = 0

    ARENA_LO = 16640
    ARENA_HI = 229344

    def sb(self, shape, dt=F32, name=None):
        self.uid += 1
        name = name or f"t{self.uid}"
        if not hasattr(self, "off"):
            self.off = self.ARENA_LO
        n = 1
        for x in shape[1:]:
            n *= x
        size = n * (2 if dt == BF16 else 4)
        size = (size + 31) // 32 * 32
        assert self.off + size <= self.ARENA_HI, f"SBUF arena overflow allocating {name} {shape}: off={self.off} size={size}"
        t = T(self.nc.alloc_sbuf_tensor_at(f"{name}_{self.uid}", list(shape), dt, offset=self.off), name)
        self.off += size
        return t

    def mark(self):
        if not hasattr(self, "off"):
            self.off = self.ARENA_LO
        return self.off

    def reset(self, mark):
        self.off = mark

    def barrier(self):
        targets = []
        for e in ("pe", "act", "dve", "pool"):
            assert not self.unsig[e], f"barrier with unsignaled op on {e}"
            if self.cnt[e] > 0:
                targets.append((self.sem[e], self.cnt[e]))
        for qn in self.dsem:
            for sm, c in zip(self.dsem[qn], self.dcnt[qn]):
                if c > 0:
                    targets.append((sm, c))
        for e in self.ENG:
            waits = []
            for (sm, val) in targets:
                if e in self.sem and sm is self.sem[e]:
                    continue
                if self.waited[e].get(id(sm), 0) >= val:
                    continue
                self.waited[e][id(sm)] = val
                waits.append((sm, val))
            if waits:
                self.q[e].append((None, waits, None))

    def ps(self, shape, dt=F32, name=None):
        self.uid += 1
        name = name or f"p{self.uid}"
        return T(self.nc.alloc_psum_tensor(f"{name}_{self.uid}", list(shape), dt), name)

    def dram(self, name, shape, dt=F32, kind="Internal"):
        return T(self.nc.dram_tensor(name, list(shape), dt, kind=kind), name)

    def _collect(self, eng, reads, writes):
        toks = []
        for t in _ts(reads):
            if t.last_w is not None:
                toks.append(t.last_w)
        for t in _ts(writes):
            if t.last_w is not None:
                toks.append(t.last_w)
            toks.extend(t.readers)
        best = {}
        for (kind, key, sem, val) in toks:
            if kind == "eng" and key == eng:
                if eng in ("pe", "sync") or not self.same_engine_sync:
                    continue
            k = id(sem)
            if k not in best or best[k][1] < val:
                best[k] = (sem, val)
        waits = []
        for k, (sem, val) in best.items():
            if self.waited[eng].get(k, 0) >= val:
                continue
            self.waited[eng][k] = val
            waits.append((sem, val))
        return waits

    def _mark(self, tok, reads, writes):
        for t in _ts(reads):
            t.readers.append(tok)
        for t in _ts(writes):
            t.last_w = tok
            t.readers = []

    def op(self, eng, fn, reads, writes, sig=True):
        waits = self._collect(eng, reads, writes)
        if sig:
            self.cnt[eng] += 1
            tok = ("eng", eng, self.sem[eng], self.cnt[eng])
            self.unsig[eng] = False
        else:
            tok = ("eng", eng, self.sem[eng], self.cnt[eng] + 1)
            self.unsig[eng] = True
        self.q[eng].append((fn, waits, (self.sem[eng], 1) if sig else None))
        self._mark(tok, reads, writes)
        self.n_inst += 1

    def dma(self, qn, out, in_, extra_reads=(), extra_writes=(), **kw):
        eng = qn
        i = self.drr[qn]
        self.drr[qn] = (i + 1) % len(self.dsem[qn])
        sem = self.dsem[qn][i]
        reads = [in_] + list(extra_reads)
        writes = [out] + list(extra_writes)
        waits = self._collect(eng, reads, writes)
        prev = self.dcnt[qn][i]
        if prev > 0 and self.waited[eng].get(id(sem), 0) < prev:
            self.waited[eng][id(sem)] = prev
            waits.append((sem, prev))
        self.dcnt[qn][i] += 16
        tok = ("dma", qn, sem, self.dcnt[qn][i])
        o, a = _ap(out), _ap(in_)
        self.q[eng].append((lambda e: e.dma_start(out=o, in_=a, **kw), waits, (sem, 16)))
        self._mark(tok, reads, writes)
        self.n_inst += 1
        return tok

    def mm(self, out, lhsT, rhs, start=True, stop=True, sig=None, f32=False):
        if sig is None:
            sig = stop
        o, l, r = _ap(out), _ap(lhsT), _ap(rhs)
        if f32:
            if l.dtype == F32R:
                l = l.bitcast(F32)
            if r.dtype == F32R:
                r = r.bitcast(F32)
        self.op("pe", lambda e: e.matmul(o, l, r, start=start, stop=stop), [lhsT, rhs], [out], sig=sig)

    def tr(self, out, in_, ident, sig=True):
        o, i, d = _ap(out), _ap(in_), _ap(ident)
        self.op("pe", lambda e: e.transpose(o, i, d), [in_, ident], [out], sig=sig)

    def act(self, out, in_, func, bias=None, scale=1.0, accum=None, eng="act"):
        o, i = _ap(out), _ap(in_)
        kw = {}
        reads = [in_]
        writes = [out]
        if bias is not None:
            kw["bias"] = _ap(bias)
            reads.append(bias)
        kw["scale"] = _ap(scale)
        if isinstance(scale, V):
            reads.append(scale)
        if accum is not None:
            kw["accum_out"] = _ap(accum)
            writes.append(accum)
        self.op("act", lambda e: e.activation(o, i, func, **kw), reads, writes)

    def tt(self, eng, out, in0, in1, op):
        o, a, b = _ap(out), _ap(in0), _ap(in1)
        self.op(eng, lambda e: e.tensor_tensor(o, a, b, op), [in0, in1], [out])

    def ts(self, eng, out, in0, s1, op0, s2=None, op1=None, accum=None):
        o, a = _ap(out), _ap(in0)
        reads = [in0] + [s for s in (s1, s2) if isinstance(s, V)]
        writes = [out] + ([accum] if accum is not None else [])
        kw = {}
        if op1 is not None:
            kw["op1"] = op1
        if accum is not None:
            kw["accum_out"] = _ap(accum)
        self.op(eng, lambda e: e.tensor_scalar(o, a, _ap(s1), _ap(s2) if s2 is not None else None, op0, **kw), reads, writes)

    def stt(self, eng, out, in0, scalar, in1, op0, op1):
        o, a, b = _ap(out), _ap(in0), _ap(in1)
        reads = [in0, in1] + ([scalar] if isinstance(scalar, V) else [])
        self.op(eng, lambda e: e.scalar_tensor_tensor(o, a, _ap(scalar), b, op0, op1), reads, [out])

    def red(self, eng, out, in_, op, axis=AX.X):
        o, a = _ap(out), _ap(in_)
        self.op(eng, lambda e: e.tensor_reduce(o, a, axis, op), [in_], [out])

    def copy(self, eng, out, in_):
        o, a = _ap(out), _ap(in_)
        if eng == "act":
            self.op(eng, lambda e: e.copy(o, a), [in_], [out])
        else:
            self.op(eng, lambda e: e.tensor_copy(o, a), [in_], [out])

    def memset(self, eng, out, val):
        o = _ap(out)
        self.op(eng, lambda e: e.memset(o, val), [], [out])

    def scan(self, out, d0, d1, init, op0=ALU.mult, op1=ALU.add):
        o, a, b, i = _ap(out), _ap(d0), _ap(d1), _ap(init)
        reads = [d0, d1] + ([init] if isinstance(init, V) else [])
        self.op("dve", lambda e: e.tensor_tensor_scan(o, a, b, i, op0, op1), reads, [out])

    def recip(self, out, in_):
        o, a = _ap(out), _ap(in_)
        self.op("dve", lambda e: e.reciprocal(o, a), [in_], [out])

    def finish(self, final_tiles):
        nc = self.nc
        toks = []
        for t in final_tiles:
            if t.last_w is not None:
                toks.append(t.last_w)
        fin = []
        best = {}
        for (_, _, sem, val) in toks:
            if id(sem) not in best or best[id(sem)][1] < val:
                best[id(sem)] = (sem, val)
        for qn in self.dsem:
            for s, c in zip(self.dsem[qn], self.dcnt[qn]):
                if c > 0:
                    best[id(s)] = (s, max(c, best.get(id(s), (s, 0))[1]))
        for e in ("pe", "act", "dve", "pool"):
            if self.cnt[e] > 0 or self.unsig[e]:
                assert not self.unsig[e], f"engine {e} ends with unsignaled instruction"
                best[id(self.sem[e])] = (self.sem[e], self.cnt[e])
        fin = list(best.values())
        q = self.q
        with nc.Block() as block:
            def replay(lst):
                def f(e):
                    for (fn, waits, inc) in lst:
                        for (sem, val) in waits:
                            e.wait_ge(sem, val)
                        if fn is None:
                            continue
                        ins = fn(e)
                        if inc is not None:
                            ins.then_inc(inc[0], inc[1])
                return f

            @block.tensor
            def _(e):
                replay(q["pe"])(e)

            @block.scalar
            def _(e):
                replay(q["act"])(e)

            @block.vector
            def _(e):
                replay(q["dve"])(e)

            @block.gpsimd
            def _(e):
                replay(q["pool"])(e)

            @block.sync
            def _(e):
                replay(q["sync"])(e)
                for (sem, val) in fin:
                    e.wait_ge(sem, val)
        return nc

import math

NCH = 34
TOK = NCH * 128
LSEQ = 4352
D = 1024
DFF = 2816
NFT = 22
GN_EPS = 64e-5
NEGC = -math.exp(-0.5)
NST = 16
NQB = 32


def prow(n):
    return n * 128 + (1 if n < 2 else 3)


class Ctx:
    pass


def setup_consts(S, ident_d):
    idf = S.sb([128, 128], F32, "idf")
    idb = S.sb([128, 128], BF16, "idb")
    S.dma("sync", idf[:], ident_d)
    S.copy("dve", idb[:], idf[:])
    return idf, idb


def mod_derive(S, modT, ngT):
    G = S.sb([128, 3, 8, 2], F32, "G")
    SH = S.sb([128, 3, 8, 2], F32, "SH")
    GATE = S.sb([128, 3, 8, 2], F32, "GATE")
    for i in range(3):
        for j in range(2):
            S.stt("dve", G[:, i, :, j], modT[:, (3 * i + 1) * 8:(3 * i + 2) * 8, j], 1.0, ngT[:, i, :], ALU.add, ALU.mult)
            S.copy("dve", SH[:, i, :, j], modT[:, (3 * i) * 8:(3 * i + 1) * 8, j])
            S.copy("dve", GATE[:, i, :, j], modT[:, (3 * i + 2) * 8:(3 * i + 3) * 8, j])
    return dict(G=G, SH=SH, GATE=GATE, modT=modT)


def mod_vectors(S, cT_d, modw_d, modbT_d, ngT_d, wst, pm):
    cT = S.sb([128, 8, 2], F32, "cT")
    sc = S.sb([128, 8, 2], F32, "sc")
    S.dma("sync", cT[:], cT_d)
    S.act(sc[:], cT[:], AF.Silu)
    modbT = S.sb([128, 72], F32, "modbT")
    S.dma("sync", modbT[:], modbT_d)
    ngT = S.sb([128, 3, 8], F32, "ngT")
    S.dma("sync", ngT[:], ngT_d)
    modT = S.sb([128, 72, 2], F32, "modT")
    for nb in range(36):
        w = wst[nb % 2]
        S.dma("sync", w[:, 0:4, :], modw_d[nb, :, 0:4, :])
        S.dma("pool", w[:, 4:8, :], modw_d[nb, :, 4:8, :])
        for ct in range(2):
            t = nb * 2 + ct
            for k in range(8):
                S.mm(pm[:, t * 2:t * 2 + 2], w[:, k, ct * 128:(ct + 1) * 128], sc[:, k, :], start=(k == 0), stop=(k == 7),
                     sig=(k == 7 and t % 2 == 1))
    for j in range(2):
        S.tt("dve", modT[:, :, j], pm[:, 0:144].re("p (t j) -> p t j", j=2)[:, :, j], modbT[:], ALU.add)
    return mod_derive(S, modT, ngT)


def gate_bcast(S, gate_col, idf, ones, ps, scale, name):
    out = S.sb([128, 1024], F32, name)
    dg = S.sb([128, 128], F32, name + "_dg")
    for k in range(8):
        S.ts("dve", dg[:], idf[:], gate_col[:, k:k + 1], ALU.mult)
        S.mm(ps[:, (k % 4) * 128:(k % 4 + 1) * 128], ones[:], dg[:], start=True, stop=True)
        S.ts("dve", out[:, k * 128:(k + 1) * 128], ps[:, (k % 4) * 128:(k % 4 + 1) * 128], float(scale), ALU.mult)
    return out


def norm_to_hT(S, C, xt, hT, col0, G, SH):
    ss = C.small[C.si % 4]; C.si += 1
    S.act(C.junk[:], xt, AF.Square, accum=ss[:, 0:1])
    S.ts("dve", ss[:, 1:2], ss[:, 0:1], 1.0 / D, ALU.mult, 1e-6, ALU.add)
    S.act(ss[:, 3:4], ss[:, 1:2], AF.Sqrt)
    S.recip(ss[:, 2:3], ss[:, 3:4])
    xn = C.xn[C.xi % 2]; C.xi += 1
    S.act(xn[:], xt, AF.Copy, scale=ss[:, 2:3])
    pt = C.pt[C.pti % 2]; C.pti += 1
    for k in range(8):
        S.tr(pt[:, k * 128:(k + 1) * 128], xn[:, k * 128:(k + 1) * 128], C.idb[:], sig=(k == 7))
    for k in range(8):
        S.ts("dve", hT[:, k, col0:col0 + 128], pt[:, k * 128:(k + 1) * 128], G[:, k:k + 1], ALU.mult, SH[:, k:k + 1], ALU.add)


def ffn(S, C, xs, xd, w1_d, w2_d, mv, ni, groups, jf, after_chunk=None):
    G, SH = mv["G"], mv["SH"]
    w2b = C.w2b
    first = True
    for grp in groups:
        nt = len(grp) * 128
        for li, ci in enumerate(grp):
            xt = C.xt[C.xti % 3]; C.xti += 1
            S.dma("sync", xt[:], xs(ci))
            j = jf(ci)
            norm_to_hT(S, C, xt[:], C.hT, li * 128, G[:, ni, :, j], SH[:, ni, :, j])
        for ft in range(NFT):
            wst = C.wst[ft % 2]
            wb = C.w1b[ft % 2]
            S.dma("sync", wst[:, 0:4, :], w1_d[ft, :, 0:4, :])
            S.dma("sync", wst[:, 4:8, :], w1_d[ft, :, 4:8, :])
            S.copy("act", wb[:, 0:4, :], wst[:, 0:4, :]); S.copy("dve", wb[:, 4:8, :], wst[:, 4:8, :])
            if first:
                w2s = C.w2st[ft % 2]
                S.dma("sync", w2s[:], w2_d[ft * 128:(ft + 1) * 128, :])
                S.copy("act", w2b[:, ft, :], w2s[:])
            for b0 in range(0, nt, 512):
                bw = min(512, nt - b0)
                pg = C.pa[C.pai % 4]; C.pai += 1
                pu = C.pa[C.pai % 4]; C.pai += 1
                for k in range(8):
                    S.mm(pg[:, 0:bw], wb[:, k, 0:128], C.hT[:, k, b0:b0 + bw], start=(k == 0), stop=(k == 7))
                for k in range(8):
                    S.mm(pu[:, 0:bw], wb[:, k, 128:256], C.hT[:, k, b0:b0 + bw], start=(k == 0), stop=(k == 7))
                sg = C.sg[C.sgi % 2]; C.sgi += 1
                S.act(sg[:, 0:bw], pg[:, 0:bw], AF.Silu)
                S.tt("dve", C.actT[:, ft, b0:b0 + bw], sg[:, 0:bw], pu[:, 0:bw], ALU.mult)
        first = False
        for li, ci in enumerate(grp):
            xt = C.xt[C.xti % 3]; C.xti += 1
            S.dma("sync", xt[:], xs(ci))
            gb = C.gateb[(ni, jf(ci))]
            for h in range(2):
                pc = C.pcs[h]
                for ft in range(NFT):
                    S.mm(pc[:], C.actT[:, ft, li * 128:(li + 1) * 128], w2b[:, ft, h * 512:(h + 1) * 512],
                         start=(ft == 0), stop=(ft == NFT - 1))
                S.tt("dve", C.tmp[:, h * 512:(h + 1) * 512], pc[:], gb[:, h * 512:(h + 1) * 512], ALU.mult)
            S.tt("dve", xt[:], xt[:], C.tmp[:], ALU.add)
            if xd is not None:
                S.dma("pool", xd(ci), xt[:])
            if after_chunk is not None:
                after_chunk(ci, li, xt)
        yield grp


def alloc_common(S, PS):
    C = Ctx()
    C.small = [S.sb([128, 4], F32, f"small{i}") for i in range(4)]; C.si = 0
    C.junk = S.sb([128, 1024], BF16, "junk")
    C.xn = [S.sb([128, 1024], BF16, f"xn{i}") for i in range(2)]; C.xi = 0
    C.pt = PS["b"]; C.pti = 0
    C.pa = PS["g"][0:4]; C.pai = 0
    C.pcs = PS["g"][4:6]
    C.xt = [S.sb([128, 1024], F32, f"xt{i}") for i in range(3)]; C.xti = 0
    C.hT = S.sb([128, 8, 1152], BF16, "hT")
    C.actT = S.sb([128, NFT, 1152], BF16, "actT")
    C.w2b = S.sb([128, NFT, 1024], BF16, "w2b")
    C.wst = [S.sb([128, 8, 256], F32, f"wst{i}") for i in range(2)]
    C.w1b = [S.sb([128, 8, 256], BF16, f"w1b{i}") for i in range(2)]
    C.w2st = [S.sb([128, 1024], F32, f"w2st{i}") for i in range(2)]
    C.sg = [S.sb([128, 512], F32, f"sg{i}") for i in range(2)]; C.sgi = 0
    C.tmp = S.sb([128, 1024], F32, "tmp")
    C.ones = S.sb([128, 128], F32, "ones")
    S.memset("dve", C.ones[:], 1.0)
    return C


GROUPS34 = [list(range(0, 9)), list(range(9, 18)), list(range(18, 26)), list(range(26, 34))]
GROUPS32 = [list(range(0, 8)), list(range(8, 16)), list(range(16, 24)), list(range(24, 32))]
JF34 = lambda ci: 0 if ci < 2 else 1


def phase1(S, PS, IN, SC):
    C = alloc_common(S, PS)
    C.idf, C.idb = setup_consts(S, IN["ident"][:])
    mv = mod_vectors(S, IN["cT"][:], IN["modw"][0], IN["modbT"][0], IN["ngT"][0], C.wst, C.pa[0])
    S.dma("sync", SC["modT0"][:], mv["modT"][:].re("p t j -> p (t j)"))
    C.gateb = {}
    for j in range(2):
        C.gateb[(0, j)] = gate_bcast(S, mv["GATE"][:, 0, :, j], C.idf, C.ones, C.pa[1 + j], 0.5, f"gb0{j}")
    zt = S.sb([2, 2304], F32, "zt")
    S.memset("dve", zt[:], 0.0)
    ppad = SC["ppad"]
    S.dma("sync", ppad[0:1, :], zt[0:1, :]); S.dma("sync", ppad[257:259, :], zt[0:2, :]); S.dma("sync", ppad[4355:4356, :], zt[0:1, :])
    pst = [S.sb([128, 256], F32, f"pst{i}") for i in range(2)]
    psti = [0]

    def xs(ci):
        return IN["ctx"][ci * 128:(ci + 1) * 128, :] if ci < 2 else IN["x"][(ci - 2) * 128:(ci - 1) * 128, :]

    def xd(ci):
        return SC["x1"][ci * 128:(ci + 1) * 128, :]

    def after(ci, li, xt):
        j = JF34(ci)
        norm_to_hT(S, C, xt[:], C.hT, li * 128, mv["G"][:, 1, :, j], mv["SH"][:, 1, :, j])

    win = IN["win_ab"]
    for grp in ffn(S, C, xs, xd, IN["w1"][0, 0], IN["w2"][0, 0], mv, 0, GROUPS34, JF34, after_chunk=after):
        for cb in range(9):
            wst = C.wst[cb % 2]; wb = C.w1b[cb % 2]
            S.dma("sync", wst[:, 0:4, :], win[cb, :, 0:4, :])
            S.dma("sync", wst[:, 4:8, :], win[cb, :, 4:8, :])
            S.copy("act", wb[:, 0:4, :], wst[:, 0:4, :]); S.copy("dve", wb[:, 4:8, :], wst[:, 4:8, :])
            for li, ci in enumerate(grp):
                pp = C.pa[C.pai % 4]; C.pai += 1
                for k in range(8):
                    S.mm(pp[:, 0:256], C.hT[:, k, li * 128:(li + 1) * 128], wb[:, k, :], start=(k == 0), stop=(k == 7))
                st = pst[psti[0] % 2]; psti[0] += 1
                S.copy("act", st[:], pp[:, 0:256])
                S.dma("sync", ppad[prow(ci):prow(ci) + 128, cb * 256:(cb + 1) * 256], st[:])


def phase2a(S, PS, IN, SC, MD=F32R, nblocks=NCH):
    ppad = SC["ppad"]

    def ld(dv, shape, nm, dt=F32):
        t = S.sb(shape, dt, nm)
        S.dma("sync", t[:], dv)
        return t
    mu0 = ld(IN["mub"][0], [128, 1536], "mu0"); mu1 = ld(IN["mub"][1], [128, 1536], "mu1")
    c0 = S.sb([128, 1536], F32, "c0")
    S.tt("dve", c0[:], mu0[:], mu1[:], ALU.add)
    S.ts("dve", c0[:], c0[:], -1.0, ALU.mult, 1.0, ALU.add)
    kkb = ld(IN["kkb"][:], [128, 512], "kkb"); kab = ld(IN["kab"][:], [128, 512], "kab")
    omka = S.sb([128, 512], F32, "omka")
    S.ts("dve", omka[:], kab[:], -1.0, ALU.mult, 1.0, ALU.add)
    g2 = ld(IN["g2"][:], [128, 512], "g2")
    msk = IN["msk"]
    mUs = ld(msk[0], [128, 128], "mUs"); mUi = ld(msk[1], [128, 128], "mUi"); mLs = ld(msk[2], [128, 128], "mLs")
    idf = ld(msk[3], [128, 128], "idf"); mLi = ld(msk[4], [128, 128], "mLi")
    cm = {}
    for nm, m_ in (("Ui", mUi), ("Us", mUs), ("Ls", mLs), ("Li", mLi)):
        cm[nm] = S.sb([128, 128], MD, "c" + nm)
        S.ts("dve", cm[nm][:], m_[:], NEGC, ALU.mult)
    negc = S.sb([128, 2], MD, "negc")
    S.ts("dve", negc[:], mUi[:, 0:2], 0.0, ALU.mult, NEGC, ALU.add)
    idm = S.sb([128, 128], MD, "idm")
    S.copy("dve", idm[:], idf[:])
    w2a = []; a2a = []
    for dr in range(2):
        t_ = ld(IN["w2a"][dr], [65, 512], f"w2a{dr}"); t2_ = S.sb([65, 512], MD, f"w2ar{dr}"); S.copy("dve", t2_[:], t_[:]); w2a.append(t2_)
        t_ = ld(IN["a2a"][dr], [65, 512], f"a2a{dr}"); t2_ = S.sb([65, 512], MD, f"a2ar{dr}"); S.copy("dve", t2_[:], t_[:]); a2a.append(t2_)
    g2r = S.sb([128, 512], MD, "g2r"); S.copy("dve", g2r[:], g2[:]); g2 = g2r

    pb = PS["g"]
    pbi = [0]

    def bank():
        b = pb[pbi[0] % 6]; pbi[0] += 1
        return b

    def t512(nm, dt=F32):
        return S.sb([128, 512], dt, nm)

    rc = S.sb([128, 1536], F32, "rc"); rp = S.sb([128, 1536], F32, "rp"); rn_ = S.sb([128, 1536], F32, "rn")
    mix = S.sb([128, 1536], MD, "mix"); mt = S.sb([128, 1536], F32, "mt")
    TW = S.sb([65, 128], MD, "TW"); AL = S.sb([65, 128], MD, "AL"); SG = S.sb([128, 128], MD, "SG")
    S.ts("dve", TW[:], mUi[0:65, :], 0.0, ALU.mult, 1.0, ALU.add); S.ts("dve", AL[:], mUi[0:65, :], 0.0, ALU.mult, 1.0, ALU.add)
    lmix = S.sb([128, 256], MD, "lmix"); lmt = S.sb([128, 256], F32, "lmt")
    lc = S.sb([128, 256], F32, "lc"); lp = S.sb([128, 256], F32, "lp"); ln_ = S.sb([128, 256], F32, "ln")
    mul0 = ld(IN["mulb"][0], [128, 256], "mul0"); mul1 = ld(IN["mulb"][1], [128, 256], "mul1")
    c0lb = S.sb([128, 256], F32, "c0lb")
    S.tt("dve", c0lb[:], mul0[:], mul1[:], ALU.add)
    S.ts("dve", c0lb[:], c0lb[:], -1.0, ALU.mult, 1.0, ALU.add)
    sig = t512("sig", MD); a_ = t512("a"); gt = t512("gt")
    kk = t512("kk"); sq = t512("sq"); ss = S.sb([128, 8], F32, "ss"); rn8 = S.sb([128, 8], F32, "rn8")
    kd = t512("kd"); tq = t512("tq"); bq = t512("bq")
    Gc = t512("G"); Gp = t512("Gp"); Gi = t512("Gi"); Ge = t512("Ge")
    A = t512("A", MD); B = t512("B", MD); K = t512("K", MD); Rq = t512("Rq", MD)
    B2m = t512("B2m", MD); K2m = t512("K2m", MD)
    AT = S.sb([64, 8, 128], MD, "AT"); BT = S.sb([64, 8, 128], MD, "BT"); KT = S.sb([64, 8, 128], MD, "KT")
    RT = S.sb([64, 8, 128], MD, "RT")
    mat = lambda nm, dt=MD: [S.sb([128, 4, 128], dt, f"{nm}{g}") for g in range(2)]
    Nm = [mat("Nm0"), mat("Nm1")]; NT = [mat("NT0"), mat("NT1")]
    Mak = mat("Mak"); Mbr = mat("Mbr"); Mkr = mat("Mkr")
    Tf = mat("Tf", MD); Tm = Tf
    WTm = t512("WTm", MD); X1Tm = t512("X1Tm", MD); UlTm = t512("UlTm", MD)
    Rpf = S.sb([64, 8, 128], MD, "Rpf")
    gC = S.sb([64, 16], F32, "gC")
    dgG = S.sb([64, 8, 64], F32, "dgG")
    Pf = [S.sb([64, 8, 64], MD, f"Pf{c}") for c in range(2)]
    QTf = [S.sb([64, 8, 64], F32, f"QTf{c}") for c in range(2)]
    Yloc = t512("Yloc")
    ST = [S.sb([64, 8, 64], MD, f"ST{i}") for i in range(2)]
    yt = [t512(f"yt{i}") for i in range(2)]
    v3 = lambda v: v.re("p (h k) -> p h k", k=64)
    m3 = lambda v: v.re("p (h t) -> p h t", t=128)
    sti = 0
    for dr in range(2):
        if dr == 0:
            m_strict, m_strictT, m_incl = mUs, mLs, mUi
            c_incl, c_strict, c_end = cm["Ui"], cm["Us"], cm["Ls"]
            border = list(range(nblocks)); corder = [0, 1]
        else:
            m_strict, m_strictT, m_incl = mLs, mUs, mLi
            c_incl, c_strict, c_end = cm["Li"], cm["Ls"], cm["Us"]
            border = [1, 0] + list(range(NCH - 1, 1, -1)); corder = [1, 0]
            border = border[:nblocks]
        S.ts("dve", ST[sti % 2][:].re("p h k -> p (h k)"), kkb[0:64, :], 0.0, ALU.mult)
        for n in border:
            r0 = prow(n)
            S.dma("sync", rc[:], ppad[r0:r0 + 128, 0:1536])
            S.dma("pool", rp[:], ppad[r0 - 1:r0 + 127, 0:1536])
            S.dma("sync", rn_[:], ppad[r0 + 1:r0 + 129, 0:1536])
            S.dma("pool", lc[:], ppad[r0:r0 + 128, 1536:1792])
            S.dma("pool", lp[:], ppad[r0 - 1:r0 + 127, 1536:1792])
            S.dma("sync", ln_[:], ppad[r0 + 1:r0 + 129, 1536:1792])
            S.tt("dve", mix[:], rc[:], c0[:], ALU.mult)
            S.tt("dve", mt[:], rp[:], mu0[:], ALU.mult)
            S.tt("dve", mix[:], mix[:], mt[:], ALU.add)
            S.tt("dve", mt[:], rn_[:], mu1[:], ALU.mult)
            S.tt("dve", mix[:], mix[:], mt[:], ALU.add)
            r = mix[:, 0:512]; k = mix[:, 512:1024]; v = mix[:, 1024:1536]
            if dr == 0:
                S.dma("pool", SC["rvo"][n * 128:(n + 1) * 128, 0:512], r)
                S.dma("pool", SC["rvo"][n * 128:(n + 1) * 128, 512:1024], v)
            S.tt("dve", lmix[:], lc[:], c0lb[:], ALU.mult)
            S.tt("dve", lmt[:], lp[:], mul0[:], ALU.mult)
            S.tt("dve", lmix[:], lmix[:], lmt[:], ALU.add)
            S.tt("dve", lmt[:], ln_[:], mul1[:], ALU.mult)
            S.tt("dve", lmix[:], lmix[:], lmt[:], ALU.add)
            pl = bank()
            S.mm(pl[0:64, 0:128], lmix[:, 0:64], idm[:], sig=False, f32=True)
            S.mm(pl[0:64, 128:256], lmix[:, 64:128], idm[:], sig=False, f32=True)
            S.mm(pl[:, 256:384], lmix[:, 128:256], idm[:])
            S.act(TW[0:64, :], pl[0:64, 0:128], AF.Tanh)
            S.copy("act", AL[0:64, :], pl[0:64, 128:256])
            S.act(SG[:], pl[:, 256:384], AF.Sigmoid)
            pw_ = bank(); S.mm(pw_[:], TW[:], w2a[dr][:])
            S.act(sig[:], pw_[:], AF.Sigmoid)
            pa_ = bank(); S.mm(pa_[:], AL[:], a2a[dr][:])
            S.act(a_[:], pa_[:], AF.Sigmoid)
            if dr == 0:
                pg_ = bank(); S.mm(pg_[:], SG[:], g2[:])
                S.copy("act", gt[:], pg_[:])
                S.dma("pool", SC["gto"][n * 128:(n + 1) * 128, :], gt[:])
            S.tt("dve", kk[:], k, kkb[:], ALU.mult)
            S.tt("dve", sq[:], kk[:], kk[:], ALU.mult)
            S.red("dve", ss[:], v3(sq[:]), ALU.add)
            S.ts("dve", ss[:], ss[:], 1e-12, ALU.max)
            S.act(ss[:], ss[:], AF.Sqrt)
            S.recip(rn8[:], ss[:])
            S.tt("dve", v3(kk[:]), v3(kk[:]), rn8[:].re("p (h o) -> p h o", o=1).bc([128, 8, 64]), ALU.mult)
            S.tt("dve", tq[:], a_[:], kab[:], ALU.mult)
            S.tt("dve", tq[:], tq[:], omka[:], ALU.add)
            S.tt("dve", kd[:], k, tq[:], ALU.mult)
            S.dma("pool", SC["kdo"][dr, n * 128:(n + 1) * 128, :], kd[:])
            S.tt("dve", bq[:], kk[:], a_[:], ALU.mult)
            pc1 = bank(); S.mm(pc1[:], c_incl[:], sig[:])
            pc2 = bank(); S.mm(pc2[:], c_strict[:], sig[:])
            pc3 = bank(); S.mm(pc3[:], c_end[:], sig[:])
            S.act(Gc[:], pc1[:], AF.Exp)
            S.act(Gi[:], pc1[:], AF.Exp, scale=-1.0)
            S.act(Gp[:], pc2[:], AF.Exp)
            S.act(Ge[:], pc3[:], AF.Exp)
            S.stt("dve", A[:], kk[:], -1.0, Gp[:], ALU.mult, ALU.mult)
            S.tt("dve", B[:], bq[:], Gi[:], ALU.mult)
            S.tt("dve", K[:], kd[:], Gi[:], ALU.mult)
            S.tt("dve", Rq[:], r, Gc[:], ALU.mult)
            S.tt("dve", B2m[:], bq[:], Ge[:], ALU.mult)
            S.tt("dve", K2m[:], kd[:], Ge[:], ALU.mult)
            Am_ = A
            Vv = v
            for (src, dst) in ((A, AT), (B, BT), (K, KT), (Rq, RT)):
                for hg in range(2):
                    p_ = bank()
                    for hh in range(4):
                        h = hg * 4 + hh
                        S.mm(p_[0:64, hh * 128:(hh + 1) * 128], src[:, h * 64:(h + 1) * 64], idm[:], sig=(hh == 3), f32=True)
                    S.copy("act", dst[:, hg * 4:(hg + 1) * 4, :], m3(p_[0:64, :]))
            RTf_ = RT

            def mmat(dst, LT, RTt, mask, hg):
                p_ = bank()
                for hh in range(4):
                    h = hg * 4 + hh
                    S.mm(p_[:, hh * 128:(hh + 1) * 128], LT[:, h, :], RTt[:, h, :], sig=(hh == 3))
                S.tt("dve", dst[hg][:], m3(p_[:]), mask[:].re("p (o t) -> p o t", o=1).bc([128, 4, 128]), ALU.mult)
            for hg in range(2):
                mmat(Nm[0], BT, AT, m_strict, hg)
                mmat(NT[0], AT, BT, m_strictT, hg)
                mmat(Mak, KT, AT, m_strict, hg)
                mmat(Mbr, BT, RT, m_incl, hg)
                mmat(Mkr, KT, RT, m_incl, hg)
            for hg in range(2):
                S.tt("dve", Tf[hg][:], Nm[0][hg][:], idf[:].re("p (o t) -> p o t", o=1).bc([128, 4, 128]), ALU.add)
            cur = 0
            for lev in range(5):
                nxt = 1 - cur
                last = lev == 4
                for hg in range(2):
                    if not last:
                        p1 = bank()
                        for hh in range(4):
                            S.mm(p1[:, hh * 128:(hh + 1) * 128], NT[cur][hg][:, hh, :], Nm[cur][hg][:, hh, :], sig=(hh == 3))
                        S.copy("act", Nm[nxt][hg][:], m3(p1[:]))
                    p2 = bank()
                    for hh in range(4):
                        S.mm(p2[:, hh * 128:(hh + 1) * 128], Nm[cur][hg][:, hh, :], NT[cur][hg][:, hh, :], sig=(hh == 3))
                    S.copy("dve" if hg == 0 else "act", NT[nxt][hg][:], m3(p2[:]))
                for hg in range(2):
                    p3 = bank()
                    for hh in range(4):
                        S.mm(p3[:, hh * 128:(hh + 1) * 128], NT[nxt][hg][:, hh, :], Tm[hg][:, hh, :], sig=(hh == 3))
                    S.tt("dve", Tf[hg][:], Tf[hg][:], m3(p3[:]), ALU.add)
                cur = nxt
            p_ = bank()
            for h in range(8):
                S.mm(p_[:, h * 64:(h + 1) * 64], Tm[h // 4][:, h % 4, :], Am_[:, h * 64:(h + 1) * 64], sig=(h == 7))
            S.copy("act", WTm[:], p_[:])
            p_ = bank()
            for h in range(8):
                S.mm(p_[:, h * 64:(h + 1) * 64], Mak[h // 4][:, h % 4, :], Vv[:, h * 64:(h + 1) * 64], sig=(h == 7))
            S.copy("dve", X1Tm[:], p_[:])
            p_ = bank()
            for h in range(8):
                S.mm(p_[:, h * 64:(h + 1) * 64], Tm[h // 4][:, h % 4, :], X1Tm[:, h * 64:(h + 1) * 64], sig=(h == 7))
            S.copy("act", UlTm[:], p_[:])
            for hg in range(2):
                p_ = bank()
                for hh in range(4):
                    h = hg * 4 + hh
                    S.mm(p_[0:64, hh * 128:(hh + 1) * 128], WTm[:, h * 64:(h + 1) * 64], Mbr[hg][:, hh, :], sig=(hh == 3), f32=True)
                S.tt("dve", Rpf[:, hg * 4:(hg + 1) * 4, :], m3(p_[0:64, :]), RTf_[:, hg * 4:(hg + 1) * 4, :], ALU.add)
            pgc = [bank(), bank()]
            for c in range(2):
                for h in range(8):
                    S.mm(pgc[c][0:64, h:h + 1], sig[c * 64:(c + 1) * 64, h * 64:(h + 1) * 64], negc[c * 64:(c + 1) * 64, 0:1], sig=(h == 7), f32=True)
                S.act(gC[:, c * 8:(c + 1) * 8], pgc[c][0:64, 0:8], AF.Exp)
            for c in range(2):
                pc = slice(c * 64, (c + 1) * 64)
                p_ = bank()
                for h in range(8):
                    S.mm(p_[0:64, h * 64:(h + 1) * 64], WTm[pc, h * 64:(h + 1) * 64], B2m[pc, h * 64:(h + 1) * 64], sig=(h == 7), f32=True)
                S.tt("dve", dgG[:], idf[0:64, 0:64].re("p (o k) -> p o k", o=1).bc([64, 8, 64]),
                     gC[:, c * 8:(c + 1) * 8].re("p (h o) -> p h o", o=1).bc([64, 8, 64]), ALU.mult)
                S.tt("dve", Pf[c][:], v3(p_[0:64, :]), dgG[:], ALU.add)
                p_ = bank()
                for h in range(8):
                    S.mm(p_[0:64, h * 64:(h + 1) * 64], B2m[pc, h * 64:(h + 1) * 64], UlTm[pc, h * 64:(h + 1) * 64], start=True, stop=False, sig=False, f32=True)
                    S.mm(p_[0:64, h * 64:(h + 1) * 64], K2m[pc, h * 64:(h + 1) * 64], Vv[pc, h * 64:(h + 1) * 64], start=False, stop=True, sig=(h == 7), f32=True)
                S.copy("act", QTf[c][:], v3(p_[0:64, :]))
            p_ = bank()
            for h in range(8):
                S.mm(p_[:, h * 64:(h + 1) * 64], Mbr[h // 4][:, h % 4, :], UlTm[:, h * 64:(h + 1) * 64], start=True, stop=False, sig=False)
                S.mm(p_[:, h * 64:(h + 1) * 64], Mkr[h // 4][:, h % 4, :], Vv[:, h * 64:(h + 1) * 64], start=False, stop=True, sig=(h == 7))
            S.copy("act", Yloc[:], p_[:])
            ytile = yt[n % 2]
            for c in corder:
                pc = slice(c * 64, (c + 1) * 64)
                st_cur = ST[sti % 2]; st_nxt = ST[(sti + 1) % 2]; sti += 1
                p_ = bank()
                for h in range(8):
                    S.mm(p_[:, h * 64:(h + 1) * 64], Rpf[:, h, :], st_cur[:, h, :], sig=(h == 7))
                S.tt("dve", ytile[pc, :], p_[pc, :], Yloc[pc, :], ALU.add)
                p2 = bank()
                for h in range(8):
                    S.mm(p2[0:64, h * 64:(h + 1) * 64], Pf[c][:, h, :], st_cur[:, h, :], sig=(h == 7), f32=True)
                S.tt("dve", st_nxt[:], v3(p2[0:64, :]), QTf[c][:], ALU.add)
            S.dma("sync", SC["yd"][dr, n * 128:(n + 1) * 128, :], ytile[:])


def phase2b(S, PS, IN, SC):
    CH = 128
    ppad = SC["ppad"]
    ident = IN["ident"]
    idf0 = S.sb([128, 128], F32, "idf0"); S.dma("sync", idf0[:], ident[:])
    Jm0 = S.sb([128, 128], F32, "Jm0"); S.dma("sync", Jm0[:], IN["msk"][5])
    idf = S.sb([128, 128], F32R, "idf"); S.copy("dve", idf[:], idf0[:])
    Jm = S.sb([128, 128], F32R, "Jm"); S.copy("dve", Jm[:], Jm0[:])
    pb = PS["g"]
    sm = lambda nm: S.sb([128, NST], F32, nm)
    big = lambda nm, dt=F32: S.sb([128, NST, 32], dt, nm)
    are = sm("are"); aim = sm("aim"); lst = sm("lst")
    bre = big("bre"); bim = big("bim"); cre0 = big("cre0"); cim = big("cim"); ncim = big("ncim", F32R); cre = big("cre", F32R)
    lre = sm("lre"); dt_ = sm("dt"); zr = sm("zr"); th = sm("th"); rho = sm("rho")
    sa = sm("sa"); sk = sm("sk"); sr = sm("sr")
    cs = sm("cs"); sn = sm("sn")
    abre = sm("abre"); abim = sm("abim"); den = sm("den"); rden = sm("rden"); t1 = sm("t1"); t2 = sm("t2")
    fre = sm("fre"); fim = sm("fim"); am1 = sm("am1")
    bbre = big("bbre", F32R); bbim = big("bbim", F32R); u1 = big("u1"); u2 = big("u2")
    BBTre = S.sb([32, NST, 128], F32R, "BBTre"); BBTim = S.sb([32, NST, 128], F32R, "BBTim")
    Ec = S.sb([128, NST, CH], F32, "Ec"); Es = S.sb([128, NST, CH], F32, "Es")
    w1 = S.sb([128, NST, CH // 2], F32, "w1"); w2 = S.sb([128, NST, CH // 2], F32, "w2")
    rhob = S.sb([128, NST, CH], F32, "rhob")
    utok = [S.sb([128, 512], F32, f"utok{i}") for i in range(2)]
    utokr = [S.sb([128, 512], F32R, f"utokr{i}") for i in range(2)]
    ut = [S.sb([32, NST, CH], F32R, f"ut{i}") for i in range(2)]
    Zre_ = [S.sb([128, NST, CH], F32, f"Zre{i}") for i in range(2)]; Zim_ = [S.sb([128, NST, CH], F32, f"Zim{i}") for i in range(2)]
    wre_ = [S.sb([128, NST, CH], F32, "wre0")] * 2; wim_ = [S.sb([128, NST, CH], F32, "wim0")] * 2
    xre = [S.sb([128, NST, CH], F32R, f"xre{i}") for i in range(2)]
    xim = [S.sb([128, NST, CH], F32R, f"xim{i}") for i in range(2)]
    ta_ = [[S.sb([128, 4, CH], F32, f"ta{q}{i}") for i in range(4)] for q in range(2)]
    tb_ = [[S.sb([128, 4, CH], F32, f"tb0{i}") for i in range(4)]] * 2
    pbi = 0
    tai = 0
    yst = [S.sb([128, 512], F32R, f"yst{i}") for i in range(2)]
    yst2 = [S.sb([128, 512], F32, f"ystb{i}") for i in range(2)]
    MAGIC = 12582912.0
    TWO_PI = 2.0 * math.pi
    pbi = 0
    cc = 0
    for dr in range(2):
        for (t_, nm) in ((are, "are"), (aim, "aim"), (lst, "lst")):
            S.dma("sync", t_[:], IN[nm][dr])
        for (t_, nm) in ((bre, "bre"), (bim, "bim"), (cre0, "cre"), (cim, "cim")):
            S.dma("sync", t_[:], IN[nm][dr])
        S.ts("dve", ncim[:], cim[:], -1.0, ALU.mult)
        S.copy("dve", cre[:], cre0[:])
        S.ts("dve", lre[:], are[:], -1e-4, ALU.min)
        S.act(dt_[:], lst[:], AF.Exp)
        S.tt("dve", zr[:], lre[:], dt_[:], ALU.mult)
        S.tt("dve", th[:], aim[:], dt_[:], ALU.mult)
        S.act(rho[:], zr[:], AF.Exp)

        def sin_reduced(out, ang, shift):
            S.ts("dve", sa[:], ang[:], float(shift), ALU.add)
            S.ts("dve", sk[:], sa[:], 1.0 / TWO_PI, ALU.mult, MAGIC, ALU.add)
            S.ts("dve", sk[:], sk[:], MAGIC, ALU.subtract)
            S.stt("dve", sr[:], sk[:], -TWO_PI, sa[:], ALU.mult, ALU.add)
            S.ts("dve", sr[:], sr[:], 3.14159, ALU.min, -3.14159, ALU.max)
            S.act(out, sr[:], AF.Sin)
        sin_reduced(sn[:], th, 0.0)
        sin_reduced(cs[:], th, math.pi / 2)
        S.tt("dve", abre[:], rho[:], cs[:], ALU.mult)
        S.tt("dve", abim[:], rho[:], sn[:], ALU.mult)
        S.tt("dve", t1[:], lre[:], lre[:], ALU.mult)
        S.tt("dve", t2[:], aim[:], aim[:], ALU.mult)
        S.tt("dve", den[:], t1[:], t2[:], ALU.add)
        S.recip(rden[:], den[:])
        S.ts("dve", am1[:], abre[:], -1.0, ALU.add)
        S.tt("dve", t1[:], am1[:], lre[:], ALU.mult)
        S.tt("dve", t2[:], abim[:], aim[:], ALU.mult)
        S.tt("dve", t1[:], t1[:], t2[:], ALU.add)
        S.tt("dve", fre[:], t1[:], rden[:], ALU.mult)
        S.tt("dve", t1[:], abim[:], lre[:], ALU.mult)
        S.tt("dve", t2[:], am1[:], aim[:], ALU.mult)
        S.tt("dve", t1[:], t1[:], t2[:], ALU.subtract)
        S.tt("dve", fim[:], t1[:], rden[:], ALU.mult)
        fre_b = fre[:].re("p (t o) -> p t o", o=1).bc([128, NST, 32])
        fim_b = fim[:].re("p (t o) -> p t o", o=1).bc([128, NST, 32])
        S.tt("dve", u1[:], bre[:], fre_b, ALU.mult)
        S.tt("dve", u2[:], bim[:], fim_b, ALU.mult)
        S.tt("dve", bbre[:], u1[:], u2[:], ALU.subtract)
        S.tt("dve", u1[:], bim[:], fre_b, ALU.mult)
        S.tt("dve", u2[:], bre[:], fim_b, ALU.mult)
        S.tt("dve", bbim[:], u1[:], u2[:], ALU.add)
        for (src, dst) in ((bbre, BBTre), (bbim, BBTim)):
            for g4 in range(4):
                p_ = pb[pbi % 6]; pbi += 1
                for jj in range(4):
                    j = g4 * 4 + jj
                    S.mm(p_[0:32, jj * 128:(jj + 1) * 128], src[:, j, :], idf[:], sig=(jj == 3), f32=True)
                S.copy("dve", dst[:, g4 * 4:(g4 + 1) * 4, :], p_[0:32, :].re("p (a b) -> p a b", b=128))
        S.copy("dve", Ec[:, :, 0], cs[:])
        S.copy("dve", Es[:, :, 0], sn[:])
        m = 1
        while m < CH:
            cb = Ec[:, :, m - 1:m].bc([128, NST, m]); sb_ = Es[:, :, m - 1:m].bc([128, NST, m])
            S.tt("dve", w1[:, :, 0:m], Ec[:, :, 0:m], cb, ALU.mult)
            S.tt("dve", w2[:, :, 0:m], Es[:, :, 0:m], sb_, ALU.mult)
            S.tt("dve", Ec[:, :, m:2 * m], w1[:, :, 0:m], w2[:, :, 0:m], ALU.subtract)
            S.tt("dve", w1[:, :, 0:m], Ec[:, :, 0:m], sb_, ALU.mult)
            S.tt("dve", w2[:, :, 0:m], Es[:, :, 0:m], cb, ALU.mult)
            S.tt("dve", Es[:, :, m:2 * m], w1[:, :, 0:m], w2[:, :, 0:m], ALU.add)
            m *= 2
        S.copy("dve", rhob[:], rho[:].re("p (t o) -> p t o", o=1).bc([128, NST, CH]))
        Pm = idf if dr == 0 else Jm
        border = list(range(NCH)) if dr == 0 else [1, 0] + list(range(NCH - 1, 1, -1))
        def stageA(ci, n, cc):
            nonlocal pbi, tai
            r0 = prow(n)
            utk0 = utok[cc % 2]
            S.dma("sync", utk0[:], ppad[r0:r0 + 128, 1792:2304])
            utk = utokr[cc % 2]
            S.copy("act", utk[:], utk0[:])
            u = ut[cc % 2]
            for g4 in range(4):
                p_ = pb[pbi % 6]; pbi += 1
                for jj in range(4):
                    j = g4 * 4 + jj
                    S.mm(p_[0:32, jj * 128:(jj + 1) * 128], utk[:, j * 32:(j + 1) * 32], Pm[:], sig=(jj == 3), f32=True)
                S.copy("act", u[:, g4 * 4:(g4 + 1) * 4, :], p_[0:32, :].re("p (a b) -> p a b", b=128))
            Zre, Zim = Zre_[cc % 2], Zim_[cc % 2]
            for g4 in range(4):
                ta = ta_[tai % 2]; tai += 1
                pr = pb[pbi % 6]; pbi += 1
                pi_ = pb[pbi % 6]; pbi += 1
                for jj in range(4):
                    j = g4 * 4 + jj
                    S.mm(pr[:, jj * CH:(jj + 1) * CH], BBTre[:, j, :], u[:, j, :], sig=False)
                for jj in range(4):
                    j = g4 * 4 + jj
                    S.mm(pi_[:, jj * CH:(jj + 1) * CH], BBTim[:, j, :], u[:, j, :], sig=(jj == 3))
                sl = slice(g4 * 4, (g4 + 1) * 4)
                prv = pr[:, :].re("p (a b) -> p a b", b=CH); piv = pi_[:, :].re("p (a b) -> p a b", b=CH)
                a0, a1, a2, a3 = ta
                S.tt("dve", a0[:], prv, Ec[:, sl, :], ALU.mult)
                S.tt("dve", a1[:], piv, Es[:, sl, :], ALU.mult)
                S.tt("dve", Zre[:, sl, :], a0[:], a1[:], ALU.add)
                S.tt("dve", a2[:], piv, Ec[:, sl, :], ALU.mult)
                S.tt("dve", a3[:], prv, Es[:, sl, :], ALU.mult)
                S.tt("dve", Zim[:, sl, :], a2[:], a3[:], ALU.subtract)

        def stageSc(ci, cc):
            Zre, Zim = Zre_[cc % 2], Zim_[cc % 2]
            xr_prev, xi_prev = xre[(cc + 1) % 2], xim[(cc + 1) % 2]
            for j in range(NST):
                for (wt, zt, xp) in ((wre_[0], Zre, xr_prev), (wim_[0], Zim, xi_prev)):
                    init = 0.0 if ci == 0 else xp[:, j, CH - 1:CH]
                    S.scan(wt[:, j, :], rhob[:, j, :], zt[:, j, :], init)

        def stageB(ci, n, cc):
            nonlocal pbi
            xr, xi = xre[cc % 2], xim[cc % 2]
            wre, wim = wre_[0], wim_[0]
            tb = tb_[0]
            for g4 in range(4):
                sl = slice(g4 * 4, (g4 + 1) * 4)
                b0, b1, b2, b3 = tb
                S.tt("pool", b0[:], wre[:, sl, :], Ec[:, sl, :], ALU.mult)
                S.tt("pool", b1[:], wim[:, sl, :], Es[:, sl, :], ALU.mult)
                S.tt("dve", xr[:, sl, :], b0[:], b1[:], ALU.subtract)
                S.tt("pool", b2[:], wim[:, sl, :], Ec[:, sl, :], ALU.mult)
                S.tt("pool", b3[:], wre[:, sl, :], Es[:, sl, :], ALU.mult)
                S.tt("dve", xi[:, sl, :], b2[:], b3[:], ALU.add)
            py = pb[pbi % 6]; pbi += 1
            for j in range(NST):
                S.mm(py[:, j * 32:(j + 1) * 32], xr[:, j, :], cre[:, j, :], start=True, stop=False, sig=False)
                S.mm(py[:, j * 32:(j + 1) * 32], xi[:, j, :], ncim[:, j, :], start=False, stop=True, sig=(j == NST - 1))
            ys = yst[cc % 2]
            S.copy("act", ys[:], py[:])
            py2 = pb[pbi % 6]; pbi += 1
            S.mm(py2[:], Pm[:], ys[:])
            ys2 = yst2[cc % 2]
            S.copy("act", ys2[:], py2[:])
            S.dma("sync", SC["ys"][dr, n * 128:(n + 1) * 128, :], ys2[:])

        stageA(0, border[0], cc)
        for ci, n in enumerate(border):
            stageSc(ci, cc)
            if ci + 1 < len(border):
                stageA(ci + 1, border[ci + 1], cc + 1)
            stageB(ci, n, cc)
            cc += 1


def phase3a(S, PS, IN, SC):
    idf, idb = setup_consts(S, IN["ident"][:])
    ones = S.sb([128, 128], F32, "ones"); S.memset("dve", ones[:], 1.0)
    pa = PS["g"]; pt = PS["b"]
    modT = S.sb([128, 72, 2], F32, "modT")
    S.dma("sync", modT[:].re("p t j -> p (t j)"), SC["modT0"][:])
    gateb = [gate_bcast(S, modT[:, 5 * 8:6 * 8, j], idf, ones, pa[j], 1.0, f"g5{j}") for j in range(2)]
    bcs = [S.sb([128, 512], F32, f"bcs{i}") for i in range(5)]
    for i in range(5):
        S.dma("sync", bcs[i][:], IN["bcs"][i])
    lnxg, lnxb, rk, s5d, glub = bcs
    S.ts("dve", rk[:], rk[:], 0.5, ALU.mult)
    wst = S.sb([128, 4, 512], F32, "wst")
    gluw = S.sb([128, 4, 512], BF16, "gluw")
    S.dma("sync", wst[:], IN["gluw"].re("(k p) n -> p k n", p=128))
    S.copy("act", gluw[:], wst[:])
    outw = S.sb([128, 8, 1024], BF16, "outw")
    wst2 = [S.sb([128, 1024], F32, f"wst2{i}") for i in range(2)]
    for k in range(8):
        S.dma("sync", wst2[k % 2][:], IN["outw_ab"][k * 128:(k + 1) * 128, :])
        S.copy("act", outw[:, k, :], wst2[k % 2][:])
    t5 = lambda nm, dt=F32: S.sb([128, 512], dt, nm)
    inr = [[t5(f"inr{i}{j}") for j in range(7)] for i in range(2)]
    ins = [[t5(f"ins{i}{j}") for j in range(3)] for i in range(2)]
    xt = [S.sb([128, 1024], F32, f"xt{i}") for i in range(2)]
    y = t5("y"); yc = t5("yc"); sq = t5("sq"); ks = t5("ks"); tq = t5("tq"); bon = t5("bon")
    s8 = S.sb([128, 8], F32, "s8"); v8 = S.sb([128, 8], F32, "v8"); b8 = S.sb([128, 8], F32, "b8")
    cat = S.sb([128, 1024], BF16, "cat"); ysum = t5("ysum"); z = t5("z"); zb = t5("zb", BF16)
    zT = S.sb([128, 4, 128], BF16, "zT"); gl = t5("gl"); catT = S.sb([128, 8, 128], BF16, "catT")
    tmp = S.sb([128, 1024], F32, "tmp")
    v3 = lambda v: v.re("p (h k) -> p h k", k=64)
    b3 = lambda t: t[:].re("p (h o) -> p h o", o=1).bc([128, 8, 64])
    for ci in range(NCH):
        j = JF34(ci)
        i2 = ci % 2
        rows = slice(ci * 128, (ci + 1) * 128)
        srcs = [SC["yd"][0, rows, :], SC["yd"][1, rows, :], SC["kdo"][0, rows, :], SC["kdo"][1, rows, :],
                SC["rvo"][rows, 0:512], SC["rvo"][rows, 512:1024], SC["gto"][rows, :]]
        for q in range(7):
            S.dma("sync" if q % 2 == 0 else "pool", inr[i2][q][:], srcs[q])
        srcs2 = [SC["ys"][0, rows, :], SC["ys"][1, rows, :], SC["ppad"][prow(ci):prow(ci) + 128, 1792:2304]]
        for q in range(3):
            S.dma("pool" if q % 2 == 0 else "sync", ins[i2][q][:], srcs2[q])
        S.dma("sync", xt[i2][:], SC["x1"][rows, :])
        y0, y1, kd0, kd1, r, v, g = inr[i2]
        S.tt("dve", y[:], y0[:], y1[:], ALU.add)
        S.red("dve", s8[:], v3(y[:]), ALU.add)
        S.ts("dve", s8[:], s8[:], 1.0 / 64, ALU.mult)
        S.tt("dve", v3(yc[:]), v3(y[:]), b3(s8), ALU.subtract)
        S.tt("dve", sq[:], yc[:], yc[:], ALU.mult)
        S.red("dve", v8[:], v3(sq[:]), ALU.add)
        S.ts("dve", v8[:], v8[:], 1.0 / 64, ALU.mult, GN_EPS, ALU.add)
        S.act(v8[:], v8[:], AF.Sqrt)
        S.recip(v8[:], v8[:])
        S.tt("dve", v3(yc[:]), v3(yc[:]), b3(v8), ALU.mult)
        S.tt("dve", yc[:], yc[:], lnxg[:], ALU.mult)
        S.tt("dve", yc[:], yc[:], lnxb[:], ALU.add)
        S.tt("dve", ks[:], kd0[:], kd1[:], ALU.add)
        S.tt("dve", tq[:], r[:], ks[:], ALU.mult)
        S.tt("dve", tq[:], tq[:], rk[:], ALU.mult)
        S.red("dve", b8[:], v3(tq[:]), ALU.add)
        S.tt("dve", v3(bon[:]), v3(v[:]), b3(b8), ALU.mult)
        S.tt("dve", yc[:], yc[:], bon[:], ALU.add)
        S.tt("dve", cat[:, 0:512], yc[:], g[:], ALU.mult)
        ys0, ys1, u = ins[i2]
        S.tt("dve", ysum[:], ys0[:], ys1[:], ALU.add)
        S.tt("dve", tq[:], u[:], s5d[:], ALU.mult)
        S.tt("dve", ysum[:], ysum[:], tq[:], ALU.add)
        S.act(z[:], ysum[:], AF.Gelu)
        S.copy("act", zb[:], z[:])
        p_ = pt[0]
        for k in range(4):
            S.tr(p_[:, k * 128:(k + 1) * 128], zb[:, k * 128:(k + 1) * 128], idb[:], sig=(k == 3))
        S.copy("dve", zT[:], p_[:, 0:512].re("p (k t) -> p k t", t=128))
        pg = pa[2]
        for k in range(4):
            S.mm(pg[:], zT[:, k, :], gluw[:, k, :], start=(k == 0), stop=(k == 3))
        S.tt("dve", gl[:], pg[:], glub[:], ALU.add)
        S.act(gl[:], gl[:], AF.Sigmoid)
        S.tt("dve", cat[:, 512:1024], z[:], gl[:], ALU.mult)
        p_ = pt[1]
        for k in range(8):
            S.tr(p_[:, k * 128:(k + 1) * 128], cat[:, k * 128:(k + 1) * 128], idb[:], sig=(k == 7))
        S.copy("dve", catT[:], p_[:].re("p (k t) -> p k t", t=128))
        for h in range(2):
            pc = pa[4 + h]
            for k in range(8):
                S.mm(pc[:], catT[:, k, :], outw[:, k, h * 512:(h + 1) * 512], start=(k == 0), stop=(k == 7))
            S.tt("dve", tmp[:, h * 512:(h + 1) * 512], pc[:], gateb[j][:, h * 512:(h + 1) * 512], ALU.mult)
        S.tt("dve", xt[i2][:], xt[i2][:], tmp[:], ALU.add)
        S.dma("pool", SC["xm"][rows, :], xt[i2][:])


def phase3b(S, PS, IN, SC):
    C = alloc_common(S, PS)
    C.idf, C.idb = setup_consts(S, IN["ident"][:])
    modT0 = S.sb([128, 72, 2], F32, "modT0")
    S.dma("sync", modT0[:].re("p t j -> p (t j)"), SC["modT0"][:])
    ngT0 = S.sb([128, 3, 8], F32, "ngT0")
    S.dma("sync", ngT0[:], IN["ngT"][0])
    mv0 = mod_derive(S, modT0, ngT0)
    C.gateb = {}
    for j in range(2):
        C.gateb[(2, j)] = gate_bcast(S, mv0["GATE"][:, 2, :, j], C.idf, C.ones, C.pa[j], 0.5, f"gb2{j}")
    rows = lambda t: (lambda ci: t[ci * 128:(ci + 1) * 128, :])
    for grp in ffn(S, C, rows(SC["xm"]), rows(SC["xl0"]), IN["w1"][0, 1], IN["w2"][0, 1], mv0, 2, GROUPS34, JF34):
        pass
    mv1 = mod_vectors(S, IN["cT"][:], IN["modw"][1], IN["modbT"][1], IN["ngT"][1], C.wst, C.pa[0])
    S.dma("sync", SC["modT1"][:], mv1["modT"][:].re("p t j -> p (t j)"))
    for j in range(2):
        C.gateb[(0, j)] = gate_bcast(S, mv1["GATE"][:, 0, :, j], C.idf, C.ones, C.pa[2 + j], 0.5, f"gb0{j}")
    cos = S.sb([128, NCH, 32], F32, "cos"); sin = S.sb([128, NCH, 32], F32, "sin")
    S.dma("sync", cos[:], IN["rope"][0].re("c p f -> p c f"))
    S.dma("sync", sin[:], IN["rope"][1].re("c p f -> p c f"))
    pst = [S.sb([128, 256], F32, f"pst{i}") for i in range(2)]
    ra = [S.sb([128, 4, 32], F32, f"ra{i}") for i in range(4)]
    psti = [0]

    def after(ci, li, xt):
        j = JF34(ci)
        norm_to_hT(S, C, xt[:], C.hT, li * 128, mv1["G"][:, 1, :, j], mv1["SH"][:, 1, :, j])
    win = IN["win_at"]
    qkv = SC["qkv"]
    for grp in ffn(S, C, rows(SC["xl0"]), rows(SC["x2"]), IN["w1"][1, 0], IN["w2"][1, 0], mv1, 0, GROUPS34, JF34, after_chunk=after):
        for cb in range(6):
            wst = C.wst[cb % 2]; wb = C.w1b[cb % 2]
            S.dma("sync", wst[:, 0:4, :], win[cb, :, 0:4, :])
            S.dma("sync", wst[:, 4:8, :], win[cb, :, 4:8, :])
            S.copy("act", wb[:, 0:4, :], wst[:, 0:4, :]); S.copy("dve", wb[:, 4:8, :], wst[:, 4:8, :])
            for li, ci in enumerate(grp):
                pp = C.pa[C.pai % 4]; C.pai += 1
                for k in range(8):
                    S.mm(pp[:, 0:256], C.hT[:, k, li * 128:(li + 1) * 128], wb[:, k, :], start=(k == 0), stop=(k == 7))
                st = pst[psti[0] % 2]; psti[0] += 1
                if cb < 5:
                    pv = pp[:, 0:256].re("p (h two f) -> p h two f", two=2, f=32)
                    sv = st[:].re("p (h two f) -> p h two f", two=2, f=32)
                    cb_ = cos[:, ci, :].re("p (o f) -> p o f", o=1).bc([128, 4, 32])
                    sb_ = sin[:, ci, :].re("p (o f) -> p o f", o=1).bc([128, 4, 32])
                    a, b, c, dd = ra
                    S.tt("dve", a[:], pv[:, :, 0, :], cb_, ALU.mult)
                    S.tt("dve", b[:], pv[:, :, 1, :], sb_, ALU.mult)
                    S.tt("pool", sv[:, :, 0, :], a[:], b[:], ALU.subtract)
                    S.tt("dve", c[:], pv[:, :, 1, :], cb_, ALU.mult)
                    S.tt("dve", dd[:], pv[:, :, 0, :], sb_, ALU.mult)
                    S.tt("pool", sv[:, :, 1, :], c[:], dd[:], ALU.add)
                else:
                    S.copy("act", st[:], pp[:, 0:256])
                S.dma("sync", qkv[ci * 128:(ci + 1) * 128, cb * 256:(cb + 1) * 256], st[:])


def phase4a(S, PS, IN, SC):
    idf, idb = setup_consts(S, IN["ident"][:])
    ones = S.sb([128, 128], F32, "ones"); S.memset("dve", ones[:], 1.0)
    pa = PS["g"][0:4]; pai = [0]
    ptb = PS["b"][0]
    pos = PS["g"][4:6]
    qkv = SC["qkv"]
    modT = S.sb([128, 72, 2], F32, "modT")
    S.dma("sync", modT[:].re("p t j -> p (t j)"), SC["modT1"][:])
    gate5 = gate_bcast(S, modT[:, 5 * 8:6 * 8, 1], idf, ones, pa[0], 1.0, "g5")
    sinkb = S.sb([128, 16], F32, "sinkb"); S.dma("sync", sinkb[:], IN["sinkb"][:])
    mt16 = S.sb([128, 16, 3], F32, "mt16")
    S.copy("dve", mt16[:, :, 2], sinkb[:])
    mstage = S.sb([128, 384], F32, "mstage")
    maskb = S.sb([128, 3, 384], BF16, "maskb")
    for i in range(3):
        S.dma("sync", mstage[:], IN["maskb"][i])
        S.copy("dve", maskb[:, i, :], mstage[:])
    outw = S.sb([128, 8, 1024], BF16, "outw")
    wst2 = [S.sb([128, 1024], F32, f"wst2{i}") for i in range(2)]
    for k in range(8):
        S.dma("sync", wst2[k % 2][:], IN["outw_at"][k * 128:(k + 1) * 128, :])
        S.copy("act", outw[:, k, :], wst2[k % 2][:])
    NKB = NQB + 2
    kT = S.sb([64, 4, NKB * 128], BF16, "kT"); kcT = S.sb([64, 4, 256], BF16, "kcT")
    vw = S.sb([128, NKB, 256], BF16, "vw"); vc = S.sb([128, 2, 256], BF16, "vc")
    for blk in (0, NKB - 1):
        S.memset("dve", kT[:, :, blk * 128:(blk + 1) * 128], 0.0)
        S.memset("dve", vw[:, blk, :], 0.0)
    kst = [S.sb([128, 512], F32, f"kst{i}") for i in range(2)]; kb = [S.sb([128, 256], BF16, f"kb{i}") for i in range(2)]
    for c in range(NCH):
        S.dma("sync", kst[c % 2][:], qkv[c * 128:(c + 1) * 128, 1024:1536])
        S.copy("pool", kb[c % 2][:], kst[c % 2][:, 0:256])
        for kv in range(4):
            S.tr(ptb[0:64, kv * 128:(kv + 1) * 128], kb[c % 2][:, kv * 64:(kv + 1) * 64], idb[:], sig=(kv == 3))
        blk = c - 1
        dstk = kcT[:, :, c * 128:(c + 1) * 128] if c < 2 else kT[:, :, blk * 128:(blk + 1) * 128]
        S.copy("act", dstk, ptb[0:64, 0:512].re("p (a t) -> p a t", t=128))
        dstv = vc[:, c, :] if c < 2 else vw[:, blk, :]
        S.copy("dve", dstv, kst[c % 2][:, 256:512])
    qst = [S.sb([128, 1024], F32, f"qst{i}") for i in range(2)]
    qb = S.sb([128, 1024], BF16, "qb")
    qT = S.sb([64, 16, 128], BF16, "qT")
    Pm = [S.sb([128, 640], BF16, f"Pm{i}") for i in range(2)]
    PT = [S.sb([128, 5, 128], BF16, f"PT{i}") for i in range(2)]
    rs = [S.sb([128, 4], F32, f"rs{i}") for i in range(2)]
    negm = [S.sb([128, 1], F32, f"negm{i}") for i in range(2)]
    rden = S.sb([128, 16], F32, "rden")
    ob = S.sb([128, 1024], BF16, "ob"); oT = S.sb([128, 8, 128], BF16, "oT")
    xt = [S.sb([128, 1024], F32, f"xt{i}") for i in range(2)]
    tmp = S.sb([128, 1024], F32, "tmp")
    for i in range(NQB):
        rows = slice((i + 2) * 128, (i + 3) * 128)
        S.dma("sync", qst[i % 2][:], qkv[rows, 0:1024])
        S.dma("pool", xt[i % 2][:], SC["x2"][rows, :])
        S.act(qb[:], qst[i % 2][:], AF.Copy, scale=0.125)
        for half in range(2):
            for hh in range(8):
                hd = half * 8 + hh
                S.tr(ptb[0:64, hh * 128:(hh + 1) * 128], qb[:, hd * 64:(hd + 1) * 64], idb[:], sig=(hh == 7))
            S.copy("act", qT[:, half * 8:(half + 1) * 8, :], ptb[0:64, :].re("p (a t) -> p a t", t=128))
        mi = 0 if i == 0 else (2 if i == NQB - 1 else 1)
        def scores(hd):
            kv = hd // 4
            pw = pa[pai[0] % 4]; pai[0] += 1
            pcx = pa[pai[0] % 4]; pai[0] += 1
            S.mm(pw[:, 0:384], qT[:, hd, :], kT[:, kv, i * 128:(i + 3) * 128], start=True, stop=False, sig=False)
            S.mm(pw[:, 0:384], idb[:], maskb[:, mi, :], start=False, stop=True)
            S.mm(pcx[:, 0:256], qT[:, hd, :], kcT[:, kv, :])
            return pw, pcx
        nxt_sc = scores(0)
        for hd in range(16):
            kv = hd // 4
            i2 = hd % 2
            pw, pcx = nxt_sc
            if hd + 1 < 16:
                nxt_sc = scores(hd + 1)
            S.red("dve", mt16[:, hd, 0:1], pw[:, 0:384], ALU.max)
            S.red("dve", mt16[:, hd, 1:2], pcx[:, 0:256], ALU.max)
            S.red("dve", negm[i2][:], mt16[:, hd, :], ALU.max)
            S.ts("dve", negm[i2][:], negm[i2][:], -1.0, ALU.mult)
            S.act(Pm[i2][:, 0:384], pw[:, 0:384], AF.Exp, bias=negm[i2][:, 0:1], accum=rs[i2][:, 0:1])
            S.act(Pm[i2][:, 384:640], pcx[:, 0:256], AF.Exp, bias=negm[i2][:, 0:1], accum=rs[i2][:, 1:2])
            S.act(rs[i2][:, 2:3], sinkb[:, hd:hd + 1], AF.Exp, bias=negm[i2][:, 0:1])
            S.red("dve", rs[i2][:, 3:4], rs[i2][:, 0:3], ALU.add)
            S.recip(rden[:, hd:hd + 1], rs[i2][:, 3:4])
            for j in range(5):
                S.tr(ptb[:, j * 128:(j + 1) * 128], Pm[i2][:, j * 128:(j + 1) * 128], idb[:], sig=(j == 4))
            S.copy("dve" if hd % 2 == 0 else "act", PT[i2][:], ptb[:, 0:640].re("p (a t) -> p a t", t=128))
            po = pos[hd // 8]
            for j in range(5):
                vsrc = vw[:, i + j, kv * 64:(kv + 1) * 64] if j < 3 else vc[:, j - 3, kv * 64:(kv + 1) * 64]
                S.mm(po[:, (hd % 8) * 64:(hd % 8 + 1) * 64], PT[i2][:, j, :], vsrc, start=(j == 0), stop=(j == 4), sig=(j == 4))
        for h2 in range(2):
            S.tt("dve", ob[:, h2 * 512:(h2 + 1) * 512].re("p (h k) -> p h k", k=64), pos[h2][:].re("p (h k) -> p h k", k=64),
                 rden[:, h2 * 8:(h2 + 1) * 8].re("p (h o) -> p h o", o=1).bc([128, 8, 64]), ALU.mult)
        for k in range(8):
            S.tr(ptb[:, k * 128:(k + 1) * 128], ob[:, k * 128:(k + 1) * 128], idb[:], sig=(k == 7))
        S.copy("act", oT[:], ptb[:].re("p (a t) -> p a t", t=128))
        for h in range(2):
            py = pa[pai[0] % 4]; pai[0] += 1
            for k in range(8):
                S.mm(py[:], oT[:, k, :], outw[:, k, h * 512:(h + 1) * 512], start=(k == 0), stop=(k == 7))
            S.tt("dve", tmp[:, h * 512:(h + 1) * 512], py[:], gate5[:, h * 512:(h + 1) * 512], ALU.mult)
        S.tt("dve", xt[i % 2][:], xt[i % 2][:], tmp[:], ALU.add)
        S.dma("pool", SC["x3"][i * 128:(i + 1) * 128, :], xt[i % 2][:])


def phase4b(S, PS, IN, SC, OUT):
    C = alloc_common(S, PS)
    C.idf, C.idb = setup_consts(S, IN["ident"][:])
    modT = S.sb([128, 72, 2], F32, "modT")
    S.dma("sync", modT[:].re("p t j -> p (t j)"), SC["modT1"][:])
    ngT = S.sb([128, 3, 8], F32, "ngT"); S.dma("sync", ngT[:], IN["ngT"][1])
    mv = mod_derive(S, modT, ngT)
    C.gateb = {(2, 1): gate_bcast(S, mv["GATE"][:, 2, :, 1], C.idf, C.ones, C.pa[0], 0.5, "gb21")}
    fing = S.sb([128, 1024], F32, "fing"); S.dma("sync", fing[:], IN["fing"][:])
    ot = [S.sb([128, 1024], F32, f"ot{i}") for i in range(2)]
    oi = [0]

    def after(ci, li, xt):
        ss = C.small[C.si % 4]; C.si += 1
        S.act(C.junk[:], xt[:], AF.Square, accum=ss[:, 0:1])
        S.ts("dve", ss[:, 1:2], ss[:, 0:1], 1.0 / D, ALU.mult, 1e-6, ALU.add)
        S.act(ss[:, 3:4], ss[:, 1:2], AF.Sqrt)
        S.recip(ss[:, 2:3], ss[:, 3:4])
        o = ot[oi[0] % 2]; oi[0] += 1
        S.stt("dve", o[:], xt[:], ss[:, 2:3], fing[:], ALU.mult, ALU.mult)
        S.dma("sync", OUT[ci * 128:(ci + 1) * 128, :], o[:])
    rows = lambda t: (lambda ci: t[ci * 128:(ci + 1) * 128, :])
    for grp in ffn(S, C, rows(SC["x3"]), None, IN["w1"][1, 1], IN["w2"][1, 1], mv, 2, GROUPS32, lambda ci: 1, after_chunk=after):
        pass


IN_SPECS = dict(
    x=[4096, D], ctx=[256, D], cT=[128, 8, 2], modw=[2, 36, 128, 8, 256], modbT=[2, 128, 72], ngT=[2, 128, 3, 8],
    w1=[2, 2, NFT, 128, 8, 256], w2=[2, 2, DFF, D], win_ab=[9, 128, 8, 256], ident=[128, 128],
    mub=[2, 128, 1536], mulb=[2, 128, 256], kkb=[128, 512], kab=[128, 512], w2a=[2, 65, 512], a2a=[2, 65, 512], g2=[128, 512], msk=[6, 128, 128],
    are=[2, 128, NST], aim=[2, 128, NST], lst=[2, 128, NST], bre=[2, 128, NST, 32], bim=[2, 128, NST, 32], cre=[2, 128, NST, 32], cim=[2, 128, NST, 32],
    bcs=[5, 128, 512], gluw=[512, 512], outw_ab=[D, D], win_at=[6, 128, 8, 256], rope=[2, NCH, 128, 32],
    maskb=[3, 128, 384], sinkb=[128, 16], outw_at=[D, D], fing=[128, D])

SC_SPECS = dict(x1=[TOK, D], ppad=[4356, 2304], yd=[2, TOK, 512], kdo=[2, TOK, 512], rvo=[TOK, 1024], gto=[TOK, 512], ys=[2, TOK, 512],
                xm=[TOK, D], xl0=[TOK, D], x2=[TOK, D], qkv=[TOK, 1536], x3=[4096, D], modT0=[128, 144], modT1=[128, 144])


def build_fused(upto=99, debug=(), ses=True):
    nc = bass.Bass("TRN2", target_bir_lowering=False)
    S = Sched(nc, same_engine_sync=ses)
    IN = {k: S.dram(k, v, F32, kind="ExternalInput") for k, v in IN_SPECS.items()}
    SC = {k: S.dram("sc_" + k, v, F32, kind=("ExternalOutput" if k in debug else "Internal")) for k, v in SC_SPECS.items()}
    OUT = S.dram("out", [4096, D], F32, kind="ExternalOutput")
    PS = dict(g=[S.ps([128, 512], F32, f"g{i}") for i in range(6)], b=[S.ps([128, 1024], BF16, f"b{i}") for i in range(2)])
    base = S.mark()
    phases = [lambda: phase1(S, PS, IN, SC), lambda: phase2a(S, PS, IN, SC), lambda: phase2b(S, PS, IN, SC), lambda: phase3a(S, PS, IN, SC),
              lambda: phase3b(S, PS, IN, SC), lambda: phase4a(S, PS, IN, SC), lambda: phase4b(S, PS, IN, SC, OUT)]
    for i, ph in enumerate(phases):
        if i > upto:
            break
        S.reset(base)
        ph()
        S.barrier()
    finals = [OUT] + [SC[k] for k in debug]
    S.finish(finals)
    return nc, S

import numpy as np
LC = 256; NLAT = 4096; L = 4352
GRID_W = 64; ROPE_BASE = 10000.0
def core_tok(seq, h):
    return np.concatenate([seq[h * 128:(h + 1) * 128], seq[256 + h * 2048:256 + (h + 1) * 2048]], 0)
def uncore_tok(parts):
    return np.concatenate([parts[0][:128], parts[1][:128], parts[0][128:], parts[1][128:]], 0)
def colT(v, k=8):
    return np.ascontiguousarray(v.reshape(k, 128).T)
def bc(v):
    return np.ascontiguousarray(np.broadcast_to(v[None, :], (128, v.shape[0])))
def rope_tables(h):
    t = np.arange(h * 2048, (h + 1) * 2048)
    row = (t // GRID_W).astype(np.float32); col = (t % GRID_W).astype(np.float32)
    inv = (ROPE_BASE ** (-np.arange(0, 32, 2, dtype=np.float32) / 32)).astype(np.float32)
    ang = np.concatenate([row[:, None] * inv, col[:, None] * inv], -1).astype(np.float32)
    cos = np.concatenate([np.ones((128, 32), np.float32), np.cos(ang)], 0).reshape(17, 128, 32)
    sin = np.concatenate([np.zeros((128, 32), np.float32), np.sin(ang)], 0).reshape(17, 128, 32)
    return np.stack([cos, sin], 0).astype(np.float32)

import numpy as np
LC = 256

def f_masks():
    m = np.zeros((6, 128, 128), np.float32)
    s = np.arange(128)[:, None]; t = np.arange(128)[None, :]
    same = (s // 64) == (t // 64)
    m[0] = same & (s < t); m[1] = same & (s <= t); m[2] = same & (s > t); m[4] = same & (s >= t)
    m[3] = np.eye(128)
    m[5] = np.eye(128)[::-1]
    return m

def f_rope():
    GRID_W = 64
    t = np.arange(4096)
    row = (t // GRID_W).astype(np.float32); col = (t % GRID_W).astype(np.float32)
    inv = (10000.0 ** (-np.arange(0, 32, 2, dtype=np.float32) / 32)).astype(np.float32)
    ang = np.concatenate([row[:, None] * inv, col[:, None] * inv], -1).astype(np.float32)
    cos = np.concatenate([np.ones((256, 32), np.float32), np.cos(ang)], 0).reshape(34, 128, 32)
    sin = np.concatenate([np.zeros((256, 32), np.float32), np.sin(ang)], 0).reshape(34, 128, 32)
    return np.ascontiguousarray(np.stack([cos, sin], 0).astype(np.float32))

def f_attn_masks():
    qi = np.arange(128)[:, None]; mj = np.arange(384)[None, :] - 128
    valid = np.abs(mj - qi) <= 128
    NEG = -30000.0
    gen = np.where(valid, 0.0, NEG).astype(np.float32)
    left_inv = gen.copy(); left_inv[:, :128] = NEG
    right_inv = gen.copy(); right_inv[:, 256:] = NEG
    return np.ascontiguousarray(np.stack([left_inv, gen, right_inv], 0))

def wblk(w):
    n = w.shape[1] // 256
    return np.ascontiguousarray(w.reshape(8, 128, n, 256).transpose(2, 1, 0, 3))

def w1blk(w):
    a = w.reshape(8, 128, 2, 22, 128).transpose(3, 1, 0, 2, 4)
    return np.ascontiguousarray(a.reshape(22, 128, 8, 256))

def f_shared(d):
    e = 0
    st = lambda a: np.ascontiguousarray(a.reshape(16, 128).T)
    def pad(a):
        out = np.zeros((128, 16, 32), np.float32)
        for g in range(32):
            out[(g % 2) * 64:(g % 2) * 64 + 64, g // 2, (g % 2) * 16:(g % 2) * 16 + 16] = a[g]
        return out
    mu = d['rwkv_mu'][e]
    sh = dict(
        modw=np.stack([wblk(d['mod_w'][l]) for l in range(2)], 0), modbT=np.ascontiguousarray(np.stack([colT(d['mod_b'][l], 72) for l in range(2)], 0)),
        ngT=np.ascontiguousarray(np.stack([np.stack([colT(d['norm_g'][l, i]) for i in range(3)], 1) for l in range(2)], 0)),
        w1=np.stack([np.stack([w1blk(d['ffn_w1'][l, j]) for j in range(2)], 0) for l in range(2)], 0), w2=d['ffn_w2'], win_ab=wblk(d['ab_in_w'][0]), ident=np.eye(128, dtype=np.float32),
        mub=np.ascontiguousarray(np.stack([bc(mu[0, :1536]), bc(mu[1, :1536])], 0)),
        mulb=np.ascontiguousarray(np.stack([bc(mu[0, 1536:1792]), bc(mu[1, 1536:1792])], 0)),
        kkb=bc(d['rwkv_k_k'][e]), kab=bc(d['rwkv_k_a'][e]),
        w2a=np.ascontiguousarray(np.stack([np.concatenate([d['rwkv_w2'][e, dr], d['rwkv_w0'][e, dr][None]], 0) for dr in range(2)], 0)),
        a2a=np.ascontiguousarray(np.stack([np.concatenate([d['rwkv_a2'][e, dr], d['rwkv_a0'][e, dr][None]], 0) for dr in range(2)], 0)),
        g2=d['rwkv_g2'][e], msk=f_masks(),
        are=np.stack([st(d['s5_a_re'][0, dr]) for dr in range(2)], 0), aim=np.stack([st(d['s5_a_im'][0, dr]) for dr in range(2)], 0),
        lst=np.stack([st(np.repeat(d['s5_log_step'][0, dr][:, None], 64, 1)) for dr in range(2)], 0),
        bre=np.stack([pad(d['s5_b_re'][0, dr]) for dr in range(2)], 0), bim=np.stack([pad(d['s5_b_im'][0, dr]) for dr in range(2)], 0),
        cre=np.stack([pad(d['s5_c_re'][0, dr].transpose(0, 2, 1)) for dr in range(2)], 0),
        cim=np.stack([pad(d['s5_c_im'][0, dr].transpose(0, 2, 1)) for dr in range(2)], 0),
        bcs=np.ascontiguousarray(np.stack([bc(d['rwkv_lnx_g'][0]), bc(d['rwkv_lnx_b'][0]), bc(d['rwkv_r_k'][0].reshape(-1)), bc(d['s5_d'][0]),
                                           bc(d['s5_glu_b'][0])], 0)),
        gluw=d['s5_glu_w'][0], outw_ab=d['ab_out_w'][0], win_at=wblk(d['attn_in_w'][0]), rope=f_rope(),
        maskb=f_attn_masks(), sinkb=bc(d['attn_sink'][0]), outw_at=d['attn_out_w'][0], fing=bc(d['final_g']))
    return {k: np.ascontiguousarray(v, dtype=np.float32) for k, v in sh.items()}

def f_core(d, b):
    return dict(x=np.ascontiguousarray(d['x'][b]), ctx=np.ascontiguousarray(d['ctx'][b]),
                cT=np.ascontiguousarray(np.stack([colT(d['c_ctx']), colT(d['c'][b])], -1)))


def kernel(**inputs):
    d = {k: np.ascontiguousarray(np.asarray(v, dtype=np.float32)) for k, v in inputs.items()}
    nc, _ = build_fused()
    sh = f_shared(d)
    in_maps = [dict(sh, **f_core(d, c % 4)) for c in range(8)]
    res = run_bass_kernel_spmd(nc, in_maps, core_ids=list(range(8)))
    out = np.stack([res.results[b]['out'] for b in range(4)], 0)
    return out.astype(np.float32)
```

```python
import numpy as np
import concourse.bass as bass
import concourse.mybir as mybir
from concourse.bass_utils import run_bass_kernel_spmd

F32 = mybir.dt.float32
BF16 = mybir.dt.bfloat16
F32R = mybir.dt.float32r
ALU = mybir.AluOpType
AF = mybir.ActivationFunctionType
AX = mybir.AxisListType


class T:
    def __init__(self, h, name=""):
        self.h = h
        self.name = name
        self.last_w = None
        self.readers = []

    def __getitem__(self, idx):
        return V(self, self.h[idx])

    def re(self, pat, **kw):
        return self[:].re(pat, **kw)


class V:
    def __init__(self, t, ap):
        self.t = t
        self.ap = ap

    def __getitem__(self, idx):
        return V(self.t, self.ap[idx])

    def re(self, pat, **kw):
        return V(self.t, self.ap.rearrange(pat, **kw))

    def bc(self, shape):
        return V(self.t, self.ap.to_broadcast(shape))


def _ap(x):
    return x.ap if isinstance(x, V) else x


def _ts(xs):
    out = []
    for x in xs:
        if isinstance(x, V):
            out.append(x.t)
        elif isinstance(x, T):
            out.append(x)
    return out


class Sched:
    ENG = ["pe", "act", "dve", "pool", "sync"]

    def __init__(self, nc, n_dma_sems=6, same_engine_sync=True):
        self.nc = nc
        self.q = {e: [] for e in self.ENG}
        self.cnt = {e: 0 for e in self.ENG}
        self.unsig = {e: False for e in self.ENG}
        self.sem = {e: nc.alloc_semaphore(f"s_{e}") for e in ["pe", "act", "dve", "pool"]}
        self.waited = {e: {} for e in self.ENG}
        self.same_engine_sync = same_engine_sync
        self.dsem = {}
        self.dcnt = {}
        self.drr = {}
        for qn in ["sync", "pool", "act"]:
            self.dsem[qn] = [nc.alloc_semaphore(f"d_{qn}{i}") for i in range(n_dma_sems)]
            self.dcnt[qn] = [0] * n_dma_sems
            self.drr[qn] = 0
        self.n_inst = 0
        self.uid = 0

    ARENA_LO = 16640
    ARENA_HI = 229344

    def sb(self, shape, dt=F32, name=None):
        self.uid += 1
        name = name or f"t{self.uid}"
        if not hasattr(self, "off"):
            self.off = self.ARENA_LO
        n = 1
        for x in shape[1:]:
            n *= x
        size = n * (2 if dt == BF16 else 4)
        size = (size + 31) // 32 * 32
        assert self.off + size <= self.ARENA_HI, f"SBUF arena overflow allocating {name} {shape}: off={self.off} size={size}"
        t = T(self.nc.alloc_sbuf_tensor_at(f"{name}_{self.uid}", list(shape), dt, offset=self.off), name)
        self.off += size
        return t

    def mark(self):
        if not hasattr(self, "off"):
            self.off = self.ARENA_LO
        return self.off

    def reset(self, mark):
        self.off = mark

    def barrier(self):
        targets = []
        for e in ("pe", "act", "dve", "pool"):
            assert not self.unsig[e], f"barrier with unsignaled op on {e}"
            if self.cnt[e] > 0:
                targets.append((self.sem[e], self.cnt[e]))
        for qn in self.dsem:
            for sm, c in zip(self.dsem[qn], self.dcnt[qn]):
                if c > 0:
                    targets.append((sm, c))
        for e in self.ENG:
            waits = []
            for (sm, val) in targets:
                if e in self.sem and sm is self.sem[e]:
                    continue
                if self.waited[e].get(id(sm), 0) >= val:
                    continue
                self.waited[e][id(sm)] = val
                waits.append((sm, val))
            if waits:
                self.q[e].append((None, waits, None))

    def ps(self, shape, dt=F32, name=None):
        self.uid += 1
        name = name or f"p{self.uid}"
        return T(self.nc.alloc_psum_tensor(f"{name}_{self.uid}", list(shape), dt), name)

    def dram(self, name, shape, dt=F32, kind="Internal"):
        return T(self.nc.dram_tensor(name, list(shape), dt, kind=kind), name)

    def _collect(self, eng, reads, writes):
        toks = []
        for t in _ts(reads):
            if t.last_w is not None:
                toks.append(t.last_w)
        for t in _ts(writes):
            if t.last_w is not None:
                toks.append(t.last_w)
            toks.extend(t.readers)
        best = {}
        for (kind, key, sem, val) in toks:
            if kind == "eng" and key == eng:
                if eng in ("pe", "sync") or not self.same_engine_sync:
                    continue
            k = id(sem)
            if k not in best or best[k][1] < val:
                best[k] = (sem, val)
        waits = []
        for k, (sem, val) in best.items():
            if self.waited[eng].get(k, 0) >= val:
                continue
            self.waited[eng][k] = val
            waits.append((sem, val))
        return waits

    def _mark(self, tok, reads, writes):
        for t in _ts(reads):
            t.readers.append(tok)
        for t in _ts(writes):
            t.last_w = tok
            t.readers = []

    def op(self, eng, fn, reads, writes, sig=True):
        waits = self._collect(eng, reads, writes)
        if sig:
            self.cnt[eng] += 1
            tok = ("eng", eng, self.sem[eng], self.cnt[eng])
            self.unsig[eng] = False
        else:
            tok = ("eng", eng, self.sem[eng], self.cnt[eng] + 1)
            self.unsig[eng] = True
        self.q[eng].append((fn, waits, (self.sem[eng], 1) if sig else None))
        self._mark(tok, reads, writes)
        self.n_inst += 1

    def dma(self, qn, out, in_, extra_reads=(), extra_writes=(), **kw):
        eng = qn
        i = self.drr[qn]
        self.drr[qn] = (i + 1) % len(self.dsem[qn])
        sem = self.dsem[qn][i]
        reads = [in_] + list(extra_reads)
        writes = [out] + list(extra_writes)
        waits = self._collect(eng, reads, writes)
        prev = self.dcnt[qn][i]
        if prev > 0 and self.waited[eng].get(id(sem), 0) < prev:
            self.waited[eng][id(sem)] = prev
            waits.append((sem, prev))
        self.dcnt[qn][i] += 16
        tok = ("dma", qn, sem, self.dcnt[qn][i])
        o, a = _ap(out), _ap(in_)
        self.q[eng].append((lambda e: e.dma_start(out=o, in_=a, **kw), waits, (sem, 16)))
        self._mark(tok, reads, writes)
        self.n_inst += 1
        return tok

    def mm(self, out, lhsT, rhs, start=True, stop=True, sig=None, f32=False):
        if sig is None:
            sig = stop
        o, l, r = _ap(out), _ap(lhsT), _ap(rhs)
        if f32:
            if l.dtype == F32R:
                l = l.bitcast(F32)
            if r.dtype == F32R:
                r = r.bitcast(F32)
        self.op("pe", lambda e: e.matmul(o, l, r, start=start, stop=stop), [lhsT, rhs], [out], sig=sig)

    def tr(self, out, in_, ident, sig=True):
        o, i, d = _ap(out), _ap(in_), _ap(ident)
        self.op("pe", lambda e: e.transpose(o, i, d), [in_, ident], [out], sig=sig)

    def act(self, out, in_, func, bias=None, scale=1.0, accum=None, eng="act"):
        o, i = _ap(out), _ap(in_)
        kw = {}
        reads = [in_]
        writes = [out]
        if bias is not None:
            kw["bias"] = _ap(bias)
            reads.append(bias)
        kw["scale"] = _ap(scale)
        if isinstance(scale, V):
            reads.append(scale)
        if accum is not None:
            kw["accum_out"] = _ap(accum)
            writes.append(accum)
        self.op("act", lambda e: e.activation(o, i, func, **kw), reads, writes)

    def tt(self, eng, out, in0, in1, op):
        o, a, b = _ap(out), _ap(in0), _ap(in1)
        self.op(eng, lambda e: e.tensor_tensor(o, a, b, op), [in0, in1], [out])

    def ts(self, eng, out, in0, s1, op0, s2=None, op1=None, accum=None):
        o, a = _ap(out), _ap(in0)
        reads = [in0] + [s for s in (s1, s2) if isinstance(s, V)]
        writes = [out] + ([accum] if accum is not None else [])
        kw = {}
        if op1 is not None:
            kw["op1"] = op1
        if accum is not None:
            kw["accum_out"] = _ap(accum)
        self.op(eng, lambda e: e.tensor_scalar(o, a, _ap(s1), _ap(s2) if s2 is not None else None, op0, **kw), reads, writes)

    def stt(self, eng, out, in0, scalar, in1, op0, op1):
        o, a, b = _ap(out), _ap(in0), _ap(in1)
        reads = [in0, in1] + ([scalar] if isinstance(scalar, V) else [])
        self.op(eng, lambda e: e.scalar_tensor_tensor(o, a, _ap(scalar), b, op0, op1), reads, [out])

    def red(self, eng, out, in_, op, axis=AX.X):
        o, a = _ap(out), _ap(in_)
        self.op(eng, lambda e: e.tensor_reduce(o, a, axis, op), [in_], [out])

    def copy(self, eng, out, in_):
        o, a = _ap(out), _ap(in_)
        if eng == "act":
            self.op(eng, lambda e: e.copy(o, a), [in_], [out])
        else:
            self.op(eng, lambda e: e.tensor_copy(o, a), [in_], [out])

    def memset(self, eng, out, val):
        o = _ap(out)
        self.op(eng, lambda e: e.memset(o, val), [], [out])

    def scan(self, out, d0, d1, init, op0=ALU.mult, op1=ALU.add):
        o, a, b, i = _ap(out), _ap(d0), _ap(d1), _ap(init)
        reads = [d0, d1] + ([init] if isinstance(init, V) else [])
        self.op("dve", lambda e: e.tensor_tensor_scan(o, a, b, i, op0, op1), reads, [out])

    def recip(self, out, in_):
        o, a = _ap(out), _ap(in_)
        self.op("dve", lambda e: e.reciprocal(o, a), [in_], [out])

    def finish(self, final_tiles):
        nc = self.nc
        toks = []
        for t in final_tiles:
            if t.last_w is not None:
                toks.append(t.last_w)
        fin = []
        best = {}
        for (_, _, sem, val) in toks:
            if id(sem) not in best or best[id(sem)][1] < val:
                best[id(sem)] = (sem, val)
        for qn in self.dsem:
            for s, c in zip(self.dsem[qn], self.dcnt[qn]):
                if c > 0:
                    best[id(s)] = (s, max(c, best.get(id(s), (s, 0))[1]))
        for e in ("pe", "act", "dve", "pool"):
            if self.cnt[e] > 0 or self.unsig[e]:
                assert not self.unsig[e], f"engine {e} ends with unsignaled instruction"
                best[id(self.sem[e])] = (self.sem[e], self.cnt[e])
        fin = list(best.values())
        q = self.q
        with nc.Block() as block:
            def replay(lst):
                def f(e):
                    for (fn, waits, inc) in lst:
                        for (sem, val) in waits:
                            e.wait_ge(sem, val)
                        if fn is None:
                            continue
                        ins = fn(e)
                        if inc is not None:
                            ins.then_inc(inc[0], inc[1])
                return f

            @block.tensor
            def _(e):
                replay(q["pe"])(e)

            @block.scalar
            def _(e):
                replay(q["act"])(e)

            @block.vector
            def _(e):
                replay(q["dve"])(e)

            @block.gpsimd
            def _(e):
                replay(q["pool"])(e)

            @block.sync
            def _(e):
                replay(q["sync"])(e)
                for (sem, val) in fin:
                    e.wait_ge(sem, val)
        return nc

import math

NCH = 34
TOK = NCH * 128
LSEQ = 4352
D = 1024
DFF = 2816
NFT = 22
GN_EPS = 64e-5
NEGC = -math.exp(-0.5)
NST = 16
NQB = 32


def prow(n):
    return n * 128 + (1 if n < 2 else 3)


class Ctx:
    pass


def setup_consts(S, ident_d):
    idf = S.sb([128, 128], F32, "idf")
    idb = S.sb([128, 128], BF16, "idb")
    S.dma("sync", idf[:], ident_d)
    S.copy("dve", idb[:], idf[:])
    return idf, idb


def mod_derive(S, modT, ngT):
    G = S.sb([128, 3, 8, 2], F32, "G")
    SH = S.sb([128, 3, 8, 2], F32, "SH")
    GATE = S.sb([128, 3, 8, 2], F32, "GATE")
    for i in range(3):
        for j in range(2):
            S.stt("dve", G[:, i, :, j], modT[:, (3 * i + 1) * 8:(3 * i + 2) * 8, j], 1.0, ngT[:, i, :], ALU.add, ALU.mult)
            S.copy("dve", SH[:, i, :, j], modT[:, (3 * i) * 8:(3 * i + 1) * 8, j])
            S.copy("dve", GATE[:, i, :, j], modT[:, (3 * i + 2) * 8:(3 * i + 3) * 8, j])
    return dict(G=G, SH=SH, GATE=GATE, modT=modT)


def mod_vectors(S, cT_d, modw_d, modbT_d, ngT_d, wst, pm):
    cT = S.sb([128, 8, 2], F32, "cT")
    sc = S.sb([128, 8, 2], F32, "sc")
    S.dma("sync", cT[:], cT_d)
    S.act(sc[:], cT[:], AF.Silu)
    modbT = S.sb([128, 72], F32, "modbT")
    S.dma("sync", modbT[:], modbT_d)
    ngT = S.sb([128, 3, 8], F32, "ngT")
    S.dma("sync", ngT[:], ngT_d)
    modT = S.sb([128, 72, 2], F32, "modT")
    for nb in range(36):
        w = wst[nb % 2]
        S.dma("sync", w[:, 0:4, :], modw_d[nb, :, 0:4, :])
        S.dma("pool", w[:, 4:8, :], modw_d[nb, :, 4:8, :])
        for ct in range(2):
            t = nb * 2 + ct
            for k in range(8):
                S.mm(pm[:, t * 2:t * 2 + 2], w[:, k, ct * 128:(ct + 1) * 128], sc[:, k, :], start=(k == 0), stop=(k == 7),
                     sig=(k == 7 and t % 2 == 1))
    for j in range(2):
        S.tt("dve", modT[:, :, j], pm[:, 0:144].re("p (t j) -> p t j", j=2)[:, :, j], modbT[:], ALU.add)
    return mod_derive(S, modT, ngT)


def gate_bcast(S, gate_col, idf, ones, ps, scale, name):
    out = S.sb([128, 1024], F32, name)
    dg = S.sb([128, 128], F32, name + "_dg")
    for k in range(8):
        S.ts("dve", dg[:], idf[:], gate_col[:, k:k + 1], ALU.mult)
        S.mm(ps[:, (k % 4) * 128:(k % 4 + 1) * 128], ones[:], dg[:], start=True, stop=True)
        S.ts("dve", out[:, k * 128:(k + 1) * 128], ps[:, (k % 4) * 128:(k % 4 + 1) * 128], float(scale), ALU.mult)
    return out


def norm_to_hT(S, C, xt, hT, col0, G, SH):
    ss = C.small[C.si % 4]; C.si += 1
    S.act(C.junk[:], xt, AF.Square, accum=ss[:, 0:1])
    S.ts("dve", ss[:, 1:2], ss[:, 0:1], 1.0 / D, ALU.mult, 1e-6, ALU.add)
    S.act(ss[:, 3:4], ss[:, 1:2], AF.Sqrt)
    S.recip(ss[:, 2:3], ss[:, 3:4])
    xn = C.xn[C.xi % 2]; C.xi += 1
    S.act(xn[:], xt, AF.Copy, scale=ss[:, 2:3])
    pt = C.pt[C.pti % 2]; C.pti += 1
    for k in range(8):
        S.tr(pt[:, k * 128:(k + 1) * 128], xn[:, k * 128:(k + 1) * 128], C.idb[:], sig=(k == 7))
    for k in range(8):
        S.ts("dve", hT[:, k, col0:col0 + 128], pt[:, k * 128:(k + 1) * 128], G[:, k:k + 1], ALU.mult, SH[:, k:k + 1], ALU.add)


def ffn(S, C, xs, xd, w1_d, w2_d, mv, ni, groups, jf, after_chunk=None):
    G, SH = mv["G"], mv["SH"]
    w2b = C.w2b
    first = True
    for grp in groups:
        nt = len(grp) * 128
        for li, ci in enumerate(grp):
            xt = C.xt[C.xti % 3]; C.xti += 1
            S.dma("sync", xt[:], xs(ci))
            j = jf(ci)
            norm_to_hT(S, C, xt[:], C.hT, li * 128, G[:, ni, :, j], SH[:, ni, :, j])
        for ft in range(NFT):
            wst = C.wst[ft % 2]
            wb = C.w1b[ft % 2]
            S.dma("sync", wst[:, 0:4, :], w1_d[ft, :, 0:4, :])
            S.dma("sync", wst[:, 4:8, :], w1_d[ft, :, 4:8, :])
            S.copy("act", wb[:, 0:4, :], wst[:, 0:4, :]); S.copy("dve", wb[:, 4:8, :], wst[:, 4:8, :])
            if first:
                w2s = C.w2st[ft % 2]
                S.dma("sync", w2s[:], w2_d[ft * 128:(ft + 1) * 128, :])
                S.copy("act", w2b[:, ft, :], w2s[:])
            for b0 in range(0, nt, 512):
                bw = min(512, nt - b0)
                pg = C.pa[C.pai % 4]; C.pai += 1
                pu = C.pa[C.pai % 4]; C.pai += 1
                for k in range(8):
                    S.mm(pg[:, 0:bw], wb[:, k, 0:128], C.hT[:, k, b0:b0 + bw], start=(k == 0), stop=(k == 7))
                for k in range(8):
                    S.mm(pu[:, 0:bw], wb[:, k, 128:256], C.hT[:, k, b0:b0 + bw], start=(k == 0), stop=(k == 7))
                sg = C.sg[C.sgi % 2]; C.sgi += 1
                S.act(sg[:, 0:bw], pg[:, 0:bw], AF.Silu)
                S.tt("dve", C.actT[:, ft, b0:b0 + bw], sg[:, 0:bw], pu[:, 0:bw], ALU.mult)
        first = False
        for li, ci in enumerate(grp):
            xt = C.xt[C.xti % 3]; C.xti += 1
            S.dma("sync", xt[:], xs(ci))
            gb = C.gateb[(ni, jf(ci))]
            for h in range(2):
                pc = C.pcs[h]
                for ft in range(NFT):
                    S.mm(pc[:], C.actT[:, ft, li * 128:(li + 1) * 128], w2b[:, ft, h * 512:(h + 1) * 512],
                         start=(ft == 0), stop=(ft == NFT - 1))
                S.tt("dve", C.tmp[:, h * 512:(h + 1) * 512], pc[:], gb[:, h * 512:(h + 1) * 512], ALU.mult)
            S.tt("dve", xt[:], xt[:], C.tmp[:], ALU.add)
            if xd is not None:
                S.dma("pool", xd(ci), xt[:])
            if after_chunk is not None:
                after_chunk(ci, li, xt)
        yield grp


def alloc_common(S, PS):
    C = Ctx()
    C.small = [S.sb([128, 4], F32, f"small{i}") for i in range(4)]; C.si = 0
    C.junk = S.sb([128, 1024], BF16, "junk")
    C.xn = [S.sb([128, 1024], BF16, f"xn{i}") for i in range(2)]; C.xi = 0
    C.pt = PS["b"]; C.pti = 0
    C.pa = PS["g"][0:4]; C.pai = 0
    C.pcs = PS["g"][4:6]
    C.xt = [S.sb([128, 1024], F32, f"xt{i}") for i in range(3)]; C.xti = 0
    C.hT = S.sb([128, 8, 1152], BF16, "hT")
    C.actT = S.sb([128, NFT, 1152], BF16, "actT")
    C.w2b = S.sb([128, NFT, 1024], BF16, "w2b")
    C.wst = [S.sb([128, 8, 256], F32, f"wst{i}") for i in range(2)]
    C.w1b = [S.sb([128, 8, 256], BF16, f"w1b{i}") for i in range(2)]
    C.w2st = [S.sb([128, 1024], F32, f"w2st{i}") for i in range(2)]
    C.sg = [S.sb([128, 512], F32, f"sg{i}") for i in range(2)]; C.sgi = 0
    C.tmp = S.sb([128, 1024], F32, "tmp")
    C.ones = S.sb([128, 128], F32, "ones")
    S.memset("dve", C.ones[:], 1.0)
    return C


GROUPS34 = [list(range(0, 9)), list(range(9, 18)), list(range(18, 26)), list(range(26, 34))]
GROUPS32 = [list(range(0, 8)), list(range(8, 16)), list(range(16, 24)), list(range(24, 32))]
JF34 = lambda ci: 0 if ci < 2 else 1


def phase1(S, PS, IN, SC):
    C = alloc_common(S, PS)
    C.idf, C.idb = setup_consts(S, IN["ident"][:])
    mv = mod_vectors(S, IN["cT"][:], IN["modw"][0], IN["modbT"][0], IN["ngT"][0], C.wst, C.pa[0])
    S.dma("sync", SC["modT0"][:], mv["modT"][:].re("p t j -> p (t j)"))
    C.gateb = {}
    for j in range(2):
        C.gateb[(0, j)] = gate_bcast(S, mv["GATE"][:, 0, :, j], C.idf, C.ones, C.pa[1 + j], 0.5, f"gb0{j}")
    zt = S.sb([2, 2304], F32, "zt")
    S.memset("dve", zt[:], 0.0)
    ppad = SC["ppad"]
    S.dma("sync", ppad[0:1, :], zt[0:1, :]); S.dma("sync", ppad[257:259, :], zt[0:2, :]); S.dma("sync", ppad[4355:4356, :], zt[0:1, :])
    pst = [S.sb([128, 256], F32, f"pst{i}") for i in range(2)]
    psti = [0]

    def xs(ci):
        return IN["ctx"][ci * 128:(ci + 1) * 128, :] if ci < 2 else IN["x"][(ci - 2) * 128:(ci - 1) * 128, :]

    def xd(ci):
        return SC["x1"][ci * 128:(ci + 1) * 128, :]

    def after(ci, li, xt):
        j = JF34(ci)
        norm_to_hT(S, C, xt[:], C.hT, li * 128, mv["G"][:, 1, :, j], mv["SH"][:, 1, :, j])

    win = IN["win_ab"]
    for grp in ffn(S, C, xs, xd, IN["w1"][0, 0], IN["w2"][0, 0], mv, 0, GROUPS34, JF34, after_chunk=after):
        for cb in range(9):
            wst = C.wst[cb % 2]; wb = C.w1b[cb % 2]
            S.dma("sync", wst[:, 0:4, :], win[cb, :, 0:4, :])
            S.dma("sync", wst[:, 4:8, :], win[cb, :, 4:8, :])
            S.copy("act", wb[:, 0:4, :], wst[:, 0:4, :]); S.copy("dve", wb[:, 4:8, :], wst[:, 4:8, :])
            for li, ci in enumerate(grp):
                pp = C.pa[C.pai % 4]; C.pai += 1
                for k in range(8):
                    S.mm(pp[:, 0:256], C.hT[:, k, li * 128:(li + 1) * 128], wb[:, k, :], start=(k == 0), stop=(k == 7))
                st = pst[psti[0] % 2]; psti[0] += 1
                S.copy("act", st[:], pp[:, 0:256])
                S.dma("sync", ppad[prow(ci):prow(ci) + 128, cb * 256:(cb + 1) * 256], st[:])


def phase2a(S, PS, IN, SC, MD=F32R, nblocks=NCH):
    ppad = SC["ppad"]

    def ld(dv, shape, nm, dt=F32):
        t = S.sb(shape, dt, nm)
        S.dma("sync", t[:], dv)
        return t
    mu0 = ld(IN["mub"][0], [128, 1536], "mu0"); mu1 = ld(IN["mub"][1], [128, 1536], "mu1")
    c0 = S.sb([128, 1536], F32, "c0")
    S.tt("dve", c0[:], mu0[:], mu1[:], ALU.add)
    S.ts("dve", c0[:], c0[:], -1.0, ALU.mult, 1.0, ALU.add)
    kkb = ld(IN["kkb"][:], [128, 512], "kkb"); kab = ld(IN["kab"][:], [128, 512], "kab")
    omka = S.sb([128, 512], F32, "omka")
    S.ts("dve", omka[:], kab[:], -1.0, ALU.mult, 1.0, ALU.add)
    g2 = ld(IN["g2"][:], [128, 512], "g2")
    msk = IN["msk"]
    mUs = ld(msk[0], [128, 128], "mUs"); mUi = ld(msk[1], [128, 128], "mUi"); mLs = ld(msk[2], [128, 128], "mLs")
    idf = ld(msk[3], [128, 128], "idf"); mLi = ld(msk[4], [128, 128], "mLi")
    cm = {}
    for nm, m_ in (("Ui", mUi), ("Us", mUs), ("Ls", mLs), ("Li", mLi)):
        cm[nm] = S.sb([128, 128], MD, "c" + nm)
        S.ts("dve", cm[nm][:], m_[:], NEGC, ALU.mult)
    negc = S.sb([128, 2], MD, "negc")
    S.ts("dve", negc[:], mUi[:, 0:2], 0.0, ALU.mult, NEGC, ALU.add)
    idm = S.sb([128, 128], MD, "idm")
    S.copy("dve", idm[:], idf[:])
    w2a = []; a2a = []
    for dr in range(2):
        t_ = ld(IN["w2a"][dr], [65, 512], f"w2a{dr}"); t2_ = S.sb([65, 512], MD, f"w2ar{dr}"); S.copy("dve", t2_[:], t_[:]); w2a.append(t2_)
        t_ = ld(IN["a2a"][dr], [65, 512], f"a2a{dr}"); t2_ = S.sb([65, 512], MD, f"a2ar{dr}"); S.copy("dve", t2_[:], t_[:]); a2a.append(t2_)
    g2r = S.sb([128, 512], MD, "g2r"); S.copy("dve", g2r[:], g2[:]); g2 = g2r

    pb = PS["g"]
    pbi = [0]

    def bank():
        b = pb[pbi[0] % 6]; pbi[0] += 1
        return b

    def t512(nm, dt=F32):
        return S.sb([128, 512], dt, nm)

    rc = S.sb([128, 1536], F32, "rc"); rp = S.sb([128, 1536], F32, "rp"); rn_ = S.sb([128, 1536], F32, "rn")
    mix = S.sb([128, 1536], MD, "mix"); mt = S.sb([128, 1536], F32, "mt")
    TW = S.sb([65, 128], MD, "TW"); AL = S.sb([65, 128], MD, "AL"); SG = S.sb([128, 128], MD, "SG")
    S.ts("dve", TW[:], mUi[0:65, :], 0.0, ALU.mult, 1.0, ALU.add); S.ts("dve", AL[:], mUi[0:65, :], 0.0, ALU.mult, 1.0, ALU.add)
    lmix = S.sb([128, 256], MD, "lmix"); lmt = S.sb([128, 256], F32, "lmt")
    lc = S.sb([128, 256], F32, "lc"); lp = S.sb([128, 256], F32, "lp"); ln_ = S.sb([128, 256], F32, "ln")
    mul0 = ld(IN["mulb"][0], [128, 256], "mul0"); mul1 = ld(IN["mulb"][1], [128, 256], "mul1")
    c0lb = S.sb([128, 256], F32, "c0lb")
    S.tt("dve", c0lb[:], mul0[:], mul1[:], ALU.add)
    S.ts("dve", c0lb[:], c0lb[:], -1.0, ALU.mult, 1.0, ALU.add)
    sig = t512("sig", MD); a_ = t512("a"); gt = t512("gt")
    kk = t512("kk"); sq = t512("sq"); ss = S.sb([128, 8], F32, "ss"); rn8 = S.sb([128, 8], F32, "rn8")
    kd = t512("kd"); tq = t512("tq"); bq = t512("bq")
    Gc = t512("G"); Gp = t512("Gp"); Gi = t512("Gi"); Ge = t512("Ge")
    A = t512("A", MD); B = t512("B", MD); K = t512("K", MD); Rq = t512("Rq", MD)
    B2m = t512("B2m", MD); K2m = t512("K2m", MD)
    AT = S.sb([64, 8, 128], MD, "AT"); BT = S.sb([64, 8, 128], MD, "BT"); KT = S.sb([64, 8, 128], MD, "KT")
    RT = S.sb([64, 8, 128], MD, "RT")
    mat = lambda nm, dt=MD: [S.sb([128, 4, 128], dt, f"{nm}{g}") for g in range(2)]
    Nm = [mat("Nm0"), mat("Nm1")]; NT = [mat("NT0"), mat("NT1")]
    Mak = mat("Mak"); Mbr = mat("Mbr"); Mkr = mat("Mkr")
    Tf = mat("Tf", MD); Tm = Tf
    WTm = t512("WTm", MD); X1Tm = t512("X1Tm", MD); UlTm = t512("UlTm", MD)
    Rpf = S.sb([64, 8, 128], MD, "Rpf")
    gC = S.sb([64, 16], F32, "gC")
    dgG = S.sb([64, 8, 64], F32, "dgG")
    Pf = [S.sb([64, 8, 64], MD, f"Pf{c}") for c in range(2)]
    QTf = [S.sb([64, 8, 64], F32, f"QTf{c}") for c in range(2)]
    Yloc = t512("Yloc")
    ST = [S.sb([64, 8, 64], MD, f"ST{i}") for i in range(2)]
    yt = [t512(f"yt{i}") for i in range(2)]
    v3 = lambda v: v.re("p (h k) -> p h k", k=64)
    m3 = lambda v: v.re("p (h t) -> p h t", t=128)
    sti = 0
    for dr in range(2):
        if dr == 0:
            m_strict, m_strictT, m_incl = mUs, mLs, mUi
            c_incl, c_strict, c_end = cm["Ui"], cm["Us"], cm["Ls"]
            border = list(range(nblocks)); corder = [0, 1]
        else:
            m_strict, m_strictT, m_incl = mLs, mUs, mLi
            c_incl, c_strict, c_end = cm["Li"], cm["Ls"], cm["Us"]
            border = [1, 0] + list(range(NCH - 1, 1, -1)); corder = [1, 0]
            border = border[:nblocks]
        S.ts("dve", ST[sti % 2][:].re("p h k -> p (h k)"), kkb[0:64, :], 0.0, ALU.mult)
        for n in border:
            r0 = prow(n)
            S.dma("sync", rc[:], ppad[r0:r0 + 128, 0:1536])
            S.dma("pool", rp[:], ppad[r0 - 1:r0 + 127, 0:1536])
            S.dma("sync", rn_[:], ppad[r0 + 1:r0 + 129, 0:1536])
            S.dma("pool", lc[:], ppad[r0:r0 + 128, 1536:1792])
            S.dma("pool", lp[:], ppad[r0 - 1:r0 + 127, 1536:1792])
            S.dma("sync", ln_[:], ppad[r0 + 1:r0 + 129, 1536:1792])
            S.tt("dve", mix[:], rc[:], c0[:], ALU.mult)
            S.tt("dve", mt[:], rp[:], mu0[:], ALU.mult)
            S.tt("dve", mix[:], mix[:], mt[:], ALU.add)
            S.tt("dve", mt[:], rn_[:], mu1[:], ALU.mult)
            S.tt("dve", mix[:], mix[:], mt[:], ALU.add)
            r = mix[:, 0:512]; k = mix[:, 512:1024]; v = mix[:, 1024:1536]
            if dr == 0:
                S.dma("pool", SC["rvo"][n * 128:(n + 1) * 128, 0:512], r)
                S.dma("pool", SC["rvo"][n * 128:(n + 1) * 128, 512:1024], v)
            S.tt("dve", lmix[:], lc[:], c0lb[:], ALU.mult)
            S.tt("dve", lmt[:], lp[:], mul0[:], ALU.mult)
            S.tt("dve", lmix[:], lmix[:], lmt[:], ALU.add)
            S.tt("dve", lmt[:], ln_[:], mul1[:], ALU.mult)
            S.tt("dve", lmix[:], lmix[:], lmt[:], ALU.add)
            pl = bank()
            S.mm(pl[0:64, 0:128], lmix[:, 0:64], idm[:], sig=False, f32=True)
            S.mm(pl[0:64, 128:256], lmix[:, 64:128], idm[:], sig=False, f32=True)
            S.mm(pl[:, 256:384], lmix[:, 128:256], idm[:])
            S.act(TW[0:64, :], pl[0:64, 0:128], AF.Tanh)
            S.copy("act", AL[0:64, :], pl[0:64, 128:256])
            S.act(SG[:], pl[:, 256:384], AF.Sigmoid)
            pw_ = bank(); S.mm(pw_[:], TW[:], w2a[dr][:])
            S.act(sig[:], pw_[:], AF.Sigmoid)
            pa_ = bank(); S.mm(pa_[:], AL[:], a2a[dr][:])
            S.act(a_[:], pa_[:], AF.Sigmoid)
            if dr == 0:
                pg_ = bank(); S.mm(pg_[:], SG[:], g2[:])
                S.copy("act", gt[:], pg_[:])
                S.dma("pool", SC["gto"][n * 128:(n + 1) * 128, :], gt[:])
            S.tt("dve", kk[:], k, kkb[:], ALU.mult)
            S.tt("dve", sq[:], kk[:], kk[:], ALU.mult)
            S.red("dve", ss[:], v3(sq[:]), ALU.add)
            S.ts("dve", ss[:], ss[:], 1e-12, ALU.max)
            S.act(ss[:], ss[:], AF.Sqrt)
            S.recip(rn8[:], ss[:])
            S.tt("dve", v3(kk[:]), v3(kk[:]), rn8[:].re("p (h o) -> p h o", o=1).bc([128, 8, 64]), ALU.mult)
            S.tt("dve", tq[:], a_[:], kab[:], ALU.mult)
            S.tt("dve", tq[:], tq[:], omka[:], ALU.add)
            S.tt("dve", kd[:], k, tq[:], ALU.mult)
            S.dma("pool", SC["kdo"][dr, n * 128:(n + 1) * 128, :], kd[:])
            S.tt("dve", bq[:], kk[:], a_[:], ALU.mult)
            pc1 = bank(); S.mm(pc1[:], c_incl[:], sig[:])
            pc2 = bank(); S.mm(pc2[:], c_strict[:], sig[:])
            pc3 = bank(); S.mm(pc3[:], c_end[:], sig[:])
            S.act(Gc[:], pc1[:], AF.Exp)
            S.act(Gi[:], pc1[:], AF.Exp, scale=-1.0)
            S.act(Gp[:], pc2[:], AF.Exp)
            S.act(Ge[:], pc3[:], AF.Exp)
            S.stt("dve", A[:], kk[:], -1.0, Gp[:], ALU.mult, ALU.mult)
            S.tt("dve", B[:], bq[:], Gi[:], ALU.mult)
            S.tt("dve", K[:], kd[:], Gi[:], ALU.mult)
            S.tt("dve", Rq[:], r, Gc[:], ALU.mult)
            S.tt("dve", B2m[:], bq[:], Ge[:], ALU.mult)
            S.tt("dve", K2m[:], kd[:], Ge[:], ALU.mult)
            Am_ = A
            Vv = v
            for (src, dst) in ((A, AT), (B, BT), (K, KT), (Rq, RT)):
                for hg in range(2):
                    p_ = bank()
                    for hh in range(4):
                        h = hg * 4 + hh
                        S.mm(p_[0:64, hh * 128:(hh + 1) * 128], src[:, h * 64:(h + 1) * 64], idm[:], sig=(hh == 3), f32=True)
                    S.copy("act", dst[:, hg * 4:(hg + 1) * 4, :], m3(p_[0:64, :]))
            RTf_ = RT

            def mmat(dst, LT, RTt, mask, hg):
                p_ = bank()
                for hh in range(4):
                    h = hg * 4 + hh
                    S.mm(p_[:, hh * 128:(hh + 1) * 128], LT[:, h, :], RTt[:, h, :], sig=(hh == 3))
                S.tt("dve", dst[hg][:], m3(p_[:]), mask[:].re("p (o t) -> p o t", o=1).bc([128, 4, 128]), ALU.mult)
            for hg in range(2):
                mmat(Nm[0], BT, AT, m_strict, hg)
                mmat(NT[0], AT, BT, m_strictT, hg)
                mmat(Mak, KT, AT, m_strict, hg)
                mmat(Mbr, BT, RT, m_incl, hg)
                mmat(Mkr, KT, RT, m_incl, hg)
            for hg in range(2):
                S.tt("dve", Tf[hg][:], Nm[0][hg][:], idf[:].re("p (o t) -> p o t", o=1).bc([128, 4, 128]), ALU.add)
            cur = 0
            for lev in range(5):
                nxt = 1 - cur
                last = lev == 4
                for hg in range(2):
                    if not last:
                        p1 = bank()
                        for hh in range(4):
                            S.mm(p1[:, hh * 128:(hh + 1) * 128], NT[cur][hg][:, hh, :], Nm[cur][hg][:, hh, :], sig=(hh == 3))
                        S.copy("act", Nm[nxt][hg][:], m3(p1[:]))
                    p2 = bank()
                    for hh in range(4):
                        S.mm(p2[:, hh * 128:(hh + 1) * 128], Nm[cur][hg][:, hh, :], NT[cur][hg][:, hh, :], sig=(hh == 3))
                    S.copy("dve" if hg == 0 else "act", NT[nxt][hg][:], m3(p2[:]))
                for hg in range(2):
                    p3 = bank()
                    for hh in range(4):
                        S.mm(p3[:, hh * 128:(hh + 1) * 128], NT[nxt][hg][:, hh, :], Tm[hg][:, hh, :], sig=(hh == 3))
                    S.tt("dve", Tf[hg][:], Tf[hg][:], m3(p3[:]), ALU.add)
                cur = nxt
            p_ = bank()
            for h in range(8):
                S.mm(p_[:, h * 64:(h + 1) * 64], Tm[h // 4][:, h % 4, :], Am_[:, h * 64:(h + 1) * 64], sig=(h == 7))
            S.copy("act", WTm[:], p_[:])
            p_ = bank()
            for h in range(8):
                S.mm(p_[:, h * 64:(h + 1) * 64], Mak[h // 4][:, h % 4, :], Vv[:, h * 64:(h + 1) * 64], sig=(h == 7))
            S.copy("dve", X1Tm[:], p_[:])
            p_ = bank()
            for h in range(8):
                S.mm(p_[:, h * 64:(h + 1) * 64], Tm[h // 4][:, h % 4, :], X1Tm[:, h * 64:(h + 1) * 64], sig=(h == 7))
            S.copy("act", UlTm[:], p_[:])
            for hg in range(2):
                p_ = bank()
                for hh in range(4):
                    h = hg * 4 + hh
                    S.mm(p_[0:64, hh * 128:(hh + 1) * 128], WTm[:, h * 64:(h + 1) * 64], Mbr[hg][:, hh, :], sig=(hh == 3), f32=True)
                S.tt("dve", Rpf[:, hg * 4:(hg + 1) * 4, :], m3(p_[0:64, :]), RTf_[:, hg * 4:(hg + 1) * 4, :], ALU.add)
            pgc = [bank(), bank()]
            for c in range(2):
                for h in range(8):
                    S.mm(pgc[c][0:64, h:h + 1], sig[c * 64:(c + 1) * 64, h * 64:(h + 1) * 64], negc[c * 64:(c + 1) * 64, 0:1], sig=(h == 7), f32=True)
                S.act(gC[:, c * 8:(c + 1) * 8], pgc[c][0:64, 0:8], AF.Exp)
            for c in range(2):
                pc = slice(c * 64, (c + 1) * 64)
                p_ = bank()
                for h in range(8):
                    S.mm(p_[0:64, h * 64:(h + 1) * 64], WTm[pc, h * 64:(h + 1) * 64], B2m[pc, h * 64:(h + 1) * 64], sig=(h == 7), f32=True)
                S.tt("dve", dgG[:], idf[0:64, 0:64].re("p (o k) -> p o k", o=1).bc([64, 8, 64]),
                     gC[:, c * 8:(c + 1) * 8].re("p (h o) -> p h o", o=1).bc([64, 8, 64]), ALU.mult)
                S.tt("dve", Pf[c][:], v3(p_[0:64, :]), dgG[:], ALU.add)
                p_ = bank()
                for h in range(8):
                    S.mm(p_[0:64, h * 64:(h + 1) * 64], B2m[pc, h * 64:(h + 1) * 64], UlTm[pc, h * 64:(h + 1) * 64], start=True, stop=False, sig=False, f32=True)
                    S.mm(p_[0:64, h * 64:(h + 1) * 64], K2m[pc, h * 64:(h + 1) * 64], Vv[pc, h * 64:(h + 1) * 64], start=False, stop=True, sig=(h == 7), f32=True)
                S.copy("act", QTf[c][:], v3(p_[0:64, :]))
            p_ = bank()
            for h in range(8):
                S.mm(p_[:, h * 64:(h + 1) * 64], Mbr[h // 4][:, h % 4, :], UlTm[:, h * 64:(h + 1) * 64], start=True, stop=False, sig=False)
                S.mm(p_[:, h * 64:(h + 1) * 64], Mkr[h // 4][:, h % 4, :], Vv[:, h * 64:(h + 1) * 64], start=False, stop=True, sig=(h == 7))
            S.copy("act", Yloc[:], p_[:])
            ytile = yt[n % 2]
            for c in corder:
                pc = slice(c * 64, (c + 1) * 64)
                st_cur = ST[sti % 2]; st_nxt = ST[(sti + 1) % 2]; sti += 1
                p_ = bank()
                for h in range(8):
                    S.mm(p_[:, h * 64:(h + 1) * 64], Rpf[:, h, :], st_cur[:, h, :], sig=(h == 7))
                S.tt("dve", ytile[pc, :], p_[pc, :], Yloc[pc, :], ALU.add)
                p2 = bank()
                for h in range(8):
                    S.mm(p2[0:64, h * 64:(h + 1) * 64], Pf[c][:, h, :], st_cur[:, h, :], sig=(h == 7), f32=True)
                S.tt("dve", st_nxt[:], v3(p2[0:64, :]), QTf[c][:], ALU.add)
            S.dma("sync", SC["yd"][dr, n * 128:(n + 1) * 128, :], ytile[:])


def phase2b(S, PS, IN, SC):
    CH = 128
    ppad = SC["ppad"]
    ident = IN["ident"]
    idf0 = S.sb([128, 128], F32, "idf0"); S.dma("sync", idf0[:], ident[:])
    Jm0 = S.sb([128, 128], F32, "Jm0"); S.dma("sync", Jm0[:], IN["msk"][5])
    idf = S.sb([128, 128], F32R, "idf"); S.copy("dve", idf[:], idf0[:])
    Jm = S.sb([128, 128], F32R, "Jm"); S.copy("dve", Jm[:], Jm0[:])
    pb = PS["g"]
    sm = lambda nm: S.sb([128, NST], F32, nm)
    big = lambda nm, dt=F32: S.sb([128, NST, 32], dt, nm)
    are = sm("are"); aim = sm("aim"); lst = sm("lst")
    bre = big("bre"); bim = big("bim"); cre0 = big("cre0"); cim = big("cim"); ncim = big("ncim", F32R); cre = big("cre", F32R)
    lre = sm("lre"); dt_ = sm("dt"); zr = sm("zr"); th = sm("th"); rho = sm("rho")
    sa = sm("sa"); sk = sm("sk"); sr = sm("sr")
    cs = sm("cs"); sn = sm("sn")
    abre = sm("abre"); abim = sm("abim"); den = sm("den"); rden = sm("rden"); t1 = sm("t1"); t2 = sm("t2")
    fre = sm("fre"); fim = sm("fim"); am1 = sm("am1")
    bbre = big("bbre", F32R); bbim = big("bbim", F32R); u1 = big("u1"); u2 = big("u2")
    BBTre = S.sb([32, NST, 128], F32R, "BBTre"); BBTim = S.sb([32, NST, 128], F32R, "BBTim")
    Ec = S.sb([128, NST, CH], F32, "Ec"); Es = S.sb([128, NST, CH], F32, "Es")
    w1 = S.sb([128, NST, CH // 2], F32, "w1"); w2 = S.sb([128, NST, CH // 2], F32, "w2")
    rhob = S.sb([128, NST, CH], F32, "rhob")
    utok = [S.sb([128, 512], F32, "utok0")] * 2
    utokr = [S.sb([128, 512], F32R, f"utokr{i}") for i in range(2)]
    ut = [S.sb([32, NST, CH], F32R, f"ut{i}") for i in range(2)]
    Zre_ = [S.sb([128, NST, CH], F32, "Zre0")] * 2; Zim_ = [S.sb([128, NST, CH], F32, "Zim0")] * 2
    wre_ = [S.sb([128, NST, CH], F32, f"wre{i}") for i in range(2)]; wim_ = [S.sb([128, NST, CH], F32, f"wim{i}") for i in range(2)]
    xlr = [S.sb([128, NST], F32, f"xlr{i}") for i in range(2)]; xli = [S.sb([128, NST], F32, f"xli{i}") for i in range(2)]
    xl1 = S.sb([128, NST], F32, "xl1"); xl2 = S.sb([128, NST], F32, "xl2")
    xre = [S.sb([128, NST, CH], F32R, f"xre{i}") for i in range(2)]
    xim = [S.sb([128, NST, CH], F32R, f"xim{i}") for i in range(2)]
    ta_ = [[S.sb([128, 4, CH], F32, f"ta{q}{i}") for i in range(4)] for q in range(2)]
    tb_ = [[S.sb([128, 4, CH], F32, f"tb0{i}") for i in range(4)]] * 2
    pbi = 0
    tai = 0
    yst = [S.sb([128, 512], F32R, f"yst{i}") for i in range(2)]
    yst2 = [S.sb([128, 512], F32, f"ystb{i}") for i in range(2)]
    MAGIC = 12582912.0
    TWO_PI = 2.0 * math.pi
    pbi = 0
    cc = 0
    for dr in range(2):
        for (t_, nm) in ((are, "are"), (aim, "aim"), (lst, "lst")):
            S.dma("sync", t_[:], IN[nm][dr])
        for (t_, nm) in ((bre, "bre"), (bim, "bim"), (cre0, "cre"), (cim, "cim")):
            S.dma("sync", t_[:], IN[nm][dr])
        S.ts("dve", ncim[:], cim[:], -1.0, ALU.mult)
        S.copy("dve", cre[:], cre0[:])
        S.ts("dve", lre[:], are[:], -1e-4, ALU.min)
        S.act(dt_[:], lst[:], AF.Exp)
        S.tt("dve", zr[:], lre[:], dt_[:], ALU.mult)
        S.tt("dve", th[:], aim[:], dt_[:], ALU.mult)
        S.act(rho[:], zr[:], AF.Exp)

        def sin_reduced(out, ang, shift):
            S.ts("dve", sa[:], ang[:], float(shift), ALU.add)
            S.ts("dve", sk[:], sa[:], 1.0 / TWO_PI, ALU.mult, MAGIC, ALU.add)
            S.ts("dve", sk[:], sk[:], MAGIC, ALU.subtract)
            S.stt("dve", sr[:], sk[:], -TWO_PI, sa[:], ALU.mult, ALU.add)
            S.ts("dve", sr[:], sr[:], 3.14159, ALU.min, -3.14159, ALU.max)
            S.act(out, sr[:], AF.Sin)
        sin_reduced(sn[:], th, 0.0)
        sin_reduced(cs[:], th, math.pi / 2)
        S.tt("dve", abre[:], rho[:], cs[:], ALU.mult)
        S.tt("dve", abim[:], rho[:], sn[:], ALU.mult)
        S.tt("dve", t1[:], lre[:], lre[:], ALU.mult)
        S.tt("dve", t2[:], aim[:], aim[:], ALU.mult)
        S.tt("dve", den[:], t1[:], t2[:], ALU.add)
        S.recip(rden[:], den[:])
        S.ts("dve", am1[:], abre[:], -1.0, ALU.add)
        S.tt("dve", t1[:], am1[:], lre[:], ALU.mult)
        S.tt("dve", t2[:], abim[:], aim[:], ALU.mult)
        S.tt("dve", t1[:], t1[:], t2[:], ALU.add)
        S.tt("dve", fre[:], t1[:], rden[:], ALU.mult)
        S.tt("dve", t1[:], abim[:], lre[:], ALU.mult)
        S.tt("dve", t2[:], am1[:], aim[:], ALU.mult)
        S.tt("dve", t1[:], t1[:], t2[:], ALU.subtract)
        S.tt("dve", fim[:], t1[:], rden[:], ALU.mult)
        fre_b = fre[:].re("p (t o) -> p t o", o=1).bc([128, NST, 32])
        fim_b = fim[:].re("p (t o) -> p t o", o=1).bc([128, NST, 32])
        S.tt("dve", u1[:], bre[:], fre_b, ALU.mult)
        S.tt("dve", u2[:], bim[:], fim_b, ALU.mult)
        S.tt("dve", bbre[:], u1[:], u2[:], ALU.subtract)
        S.tt("dve", u1[:], bim[:], fre_b, ALU.mult)
        S.tt("dve", u2[:], bre[:], fim_b, ALU.mult)
        S.tt("dve", bbim[:], u1[:], u2[:], ALU.add)
        for (src, dst) in ((bbre, BBTre), (bbim, BBTim)):
            for g4 in range(4):
                p_ = pb[pbi % 6]; pbi += 1
                for jj in range(4):
                    j = g4 * 4 + jj
                    S.mm(p_[0:32, jj * 128:(jj + 1) * 128], src[:, j, :], idf[:], sig=(jj == 3), f32=True)
                S.copy("dve", dst[:, g4 * 4:(g4 + 1) * 4, :], p_[0:32, :].re("p (a b) -> p a b", b=128))
        S.copy("dve", Ec[:, :, 0], cs[:])
        S.copy("dve", Es[:, :, 0], sn[:])
        m = 1
        while m < CH:
            cb = Ec[:, :, m - 1:m].bc([128, NST, m]); sb_ = Es[:, :, m - 1:m].bc([128, NST, m])
            S.tt("dve", w1[:, :, 0:m], Ec[:, :, 0:m], cb, ALU.mult)
            S.tt("dve", w2[:, :, 0:m], Es[:, :, 0:m], sb_, ALU.mult)
            S.tt("dve", Ec[:, :, m:2 * m], w1[:, :, 0:m], w2[:, :, 0:m], ALU.subtract)
            S.tt("dve", w1[:, :, 0:m], Ec[:, :, 0:m], sb_, ALU.mult)
            S.tt("dve", w2[:, :, 0:m], Es[:, :, 0:m], cb, ALU.mult)
            S.tt("dve", Es[:, :, m:2 * m], w1[:, :, 0:m], w2[:, :, 0:m], ALU.add)
            m *= 2
        S.copy("dve", rhob[:], rho[:].re("p (t o) -> p t o", o=1).bc([128, NST, CH]))
        Pm = idf if dr == 0 else Jm
        border = list(range(NCH)) if dr == 0 else [1, 0] + list(range(NCH - 1, 1, -1))
        def stageA(ci, n, cc):
            nonlocal pbi, tai
            r0 = prow(n)
            utk0 = utok[cc % 2]
            S.dma("sync", utk0[:], ppad[r0:r0 + 128, 1792:2304])
            utk = utokr[cc % 2]
            S.copy("act", utk[:], utk0[:])
            u = ut[cc % 2]
            for g4 in range(4):
                p_ = pb[pbi % 6]; pbi += 1
                for jj in range(4):
                    j = g4 * 4 + jj
                    S.mm(p_[0:32, jj * 128:(jj + 1) * 128], utk[:, j * 32:(j + 1) * 32], Pm[:], sig=(jj == 3), f32=True)
                S.copy("act", u[:, g4 * 4:(g4 + 1) * 4, :], p_[0:32, :].re("p (a b) -> p a b", b=128))
            Zre, Zim = Zre_[cc % 2], Zim_[cc % 2]
            for g4 in range(4):
                ta = ta_[tai % 2]; tai += 1
                pr = pb[pbi % 6]; pbi += 1
                pi_ = pb[pbi % 6]; pbi += 1
                for jj in range(4):
                    j = g4 * 4 + jj
                    S.mm(pr[:, jj * CH:(jj + 1) * CH], BBTre[:, j, :], u[:, j, :], sig=False)
                for jj in range(4):
                    j = g4 * 4 + jj
                    S.mm(pi_[:, jj * CH:(jj + 1) * CH], BBTim[:, j, :], u[:, j, :], sig=(jj == 3))
                sl = slice(g4 * 4, (g4 + 1) * 4)
                prv = pr[:, :].re("p (a b) -> p a b", b=CH); piv = pi_[:, :].re("p (a b) -> p a b", b=CH)
                a0, a1, a2, a3 = ta
                S.tt("dve", a0[:], prv, Ec[:, sl, :], ALU.mult)
                S.tt("dve", a1[:], piv, Es[:, sl, :], ALU.mult)
                S.tt("dve", Zre[:, sl, :], a0[:], a1[:], ALU.add)
                S.tt("dve", a2[:], piv, Ec[:, sl, :], ALU.mult)
                S.tt("dve", a3[:], prv, Es[:, sl, :], ALU.mult)
                S.tt("dve", Zim[:, sl, :], a2[:], a3[:], ALU.subtract)

        def stageSc(ci, cc):
            Zre, Zim = Zre_[cc % 2], Zim_[cc % 2]
            wre, wim = wre_[cc % 2], wim_[cc % 2]
            for j in range(NST):
                for (wt, zt, xp) in ((wre, Zre, xlr[(cc + 1) % 2]), (wim, Zim, xli[(cc + 1) % 2])):
                    init = 0.0 if ci == 0 else xp[:, j:j + 1]
                    S.scan(wt[:, j, :], rhob[:, j, :], zt[:, j, :], init)
            L_ = CH - 1
            S.tt("dve", xl1[:], wre[:, :, L_], Ec[:, :, L_], ALU.mult)
            S.tt("dve", xl2[:], wim[:, :, L_], Es[:, :, L_], ALU.mult)
            S.tt("dve", xlr[cc % 2][:], xl1[:], xl2[:], ALU.subtract)
            S.tt("dve", xl1[:], wim[:, :, L_], Ec[:, :, L_], ALU.mult)
            S.tt("dve", xl2[:], wre[:, :, L_], Es[:, :, L_], ALU.mult)
            S.tt("dve", xli[cc % 2][:], xl1[:], xl2[:], ALU.add)

        def stageB(ci, n, cc):
            nonlocal pbi
            xr, xi = xre[cc % 2], xim[cc % 2]
            wre, wim = wre_[cc % 2], wim_[cc % 2]
            tb = tb_[0]
            for g4 in range(4):
                sl = slice(g4 * 4, (g4 + 1) * 4)
                b0, b1, b2, b3 = tb
                S.tt("pool", b0[:], wre[:, sl, :], Ec[:, sl, :], ALU.mult)
                S.tt("pool", b1[:], wim[:, sl, :], Es[:, sl, :], ALU.mult)
                S.tt("pool", xr[:, sl, :], b0[:], b1[:], ALU.subtract)
                S.tt("pool", b2[:], wim[:, sl, :], Ec[:, sl, :], ALU.mult)
                S.tt("pool", b3[:], wre[:, sl, :], Es[:, sl, :], ALU.mult)
                S.tt("pool", xi[:, sl, :], b2[:], b3[:], ALU.add)
            py = pb[pbi % 6]; pbi += 1
            for j in range(NST):
                S.mm(py[:, j * 32:(j + 1) * 32], xr[:, j, :], cre[:, j, :], start=True, stop=False, sig=False)
                S.mm(py[:, j * 32:(j + 1) * 32], xi[:, j, :], ncim[:, j, :], start=False, stop=True, sig=(j == NST - 1))
            ys = yst[cc % 2]
            S.copy("act", ys[:], py[:])
            py2 = pb[pbi % 6]; pbi += 1
            S.mm(py2[:], Pm[:], ys[:])
            ys2 = yst2[cc % 2]
            S.copy("act", ys2[:], py2[:])
            S.dma("sync", SC["ys"][dr, n * 128:(n + 1) * 128, :], ys2[:])

        stageA(0, border[0], cc)
        for ci, n in enumerate(border):
            stageSc(ci, cc)
            if ci + 1 < len(border):
                stageA(ci + 1, border[ci + 1], cc + 1)
            stageB(ci, n, cc)
            cc += 1


def phase3a(S, PS, IN, SC):
    idf, idb = setup_consts(S, IN["ident"][:])
    ones = S.sb([128, 128], F32, "ones"); S.memset("dve", ones[:], 1.0)
    pa = PS["g"]; pt = PS["b"]
    modT = S.sb([128, 72, 2], F32, "modT")
    S.dma("sync", modT[:].re("p t j -> p (t j)"), SC["modT0"][:])
    gateb = [gate_bcast(S, modT[:, 5 * 8:6 * 8, j], idf, ones, pa[j], 1.0, f"g5{j}") for j in range(2)]
    bcs = [S.sb([128, 512], F32, f"bcs{i}") for i in range(5)]
    for i in range(5):
        S.dma("sync", bcs[i][:], IN["bcs"][i])
    lnxg, lnxb, rk, s5d, glub = bcs
    S.ts("dve", rk[:], rk[:], 0.5, ALU.mult)
    wst = S.sb([128, 4, 512], F32, "wst")
    gluw = S.sb([128, 4, 512], BF16, "gluw")
    S.dma("sync", wst[:], IN["gluw"].re("(k p) n -> p k n", p=128))
    S.copy("act", gluw[:], wst[:])
    outw = S.sb([128, 8, 1024], BF16, "outw")
    wst2 = [S.sb([128, 1024], F32, f"wst2{i}") for i in range(2)]
    for k in range(8):
        S.dma("sync", wst2[k % 2][:], IN["outw_ab"][k * 128:(k + 1) * 128, :])
        S.copy("act", outw[:, k, :], wst2[k % 2][:])
    t5 = lambda nm, dt=F32: S.sb([128, 512], dt, nm)
    inr = [[t5(f"inr{i}{j}") for j in range(7)] for i in range(2)]
    ins = [[t5(f"ins{i}{j}") for j in range(3)] for i in range(2)]
    xt = [S.sb([128, 1024], F32, f"xt{i}") for i in range(2)]
    y = t5("y"); yc = t5("yc"); sq = t5("sq"); ks = t5("ks"); tq = t5("tq"); bon = t5("bon")
    s8 = S.sb([128, 8], F32, "s8"); v8 = S.sb([128, 8], F32, "v8"); b8 = S.sb([128, 8], F32, "b8")
    cat = S.sb([128, 1024], BF16, "cat"); ysum = t5("ysum"); z = t5("z"); zb = t5("zb", BF16)
    zT = S.sb([128, 4, 128], BF16, "zT"); gl = t5("gl"); catT = S.sb([128, 8, 128], BF16, "catT")
    tmp = S.sb([128, 1024], F32, "tmp")
    v3 = lambda v: v.re("p (h k) -> p h k", k=64)
    b3 = lambda t: t[:].re("p (h o) -> p h o", o=1).bc([128, 8, 64])
    for ci in range(NCH):
        j = JF34(ci)
        i2 = ci % 2
        rows = slice(ci * 128, (ci + 1) * 128)
        srcs = [SC["yd"][0, rows, :], SC["yd"][1, rows, :], SC["kdo"][0, rows, :], SC["kdo"][1, rows, :],
                SC["rvo"][rows, 0:512], SC["rvo"][rows, 512:1024], SC["gto"][rows, :]]
        for q in range(7):
            S.dma("sync" if q % 2 == 0 else "pool", inr[i2][q][:], srcs[q])
        srcs2 = [SC["ys"][0, rows, :], SC["ys"][1, rows, :], SC["ppad"][prow(ci):prow(ci) + 128, 1792:2304]]
        for q in range(3):
            S.dma("pool" if q % 2 == 0 else "sync", ins[i2][q][:], srcs2[q])
        S.dma("sync", xt[i2][:], SC["x1"][rows, :])
        y0, y1, kd0, kd1, r, v, g = inr[i2]
        S.tt("dve", y[:], y0[:], y1[:], ALU.add)
        S.red("dve", s8[:], v3(y[:]), ALU.add)
        S.ts("dve", s8[:], s8[:], 1.0 / 64, ALU.mult)
        S.tt("dve", v3(yc[:]), v3(y[:]), b3(s8), ALU.subtract)
        S.tt("dve", sq[:], yc[:], yc[:], ALU.mult)
        S.red("dve", v8[:], v3(sq[:]), ALU.add)
        S.ts("dve", v8[:], v8[:], 1.0 / 64, ALU.mult, GN_EPS, ALU.add)
        S.act(v8[:], v8[:], AF.Sqrt)
        S.recip(v8[:], v8[:])
        S.tt("dve", v3(yc[:]), v3(yc[:]), b3(v8), ALU.mult)
        S.tt("dve", yc[:], yc[:], lnxg[:], ALU.mult)
        S.tt("dve", yc[:], yc[:], lnxb[:], ALU.add)
        S.tt("dve", ks[:], kd0[:], kd1[:], ALU.add)
        S.tt("dve", tq[:], r[:], ks[:], ALU.mult)
        S.tt("dve", tq[:], tq[:], rk[:], ALU.mult)
        S.red("dve", b8[:], v3(tq[:]), ALU.add)
        S.tt("dve", v3(bon[:]), v3(v[:]), b3(b8), ALU.mult)
        S.tt("dve", yc[:], yc[:], bon[:], ALU.add)
        S.tt("dve", cat[:, 0:512], yc[:], g[:], ALU.mult)
        ys0, ys1, u = ins[i2]
        S.tt("dve", ysum[:], ys0[:], ys1[:], ALU.add)
        S.tt("dve", tq[:], u[:], s5d[:], ALU.mult)
        S.tt("dve", ysum[:], ysum[:], tq[:], ALU.add)
        S.act(z[:], ysum[:], AF.Gelu)
        S.copy("act", zb[:], z[:])
        p_ = pt[0]
        for k in range(4):
            S.tr(p_[:, k * 128:(k + 1) * 128], zb[:, k * 128:(k + 1) * 128], idb[:], sig=(k == 3))
        S.copy("dve", zT[:], p_[:, 0:512].re("p (k t) -> p k t", t=128))
        pg = pa[2]
        for k in range(4):
            S.mm(pg[:], zT[:, k, :], gluw[:, k, :], start=(k == 0), stop=(k == 3))
        S.tt("dve", gl[:], pg[:], glub[:], ALU.add)
        S.act(gl[:], gl[:], AF.Sigmoid)
        S.tt("dve", cat[:, 512:1024], z[:], gl[:], ALU.mult)
        p_ = pt[1]
        for k in range(8):
            S.tr(p_[:, k * 128:(k + 1) * 128], cat[:, k * 128:(k + 1) * 128], idb[:], sig=(k == 7))
        S.copy("dve", catT[:], p_[:].re("p (k t) -> p k t", t=128))
        for h in range(2):
            pc = pa[4 + h]
            for k in range(8):
                S.mm(pc[:], catT[:, k, :], outw[:, k, h * 512:(h + 1) * 512], start=(k == 0), stop=(k == 7))
            S.tt("dve", tmp[:, h * 512:(h + 1) * 512], pc[:], gateb[j][:, h * 512:(h + 1) * 512], ALU.mult)
        S.tt("dve", xt[i2][:], xt[i2][:], tmp[:], ALU.add)
        S.dma("pool", SC["xm"][rows, :], xt[i2][:])


def phase3b(S, PS, IN, SC):
    C = alloc_common(S, PS)
    C.idf, C.idb = setup_consts(S, IN["ident"][:])
    modT0 = S.sb([128, 72, 2], F32, "modT0")
    S.dma("sync", modT0[:].re("p t j -> p (t j)"), SC["modT0"][:])
    ngT0 = S.sb([128, 3, 8], F32, "ngT0")
    S.dma("sync", ngT0[:], IN["ngT"][0])
    mv0 = mod_derive(S, modT0, ngT0)
    C.gateb = {}
    for j in range(2):
        C.gateb[(2, j)] = gate_bcast(S, mv0["GATE"][:, 2, :, j], C.idf, C.ones, C.pa[j], 0.5, f"gb2{j}")
    rows = lambda t: (lambda ci: t[ci * 128:(ci + 1) * 128, :])
    for grp in ffn(S, C, rows(SC["xm"]), rows(SC["xl0"]), IN["w1"][0, 1], IN["w2"][0, 1], mv0, 2, GROUPS34, JF34):
        pass
    mv1 = mod_vectors(S, IN["cT"][:], IN["modw"][1], IN["modbT"][1], IN["ngT"][1], C.wst, C.pa[0])
    S.dma("sync", SC["modT1"][:], mv1["modT"][:].re("p t j -> p (t j)"))
    for j in range(2):
        C.gateb[(0, j)] = gate_bcast(S, mv1["GATE"][:, 0, :, j], C.idf, C.ones, C.pa[2 + j], 0.5, f"gb0{j}")
    cos = S.sb([128, NCH, 32], F32, "cos"); sin = S.sb([128, NCH, 32], F32, "sin")
    S.dma("sync", cos[:], IN["rope"][0].re("c p f -> p c f"))
    S.dma("sync", sin[:], IN["rope"][1].re("c p f -> p c f"))
    pst = [S.sb([128, 256], F32, f"pst{i}") for i in range(2)]
    ra = [S.sb([128, 4, 32], F32, f"ra{i}") for i in range(4)]
    psti = [0]

    def after(ci, li, xt):
        j = JF34(ci)
        norm_to_hT(S, C, xt[:], C.hT, li * 128, mv1["G"][:, 1, :, j], mv1["SH"][:, 1, :, j])
    win = IN["win_at"]
    qkv = SC["qkv"]
    for grp in ffn(S, C, rows(SC["xl0"]), rows(SC["x2"]), IN["w1"][1, 0], IN["w2"][1, 0], mv1, 0, GROUPS34, JF34, after_chunk=after):
        for cb in range(6):
            wst = C.wst[cb % 2]; wb = C.w1b[cb % 2]
            S.dma("sync", wst[:, 0:4, :], win[cb, :, 0:4, :])
            S.dma("sync", wst[:, 4:8, :], win[cb, :, 4:8, :])
            S.copy("act", wb[:, 0:4, :], wst[:, 0:4, :]); S.copy("dve", wb[:, 4:8, :], wst[:, 4:8, :])
            for li, ci in enumerate(grp):
                pp = C.pa[C.pai % 4]; C.pai += 1
                for k in range(8):
                    S.mm(pp[:, 0:256], C.hT[:, k, li * 128:(li + 1) * 128], wb[:, k, :], start=(k == 0), stop=(k == 7))
                st = pst[psti[0] % 2]; psti[0] += 1
                if cb < 5:
                    pv = pp[:, 0:256].re("p (h two f) -> p h two f", two=2, f=32)
                    sv = st[:].re("p (h two f) -> p h two f", two=2, f=32)
                    cb_ = cos[:, ci, :].re("p (o f) -> p o f", o=1).bc([128, 4, 32])
                    sb_ = sin[:, ci, :].re("p (o f) -> p o f", o=1).bc([128, 4, 32])
                    a, b, c, dd = ra
                    S.tt("dve", a[:], pv[:, :, 0, :], cb_, ALU.mult)
                    S.tt("dve", b[:], pv[:, :, 1, :], sb_, ALU.mult)
                    S.tt("pool", sv[:, :, 0, :], a[:], b[:], ALU.subtract)
                    S.tt("dve", c[:], pv[:, :, 1, :], cb_, ALU.mult)
                    S.tt("dve", dd[:], pv[:, :, 0, :], sb_, ALU.mult)
                    S.tt("pool", sv[:, :, 1, :], c[:], dd[:], ALU.add)
                else:
                    S.copy("act", st[:], pp[:, 0:256])
                S.dma("sync", qkv[ci * 128:(ci + 1) * 128, cb * 256:(cb + 1) * 256], st[:])


def phase4a(S, PS, IN, SC):
    idf, idb = setup_consts(S, IN["ident"][:])
    ones = S.sb([128, 128], F32, "ones"); S.memset("dve", ones[:], 1.0)
    pa = PS["g"][0:4]; pai = [0]
    ptb = PS["b"][0]
    pos = PS["g"][4:6]
    qkv = SC["qkv"]
    modT = S.sb([128, 72, 2], F32, "modT")
    S.dma("sync", modT[:].re("p t j -> p (t j)"), SC["modT1"][:])
    gate5 = gate_bcast(S, modT[:, 5 * 8:6 * 8, 1], idf, ones, pa[0], 1.0, "g5")
    sinkb = S.sb([128, 16], F32, "sinkb"); S.dma("sync", sinkb[:], IN["sinkb"][:])
    mt16 = S.sb([128, 16, 3], F32, "mt16")
    S.copy("dve", mt16[:, :, 2], sinkb[:])
    mstage = S.sb([128, 384], F32, "mstage")
    maskb = S.sb([128, 3, 384], BF16, "maskb")
    for i in range(3):
        S.dma("sync", mstage[:], IN["maskb"][i])
        S.copy("dve", maskb[:, i, :], mstage[:])
    outw = S.sb([128, 8, 1024], BF16, "outw")
    wst2 = [S.sb([128, 1024], F32, f"wst2{i}") for i in range(2)]
    for k in range(8):
        S.dma("sync", wst2[k % 2][:], IN["outw_at"][k * 128:(k + 1) * 128, :])
        S.copy("act", outw[:, k, :], wst2[k % 2][:])
    NKB = NQB + 2
    kT = S.sb([64, 4, NKB * 128], BF16, "kT"); kcT = S.sb([64, 4, 256], BF16, "kcT")
    vw = S.sb([128, NKB, 256], BF16, "vw"); vc = S.sb([128, 2, 256], BF16, "vc")
    for blk in (0, NKB - 1):
        S.memset("dve", kT[:, :, blk * 128:(blk + 1) * 128], 0.0)
        S.memset("dve", vw[:, blk, :], 0.0)
    kst = [S.sb([128, 512], F32, f"kst{i}") for i in range(2)]; kb = [S.sb([128, 256], BF16, f"kb{i}") for i in range(2)]
    for c in range(NCH):
        S.dma("sync", kst[c % 2][:], qkv[c * 128:(c + 1) * 128, 1024:1536])
        S.copy("pool", kb[c % 2][:], kst[c % 2][:, 0:256])
        for kv in range(4):
            S.tr(ptb[0:64, kv * 128:(kv + 1) * 128], kb[c % 2][:, kv * 64:(kv + 1) * 64], idb[:], sig=(kv == 3))
        blk = c - 1
        dstk = kcT[:, :, c * 128:(c + 1) * 128] if c < 2 else kT[:, :, blk * 128:(blk + 1) * 128]
        S.copy("act", dstk, ptb[0:64, 0:512].re("p (a t) -> p a t", t=128))
        dstv = vc[:, c, :] if c < 2 else vw[:, blk, :]
        S.copy("dve", dstv, kst[c % 2][:, 256:512])
    qst = [S.sb([128, 1024], F32, f"qst{i}") for i in range(2)]
    qb = S.sb([128, 1024], BF16, "qb")
    qT = S.sb([64, 16, 128], BF16, "qT")
    Pm = [S.sb([128, 640], BF16, f"Pm{i}") for i in range(2)]
    PT = [S.sb([128, 5, 128], BF16, f"PT{i}") for i in range(2)]
    rs = [S.sb([128, 4], F32, f"rs{i}") for i in range(2)]
    negm = [S.sb([128, 1], F32, f"negm{i}") for i in range(2)]
    rden = S.sb([128, 16], F32, "rden")
    ob = S.sb([128, 1024], BF16, "ob"); oT = S.sb([128, 8, 128], BF16, "oT")
    xt = [S.sb([128, 1024], F32, f"xt{i}") for i in range(2)]
    tmp = S.sb([128, 1024], F32, "tmp")
    for i in range(NQB):
        rows = slice((i + 2) * 128, (i + 3) * 128)
        S.dma("sync", qst[i % 2][:], qkv[rows, 0:1024])
        S.dma("pool", xt[i % 2][:], SC["x2"][rows, :])
        S.act(qb[:], qst[i % 2][:], AF.Copy, scale=0.125)
        for half in range(2):
            for hh in range(8):
                hd = half * 8 + hh
                S.tr(ptb[0:64, hh * 128:(hh + 1) * 128], qb[:, hd * 64:(hd + 1) * 64], idb[:], sig=(hh == 7))
            S.copy("act", qT[:, half * 8:(half + 1) * 8, :], ptb[0:64, :].re("p (a t) -> p a t", t=128))
        mi = 0 if i == 0 else (2 if i == NQB - 1 else 1)
        def scores(hd):
            kv = hd // 4
            pw = pa[pai[0] % 4]; pai[0] += 1
            pcx = pa[pai[0] % 4]; pai[0] += 1
            S.mm(pw[:, 0:384], qT[:, hd, :], kT[:, kv, i * 128:(i + 3) * 128], start=True, stop=False, sig=False)
            S.mm(pw[:, 0:384], idb[:], maskb[:, mi, :], start=False, stop=True)
            S.mm(pcx[:, 0:256], qT[:, hd, :], kcT[:, kv, :])
            return pw, pcx
        nxt_sc = scores(0)
        for hd in range(16):
            kv = hd // 4
            i2 = hd % 2
            pw, pcx = nxt_sc
            if hd + 1 < 16:
                nxt_sc = scores(hd + 1)
            S.red("dve", mt16[:, hd, 0:1], pw[:, 0:384], ALU.max)
            S.red("dve", mt16[:, hd, 1:2], pcx[:, 0:256], ALU.max)
            S.red("dve", negm[i2][:], mt16[:, hd, :], ALU.max)
            S.ts("dve", negm[i2][:], negm[i2][:], -1.0, ALU.mult)
            S.act(Pm[i2][:, 0:384], pw[:, 0:384], AF.Exp, bias=negm[i2][:, 0:1], accum=rs[i2][:, 0:1])
            S.act(Pm[i2][:, 384:640], pcx[:, 0:256], AF.Exp, bias=negm[i2][:, 0:1], accum=rs[i2][:, 1:2])
            S.act(rs[i2][:, 2:3], sinkb[:, hd:hd + 1], AF.Exp, bias=negm[i2][:, 0:1])
            S.red("dve", rs[i2][:, 3:4], rs[i2][:, 0:3], ALU.add)
            S.recip(rden[:, hd:hd + 1], rs[i2][:, 3:4])
            for j in range(5):
                S.tr(ptb[:, j * 128:(j + 1) * 128], Pm[i2][:, j * 128:(j + 1) * 128], idb[:], sig=(j == 4))
            S.copy("dve" if hd % 2 == 0 else "act", PT[i2][:], ptb[:, 0:640].re("p (a t) -> p a t", t=128))
            po = pos[hd // 8]
            for j in range(5):
                vsrc = vw[:, i + j, kv * 64:(kv + 1) * 64] if j < 3 else vc[:, j - 3, kv * 64:(kv + 1) * 64]
                S.mm(po[:, (hd % 8) * 64:(hd % 8 + 1) * 64], PT[i2][:, j, :], vsrc, start=(j == 0), stop=(j == 4), sig=(j == 4))
        for h2 in range(2):
            S.tt("dve", ob[:, h2 * 512:(h2 + 1) * 512].re("p (h k) -> p h k", k=64), pos[h2][:].re("p (h k) -> p h k", k=64),
                 rden[:, h2 * 8:(h2 + 1) * 8].re("p (h o) -> p h o", o=1).bc([128, 8, 64]), ALU.mult)
        for k in range(8):
            S.tr(ptb[:, k * 128:(k + 1) * 128], ob[:, k * 128:(k + 1) * 128], idb[:], sig=(k == 7))
        S.copy("act", oT[:], ptb[:].re("p (a t) -> p a t", t=128))
        for h in range(2):
            py = pa[pai[0] % 4]; pai[0] += 1
            for k in range(8):
                S.mm(py[:], oT[:, k, :], outw[:, k, h * 512:(h + 1) * 512], start=(k == 0), stop=(k == 7))
            S.tt("dve", tmp[:, h * 512:(h + 1) * 512], py[:], gate5[:, h * 512:(h + 1) * 512], ALU.mult)
        S.tt("dve", xt[i % 2][:], xt[i % 2][:], tmp[:], ALU.add)
        S.dma("pool", SC["x3"][i * 128:(i + 1) * 128, :], xt[i % 2][:])


def phase4b(S, PS, IN, SC, OUT):
    C = alloc_common(S, PS)
    C.idf, C.idb = setup_consts(S, IN["ident"][:])
    modT = S.sb([128, 72, 2], F32, "modT")
    S.dma("sync", modT[:].re("p t j -> p (t j)"), SC["modT1"][:])
    ngT = S.sb([128, 3, 8], F32, "ngT"); S.dma("sync", ngT[:], IN["ngT"][1])
    mv = mod_derive(S, modT, ngT)
    C.gateb = {(2, 1): gate_bcast(S, mv["GATE"][:, 2, :, 1], C.idf, C.ones, C.pa[0], 0.5, "gb21")}
    fing = S.sb([128, 1024], F32, "fing"); S.dma("sync", fing[:], IN["fing"][:])
    ot = [S.sb([128, 1024], F32, f"ot{i}") for i in range(2)]
    oi = [0]

    def after(ci, li, xt):
        ss = C.small[C.si % 4]; C.si += 1
        S.act(C.junk[:], xt[:], AF.Square, accum=ss[:, 0:1])
        S.ts("dve", ss[:, 1:2], ss[:, 0:1], 1.0 / D, ALU.mult, 1e-6, ALU.add)
        S.act(ss[:, 3:4], ss[:, 1:2], AF.Sqrt)
        S.recip(ss[:, 2:3], ss[:, 3:4])
        o = ot[oi[0] % 2]; oi[0] += 1
        S.stt("dve", o[:], xt[:], ss[:, 2:3], fing[:], ALU.mult, ALU.mult)
        S.dma("sync", OUT[ci * 128:(ci + 1) * 128, :], o[:])
    rows = lambda t: (lambda ci: t[ci * 128:(ci + 1) * 128, :])
    for grp in ffn(S, C, rows(SC["x3"]), None, IN["w1"][1, 1], IN["w2"][1, 1], mv, 2, GROUPS32, lambda ci: 1, after_chunk=after):
        pass


IN_SPECS = dict(
    x=[4096, D], ctx=[256, D], cT=[128, 8, 2], modw=[2, 36, 128, 8, 256], modbT=[2, 128, 72], ngT=[2, 128, 3, 8],
    w1=[2, 2, NFT, 128, 8, 256], w2=[2, 2, DFF, D], win_ab=[9, 128, 8, 256], ident=[128, 128],
    mub=[2, 128, 1536], mulb=[2, 128, 256], kkb=[128, 512], kab=[128, 512], w2a=[2, 65, 512], a2a=[2, 65, 512], g2=[128, 512], msk=[6, 128, 128],
    are=[2, 128, NST], aim=[2, 128, NST], lst=[2, 128, NST], bre=[2, 128, NST, 32], bim=[2, 128, NST, 32], cre=[2, 128, NST, 32], cim=[2, 128, NST, 32],
    bcs=[5, 128, 512], gluw=[512, 512], outw_ab=[D, D], win_at=[6, 128, 8, 256], rope=[2, NCH, 128, 32],
    maskb=[3, 128, 384], sinkb=[128, 16], outw_at=[D, D], fing=[128, D])

SC_SPECS = dict(x1=[TOK, D], ppad=[4356, 2304], yd=[2, TOK, 512], kdo=[2, TOK, 512], rvo=[TOK, 1024], gto=[TOK, 512], ys=[2, TOK, 512],
                xm=[TOK, D], xl0=[TOK, D], x2=[TOK, D], qkv=[TOK, 1536], x3=[4096, D], modT0=[128, 144], modT1=[128, 144])


def build_fused(upto=99, debug=(), ses=True):
    nc = bass.Bass("TRN2", target_bir_lowering=False)
    S = Sched(nc, same_engine_sync=ses)
    IN = {k: S.dram(k, v, F32, kind="ExternalInput") for k, v in IN_SPECS.items()}
    SC = {k: S.dram("sc_" + k, v, F32, kind=("ExternalOutput" if k in debug else "Internal")) for k, v in SC_SPECS.items()}
    OUT = S.dram("out", [4096, D], F32, kind="ExternalOutput")
    PS = dict(g=[S.ps([128, 512], F32, f"g{i}") for i in range(6)], b=[S.ps([128, 1024], BF16, f"b{i}") for i in range(2)])
    base = S.mark()
    phases = [lambda: phase1(S, PS, IN, SC), lambda: phase2a(S, PS, IN, SC), lambda: phase2b(S, PS, IN, SC), lambda: phase3a(S, PS, IN, SC),
              lambda: phase3b(S, PS, IN, SC), lambda: phase4a(S, PS, IN, SC), lambda: phase4b(S, PS, IN, SC, OUT)]
    for i, ph in enumerate(phases):
        if i > upto:
            break
        S.reset(base)
        ph()
        S.barrier()
    finals = [OUT] + [SC[k] for k in debug]
    S.finish(finals)
    return nc, S

import numpy as np
LC = 256; NLAT = 4096; L = 4352
GRID_W = 64; ROPE_BASE = 10000.0
def core_tok(seq, h):
    return np.concatenate([seq[h * 128:(h + 1) * 128], seq[256 + h * 2048:256 + (h + 1) * 2048]], 0)
def uncore_tok(parts):
    return np.concatenate([parts[0][:128], parts[1][:128], parts[0][128:], parts[1][128:]], 0)
def colT(v, k=8):
    return np.ascontiguousarray(v.reshape(k, 128).T)
def bc(v):
    return np.ascontiguousarray(np.broadcast_to(v[None, :], (128, v.shape[0])))
def rope_tables(h):
    t = np.arange(h * 2048, (h + 1) * 2048)
    row = (t // GRID_W).astype(np.float32); col = (t % GRID_W).astype(np.float32)
    inv = (ROPE_BASE ** (-np.arange(0, 32, 2, dtype=np.float32) / 32)).astype(np.float32)
    ang = np.concatenate([row[:, None] * inv, col[:, None] * inv], -1).astype(np.float32)
    cos = np.concatenate([np.ones((128, 32), np.float32), np.cos(ang)], 0).reshape(17, 128, 32)
    sin = np.concatenate([np.zeros((128, 32), np.float32), np.sin(ang)], 0).reshape(17, 128, 32)
    return np.stack([cos, sin], 0).astype(np.float32)

import numpy as np
LC = 256

def f_masks():
    m = np.zeros((6, 128, 128), np.float32)
    s = np.arange(128)[:, None]; t = np.arange(128)[None, :]
    same = (s // 64) == (t // 64)
    m[0] = same & (s < t); m[1] = same & (s <= t); m[2] = same & (s > t); m[4] = same & (s >= t)
    m[3] = np.eye(128)
    m[5] = np.eye(128)[::-1]
    return m

def f_rope():
    GRID_W = 64
    t = np.arange(4096)
    row = (t // GRID_W).astype(np.float32); col = (t % GRID_W).astype(np.float32)
    inv = (10000.0 ** (-np.arange(0, 32, 2, dtype=np.float32) / 32)).astype(np.float32)
    ang = np.concatenate([row[:, None] * inv, col[:, None] * inv], -1).astype(np.float32)
    cos = np.concatenate([np.ones((256, 32), np.float32), np.cos(ang)], 0).reshape(34, 128, 32)
    sin = np.concatenate([np.zeros((256, 32), np.float32), np.sin(ang)], 0).reshape(34, 128, 32)
    return np.ascontiguousarray(np.stack([cos, sin], 0).astype(np.float32))

def f_attn_masks():
    qi = np.arange(128)[:, None]; mj = np.arange(384)[None, :] - 128
    valid = np.abs(mj - qi) <= 128
    NEG = -30000.0
    gen = np.where(valid, 0.0, NEG).astype(np.float32)
    left_inv = gen.copy(); left_inv[:, :128] = NEG
    right_inv = gen.copy(); right_inv[:, 256:] = NEG
    return np.ascontiguousarray(np.stack([left_inv, gen, right_inv], 0))

def wblk(w):
    n = w.shape[1] // 256
    return np.ascontiguousarray(w.reshape(8, 128, n, 256).transpose(2, 1, 0, 3))

def w1blk(w):
    a = w.reshape(8, 128, 2, 22, 128).transpose(3, 1, 0, 2, 4)
    return np.ascontiguousarray(a.reshape(22, 128, 8, 256))

def f_shared(d):
    e = 0
    st = lambda a: np.ascontiguousarray(a.reshape(16, 128).T)
    def pad(a):
        out = np.zeros((128, 16, 32), np.float32)
        for g in range(32):
            out[(g % 2) * 64:(g % 2) * 64 + 64, g // 2, (g % 2) * 16:(g % 2) * 16 + 16] = a[g]
        return out
    mu = d['rwkv_mu'][e]
    sh = dict(
        modw=np.stack([wblk(d['mod_w'][l]) for l in range(2)], 0), modbT=np.ascontiguousarray(np.stack([colT(d['mod_b'][l], 72) for l in range(2)], 0)),
        ngT=np.ascontiguousarray(np.stack([np.stack([colT(d['norm_g'][l, i]) for i in range(3)], 1) for l in range(2)], 0)),
        w1=np.stack([np.stack([w1blk(d['ffn_w1'][l, j]) for j in range(2)], 0) for l in range(2)], 0), w2=d['ffn_w2'], win_ab=wblk(d['ab_in_w'][0]), ident=np.eye(128, dtype=np.float32),
        mub=np.ascontiguousarray(np.stack([bc(mu[0, :1536]), bc(mu[1, :1536])], 0)),
        mulb=np.ascontiguousarray(np.stack([bc(mu[0, 1536:1792]), bc(mu[1, 1536:1792])], 0)),
        kkb=bc(d['rwkv_k_k'][e]), kab=bc(d['rwkv_k_a'][e]),
        w2a=np.ascontiguousarray(np.stack([np.concatenate([d['rwkv_w2'][e, dr], d['rwkv_w0'][e, dr][None]], 0) for dr in range(2)], 0)),
        a2a=np.ascontiguousarray(np.stack([np.concatenate([d['rwkv_a2'][e, dr], d['rwkv_a0'][e, dr][None]], 0) for dr in range(2)], 0)),
        g2=d['rwkv_g2'][e], msk=f_masks(),
        are=np.stack([st(d['s5_a_re'][0, dr]) for dr in range(2)], 0), aim=np.stack([st(d['s5_a_im'][0, dr]) for dr in range(2)], 0),
        lst=np.stack([st(np.repeat(d['s5_log_step'][0, dr][:, None], 64, 1)) for dr in range(2)], 0),
        bre=np.stack([pad(d['s5_b_re'][0, dr]) for dr in range(2)], 0), bim=np.stack([pad(d['s5_b_im'][0, dr]) for dr in range(2)], 0),
        cre=np.stack([pad(d['s5_c_re'][0, dr].transpose(0, 2, 1)) for dr in range(2)], 0),
        cim=np.stack([pad(d['s5_c_im'][0, dr].transpose(0, 2, 1)) for dr in range(2)], 0),
        bcs=np.ascontiguousarray(np.stack([bc(d['rwkv_lnx_g'][0]), bc(d['rwkv_lnx_b'][0]), bc(d['rwkv_r_k'][0].reshape(-1)), bc(d['s5_d'][0]),
                                           bc(d['s5_glu_b'][0])], 0)),
        gluw=d['s5_glu_w'][0], outw_ab=d['ab_out_w'][0], win_at=wblk(d['attn_in_w'][0]), rope=f_rope(),
        maskb=f_attn_masks(), sinkb=bc(d['attn_sink'][0]), outw_at=d['attn_out_w'][0], fing=bc(d['final_g']))
    return {k: np.ascontiguousarray(v, dtype=np.float32) for k, v in sh.items()}

def f_core(d, b):
    return dict(x=np.ascontiguousarray(d['x'][b]), ctx=np.ascontiguousarray(d['ctx'][b]),
                cT=np.ascontiguousarray(np.stack([colT(d['c_ctx']), colT(d['c'][b])], -1)))


def kernel(**inputs):
    d = {k: np.ascontiguousarray(np.asarray(v, dtype=np.float32)) for k, v in inputs.items()}
    nc, _ = build_fused()
    sh = f_shared(d)
    in_maps = [dict(sh, **f_core(d, c % 4)) for c in range(8)]
    res = run_bass_kernel_spmd(nc, in_maps, core_ids=list(range(8)))
    out = np.stack([res.results[b]['out'] for b in range(4)], 0)
    return out.astype(np.float32)
```

```python
import numpy as np
import concourse.bass as bass
import concourse.mybir as mybir
from concourse.bass_utils import run_bass_kernel_spmd

F32 = mybir.dt.float32
BF16 = mybir.dt.bfloat16
F32R = mybir.dt.float32r
ALU = mybir.AluOpType
AF = mybir.ActivationFunctionType
AX = mybir.AxisListType


class T:
    def __init__(self, h, name=""):
        self.h = h
        self.name = name
        self.last_w = None
        self.readers = []

    def __getitem__(self, idx):
        return V(self, self.h[idx])

    def re(self, pat, **kw):
        return self[:].re(pat, **kw)


class V:
    def __init__(self, t, ap):
        self.t = t
        self.ap = ap

    def __getitem__(self, idx):
        return V(self.t, self.ap[idx])

    def re(self, pat, **kw):
        return V(self.t, self.ap.rearrange(pat, **kw))

    def bc(self, shape):
        return V(self.t, self.ap.to_broadcast(shape))


def _ap(x):
    return x.ap if isinstance(x, V) else x


def _ts(xs):
    out = []
    for x in xs:
        if isinstance(x, V):
            out.append(x.t)
        elif isinstance(x, T):
            out.append(x)
    return out


class Sched:
    ENG = ["pe", "act", "dve", "pool", "sync"]

    def __init__(self, nc, n_dma_sems=6, same_engine_sync=True):
        self.nc = nc
        self.q = {e: [] for e in self.ENG}
        self.cnt = {e: 0 for e in self.ENG}
        self.unsig = {e: False for e in self.ENG}
        self.sem = {e: nc.alloc_semaphore(f"s_{e}") for e in ["pe", "act", "dve", "pool"]}
        self.waited = {e: {} for e in self.ENG}
        self.same_engine_sync = same_engine_sync
        self.dsem = {}
        self.dcnt = {}
        self.drr = {}
        for qn in ["sync", "pool", "act"]:
            self.dsem[qn] = [nc.alloc_semaphore(f"d_{qn}{i}") for i in range(n_dma_sems)]
            self.dcnt[qn] = [0] * n_dma_sems
            self.drr[qn] = 0
        self.n_inst = 0
        self.uid = 0

    ARENA_LO = 16640
    ARENA_HI = 229344

    def sb(self, shape, dt=F32, name=None):
        self.uid += 1
        name = name or f"t{self.uid}"
        if not hasattr(self, "off"):
            self.off = self.ARENA_LO
        n = 1
        for x in shape[1:]:
            n *= x
        size = n * (2 if dt == BF16 else 4)
        size = (size + 31) // 32 * 32
        assert self.off + size <= self.ARENA_HI, f"SBUF arena overflow allocating {name} {shape}: off={self.off} size={size}"
        t = T(self.nc.alloc_sbuf_tensor_at(f"{name}_{self.uid}", list(shape), dt, offset=self.off), name)
        self.off += size
        return t

    def mark(self):
        if not hasattr(self, "off"):
            self.off = self.ARENA_LO
        return self.off

    def reset(self, mark):
        self.off = mark

    def barrier(self):
        targets = []
        for e in ("pe", "act", "dve", "pool"):
            assert not self.unsig[e], f"barrier with unsignaled op on {e}"
            if self.cnt[e] > 0:
                targets.append((self.sem[e], self.cnt[e]))
        for qn in self.dsem:
            for sm, c in zip(self.dsem[qn], self.dcnt[qn]):
                if c > 0:
                    targets.append((sm, c))
        for e in self.ENG:
            waits = []
            for (sm, val) in targets:
                if e in self.sem and sm is self.sem[e]:
                    continue
                if self.waited[e].get(id(sm), 0) >= val:
                    continue
                self.waited[e][id(sm)] = val
                waits.append((sm, val))
            if waits:
                self.q[e].append((None, waits, None))

    def ps(self, shape, dt=F32, name=None):
        self.uid += 1
        name = name or f"p{self.uid}"
        return T(self.nc.alloc_psum_tensor(f"{name}_{self.uid}", list(shape), dt), name)

    def dram(self, name, shape, dt=F32, kind="Internal"):
        return T(self.nc.dram_tensor(name, list(shape), dt, kind=kind), name)

    def _collect(self, eng, reads, writes):
        toks = []
        for t in _ts(reads):
            if t.last_w is not None:
                toks.append(t.last_w)
        for t in _ts(writes):
            if t.last_w is not None:
                toks.append(t.last_w)
            toks.extend(t.readers)
        best = {}
        for (kind, key, sem, val) in toks:
            if kind == "eng" and key == eng:
                if eng in ("pe", "sync") or not self.same_engine_sync:
                    continue
            k = id(sem)
            if k not in best or best[k][1] < val:
                best[k] = (sem, val)
        waits = []
        for k, (sem, val) in best.items():
            if self.waited[eng].get(k, 0) >= val:
                continue
            self.waited[eng][k] = val
            waits.append((sem, val))
        return waits

    def _mark(self, tok, reads, writes):
        for t in _ts(reads):
            t.readers.append(tok)
        for t in _ts(writes):
            t.last_w = tok
            t.readers = []

    def op(self, eng, fn, reads, writes, sig=True):
        waits = self._collect(eng, reads, writes)
        if sig:
            self.cnt[eng] += 1
            tok = ("eng", eng, self.sem[eng], self.cnt[eng])
            self.unsig[eng] = False
        else:
            tok = ("eng", eng, self.sem[eng], self.cnt[eng] + 1)
            self.unsig[eng] = True
        self.q[eng].append((fn, waits, (self.sem[eng], 1) if sig else None))
        self._mark(tok, reads, writes)
        self.n_inst += 1

    def dma(self, qn, out, in_, extra_reads=(), extra_writes=(), **kw):
        eng = qn
        i = self.drr[qn]
        self.drr[qn] = (i + 1) % len(self.dsem[qn])
        sem = self.dsem[qn][i]
        reads = [in_] + list(extra_reads)
        writes = [out] + list(extra_writes)
        waits = self._collect(eng, reads, writes)
        prev = self.dcnt[qn][i]
        if prev > 0 and self.waited[eng].get(id(sem), 0) < prev:
            self.waited[eng][id(sem)] = prev
            waits.append((sem, prev))
        self.dcnt[qn][i] += 16
        tok = ("dma", qn, sem, self.dcnt[qn][i])
        o, a = _ap(out), _ap(in_)
        self.q[eng].append((lambda e: e.dma_start(out=o, in_=a, **kw), waits, (sem, 16)))
        self._mark(tok, reads, writes)
        self.n_inst += 1
        return tok

    def mm(self, out, lhsT, rhs, start=True, stop=True, sig=None, f32=False):
        if sig is None:
            sig = stop
        o, l, r = _ap(out), _ap(lhsT), _ap(rhs)
        if f32:
            if l.dtype == F32R:
                l = l.bitcast(F32)
            if r.dtype == F32R:
                r = r.bitcast(F32)
        self.op("pe", lambda e: e.matmul(o, l, r, start=start, stop=stop), [lhsT, rhs], [out], sig=sig)

    def tr(self, out, in_, ident, sig=True):
        o, i, d = _ap(out), _ap(in_), _ap(ident)
        self.op("pe", lambda e: e.transpose(o, i, d), [in_, ident], [out], sig=sig)

    def act(self, out, in_, func, bias=None, scale=1.0, accum=None, eng="act"):
        o, i = _ap(out), _ap(in_)
        kw = {}
        reads = [in_]
        writes = [out]
        if bias is not None:
            kw["bias"] = _ap(bias)
            reads.append(bias)
        kw["scale"] = _ap(scale)
        if isinstance(scale, V):
            reads.append(scale)
        if accum is not None:
            kw["accum_out"] = _ap(accum)
            writes.append(accum)
        self.op("act", lambda e: e.activation(o, i, func, **kw), reads, writes)

    def tt(self, eng, out, in0, in1, op):
        o, a, b = _ap(out), _ap(in0), _ap(in1)
        self.op(eng, lambda e: e.tensor_tensor(o, a, b, op), [in0, in1], [out])

    def ts(self, eng, out, in0, s1, op0, s2=None, op1=None, accum=None):
        o, a = _ap(out), _ap(in0)
        reads = [in0] + [s for s in (s1, s2) if isinstance(s, V)]
        writes = [out] + ([accum] if accum is not None else [])
        kw = {}
        if op1 is not None:
            kw["op1"] = op1
        if accum is not None:
            kw["accum_out"] = _ap(accum)
        self.op(eng, lambda e: e.tensor_scalar(o, a, _ap(s1), _ap(s2) if s2 is not None else None, op0, **kw), reads, writes)

    def stt(self, eng, out, in0, scalar, in1, op0, op1):
        o, a, b = _ap(out), _ap(in0), _ap(in1)
        reads = [in0, in1] + ([scalar] if isinstance(scalar, V) else [])
        self.op(eng, lambda e: e.scalar_tensor_tensor(o, a, _ap(scalar), b, op0, op1), reads, [out])

    def red(self, eng, out, in_, op, axis=AX.X):
        o, a = _ap(out), _ap(in_)
        self.op(eng, lambda e: e.tensor_reduce(o, a, axis, op), [in_], [out])

    def copy(self, eng, out, in_):
        o, a = _ap(out), _ap(in_)
        if eng == "act":
            self.op(eng, lambda e: e.copy(o, a), [in_], [out])
        else:
            self.op(eng, lambda e: e.tensor_copy(o, a), [in_], [out])

    def memset(self, eng, out, val):
        o = _ap(out)
        self.op(eng, lambda e: e.memset(o, val), [], [out])

    def scan(self, out, d0, d1, init, op0=ALU.mult, op1=ALU.add):
        o, a, b, i = _ap(out), _ap(d0), _ap(d1), _ap(init)
        reads = [d0, d1] + ([init] if isinstance(init, V) else [])
        self.op("dve", lambda e: e.tensor_tensor_scan(o, a, b, i, op0, op1), reads, [out])

    def recip(self, out, in_):
        o, a = _ap(out), _ap(in_)
        self.op("dve", lambda e: e.reciprocal(o, a), [in_], [out])

    def finish(self, final_tiles):
        nc = self.nc
        toks = []
        for t in final_tiles:
            if t.last_w is not None:
                toks.append(t.last_w)
        fin = []
        best = {}
        for (_, _, sem, val) in toks:
            if id(sem) not in best or best[id(sem)][1] < val:
                best[id(sem)] = (sem, val)
        for qn in self.dsem:
            for s, c in zip(self.dsem[qn], self.dcnt[qn]):
                if c > 0:
                    best[id(s)] = (s, max(c, best.get(id(s), (s, 0))[1]))
        for e in ("pe", "act", "dve", "pool"):
            if self.cnt[e] > 0 or self.unsig[e]:
                assert not self.unsig[e], f"engine {e} ends with unsignaled instruction"
                best[id(self.sem[e])] = (self.sem[e], self.cnt[e])
        fin = list(best.values())
        q = self.q
        with nc.Block() as block:
            def replay(lst):
                def f(e):
                    for (fn, waits, inc) in lst:
                        for (sem, val) in waits:
                            e.wait_ge(sem, val)
                        if fn is None:
                            continue
                        ins = fn(e)
                        if inc is not None:
                            ins.then_inc(inc[0], inc[1])
                return f

            @block.tensor
            def _(e):
                replay(q["pe"])(e)

            @block.scalar
            def _(e):
                replay(q["act"])(e)

            @block.vector
            def _(e):
                replay(q["dve"])(e)

            @block.gpsimd
            def _(e):
                replay(q["pool"])(e)

            @block.sync
            def _(e):
                replay(q["sync"])(e)
                for (sem, val) in fin:
                    e.wait_ge(sem, val)
        return nc

import math

NCH = 34
TOK = NCH * 128
LSEQ = 4352
D = 1024
DFF = 2816
NFT = 22
GN_EPS = 64e-5
NEGC = -math.exp(-0.5)
NST = 16
NQB = 32


def prow(n):
    return n * 128 + (1 if n < 2 else 3)


class Ctx:
    pass


def setup_consts(S, ident_d):
    idf = S.sb([128, 128], F32, "idf")
    idb = S.sb([128, 128], BF16, "idb")
    S.dma("sync", idf[:], ident_d)
    S.copy("dve", idb[:], idf[:])
    return idf, idb


def mod_derive(S, modT, ngT):
    G = S.sb([128, 3, 8, 2], F32, "G")
    SH = S.sb([128, 3, 8, 2], F32, "SH")
    GATE = S.sb([128, 3, 8, 2], F32, "GATE")
    for i in range(3):
        for j in range(2):
            S.stt("dve", G[:, i, :, j], modT[:, (3 * i + 1) * 8:(3 * i + 2) * 8, j], 1.0, ngT[:, i, :], ALU.add, ALU.mult)
            S.copy("dve", SH[:, i, :, j], modT[:, (3 * i) * 8:(3 * i + 1) * 8, j])
            S.copy("dve", GATE[:, i, :, j], modT[:, (3 * i + 2) * 8:(3 * i + 3) * 8, j])
    return dict(G=G, SH=SH, GATE=GATE, modT=modT)


def mod_vectors(S, cT_d, modw_d, modbT_d, ngT_d, wst, pm):
    cT = S.sb([128, 8, 2], F32, "cT")
    sc = S.sb([128, 8, 2], F32, "sc")
    S.dma("sync", cT[:], cT_d)
    S.act(sc[:], cT[:], AF.Silu)
    modbT = S.sb([128, 72], F32, "modbT")
    S.dma("sync", modbT[:], modbT_d)
    ngT = S.sb([128, 3, 8], F32, "ngT")
    S.dma("sync", ngT[:], ngT_d)
    modT = S.sb([128, 72, 2], F32, "modT")
    for nb in range(36):
        w = wst[nb % 2]
        S.dma("sync", w[:, 0:4, :], modw_d[nb, :, 0:4, :])
        S.dma("pool", w[:, 4:8, :], modw_d[nb, :, 4:8, :])
        for ct in range(2):
            t = nb * 2 + ct
            for k in range(8):
                S.mm(pm[:, t * 2:t * 2 + 2], w[:, k, ct * 128:(ct + 1) * 128], sc[:, k, :], start=(k == 0), stop=(k == 7),
                     sig=(k == 7 and t % 2 == 1))
    for j in range(2):
        S.tt("dve", modT[:, :, j], pm[:, 0:144].re("p (t j) -> p t j", j=2)[:, :, j], modbT[:], ALU.add)
    return mod_derive(S, modT, ngT)


def gate_bcast(S, gate_col, idf, ones, ps, scale, name):
    out = S.sb([128, 1024], F32, name)
    dg = S.sb([128, 128], F32, name + "_dg")
    for k in range(8):
        S.ts("dve", dg[:], idf[:], gate_col[:, k:k + 1], ALU.mult)
        S.mm(ps[:, (k % 4) * 128:(k % 4 + 1) * 128], ones[:], dg[:], start=True, stop=True)
        S.ts("dve", out[:, k * 128:(k + 1) * 128], ps[:, (k % 4) * 128:(k % 4 + 1) * 128], float(scale), ALU.mult)
    return out


def norm_to_hT(S, C, xt, hT, col0, G, SH):
    ss = C.small[C.si % 4]; C.si += 1
    S.act(C.junk[:], xt, AF.Square, accum=ss[:, 0:1])
    S.ts("dve", ss[:, 1:2], ss[:, 0:1], 1.0 / D, ALU.mult, 1e-6, ALU.add)
    S.act(ss[:, 3:4], ss[:, 1:2], AF.Sqrt)
    S.recip(ss[:, 2:3], ss[:, 3:4])
    xn = C.xn[C.xi % 2]; C.xi += 1
    S.act(xn[:], xt, AF.Copy, scale=ss[:, 2:3])
    pt = C.pt[C.pti % 2]; C.pti += 1
    for k in range(8):
        S.tr(pt[:, k * 128:(k + 1) * 128], xn[:, k * 128:(k + 1) * 128], C.idb[:], sig=(k == 7))
    for k in range(8):
        S.ts("dve", hT[:, k, col0:col0 + 128], pt[:, k * 128:(k + 1) * 128], G[:, k:k + 1], ALU.mult, SH[:, k:k + 1], ALU.add)


def ffn(S, C, xs, xd, w1_d, w2_d, mv, ni, groups, jf, after_chunk=None):
    G, SH = mv["G"], mv["SH"]
    w2b = C.w2b
    first = True
    for grp in groups:
        nt = len(grp) * 128
        for li, ci in enumerate(grp):
            xt = C.xt[C.xti % 3]; C.xti += 1
            S.dma("sync", xt[:], xs(ci))
            j = jf(ci)
            norm_to_hT(S, C, xt[:], C.hT, li * 128, G[:, ni, :, j], SH[:, ni, :, j])
        def w_dma(ft_):
            wst_ = C.wst[ft_ % 2]
            S.dma("sync", wst_[:, 0:4, :], w1_d[ft_, :, 0:4, :])
            S.dma("sync", wst_[:, 4:8, :], w1_d[ft_, :, 4:8, :])

        def w_cast(ft_):
            wst_ = C.wst[ft_ % 2]; wb_ = C.w1b[ft_ % 2]
            S.copy("act", wb_[:, 0:4, :], wst_[:, 0:4, :]); S.copy("dve", wb_[:, 4:8, :], wst_[:, 4:8, :])
        w_dma(0)
        w_cast(0)
        w_dma(1)
        for ft in range(NFT):
            wb = C.w1b[ft % 2]
            if ft + 1 < NFT:
                w_cast(ft + 1)
            if ft + 2 < NFT:
                w_dma(ft + 2)
            if first:
                w2s = C.w2st[ft % 2]
                S.dma("sync", w2s[:], w2_d[ft * 128:(ft + 1) * 128, :])
                S.copy("act", w2b[:, ft, :], w2s[:])
            for b0 in range(0, nt, 512):
                bw = min(512, nt - b0)
                pg = C.pa[C.pai % 4]; C.pai += 1
                pu = C.pa[C.pai % 4]; C.pai += 1
                for k in range(8):
                    S.mm(pg[:, 0:bw], wb[:, k, 0:128], C.hT[:, k, b0:b0 + bw], start=(k == 0), stop=(k == 7))
                for k in range(8):
                    S.mm(pu[:, 0:bw], wb[:, k, 128:256], C.hT[:, k, b0:b0 + bw], start=(k == 0), stop=(k == 7))
                sg = C.sg[C.sgi % 2]; C.sgi += 1
                S.act(sg[:, 0:bw], pg[:, 0:bw], AF.Silu)
                S.tt("dve", C.actT[:, ft, b0:b0 + bw], sg[:, 0:bw], pu[:, 0:bw], ALU.mult)
        first = False
        for li, ci in enumerate(grp):
            xt = C.xt[C.xti % 3]; C.xti += 1
            S.dma("sync", xt[:], xs(ci))
            gb = C.gateb[(ni, jf(ci))]
            for h in range(2):
                pc = C.pcs[h]
                for ft in range(NFT):
                    S.mm(pc[:], C.actT[:, ft, li * 128:(li + 1) * 128], w2b[:, ft, h * 512:(h + 1) * 512],
                         start=(ft == 0), stop=(ft == NFT - 1))
                S.tt("dve", C.tmp[:, h * 512:(h + 1) * 512], pc[:], gb[:, h * 512:(h + 1) * 512], ALU.mult)
            S.tt("dve", xt[:], xt[:], C.tmp[:], ALU.add)
            if xd is not None:
                S.dma("pool", xd(ci), xt[:])
            if after_chunk is not None:
                after_chunk(ci, li, xt)
        yield grp


def alloc_common(S, PS):
    C = Ctx()
    C.small = [S.sb([128, 4], F32, f"small{i}") for i in range(4)]; C.si = 0
    C.junk = S.sb([128, 1024], BF16, "junk")
    C.xn = [S.sb([128, 1024], BF16, f"xn{i}") for i in range(2)]; C.xi = 0
    C.pt = PS["b"]; C.pti = 0
    C.pa = PS["g"][0:4]; C.pai = 0
    C.pcs = PS["g"][4:6]
    C.xt = [S.sb([128, 1024], F32, f"xt{i}") for i in range(3)]; C.xti = 0
    C.hT = S.sb([128, 8, 1152], BF16, "hT")
    C.actT = S.sb([128, NFT, 1152], BF16, "actT")
    C.w2b = S.sb([128, NFT, 1024], BF16, "w2b")
    C.wst = [S.sb([128, 8, 256], F32, f"wst{i}") for i in range(2)]
    C.w1b = [S.sb([128, 8, 256], BF16, f"w1b{i}") for i in range(2)]
    C.w2st = [S.sb([128, 1024], F32, f"w2st{i}") for i in range(2)]
    C.sg = [S.sb([128, 512], F32, f"sg{i}") for i in range(2)]; C.sgi = 0
    C.tmp = S.sb([128, 1024], F32, "tmp")
    C.ones = S.sb([128, 128], F32, "ones")
    S.memset("dve", C.ones[:], 1.0)
    return C


GROUPS34 = [list(range(0, 9)), list(range(9, 18)), list(range(18, 26)), list(range(26, 34))]
GROUPS32 = [list(range(0, 8)), list(range(8, 16)), list(range(16, 24)), list(range(24, 32))]
JF34 = lambda ci: 0 if ci < 2 else 1


def phase1(S, PS, IN, SC):
    C = alloc_common(S, PS)
    C.idf, C.idb = setup_consts(S, IN["ident"][:])
    mv = mod_vectors(S, IN["cT"][:], IN["modw"][0], IN["modbT"][0], IN["ngT"][0], C.wst, C.pa[0])
    S.dma("sync", SC["modT0"][:], mv["modT"][:].re("p t j -> p (t j)"))
    C.gateb = {}
    for j in range(2):
        C.gateb[(0, j)] = gate_bcast(S, mv["GATE"][:, 0, :, j], C.idf, C.ones, C.pa[1 + j], 0.5, f"gb0{j}")
    zt = S.sb([2, 2304], F32, "zt")
    S.memset("dve", zt[:], 0.0)
    ppad = SC["ppad"]
    S.dma("sync", ppad[0:1, :], zt[0:1, :]); S.dma("sync", ppad[257:259, :], zt[0:2, :]); S.dma("sync", ppad[4355:4356, :], zt[0:1, :])
    pst = [S.sb([128, 256], F32, f"pst{i}") for i in range(2)]
    psti = [0]

    def xs(ci):
        return IN["ctx"][ci * 128:(ci + 1) * 128, :] if ci < 2 else IN["x"][(ci - 2) * 128:(ci - 1) * 128, :]

    def xd(ci):
        return SC["x1"][ci * 128:(ci + 1) * 128, :]

    def after(ci, li, xt):
        j = JF34(ci)
        norm_to_hT(S, C, xt[:], C.hT, li * 128, mv["G"][:, 1, :, j], mv["SH"][:, 1, :, j])

    win = IN["win_ab"]
    for grp in ffn(S, C, xs, xd, IN["w1"][0, 0], IN["w2"][0, 0], mv, 0, GROUPS34, JF34, after_chunk=after):
        for cb in range(9):
            wst = C.wst[cb % 2]; wb = C.w1b[cb % 2]
            S.dma("sync", wst[:, 0:4, :], win[cb, :, 0:4, :])
            S.dma("sync", wst[:, 4:8, :], win[cb, :, 4:8, :])
            S.copy("act", wb[:, 0:4, :], wst[:, 0:4, :]); S.copy("dve", wb[:, 4:8, :], wst[:, 4:8, :])
            for li, ci in enumerate(grp):
                pp = C.pa[C.pai % 4]; C.pai += 1
                for k in range(8):
                    S.mm(pp[:, 0:256], C.hT[:, k, li * 128:(li + 1) * 128], wb[:, k, :], start=(k == 0), stop=(k == 7))
                st = pst[psti[0] % 2]; psti[0] += 1
                S.copy("act", st[:], pp[:, 0:256])
                S.dma("sync", ppad[prow(ci):prow(ci) + 128, cb * 256:(cb + 1) * 256], st[:])


def phase2a(S, PS, IN, SC, MD=F32R, nblocks=NCH):
    ppad = SC["ppad"]

    def ld(dv, shape, nm, dt=F32):
        t = S.sb(shape, dt, nm)
        S.dma("sync", t[:], dv)
        return t
    mu0 = ld(IN["mub"][0], [128, 1536], "mu0"); mu1 = ld(IN["mub"][1], [128, 1536], "mu1")
    c0 = S.sb([128, 1536], F32, "c0")
    S.tt("dve", c0[:], mu0[:], mu1[:], ALU.add)
    S.ts("dve", c0[:], c0[:], -1.0, ALU.mult, 1.0, ALU.add)
    kkb = ld(IN["kkb"][:], [128, 512], "kkb"); kab = ld(IN["kab"][:], [128, 512], "kab")
    omka = S.sb([128, 512], F32, "omka")
    S.ts("dve", omka[:], kab[:], -1.0, ALU.mult, 1.0, ALU.add)
    g2 = ld(IN["g2"][:], [128, 512], "g2")
    msk = IN["msk"]
    mUs = ld(msk[0], [128, 128], "mUs"); mUi = ld(msk[1], [128, 128], "mUi"); mLs = ld(msk[2], [128, 128], "mLs")
    idf = ld(msk[3], [128, 128], "idf"); mLi = ld(msk[4], [128, 128], "mLi")
    cm = {}
    for nm, m_ in (("Ui", mUi), ("Us", mUs), ("Ls", mLs), ("Li", mLi)):
        cm[nm] = S.sb([128, 128], MD, "c" + nm)
        S.ts("dve", cm[nm][:], m_[:], NEGC, ALU.mult)
    negc = S.sb([128, 2], MD, "negc")
    S.ts("dve", negc[:], mUi[:, 0:2], 0.0, ALU.mult, NEGC, ALU.add)
    idm = S.sb([128, 128], MD, "idm")
    S.copy("dve", idm[:], idf[:])
    w2a = []; a2a = []
    for dr in range(2):
        t_ = ld(IN["w2a"][dr], [65, 512], f"w2a{dr}"); t2_ = S.sb([65, 512], MD, f"w2ar{dr}"); S.copy("dve", t2_[:], t_[:]); w2a.append(t2_)
        t_ = ld(IN["a2a"][dr], [65, 512], f"a2a{dr}"); t2_ = S.sb([65, 512], MD, f"a2ar{dr}"); S.copy("dve", t2_[:], t_[:]); a2a.append(t2_)
    g2r = S.sb([128, 512], MD, "g2r"); S.copy("dve", g2r[:], g2[:]); g2 = g2r

    pb = PS["g"]
    pbi = [0]

    def bank():
        b = pb[pbi[0] % 6]; pbi[0] += 1
        return b

    def t512(nm, dt=F32):
        return S.sb([128, 512], dt, nm)

    rc = S.sb([128, 1536], F32, "rc"); rp = S.sb([128, 1536], F32, "rp"); rn_ = S.sb([128, 1536], F32, "rn")
    mix = S.sb([128, 1536], MD, "mix"); mt = S.sb([128, 1536], F32, "mt")
    TW = S.sb([65, 128], MD, "TW"); AL = S.sb([65, 128], MD, "AL"); SG = S.sb([128, 128], MD, "SG")
    S.ts("dve", TW[:], mUi[0:65, :], 0.0, ALU.mult, 1.0, ALU.add); S.ts("dve", AL[:], mUi[0:65, :], 0.0, ALU.mult, 1.0, ALU.add)
    lmix = S.sb([128, 256], MD, "lmix"); lmt = S.sb([128, 256], F32, "lmt")
    lc = S.sb([128, 256], F32, "lc"); lp = S.sb([128, 256], F32, "lp"); ln_ = S.sb([128, 256], F32, "ln")
    mul0 = ld(IN["mulb"][0], [128, 256], "mul0"); mul1 = ld(IN["mulb"][1], [128, 256], "mul1")
    c0lb = S.sb([128, 256], F32, "c0lb")
    S.tt("dve", c0lb[:], mul0[:], mul1[:], ALU.add)
    S.ts("dve", c0lb[:], c0lb[:], -1.0, ALU.mult, 1.0, ALU.add)
    sig = t512("sig", MD); a_ = t512("a"); gt = t512("gt")
    kk = t512("kk"); sq = t512("sq"); ss = S.sb([128, 8], F32, "ss"); rn8 = S.sb([128, 8], F32, "rn8")
    kd = t512("kd"); tq = t512("tq"); bq = t512("bq")
    Gc = t512("G"); Gp = t512("Gp"); Gi = t512("Gi"); Ge = t512("Ge")
    A = t512("A", MD); B = t512("B", MD); K = t512("K", MD); Rq = t512("Rq", MD)
    B2m = t512("B2m", MD); K2m = t512("K2m", MD)
    AT = S.sb([64, 8, 128], MD, "AT"); BT = S.sb([64, 8, 128], MD, "BT"); KT = S.sb([64, 8, 128], MD, "KT")
    RT = S.sb([64, 8, 128], MD, "RT")
    mat = lambda nm, dt=MD: [S.sb([128, 4, 128], dt, f"{nm}{g}") for g in range(2)]
    Nm = [mat("Nm0"), mat("Nm1")]; NT = [mat("NT0"), mat("NT1")]
    Mak = mat("Mak"); Mbr = mat("Mbr"); Mkr = mat("Mkr")
    Tf = mat("Tf", MD); Tm = Tf
    WTm = t512("WTm", MD); X1Tm = t512("X1Tm", MD); UlTm = t512("UlTm", MD)
    Rpf = S.sb([64, 8, 128], MD, "Rpf")
    gC = S.sb([64, 16], F32, "gC")
    dgG = S.sb([64, 8, 64], F32, "dgG")
    Pf = [S.sb([64, 8, 64], MD, f"Pf{c}") for c in range(2)]
    QTf = [S.sb([64, 8, 64], F32, f"QTf{c}") for c in range(2)]
    Yloc = t512("Yloc")
    ST = [S.sb([64, 8, 64], MD, f"ST{i}") for i in range(2)]
    yt = [t512(f"yt{i}") for i in range(2)]
    v3 = lambda v: v.re("p (h k) -> p h k", k=64)
    m3 = lambda v: v.re("p (h t) -> p h t", t=128)
    sti = 0
    for dr in range(2):
        if dr == 0:
            m_strict, m_strictT, m_incl = mUs, mLs, mUi
            c_incl, c_strict, c_end = cm["Ui"], cm["Us"], cm["Ls"]
            border = list(range(nblocks)); corder = [0, 1]
        else:
            m_strict, m_strictT, m_incl = mLs, mUs, mLi
            c_incl, c_strict, c_end = cm["Li"], cm["Ls"], cm["Us"]
            border = [1, 0] + list(range(NCH - 1, 1, -1)); corder = [1, 0]
            border = border[:nblocks]
        S.ts("dve", ST[sti % 2][:].re("p h k -> p (h k)"), kkb[0:64, :], 0.0, ALU.mult)
        for n in border:
            r0 = prow(n)
            S.dma("sync", rc[:], ppad[r0:r0 + 128, 0:1536])
            S.dma("pool", rp[:], ppad[r0 - 1:r0 + 127, 0:1536])
            S.dma("sync", rn_[:], ppad[r0 + 1:r0 + 129, 0:1536])
            S.dma("pool", lc[:], ppad[r0:r0 + 128, 1536:1792])
            S.dma("pool", lp[:], ppad[r0 - 1:r0 + 127, 1536:1792])
            S.dma("sync", ln_[:], ppad[r0 + 1:r0 + 129, 1536:1792])
            S.tt("dve", mix[:], rc[:], c0[:], ALU.mult)
            S.tt("dve", mt[:], rp[:], mu0[:], ALU.mult)
            S.tt("dve", mix[:], mix[:], mt[:], ALU.add)
            S.tt("dve", mt[:], rn_[:], mu1[:], ALU.mult)
            S.tt("dve", mix[:], mix[:], mt[:], ALU.add)
            r = mix[:, 0:512]; k = mix[:, 512:1024]; v = mix[:, 1024:1536]
            if dr == 0:
                S.dma("pool", SC["rvo"][n * 128:(n + 1) * 128, 0:512], r)
                S.dma("pool", SC["rvo"][n * 128:(n + 1) * 128, 512:1024], v)
            S.tt("dve", lmix[:], lc[:], c0lb[:], ALU.mult)
            S.tt("dve", lmt[:], lp[:], mul0[:], ALU.mult)
            S.tt("dve", lmix[:], lmix[:], lmt[:], ALU.add)
            S.tt("dve", lmt[:], ln_[:], mul1[:], ALU.mult)
            S.tt("dve", lmix[:], lmix[:], lmt[:], ALU.add)
            pl = bank()
            S.mm(pl[0:64, 0:128], lmix[:, 0:64], idm[:], sig=False, f32=True)
            S.mm(pl[0:64, 128:256], lmix[:, 64:128], idm[:], sig=False, f32=True)
            S.mm(pl[:, 256:384], lmix[:, 128:256], idm[:])
            S.act(TW[0:64, :], pl[0:64, 0:128], AF.Tanh)
            S.copy("act", AL[0:64, :], pl[0:64, 128:256])
            S.act(SG[:], pl[:, 256:384], AF.Sigmoid)
            pw_ = bank(); S.mm(pw_[:], TW[:], w2a[dr][:])
            S.act(sig[:], pw_[:], AF.Sigmoid)
            pa_ = bank(); S.mm(pa_[:], AL[:], a2a[dr][:])
            S.act(a_[:], pa_[:], AF.Sigmoid)
            if dr == 0:
                pg_ = bank(); S.mm(pg_[:], SG[:], g2[:])
                S.copy("act", gt[:], pg_[:])
                S.dma("pool", SC["gto"][n * 128:(n + 1) * 128, :], gt[:])
            S.tt("dve", kk[:], k, kkb[:], ALU.mult)
            S.tt("dve", sq[:], kk[:], kk[:], ALU.mult)
            S.red("dve", ss[:], v3(sq[:]), ALU.add)
            S.ts("dve", ss[:], ss[:], 1e-12, ALU.max)
            S.act(ss[:], ss[:], AF.Sqrt)
            S.recip(rn8[:], ss[:])
            S.tt("dve", v3(kk[:]), v3(kk[:]), rn8[:].re("p (h o) -> p h o", o=1).bc([128, 8, 64]), ALU.mult)
            S.tt("dve", tq[:], a_[:], kab[:], ALU.mult)
            S.tt("dve", tq[:], tq[:], omka[:], ALU.add)
            S.tt("dve", kd[:], k, tq[:], ALU.mult)
            S.dma("pool", SC["kdo"][dr, n * 128:(n + 1) * 128, :], kd[:])
            S.tt("dve", bq[:], kk[:], a_[:], ALU.mult)
            pc1 = bank(); S.mm(pc1[:], c_incl[:], sig[:])
            pc2 = bank(); S.mm(pc2[:], c_strict[:], sig[:])
            pc3 = bank(); S.mm(pc3[:], c_end[:], sig[:])
            S.act(Gc[:], pc1[:], AF.Exp)
            S.act(Gi[:], pc1[:], AF.Exp, scale=-1.0)
            S.act(Gp[:], pc2[:], AF.Exp)
            S.act(Ge[:], pc3[:], AF.Exp)
            S.stt("dve", A[:], kk[:], -1.0, Gp[:], ALU.mult, ALU.mult)
            S.tt("dve", B[:], bq[:], Gi[:], ALU.mult)
            S.tt("dve", K[:], kd[:], Gi[:], ALU.mult)
            S.tt("dve", Rq[:], r, Gc[:], ALU.mult)
            S.tt("dve", B2m[:], bq[:], Ge[:], ALU.mult)
            S.tt("dve", K2m[:], kd[:], Ge[:], ALU.mult)
            Am_ = A
            Vv = v
            for (src, dst) in ((A, AT), (B, BT), (K, KT), (Rq, RT)):
                for hg in range(2):
                    p_ = bank()
                    for hh in range(4):
                        h = hg * 4 + hh
                        S.mm(p_[0:64, hh * 128:(hh + 1) * 128], src[:, h * 64:(h + 1) * 64], idm[:], sig=(hh == 3), f32=True)
                    S.copy("act", dst[:, hg * 4:(hg + 1) * 4, :], m3(p_[0:64, :]))
            RTf_ = RT

            def mmat(dst, LT, RTt, mask, hg):
                p_ = bank()
                for hh in range(4):
                    h = hg * 4 + hh
                    S.mm(p_[:, hh * 128:(hh + 1) * 128], LT[:, h, :], RTt[:, h, :], sig=(hh == 3))
                S.tt("dve", dst[hg][:], m3(p_[:]), mask[:].re("p (o t) -> p o t", o=1).bc([128, 4, 128]), ALU.mult)
            for hg in range(2):
                mmat(Nm[0], BT, AT, m_strict, hg)
                mmat(NT[0], AT, BT, m_strictT, hg)
                mmat(Mak, KT, AT, m_strict, hg)
                mmat(Mbr, BT, RT, m_incl, hg)
                mmat(Mkr, KT, RT, m_incl, hg)
            for hg in range(2):
                S.tt("dve", Tf[hg][:], Nm[0][hg][:], idf[:].re("p (o t) -> p o t", o=1).bc([128, 4, 128]), ALU.add)
            cur = 0
            for lev in range(5):
                nxt = 1 - cur
                last = lev == 4
                for hg in range(2):
                    if not last:
                        p1 = bank()
                        for hh in range(4):
                            S.mm(p1[:, hh * 128:(hh + 1) * 128], NT[cur][hg][:, hh, :], Nm[cur][hg][:, hh, :], sig=(hh == 3))
                        S.copy("act", Nm[nxt][hg][:], m3(p1[:]))
                    p2 = bank()
                    for hh in range(4):
                        S.mm(p2[:, hh * 128:(hh + 1) * 128], Nm[cur][hg][:, hh, :], NT[cur][hg][:, hh, :], sig=(hh == 3))
                    S.copy("dve" if hg == 0 else "act", NT[nxt][hg][:], m3(p2[:]))
                for hg in range(2):
                    p3 = bank()
                    for hh in range(4):
                        S.mm(p3[:, hh * 128:(hh + 1) * 128], NT[nxt][hg][:, hh, :], Tm[hg][:, hh, :], sig=(hh == 3))
                    S.tt("dve", Tf[hg][:], Tf[hg][:], m3(p3[:]), ALU.add)
                cur = nxt
            p_ = bank()
            for h in range(8):
                S.mm(p_[:, h * 64:(h + 1) * 64], Tm[h // 4][:, h % 4, :], Am_[:, h * 64:(h + 1) * 64], sig=(h == 7))
            S.copy("act", WTm[:], p_[:])
            p_ = bank()
            for h in range(8):
                S.mm(p_[:, h * 64:(h + 1) * 64], Mak[h // 4][:, h % 4, :], Vv[:, h * 64:(h + 1) * 64], sig=(h == 7))
            S.copy("dve", X1Tm[:], p_[:])
            p_ = bank()
            for h in range(8):
                S.mm(p_[:, h * 64:(h + 1) * 64], Tm[h // 4][:, h % 4, :], X1Tm[:, h * 64:(h + 1) * 64], sig=(h == 7))
            S.copy("act", UlTm[:], p_[:])
            for hg in range(2):
                p_ = bank()
                for hh in range(4):
                    h = hg * 4 + hh
                    S.mm(p_[0:64, hh * 128:(hh + 1) * 128], WTm[:, h * 64:(h + 1) * 64], Mbr[hg][:, hh, :], sig=(hh == 3), f32=True)
                S.tt("dve", Rpf[:, hg * 4:(hg + 1) * 4, :], m3(p_[0:64, :]), RTf_[:, hg * 4:(hg + 1) * 4, :], ALU.add)
            pgc = [bank(), bank()]
            for c in range(2):
                for h in range(8):
                    S.mm(pgc[c][0:64, h:h + 1], sig[c * 64:(c + 1) * 64, h * 64:(h + 1) * 64], negc[c * 64:(c + 1) * 64, 0:1], sig=(h == 7), f32=True)
                S.act(gC[:, c * 8:(c + 1) * 8], pgc[c][0:64, 0:8], AF.Exp)
            for c in range(2):
                pc = slice(c * 64, (c + 1) * 64)
                p_ = bank()
                for h in range(8):
                    S.mm(p_[0:64, h * 64:(h + 1) * 64], WTm[pc, h * 64:(h + 1) * 64], B2m[pc, h * 64:(h + 1) * 64], sig=(h == 7), f32=True)
                S.tt("dve", dgG[:], idf[0:64, 0:64].re("p (o k) -> p o k", o=1).bc([64, 8, 64]),
                     gC[:, c * 8:(c + 1) * 8].re("p (h o) -> p h o", o=1).bc([64, 8, 64]), ALU.mult)
                S.tt("dve", Pf[c][:], v3(p_[0:64, :]), dgG[:], ALU.add)
                p_ = bank()
                for h in range(8):
                    S.mm(p_[0:64, h * 64:(h + 1) * 64], B2m[pc, h * 64:(h + 1) * 64], UlTm[pc, h * 64:(h + 1) * 64], start=True, stop=False, sig=False, f32=True)
                    S.mm(p_[0:64, h * 64:(h + 1) * 64], K2m[pc, h * 64:(h + 1) * 64], Vv[pc, h * 64:(h + 1) * 64], start=False, stop=True, sig=(h == 7), f32=True)
                S.copy("act", QTf[c][:], v3(p_[0:64, :]))
            p_ = bank()
            for h in range(8):
                S.mm(p_[:, h * 64:(h + 1) * 64], Mbr[h // 4][:, h % 4, :], UlTm[:, h * 64:(h + 1) * 64], start=True, stop=False, sig=False)
                S.mm(p_[:, h * 64:(h + 1) * 64], Mkr[h // 4][:, h % 4, :], Vv[:, h * 64:(h + 1) * 64], start=False, stop=True, sig=(h == 7))
            S.copy("act", Yloc[:], p_[:])
            ytile = yt[n % 2]
            for c in corder:
                pc = slice(c * 64, (c + 1) * 64)
                st_cur = ST[sti % 2]; st_nxt = ST[(sti + 1) % 2]; sti += 1
                p_ = bank()
                for h in range(8):
                    S.mm(p_[:, h * 64:(h + 1) * 64], Rpf[:, h, :], st_cur[:, h, :], sig=(h == 7))
                S.tt("dve", ytile[pc, :], p_[pc, :], Yloc[pc, :], ALU.add)
                p2 = bank()
                for h in range(8):
                    S.mm(p2[0:64, h * 64:(h + 1) * 64], Pf[c][:, h, :], st_cur[:, h, :], sig=(h == 7), f32=True)
                S.tt("dve", st_nxt[:], v3(p2[0:64, :]), QTf[c][:], ALU.add)
            S.dma("sync", SC["yd"][dr, n * 128:(n + 1) * 128, :], ytile[:])


def phase2b(S, PS, IN, SC):
    CH = 128
    ppad = SC["ppad"]
    ident = IN["ident"]
    idf0 = S.sb([128, 128], F32, "idf0"); S.dma("sync", idf0[:], ident[:])
    Jm0 = S.sb([128, 128], F32, "Jm0"); S.dma("sync", Jm0[:], IN["msk"][5])
    idf = S.sb([128, 128], F32R, "idf"); S.copy("dve", idf[:], idf0[:])
    Jm = S.sb([128, 128], F32R, "Jm"); S.copy("dve", Jm[:], Jm0[:])
    pb = PS["g"]
    sm = lambda nm: S.sb([128, NST], F32, nm)
    big = lambda nm, dt=F32: S.sb([128, NST, 32], dt, nm)
    are = sm("are"); aim = sm("aim"); lst = sm("lst")
    bre = big("bre"); bim = big("bim"); cre0 = big("cre0"); cim = big("cim"); ncim = big("ncim", F32R); cre = big("cre", F32R)
    lre = sm("lre"); dt_ = sm("dt"); zr = sm("zr"); th = sm("th"); rho = sm("rho")
    sa = sm("sa"); sk = sm("sk"); sr = sm("sr")
    cs = sm("cs"); sn = sm("sn")
    abre = sm("abre"); abim = sm("abim"); den = sm("den"); rden = sm("rden"); t1 = sm("t1"); t2 = sm("t2")
    fre = sm("fre"); fim = sm("fim"); am1 = sm("am1")
    bbre = big("bbre", F32R); bbim = big("bbim", F32R); u1 = big("u1"); u2 = big("u2")
    BBTre = S.sb([32, NST, 128], F32R, "BBTre"); BBTim = S.sb([32, NST, 128], F32R, "BBTim")
    Ec = S.sb([128, NST, CH], F32, "Ec"); Es = S.sb([128, NST, CH], F32, "Es")
    w1 = S.sb([128, NST, CH // 2], F32, "w1"); w2 = S.sb([128, NST, CH // 2], F32, "w2")
    rhob = S.sb([128, NST, CH], F32, "rhob")
    utok = [S.sb([128, 512], F32, "utok0")] * 2
    utokr = [S.sb([128, 512], F32R, f"utokr{i}") for i in range(2)]
    ut = [S.sb([32, NST, CH], F32R, f"ut{i}") for i in range(2)]
    Zre_ = [S.sb([128, NST, CH], F32, "Zre0")] * 2; Zim_ = [S.sb([128, NST, CH], F32, "Zim0")] * 2
    wre_ = [S.sb([128, NST, CH], F32, f"wre{i}") for i in range(2)]; wim_ = [S.sb([128, NST, CH], F32, f"wim{i}") for i in range(2)]
    xlr = [S.sb([128, NST], F32, f"xlr{i}") for i in range(2)]; xli = [S.sb([128, NST], F32, f"xli{i}") for i in range(2)]
    xl1 = S.sb([128, NST], F32, "xl1"); xl2 = S.sb([128, NST], F32, "xl2")
    xre = [S.sb([128, NST, CH], F32R, f"xre{i}") for i in range(2)]
    xim = [S.sb([128, NST, CH], F32R, f"xim{i}") for i in range(2)]
    ta_ = [[S.sb([128, 4, CH], F32, f"ta{q}{i}") for i in range(4)] for q in range(2)]
    tb_ = [[S.sb([128, 4, CH], F32, f"tb0{i}") for i in range(4)]] * 2
    pbi = 0
    tai = 0
    yst = [S.sb([128, 512], F32R, f"yst{i}") for i in range(2)]
    yst2 = [S.sb([128, 512], F32, f"ystb{i}") for i in range(2)]
    MAGIC = 12582912.0
    TWO_PI = 2.0 * math.pi
    pbi = 0
    cc = 0
    for dr in range(2):
        for (t_, nm) in ((are, "are"), (aim, "aim"), (lst, "lst")):
            S.dma("sync", t_[:], IN[nm][dr])
        for (t_, nm) in ((bre, "bre"), (bim, "bim"), (cre0, "cre"), (cim, "cim")):
            S.dma("sync", t_[:], IN[nm][dr])
        S.ts("dve", ncim[:], cim[:], -1.0, ALU.mult)
        S.copy("dve", cre[:], cre0[:])
        S.ts("dve", lre[:], are[:], -1e-4, ALU.min)
        S.act(dt_[:], lst[:], AF.Exp)
        S.tt("dve", zr[:], lre[:], dt_[:], ALU.mult)
        S.tt("dve", th[:], aim[:], dt_[:], ALU.mult)
        S.act(rho[:], zr[:], AF.Exp)

        def sin_reduced(out, ang, shift):
            S.ts("dve", sa[:], ang[:], float(shift), ALU.add)
            S.ts("dve", sk[:], sa[:], 1.0 / TWO_PI, ALU.mult, MAGIC, ALU.add)
            S.ts("dve", sk[:], sk[:], MAGIC, ALU.subtract)
            S.stt("dve", sr[:], sk[:], -TWO_PI, sa[:], ALU.mult, ALU.add)
            S.ts("dve", sr[:], sr[:], 3.14159, ALU.min, -3.14159, ALU.max)
            S.act(out, sr[:], AF.Sin)
        sin_reduced(sn[:], th, 0.0)
        sin_reduced(cs[:], th, math.pi / 2)
        S.tt("dve", abre[:], rho[:], cs[:], ALU.mult)
        S.tt("dve", abim[:], rho[:], sn[:], ALU.mult)
        S.tt("dve", t1[:], lre[:], lre[:], ALU.mult)
        S.tt("dve", t2[:], aim[:], aim[:], ALU.mult)
        S.tt("dve", den[:], t1[:], t2[:], ALU.add)
        S.recip(rden[:], den[:])
        S.ts("dve", am1[:], abre[:], -1.0, ALU.add)
        S.tt("dve", t1[:], am1[:], lre[:], ALU.mult)
        S.tt("dve", t2[:], abim[:], aim[:], ALU.mult)
        S.tt("dve", t1[:], t1[:], t2[:], ALU.add)
        S.tt("dve", fre[:], t1[:], rden[:], ALU.mult)
        S.tt("dve", t1[:], abim[:], lre[:], ALU.mult)
        S.tt("dve", t2[:], am1[:], aim[:], ALU.mult)
        S.tt("dve", t1[:], t1[:], t2[:], ALU.subtract)
        S.tt("dve", fim[:], t1[:], rden[:], ALU.mult)
        fre_b = fre[:].re("p (t o) -> p t o", o=1).bc([128, NST, 32])
        fim_b = fim[:].re("p (t o) -> p t o", o=1).bc([128, NST, 32])
        S.tt("dve", u1[:], bre[:], fre_b, ALU.mult)
        S.tt("dve", u2[:], bim[:], fim_b, ALU.mult)
        S.tt("dve", bbre[:], u1[:], u2[:], ALU.subtract)
        S.tt("dve", u1[:], bim[:], fre_b, ALU.mult)
        S.tt("dve", u2[:], bre[:], fim_b, ALU.mult)
        S.tt("dve", bbim[:], u1[:], u2[:], ALU.add)
        for (src, dst) in ((bbre, BBTre), (bbim, BBTim)):
            for g4 in range(4):
                p_ = pb[pbi % 6]; pbi += 1
                for jj in range(4):
                    j = g4 * 4 + jj
                    S.mm(p_[0:32, jj * 128:(jj + 1) * 128], src[:, j, :], idf[:], sig=(jj == 3), f32=True)
                S.copy("dve", dst[:, g4 * 4:(g4 + 1) * 4, :], p_[0:32, :].re("p (a b) -> p a b", b=128))
        S.copy("dve", Ec[:, :, 0], cs[:])
        S.copy("dve", Es[:, :, 0], sn[:])
        m = 1
        while m < CH:
            cb = Ec[:, :, m - 1:m].bc([128, NST, m]); sb_ = Es[:, :, m - 1:m].bc([128, NST, m])
            S.tt("dve", w1[:, :, 0:m], Ec[:, :, 0:m], cb, ALU.mult)
            S.tt("dve", w2[:, :, 0:m], Es[:, :, 0:m], sb_, ALU.mult)
            S.tt("dve", Ec[:, :, m:2 * m], w1[:, :, 0:m], w2[:, :, 0:m], ALU.subtract)
            S.tt("dve", w1[:, :, 0:m], Ec[:, :, 0:m], sb_, ALU.mult)
            S.tt("dve", w2[:, :, 0:m], Es[:, :, 0:m], cb, ALU.mult)
            S.tt("dve", Es[:, :, m:2 * m], w1[:, :, 0:m], w2[:, :, 0:m], ALU.add)
            m *= 2
        S.copy("dve", rhob[:], rho[:].re("p (t o) -> p t o", o=1).bc([128, NST, CH]))
        Pm = idf if dr == 0 else Jm
        border = list(range(NCH)) if dr == 0 else [1, 0] + list(range(NCH - 1, 1, -1))
        def stageA(ci, n, cc):
            nonlocal pbi, tai
            r0 = prow(n)
            utk0 = utok[cc % 2]
            S.dma("sync", utk0[:], ppad[r0:r0 + 128, 1792:2304])
            utk = utokr[cc % 2]
            S.copy("act", utk[:], utk0[:])
            u = ut[cc % 2]
            for g4 in range(4):
                p_ = pb[pbi % 6]; pbi += 1
                for jj in range(4):
                    j = g4 * 4 + jj
                    S.mm(p_[0:32, jj * 128:(jj + 1) * 128], utk[:, j * 32:(j + 1) * 32], Pm[:], sig=(jj == 3), f32=True)
                S.copy("act", u[:, g4 * 4:(g4 + 1) * 4, :], p_[0:32, :].re("p (a b) -> p a b", b=128))
            Zre, Zim = Zre_[cc % 2], Zim_[cc % 2]
            for g4 in range(4):
                ta = ta_[tai % 2]; tai += 1
                pr = pb[pbi % 6]; pbi += 1
                pi_ = pb[pbi % 6]; pbi += 1
                for jj in range(4):
                    j = g4 * 4 + jj
                    S.mm(pr[:, jj * CH:(jj + 1) * CH], BBTre[:, j, :], u[:, j, :], sig=False)
                for jj in range(4):
                    j = g4 * 4 + jj
                    S.mm(pi_[:, jj * CH:(jj + 1) * CH], BBTim[:, j, :], u[:, j, :], sig=(jj == 3))
                sl = slice(g4 * 4, (g4 + 1) * 4)
                prv = pr[:, :].re("p (a b) -> p a b", b=CH); piv = pi_[:, :].re("p (a b) -> p a b", b=CH)
                a0, a1, a2, a3 = ta
                S.tt("dve", a0[:], prv, Ec[:, sl, :], ALU.mult)
                S.tt("dve", a1[:], piv, Es[:, sl, :], ALU.mult)
                S.tt("dve", Zre[:, sl, :], a0[:], a1[:], ALU.add)
                S.tt("dve", a2[:], piv, Ec[:, sl, :], ALU.mult)
                S.tt("dve", a3[:], prv, Es[:, sl, :], ALU.mult)
                S.tt("dve", Zim[:, sl, :], a2[:], a3[:], ALU.subtract)

        def stageSc(ci, cc):
            Zre, Zim = Zre_[cc % 2], Zim_[cc % 2]
            wre, wim = wre_[cc % 2], wim_[cc % 2]
            for j in range(NST):
                for (wt, zt, xp) in ((wre, Zre, xlr[(cc + 1) % 2]), (wim, Zim, xli[(cc + 1) % 2])):
                    init = 0.0 if ci == 0 else xp[:, j:j + 1]
                    S.scan(wt[:, j, :], rhob[:, j, :], zt[:, j, :], init)
            L_ = CH - 1
            S.tt("dve", xl1[:], wre[:, :, L_], Ec[:, :, L_], ALU.mult)
            S.tt("dve", xl2[:], wim[:, :, L_], Es[:, :, L_], ALU.mult)
            S.tt("dve", xlr[cc % 2][:], xl1[:], xl2[:], ALU.subtract)
            S.tt("dve", xl1[:], wim[:, :, L_], Ec[:, :, L_], ALU.mult)
            S.tt("dve", xl2[:], wre[:, :, L_], Es[:, :, L_], ALU.mult)
            S.tt("dve", xli[cc % 2][:], xl1[:], xl2[:], ALU.add)

        def stageB(ci, n, cc):
            nonlocal pbi
            xr, xi = xre[cc % 2], xim[cc % 2]
            wre, wim = wre_[cc % 2], wim_[cc % 2]
            tb = tb_[0]
            for g4 in range(4):
                sl = slice(g4 * 4, (g4 + 1) * 4)
                b0, b1, b2, b3 = tb
                S.tt("pool", b0[:], wre[:, sl, :], Ec[:, sl, :], ALU.mult)
                S.tt("pool", b1[:], wim[:, sl, :], Es[:, sl, :], ALU.mult)
                S.tt("pool", xr[:, sl, :], b0[:], b1[:], ALU.subtract)
                S.tt("pool", b2[:], wim[:, sl, :], Ec[:, sl, :], ALU.mult)
                S.tt("pool", b3[:], wre[:, sl, :], Es[:, sl, :], ALU.mult)
                S.tt("pool", xi[:, sl, :], b2[:], b3[:], ALU.add)
            py = pb[pbi % 6]; pbi += 1
            for j in range(NST):
                S.mm(py[:, j * 32:(j + 1) * 32], xr[:, j, :], cre[:, j, :], start=True, stop=False, sig=False)
                S.mm(py[:, j * 32:(j + 1) * 32], xi[:, j, :], ncim[:, j, :], start=False, stop=True, sig=(j == NST - 1))
            ys = yst[cc % 2]
            S.copy("act", ys[:], py[:])
            py2 = pb[pbi % 6]; pbi += 1
            S.mm(py2[:], Pm[:], ys[:])
            ys2 = yst2[cc % 2]
            S.copy("act", ys2[:], py2[:])
            S.dma("sync", SC["ys"][dr, n * 128:(n + 1) * 128, :], ys2[:])

        stageA(0, border[0], cc)
        for ci, n in enumerate(border):
            stageSc(ci, cc)
            if ci + 1 < len(border):
                stageA(ci + 1, border[ci + 1], cc + 1)
            stageB(ci, n, cc)
            cc += 1


def phase3a(S, PS, IN, SC):
    idf, idb = setup_consts(S, IN["ident"][:])
    ones = S.sb([128, 128], F32, "ones"); S.memset("dve", ones[:], 1.0)
    pa = PS["g"]; pt = PS["b"]
    modT = S.sb([128, 72, 2], F32, "modT")
    S.dma("sync", modT[:].re("p t j -> p (t j)"), SC["modT0"][:])
    gateb = [gate_bcast(S, modT[:, 5 * 8:6 * 8, j], idf, ones, pa[j], 1.0, f"g5{j}") for j in range(2)]
    bcs = [S.sb([128, 512], F32, f"bcs{i}") for i in range(5)]
    for i in range(5):
        S.dma("sync", bcs[i][:], IN["bcs"][i])
    lnxg, lnxb, rk, s5d, glub = bcs
    S.ts("dve", rk[:], rk[:], 0.5, ALU.mult)
    wst = S.sb([128, 4, 512], F32, "wst")
    gluw = S.sb([128, 4, 512], BF16, "gluw")
    S.dma("sync", wst[:], IN["gluw"].re("(k p) n -> p k n", p=128))
    S.copy("act", gluw[:], wst[:])
    outw = S.sb([128, 8, 1024], BF16, "outw")
    wst2 = [S.sb([128, 1024], F32, f"wst2{i}") for i in range(2)]
    for k in range(8):
        S.dma("sync", wst2[k % 2][:], IN["outw_ab"][k * 128:(k + 1) * 128, :])
        S.copy("act", outw[:, k, :], wst2[k % 2][:])
    t5 = lambda nm, dt=F32: S.sb([128, 512], dt, nm)
    inr = [[t5(f"inr{i}{j}") for j in range(7)] for i in range(2)]
    ins = [[t5(f"ins{i}{j}") for j in range(3)] for i in range(2)]
    xt = [S.sb([128, 1024], F32, f"xt{i}") for i in range(2)]
    y = t5("y"); yc = t5("yc"); sq = t5("sq"); ks = t5("ks"); tq = t5("tq"); bon = t5("bon")
    s8 = S.sb([128, 8], F32, "s8"); v8 = S.sb([128, 8], F32, "v8"); b8 = S.sb([128, 8], F32, "b8")
    cat = S.sb([128, 1024], BF16, "cat"); ysum = t5("ysum"); z = t5("z"); zb = t5("zb", BF16)
    zT = S.sb([128, 4, 128], BF16, "zT"); gl = t5("gl"); catT = S.sb([128, 8, 128], BF16, "catT")
    tmp = S.sb([128, 1024], F32, "tmp")
    v3 = lambda v: v.re("p (h k) -> p h k", k=64)
    b3 = lambda t: t[:].re("p (h o) -> p h o", o=1).bc([128, 8, 64])
    for ci in range(NCH):
        j = JF34(ci)
        i2 = ci % 2
        rows = slice(ci * 128, (ci + 1) * 128)
        srcs = [SC["yd"][0, rows, :], SC["yd"][1, rows, :], SC["kdo"][0, rows, :], SC["kdo"][1, rows, :],
                SC["rvo"][rows, 0:512], SC["rvo"][rows, 512:1024], SC["gto"][rows, :]]
        for q in range(7):
            S.dma("sync" if q % 2 == 0 else "pool", inr[i2][q][:], srcs[q])
        srcs2 = [SC["ys"][0, rows, :], SC["ys"][1, rows, :], SC["ppad"][prow(ci):prow(ci) + 128, 1792:2304]]
        for q in range(3):
            S.dma("pool" if q % 2 == 0 else "sync", ins[i2][q][:], srcs2[q])
        S.dma("sync", xt[i2][:], SC["x1"][rows, :])
        y0, y1, kd0, kd1, r, v, g = inr[i2]
        S.tt("dve", y[:], y0[:], y1[:], ALU.add)
        S.red("dve", s8[:], v3(y[:]), ALU.add)
        S.ts("dve", s8[:], s8[:], 1.0 / 64, ALU.mult)
        S.tt("dve", v3(yc[:]), v3(y[:]), b3(s8), ALU.subtract)
        S.tt("dve", sq[:], yc[:], yc[:], ALU.mult)
        S.red("dve", v8[:], v3(sq[:]), ALU.add)
        S.ts("dve", v8[:], v8[:], 1.0 / 64, ALU.mult, GN_EPS, ALU.add)
        S.act(v8[:], v8[:], AF.Sqrt)
        S.recip(v8[:], v8[:])
        S.tt("dve", v3(yc[:]), v3(yc[:]), b3(v8), ALU.mult)
        S.tt("dve", yc[:], yc[:], lnxg[:], ALU.mult)
        S.tt("dve", yc[:], yc[:], lnxb[:], ALU.add)
        S.tt("dve", ks[:], kd0[:], kd1[:], ALU.add)
        S.tt("dve", tq[:], r[:], ks[:], ALU.mult)
        S.tt("dve", tq[:], tq[:], rk[:], ALU.mult)
        S.red("dve", b8[:], v3(tq[:]), ALU.add)
        S.tt("dve", v3(bon[:]), v3(v[:]), b3(b8), ALU.mult)
        S.tt("dve", yc[:], yc[:], bon[:], ALU.add)
        S.tt("dve", cat[:, 0:512], yc[:], g[:], ALU.mult)
        ys0, ys1, u = ins[i2]
        S.tt("dve", ysum[:], ys0[:], ys1[:], ALU.add)
        S.tt("dve", tq[:], u[:], s5d[:], ALU.mult)
        S.tt("dve", ysum[:], ysum[:], tq[:], ALU.add)
        S.act(z[:], ysum[:], AF.Gelu)
        S.copy("act", zb[:], z[:])
        p_ = pt[0]
        for k in range(4):
            S.tr(p_[:, k * 128:(k + 1) * 128], zb[:, k * 128:(k + 1) * 128], idb[:], sig=(k == 3))
        S.copy("dve", zT[:], p_[:, 0:512].re("p (k t) -> p k t", t=128))
        pg = pa[2]
        for k in range(4):
            S.mm(pg[:], zT[:, k, :], gluw[:, k, :], start=(k == 0), stop=(k == 3))
        S.tt("dve", gl[:], pg[:], glub[:], ALU.add)
        S.act(gl[:], gl[:], AF.Sigmoid)
        S.tt("dve", cat[:, 512:1024], z[:], gl[:], ALU.mult)
        p_ = pt[1]
        for k in range(8):
            S.tr(p_[:, k * 128:(k + 1) * 128], cat[:, k * 128:(k + 1) * 128], idb[:], sig=(k == 7))
        S.copy("dve", catT[:], p_[:].re("p (k t) -> p k t", t=128))
        for h in range(2):
            pc = pa[4 + h]
            for k in range(8):
                S.mm(pc[:], catT[:, k, :], outw[:, k, h * 512:(h + 1) * 512], start=(k == 0), stop=(k == 7))
            S.tt("dve", tmp[:, h * 512:(h + 1) * 512], pc[:], gateb[j][:, h * 512:(h + 1) * 512], ALU.mult)
        S.tt("dve", xt[i2][:], xt[i2][:], tmp[:], ALU.add)
        S.dma("pool", SC["xm"][rows, :], xt[i2][:])


def phase3b(S, PS, IN, SC):
    C = alloc_common(S, PS)
    C.idf, C.idb = setup_consts(S, IN["ident"][:])
    modT0 = S.sb([128, 72, 2], F32, "modT0")
    S.dma("sync", modT0[:].re("p t j -> p (t j)"), SC["modT0"][:])
    ngT0 = S.sb([128, 3, 8], F32, "ngT0")
    S.dma("sync", ngT0[:], IN["ngT"][0])
    mv0 = mod_derive(S, modT0, ngT0)
    C.gateb = {}
    for j in range(2):
        C.gateb[(2, j)] = gate_bcast(S, mv0["GATE"][:, 2, :, j], C.idf, C.ones, C.pa[j], 0.5, f"gb2{j}")
    rows = lambda t: (lambda ci: t[ci * 128:(ci + 1) * 128, :])
    for grp in ffn(S, C, rows(SC["xm"]), rows(SC["xl0"]), IN["w1"][0, 1], IN["w2"][0, 1], mv0, 2, GROUPS34, JF34):
        pass
    mv1 = mod_vectors(S, IN["cT"][:], IN["modw"][1], IN["modbT"][1], IN["ngT"][1], C.wst, C.pa[0])
    S.dma("sync", SC["modT1"][:], mv1["modT"][:].re("p t j -> p (t j)"))
    for j in range(2):
        C.gateb[(0, j)] = gate_bcast(S, mv1["GATE"][:, 0, :, j], C.idf, C.ones, C.pa[2 + j], 0.5, f"gb0{j}")
    cos = S.sb([128, NCH, 32], F32, "cos"); sin = S.sb([128, NCH, 32], F32, "sin")
    S.dma("sync", cos[:], IN["rope"][0].re("c p f -> p c f"))
    S.dma("sync", sin[:], IN["rope"][1].re("c p f -> p c f"))
    pst = [S.sb([128, 256], F32, f"pst{i}") for i in range(2)]
    ra = [S.sb([128, 4, 32], F32, f"ra{i}") for i in range(4)]
    psti = [0]

    def after(ci, li, xt):
        j = JF34(ci)
        norm_to_hT(S, C, xt[:], C.hT, li * 128, mv1["G"][:, 1, :, j], mv1["SH"][:, 1, :, j])
    win = IN["win_at"]
    qkv = SC["qkv"]
    for grp in ffn(S, C, rows(SC["xl0"]), rows(SC["x2"]), IN["w1"][1, 0], IN["w2"][1, 0], mv1, 0, GROUPS34, JF34, after_chunk=after):
        for cb in range(6):
            wst = C.wst[cb % 2]; wb = C.w1b[cb % 2]
            S.dma("sync", wst[:, 0:4, :], win[cb, :, 0:4, :])
            S.dma("sync", wst[:, 4:8, :], win[cb, :, 4:8, :])
            S.copy("act", wb[:, 0:4, :], wst[:, 0:4, :]); S.copy("dve", wb[:, 4:8, :], wst[:, 4:8, :])
            for li, ci in enumerate(grp):
                pp = C.pa[C.pai % 4]; C.pai += 1
                for k in range(8):
                    S.mm(pp[:, 0:256], C.hT[:, k, li * 128:(li + 1) * 128], wb[:, k, :], start=(k == 0), stop=(k == 7))
                st = pst[psti[0] % 2]; psti[0] += 1
                if cb < 5:
                    pv = pp[:, 0:256].re("p (h two f) -> p h two f", two=2, f=32)
                    sv = st[:].re("p (h two f) -> p h two f", two=2, f=32)
                    cb_ = cos[:, ci, :].re("p (o f) -> p o f", o=1).bc([128, 4, 32])
                    sb_ = sin[:, ci, :].re("p (o f) -> p o f", o=1).bc([128, 4, 32])
                    a, b, c, dd = ra
                    S.tt("dve", a[:], pv[:, :, 0, :], cb_, ALU.mult)
                    S.tt("dve", b[:], pv[:, :, 1, :], sb_, ALU.mult)
                    S.tt("pool", sv[:, :, 0, :], a[:], b[:], ALU.subtract)
                    S.tt("dve", c[:], pv[:, :, 1, :], cb_, ALU.mult)
                    S.tt("dve", dd[:], pv[:, :, 0, :], sb_, ALU.mult)
                    S.tt("pool", sv[:, :, 1, :], c[:], dd[:], ALU.add)
                else:
                    S.copy("act", st[:], pp[:, 0:256])
                S.dma("sync", qkv[ci * 128:(ci + 1) * 128, cb * 256:(cb + 1) * 256], st[:])


def phase4a(S, PS, IN, SC):
    idf, idb = setup_consts(S, IN["ident"][:])
    ones = S.sb([128, 128], F32, "ones"); S.memset("dve", ones[:], 1.0)
    pa = PS["g"][0:4]; pai = [0]
    ptb = PS["b"][0]
    pos = PS["g"][4:6]
    qkv = SC["qkv"]
    modT = S.sb([128, 72, 2], F32, "modT")
    S.dma("sync", modT[:].re("p t j -> p (t j)"), SC["modT1"][:])
    gate5 = gate_bcast(S, modT[:, 5 * 8:6 * 8, 1], idf, ones, pa[0], 1.0, "g5")
    sinkb = S.sb([128, 16], F32, "sinkb"); S.dma("sync", sinkb[:], IN["sinkb"][:])
    mt16 = S.sb([128, 16, 3], F32, "mt16")
    S.copy("dve", mt16[:, :, 2], sinkb[:])
    mstage = S.sb([128, 384], F32, "mstage")
    maskb = S.sb([128, 3, 384], BF16, "maskb")
    for i in range(3):
        S.dma("sync", mstage[:], IN["maskb"][i])
        S.copy("dve", maskb[:, i, :], mstage[:])
    outw = S.sb([128, 8, 1024], BF16, "outw")
    wst2 = [S.sb([128, 1024], F32, f"wst2{i}") for i in range(2)]
    for k in range(8):
        S.dma("sync", wst2[k % 2][:], IN["outw_at"][k * 128:(k + 1) * 128, :])
        S.copy("act", outw[:, k, :], wst2[k % 2][:])
    NKB = NQB + 2
    kT = S.sb([64, 4, NKB * 128], BF16, "kT"); kcT = S.sb([64, 4, 256], BF16, "kcT")
    vw = S.sb([128, NKB, 256], BF16, "vw"); vc = S.sb([128, 2, 256], BF16, "vc")
    for blk in (0, NKB - 1):
        S.memset("dve", kT[:, :, blk * 128:(blk + 1) * 128], 0.0)
        S.memset("dve", vw[:, blk, :], 0.0)
    kst = [S.sb([128, 512], F32, f"kst{i}") for i in range(2)]; kb = [S.sb([128, 256], BF16, f"kb{i}") for i in range(2)]
    for c in range(NCH):
        S.dma("sync", kst[c % 2][:], qkv[c * 128:(c + 1) * 128, 1024:1536])
        S.copy("pool", kb[c % 2][:], kst[c % 2][:, 0:256])
        for kv in range(4):
            S.tr(ptb[0:64, kv * 128:(kv + 1) * 128], kb[c % 2][:, kv * 64:(kv + 1) * 64], idb[:], sig=(kv == 3))
        blk = c - 1
        dstk = kcT[:, :, c * 128:(c + 1) * 128] if c < 2 else kT[:, :, blk * 128:(blk + 1) * 128]
        S.copy("act", dstk, ptb[0:64, 0:512].re("p (a t) -> p a t", t=128))
        dstv = vc[:, c, :] if c < 2 else vw[:, blk, :]
        S.copy("dve", dstv, kst[c % 2][:, 256:512])
    qst = [S.sb([128, 1024], F32, f"qst{i}") for i in range(2)]
    qb = S.sb([128, 1024], BF16, "qb")
    qT = S.sb([64, 16, 128], BF16, "qT")
    Pm = [S.sb([128, 640], BF16, f"Pm{i}") for i in range(2)]
    PT = [S.sb([128, 5, 128], BF16, f"PT{i}") for i in range(2)]
    rs = [S.sb([128, 4], F32, f"rs{i}") for i in range(2)]
    negm = [S.sb([128, 1], F32, f"negm{i}") for i in range(2)]
    rden = S.sb([128, 16], F32, "rden")
    ob = S.sb([128, 1024], BF16, "ob"); oT = S.sb([128, 8, 128], BF16, "oT")
    xt = [S.sb([128, 1024], F32, f"xt{i}") for i in range(2)]
    tmp = S.sb([128, 1024], F32, "tmp")
    for i in range(NQB):
        rows = slice((i + 2) * 128, (i + 3) * 128)
        S.dma("sync", qst[i % 2][:], qkv[rows, 0:1024])
        S.dma("pool", xt[i % 2][:], SC["x2"][rows, :])
        S.act(qb[:], qst[i % 2][:], AF.Copy, scale=0.125)
        for half in range(2):
            for hh in range(8):
                hd = half * 8 + hh
                S.tr(ptb[0:64, hh * 128:(hh + 1) * 128], qb[:, hd * 64:(hd + 1) * 64], idb[:], sig=(hh == 7))
            S.copy("act", qT[:, half * 8:(half + 1) * 8, :], ptb[0:64, :].re("p (a t) -> p a t", t=128))
        mi = 0 if i == 0 else (2 if i == NQB - 1 else 1)
        def scores(hd):
            kv = hd // 4
            pw = pa[pai[0] % 4]; pai[0] += 1
            pcx = pa[pai[0] % 4]; pai[0] += 1
            S.mm(pw[:, 0:384], qT[:, hd, :], kT[:, kv, i * 128:(i + 3) * 128], start=True, stop=False, sig=False)
            S.mm(pw[:, 0:384], idb[:], maskb[:, mi, :], start=False, stop=True)
            S.mm(pcx[:, 0:256], qT[:, hd, :], kcT[:, kv, :])
            return pw, pcx
        def softmax(hd, pw, pcx):
            i2 = hd % 2
            S.red("dve", mt16[:, hd, 0:1], pw[:, 0:384], ALU.max)
            S.red("dve", mt16[:, hd, 1:2], pcx[:, 0:256], ALU.max)
            S.red("dve", negm[i2][:], mt16[:, hd, :], ALU.max)
            S.ts("dve", negm[i2][:], negm[i2][:], -1.0, ALU.mult)
            S.act(Pm[i2][:, 0:384], pw[:, 0:384], AF.Exp, bias=negm[i2][:, 0:1], accum=rs[i2][:, 0:1])
            S.act(Pm[i2][:, 384:640], pcx[:, 0:256], AF.Exp, bias=negm[i2][:, 0:1], accum=rs[i2][:, 1:2])
            S.act(rs[i2][:, 2:3], sinkb[:, hd:hd + 1], AF.Exp, bias=negm[i2][:, 0:1])
            S.red("dve", rs[i2][:, 3:4], rs[i2][:, 0:3], ALU.add)
            S.recip(rden[:, hd:hd + 1], rs[i2][:, 3:4])

        def pv(hd):
            kv = hd // 4
            i2 = hd % 2
            for j in range(5):
                S.tr(ptb[:, j * 128:(j + 1) * 128], Pm[i2][:, j * 128:(j + 1) * 128], idb[:], sig=(j == 4))
            S.copy("dve" if hd % 2 == 0 else "act", PT[i2][:], ptb[:, 0:640].re("p (a t) -> p a t", t=128))
            po = pos[hd // 8]
            for j in range(5):
                vsrc = vw[:, i + j, kv * 64:(kv + 1) * 64] if j < 3 else vc[:, j - 3, kv * 64:(kv + 1) * 64]
                S.mm(po[:, (hd % 8) * 64:(hd % 8 + 1) * 64], PT[i2][:, j, :], vsrc, start=(j == 0), stop=(j == 4), sig=(j == 4))

        nxt_sc = scores(0)
        for hd in range(16):
            pw, pcx = nxt_sc
            if hd + 1 < 16:
                nxt_sc = scores(hd + 1)
            softmax(hd, pw, pcx)
            if hd >= 1:
                pv(hd - 1)
        pv(15)
        for h2 in range(2):
            S.tt("dve", ob[:, h2 * 512:(h2 + 1) * 512].re("p (h k) -> p h k", k=64), pos[h2][:].re("p (h k) -> p h k", k=64),
                 rden[:, h2 * 8:(h2 + 1) * 8].re("p (h o) -> p h o", o=1).bc([128, 8, 64]), ALU.mult)
        for k in range(8):
            S.tr(ptb[:, k * 128:(k + 1) * 128], ob[:, k * 128:(k + 1) * 128], idb[:], sig=(k == 7))
        S.copy("act", oT[:], ptb[:].re("p (a t) -> p a t", t=128))
        for h in range(2):
            py = pa[pai[0] % 4]; pai[0] += 1
            for k in range(8):
                S.mm(py[:], oT[:, k, :], outw[:, k, h * 512:(h + 1) * 512], start=(k == 0), stop=(k == 7))
            S.tt("dve", tmp[:, h * 512:(h + 1) * 512], py[:], gate5[:, h * 512:(h + 1) * 512], ALU.mult)
        S.tt("dve", xt[i % 2][:], xt[i % 2][:], tmp[:], ALU.add)
        S.dma("pool", SC["x3"][i * 128:(i + 1) * 128, :], xt[i % 2][:])


def phase4b(S, PS, IN, SC, OUT):
    C = alloc_common(S, PS)
    C.idf, C.idb = setup_consts(S, IN["ident"][:])
    modT = S.sb([128, 72, 2], F32, "modT")
    S.dma("sync", modT[:].re("p t j -> p (t j)"), SC["modT1"][:])
    ngT = S.sb([128, 3, 8], F32, "ngT"); S.dma("sync", ngT[:], IN["ngT"][1])
    mv = mod_derive(S, modT, ngT)
    C.gateb = {(2, 1): gate_bcast(S, mv["GATE"][:, 2, :, 1], C.idf, C.ones, C.pa[0], 0.5, "gb21")}
    fing = S.sb([128, 1024], F32, "fing"); S.dma("sync", fing[:], IN["fing"][:])
    ot = [S.sb([128, 1024], F32, f"ot{i}") for i in range(2)]
    oi = [0]

    def after(ci, li, xt):
        ss = C.small[C.si % 4]; C.si += 1
        S.act(C.junk[:], xt[:], AF.Square, accum=ss[:, 0:1])
        S.ts("dve", ss[:, 1:2], ss[:, 0:1], 1.0 / D, ALU.mult, 1e-6, ALU.add)
        S.act(ss[:, 3:4], ss[:, 1:2], AF.Sqrt)
        S.recip(ss[:, 2:3], ss[:, 3:4])
        o = ot[oi[0] % 2]; oi[0] += 1
        S.stt("dve", o[:], xt[:], ss[:, 2:3], fing[:], ALU.mult, ALU.mult)
        S.dma("sync", OUT[ci * 128:(ci + 1) * 128, :], o[:])
    rows = lambda t: (lambda ci: t[ci * 128:(ci + 1) * 128, :])
    for grp in ffn(S, C, rows(SC["x3"]), None, IN["w1"][1, 1], IN["w2"][1, 1], mv, 2, GROUPS32, lambda ci: 1, after_chunk=after):
        pass


IN_SPECS = dict(
    x=[4096, D], ctx=[256, D], cT=[128, 8, 2], modw=[2, 36, 128, 8, 256], modbT=[2, 128, 72], ngT=[2, 128, 3, 8],
    w1=[2, 2, NFT, 128, 8, 256], w2=[2, 2, DFF, D], win_ab=[9, 128, 8, 256], ident=[128, 128],
    mub=[2, 128, 1536], mulb=[2, 128, 256], kkb=[128, 512], kab=[128, 512], w2a=[2, 65, 512], a2a=[2, 65, 512], g2=[128, 512], msk=[6, 128, 128],
    are=[2, 128, NST], aim=[2, 128, NST], lst=[2, 128, NST], bre=[2, 128, NST, 32], bim=[2, 128, NST, 32], cre=[2, 128, NST, 32], cim=[2, 128, NST, 32],
    bcs=[5, 128, 512], gluw=[512, 512], outw_ab=[D, D], win_at=[6, 128, 8, 256], rope=[2, NCH, 128, 32],
    maskb=[3, 128, 384], sinkb=[128, 16], outw_at=[D, D], fing=[128, D])

SC_SPECS = dict(x1=[TOK, D], ppad=[4356, 2304], yd=[2, TOK, 512], kdo=[2, TOK, 512], rvo=[TOK, 1024], gto=[TOK, 512], ys=[2, TOK, 512],
                xm=[TOK, D], xl0=[TOK, D], x2=[TOK, D], qkv=[TOK, 1536], x3=[4096, D], modT0=[128, 144], modT1=[128, 144])


def build_fused(upto=99, debug=(), ses=True):
    nc = bass.Bass("TRN2", target_bir_lowering=False)
    S = Sched(nc, same_engine_sync=ses)
    IN = {k: S.dram(k, v, F32, kind="ExternalInput") for k, v in IN_SPECS.items()}
    SC = {k: S.dram("sc_" + k, v, F32, kind=("ExternalOutput" if k in debug else "Internal")) for k, v in SC_SPECS.items()}
    OUT = S.dram("out", [4096, D], F32, kind="ExternalOutput")
    PS = dict(g=[S.ps([128, 512], F32, f"g{i}") for i in range(6)], b=[S.ps([128, 1024], BF16, f"b{i}") for i in range(2)])
    base = S.mark()
    phases = [lambda: phase1(S, PS, IN, SC), lambda: phase2a(S, PS, IN, SC), lambda: phase2b(S, PS, IN, SC), lambda: phase3a(S, PS, IN, SC),
              lambda: phase3b(S, PS, IN, SC), lambda: phase4a(S, PS, IN, SC), lambda: phase4b(S, PS, IN, SC, OUT)]
    for i, ph in enumerate(phases):
        if i > upto:
            break
        S.reset(base)
        ph()
        S.barrier()
    finals = [OUT] + [SC[k] for k in debug]
    S.finish(finals)
    return nc, S

import numpy as np
LC = 256; NLAT = 4096; L = 4352
GRID_W = 64; ROPE_BASE = 10000.0
def core_tok(seq, h):
    return np.concatenate([seq[h * 128:(h + 1) * 128], seq[256 + h * 2048:256 + (h + 1) * 2048]], 0)
def uncore_tok(parts):
    return np.concatenate([parts[0][:128], parts[1][:128], parts[0][128:], parts[1][128:]], 0)
def colT(v, k=8):
    return np.ascontiguousarray(v.reshape(k, 128).T)
def bc(v):
    return np.ascontiguousarray(np.broadcast_to(v[None, :], (128, v.shape[0])))
def rope_tables(h):
    t = np.arange(h * 2048, (h + 1) * 2048)
    row = (t // GRID_W).astype(np.float32); col = (t % GRID_W).astype(np.float32)
    inv = (ROPE_BASE ** (-np.arange(0, 32, 2, dtype=np.float32) / 32)).astype(np.float32)
    ang = np.concatenate([row[:, None] * inv, col[:, None] * inv], -1).astype(np.float32)
    cos = np.concatenate([np.ones((128, 32), np.float32), np.cos(ang)], 0).reshape(17, 128, 32)
    sin = np.concatenate([np.zeros((128, 32), np.float32), np.sin(ang)], 0).reshape(17, 128, 32)
    return np.stack([cos, sin], 0).astype(np.float32)

import numpy as np
LC = 256

def f_masks():
    m = np.zeros((6, 128, 128), np.float32)
    s = np.arange(128)[:, None]; t = np.arange(128)[None, :]
    same = (s // 64) == (t // 64)
    m[0] = same & (s < t); m[1] = same & (s <= t); m[2] = same & (s > t); m[4] = same & (s >= t)
    m[3] = np.eye(128)
    m[5] = np.eye(128)[::-1]
    return m

def f_rope():
    GRID_W = 64
    t = np.arange(4096)
    row = (t // GRID_W).astype(np.float32); col = (t % GRID_W).astype(np.float32)
    inv = (10000.0 ** (-np.arange(0, 32, 2, dtype=np.float32) / 32)).astype(np.float32)
    ang = np.concatenate([row[:, None] * inv, col[:, None] * inv], -1).astype(np.float32)
    cos = np.concatenate([np.ones((256, 32), np.float32), np.cos(ang)], 0).reshape(34, 128, 32)
    sin = np.concatenate([np.zeros((256, 32), np.float32), np.sin(ang)], 0).reshape(34, 128, 32)
    return np.ascontiguousarray(np.stack([cos, sin], 0).astype(np.float32))

def f_attn_masks():
    qi = np.arange(128)[:, None]; mj = np.arange(384)[None, :] - 128
    valid = np.abs(mj - qi) <= 128
    NEG = -30000.0
    gen = np.where(valid, 0.0, NEG).astype(np.float32)
    left_inv = gen.copy(); left_inv[:, :128] = NEG
    right_inv = gen.copy(); right_inv[:, 256:] = NEG
    return np.ascontiguousarray(np.stack([left_inv, gen, right_inv], 0))

def wblk(w):
    n = w.shape[1] // 256
    return np.ascontiguousarray(w.reshape(8, 128, n, 256).transpose(2, 1, 0, 3))

def w1blk(w):
    a = w.reshape(8, 128, 2, 22, 128).transpose(3, 1, 0, 2, 4)
    return np.ascontiguousarray(a.reshape(22, 128, 8, 256))

def f_shared(d):
    e = 0
    st = lambda a: np.ascontiguousarray(a.reshape(16, 128).T)
    def pad(a):
        out = np.zeros((128, 16, 32), np.float32)
        for g in range(32):
            out[(g % 2) * 64:(g % 2) * 64 + 64, g // 2, (g % 2) * 16:(g % 2) * 16 + 16] = a[g]
        return out
    mu = d['rwkv_mu'][e]
    sh = dict(
        modw=np.stack([wblk(d['mod_w'][l]) for l in range(2)], 0), modbT=np.ascontiguousarray(np.stack([colT(d['mod_b'][l], 72) for l in range(2)], 0)),
        ngT=np.ascontiguousarray(np.stack([np.stack([colT(d['norm_g'][l, i]) for i in range(3)], 1) for l in range(2)], 0)),
        w1=np.stack([np.stack([w1blk(d['ffn_w1'][l, j]) for j in range(2)], 0) for l in range(2)], 0), w2=d['ffn_w2'], win_ab=wblk(d['ab_in_w'][0]), ident=np.eye(128, dtype=np.float32),
        mub=np.ascontiguousarray(np.stack([bc(mu[0, :1536]), bc(mu[1, :1536])], 0)),
        mulb=np.ascontiguousarray(np.stack([bc(mu[0, 1536:1792]), bc(mu[1, 1536:1792])], 0)),
        kkb=bc(d['rwkv_k_k'][e]), kab=bc(d['rwkv_k_a'][e]),
        w2a=np.ascontiguousarray(np.stack([np.concatenate([d['rwkv_w2'][e, dr], d['rwkv_w0'][e, dr][None]], 0) for dr in range(2)], 0)),
        a2a=np.ascontiguousarray(np.stack([np.concatenate([d['rwkv_a2'][e, dr], d['rwkv_a0'][e, dr][None]], 0) for dr in range(2)], 0)),
        g2=d['rwkv_g2'][e], msk=f_masks(),
        are=np.stack([st(d['s5_a_re'][0, dr]) for dr in range(2)], 0), aim=np.stack([st(d['s5_a_im'][0, dr]) for dr in range(2)], 0),
        lst=np.stack([st(np.repeat(d['s5_log_step'][0, dr][:, None], 64, 1)) for dr in range(2)], 0),
        bre=np.stack([pad(d['s5_b_re'][0, dr]) for dr in range(2)], 0), bim=np.stack([pad(d['s5_b_im'][0, dr]) for dr in range(2)], 0),
        cre=np.stack([pad(d['s5_c_re'][0, dr].transpose(0, 2, 1)) for dr in range(2)], 0),
        cim=np.stack([pad(d['s5_c_im'][0, dr].transpose(0, 2, 1)) for dr in range(2)], 0),
        bcs=np.ascontiguousarray(np.stack([bc(d['rwkv_lnx_g'][0]), bc(d['rwkv_lnx_b'][0]), bc(d['rwkv_r_k'][0].reshape(-1)), bc(d['s5_d'][0]),
                                           bc(d['s5_glu_b'][0])], 0)),
        gluw=d['s5_glu_w'][0], outw_ab=d['ab_out_w'][0], win_at=wblk(d['attn_in_w'][0]), rope=f_rope(),
        maskb=f_attn_masks(), sinkb=bc(d['attn_sink'][0]), outw_at=d['attn_out_w'][0], fing=bc(d['final_g']))
    return {k: np.ascontiguousarray(v, dtype=np.float32) for k, v in sh.items()}

def f_core(d, b):
    return dict(x=np.ascontiguousarray(d['x'][b]), ctx=np.ascontiguousarray(d['ctx'][b]),
                cT=np.ascontiguousarray(np.stack([colT(d['c_ctx']), colT(d['c'][b])], -1)))


def kernel(**inputs):
    d = {k: np.ascontiguousarray(np.asarray(v, dtype=np.float32)) for k, v in inputs.items()}
    nc, _ = build_fused()
    sh = f_shared(d)
    in_maps = [dict(sh, **f_core(d, c % 4)) for c in range(8)]
    res = run_bass_kernel_spmd(nc, in_maps, core_ids=list(range(8)))
    out = np.stack([res.results[b]['out'] for b in range(4)], 0)
    return out.astype(np.float32)
```
